# Optimizing a Trainium2 kernel written in Bass

```python
import math, functools
import jax, jax.numpy as jnp
from jax import lax
import numpy as np

D_MODEL = 2048
BATCH = 16
SEQ = 2048
DEPTH = 4

GRID_W = 64
CTX_LEN = 256
MIX_HALF = D_MODEL // 2
A_HEADS = 8
A_HEAD_K = 128
A_HEAD_V = MIX_HALF // A_HEADS
A_KEY = A_HEADS * A_HEAD_K
A_VAL = A_HEADS * A_HEAD_V
A_CHUNK = 32
B_HEADS = 8
B_HEAD_K = 128
B_HEAD_V = MIX_HALF // B_HEADS
B_KEY = B_HEADS * B_HEAD_K
B_VAL = B_HEADS * B_HEAD_V
B_CONV_W = 5
B_CHUNK = 64
C_HEADS = 16
C_HEAD_DIM = D_MODEL // C_HEADS
WIN_R = 8
WIN_C = 16
QB_C = 16
ROPE_BASE = 10000.0
D_FF = ((8 * D_MODEL // 3 + 127) // 128) * 128
FFN_CONV_W = 3
ALPHA = (2 * DEPTH) ** 0.25
BETA = (8 * DEPTH) ** -0.25
N_EVEN = (DEPTH + 1) // 2
N_ODD = DEPTH // 2
EPS = 1e-6
AB_SIZES = (A_KEY, A_KEY, A_KEY, A_VAL, A_VAL, B_KEY, B_KEY, B_VAL, B_VAL, 4 * B_HEADS)
AB_SPLITS = tuple(int(v) for v in np.cumsum(AB_SIZES)[:-1])
N_IN_AB = int(sum(AB_SIZES))

kernel_name = 'hybrid_hgrn2_gdn_natten_convglu_deepnorm'

F32 = jnp.float32


def _layernorm(x, g, b):
    xf = x.astype(F32)
    xc = xf - jnp.mean(xf, -1, keepdims=True)
    var = jnp.mean(xc * xc, -1, keepdims=True)
    return (xc * lax.rsqrt(var + EPS) * g.astype(F32) + b.astype(F32)).astype(x.dtype)


def _rmsnorm(x, w):
    xf = x.astype(F32)
    return xf * lax.rsqrt(jnp.mean(xf * xf, -1, keepdims=True) + EPS) * w.astype(F32)


def _l2norm(x):
    xf = x.astype(F32)
    return xf * lax.rsqrt(jnp.sum(xf * xf, -1, keepdims=True) + EPS)


def _heads(t, h):
    return t.reshape(t.shape[:-1] + (h, t.shape[-1] // h))


def _dwconv(x, w):
    pad = w.shape[0] // 2
    return lax.conv_general_dilated(x, w[:, None, :].astype(x.dtype), (1,), [(pad, pad)],
                                    dimension_numbers=('NWC', 'WIO', 'NWC'),
                                    feature_group_count=x.shape[-1])


def _to_chunks(t, chunk):
    b, l, h = t.shape[:3]
    t = t.astype(F32).reshape((b, l // chunk, chunk, h) + t.shape[3:])
    return jnp.swapaxes(jnp.moveaxis(t, 1, 0), 2, 3)


def _from_chunks(t):
    n, b, h, c = t.shape[:4]
    t = jnp.moveaxis(jnp.swapaxes(t, 2, 3), 0, 1)
    return t.reshape((b, n * c, h) + t.shape[4:])


def _gla_chunked(q, k, v, log_f, s0, chunk):
    qc, kc, vc, gc = (_to_chunks(t, chunk) for t in (q, k, v, log_f))
    cum = jnp.cumsum(gc, axis=3)
    ref = cum[:, :, :, chunk // 2 - 1:chunk // 2]
    causal = jnp.tril(jnp.ones((chunk, chunk), bool))

    def step(s, inp):
        q_, k_, v_, b_, m_ = inp
        a = jnp.einsum('bhtd,bhsd->bhts', q_ * jnp.exp(b_ - m_), k_ * jnp.exp(m_ - b_))
        a = jnp.where(causal, a, 0.0)
        o = (jnp.einsum('bhtd,bhdv->bhtv', q_ * jnp.exp(b_), s)
             + jnp.einsum('bhts,bhsv->bhtv', a, v_))
        b_last = b_[:, :, -1:, :]
        s = (s * jnp.swapaxes(jnp.exp(b_last), 2, 3)
             + jnp.einsum('bhsd,bhsv->bhdv', k_ * jnp.exp(b_last - b_), v_))
        return s, o

    s_fin, o = lax.scan(step, s0, (qc, kc, vc, cum, ref))
    return _from_chunks(o).astype(v.dtype), s_fin


def _gdn_chunked(q, k, v, g, beta, s0, chunk):
    qc, kc, vc, gc, bc = (_to_chunks(t, chunk) for t in (q, k, v, g, beta))
    gc = jnp.cumsum(gc, axis=-1)
    dv = v.shape[-1]
    incl = jnp.tril(jnp.ones((chunk, chunk), bool))
    strict = jnp.tril(jnp.ones((chunk, chunk), bool), -1)
    eye = jnp.eye(chunk, dtype=F32)

    def step(s, inp):
        q_, k_, v_, g_, b_ = inp
        decay = jnp.exp(jnp.where(incl, g_[..., :, None] - g_[..., None, :], -jnp.inf))
        kb = k_ * b_[..., None]
        lmat = jnp.where(strict, jnp.einsum('bhtd,bhsd->bhts', kb, k_) * decay, 0.0)
        rhs = jnp.concatenate([v_ * b_[..., None], kb * jnp.exp(g_)[..., None]], -1)
        sol = lax.linalg.triangular_solve(eye + lmat, rhs, left_side=True, lower=True,
                                          unit_diagonal=True)
        u, w = sol[..., :dv], sol[..., dv:]
        v_new = u - jnp.einsum('bhtd,bhdv->bhtv', w, s)
        attn = jnp.einsum('bhtd,bhsd->bhts', q_, k_) * decay
        o = (jnp.einsum('bhtd,bhdv->bhtv', q_ * jnp.exp(g_)[..., None], s)
             + jnp.einsum('bhts,bhsv->bhtv', attn, v_new))
        g_last = g_[..., -1:]
        s = (s * jnp.exp(g_last)[..., None]
             + jnp.einsum('bhsd,bhsv->bhdv', k_ * jnp.exp(g_last - g_)[..., None], v_new))
        return s, o

    s_fin, o = lax.scan(step, s0, (qc, kc, vc, gc, bc))
    return _from_chunks(o).astype(v.dtype), s_fin


def _ctx_then_lat(scan_fn, ctx_in, lat_in, reverse):
    if reverse:
        ctx_in = tuple(jnp.flip(t, 1) for t in ctx_in)
        lat_in = tuple(jnp.flip(t, 1) for t in lat_in)
    b, _, h, dk = ctx_in[1].shape
    s0 = jnp.zeros((b, h, dk, ctx_in[2].shape[-1]), F32)
    o_ctx, s_ctx = scan_fn(*ctx_in, s0)
    o_lat, _ = scan_fn(*lat_in, s_ctx)
    if reverse:
        o_ctx, o_lat = jnp.flip(o_ctx, 1), jnp.flip(o_lat, 1)
    return o_ctx, o_lat


def _mix_ab(h_ctx, h_lat, w_in, lb, a_norm, conv_w, a_log, dt_bias, b_norm, w_out, need_ctx):
    gla = functools.partial(_gla_chunked, chunk=A_CHUNK)
    gdn = functools.partial(_gdn_chunked, chunk=B_CHUNK)

    def prepare(h):
        (qa, fa_f, fa_b, ia, ga, qb, kb, vb, zb, gb) = jnp.split(h @ w_in, AB_SPLITS, axis=-1)
        qa = _heads(jax.nn.silu(qa), A_HEADS)
        ia = _heads(ia, A_HEADS)
        a_dirs = []
        for d, zf in enumerate((fa_f, fa_b)):
            f = lb[d] + (1.0 - lb[d]) * jax.nn.sigmoid(zf.astype(F32))
            a_dirs.append((qa, _heads(1.0 - f, A_HEADS), ia, _heads(jnp.log(f), A_HEADS)))
        qkv = jax.nn.silu(_dwconv(jnp.concatenate([qb, kb, vb], -1), conv_w))
        qb, kb, vb = jnp.split(qkv, [B_KEY, 2 * B_KEY], axis=-1)
        qb = _l2norm(_heads(qb, B_HEADS)) * B_HEAD_K ** -0.5
        kb = _l2norm(_heads(kb, B_HEADS))
        vb = _heads(vb, B_HEADS)
        a_f, a_b, beta_f, beta_b = jnp.split(gb.astype(F32), 4, axis=-1)
        b_dirs = []
        for d, (a, beta) in enumerate(((a_f, beta_f), (a_b, beta_b))):
            g = -jnp.exp(a_log[d].astype(F32)) * jax.nn.softplus(a + dt_bias[d].astype(F32))
            b_dirs.append((qb, kb, vb, g, jax.nn.sigmoid(beta)))
        return a_dirs, b_dirs, ga, zb

    a_ctx, b_ctx, ga_ctx, zb_ctx = prepare(h_ctx)
    a_lat, b_lat, ga_lat, zb_lat = prepare(h_lat)
    oa = [_ctx_then_lat(gla, a_ctx[d], a_lat[d], d == 1) for d in range(2)]
    ob = [_ctx_then_lat(gdn, b_ctx[d], b_lat[d], d == 1) for d in range(2)]

    def merge(o_a, o_b, ga, zb):
        ya = _rmsnorm(o_a, a_norm) * jax.nn.silu(_heads(ga, A_HEADS).astype(F32))
        yb = _rmsnorm(o_b, b_norm) * jax.nn.silu(_heads(zb, B_HEADS).astype(F32))
        y = jnp.concatenate([ya.reshape(ya.shape[:2] + (A_VAL,)),
                             yb.reshape(yb.shape[:2] + (B_VAL,))], -1)
        return y.astype(w_out.dtype) @ w_out

    y_lat = merge(oa[0][1] + oa[1][1], ob[0][1] + ob[1][1], ga_lat, zb_lat)
    y_ctx = None
    if need_ctx:
        y_ctx = merge(oa[0][0] + oa[1][0], ob[0][0] + ob[1][0], ga_ctx, zb_ctx)
    return y_ctx, y_lat


def _axial_rope(x, row, col):
    half = x.shape[-1] // 2
    quarter = half // 2
    inv = ROPE_BASE ** (-jnp.arange(quarter, dtype=F32) / quarter)

    def rot(t, pos):
        ang = pos[:, None] * inv[None, :]
        cos, sin = jnp.cos(ang)[None, :, None, :], jnp.sin(ang)[None, :, None, :]
        t1, t2 = t[..., :quarter].astype(F32), t[..., quarter:].astype(F32)
        return jnp.concatenate([t1 * cos - t2 * sin, t1 * sin + t2 * cos], -1)

    return jnp.concatenate([rot(x[..., :half], row), rot(x[..., half:], col)], -1).astype(x.dtype)


def _softmax_attn(q, k, v):
    s = jnp.einsum('bqhd,bkhd->bhqk', q, k).astype(F32) * q.shape[-1] ** -0.5
    p = jax.nn.softmax(s, axis=-1).astype(v.dtype)
    return jnp.einsum('bhqk,bkhd->bqhd', p, v)


def _neigh_attn(q, k, v, k_ctx, v_ctx, rel_bias):
    b, l, h, dh = q.shape
    rows = l // GRID_W
    wr = min(WIN_R, rows)
    nj = GRID_W // QB_C
    band = QB_C + WIN_C
    n_lat = wr * band
    q = q * dh ** -0.5
    kg = k.reshape(b, rows, GRID_W, h, dh)
    vg = v.reshape(b, rows, GRID_W, h, dh)
    q_rows = jnp.moveaxis(q.reshape(b, rows, nj, QB_C, h, dh), 1, 0)
    r_idx = jnp.arange(rows)
    r0 = jnp.clip(r_idx - wr // 2, 0, rows - wr)
    dr_idx = r0[:, None] + jnp.arange(wr)[None, :] - r_idx[:, None] + (WIN_R - 1)
    j_idx = jnp.arange(nj)
    key_cols = jnp.clip(j_idx * QB_C - WIN_C // 2, 0, GRID_W - band)[:, None] + jnp.arange(band)
    q_cols = j_idx[:, None] * QB_C + jnp.arange(QB_C)
    c0 = jnp.clip(q_cols - WIN_C // 2, 0, GRID_W - WIN_C)
    kc = key_cols[:, None, :]
    col_ok = (kc >= c0[..., None]) & (kc < c0[..., None] + WIN_C)
    dc_idx = jnp.clip(kc - q_cols[..., None] + (WIN_C - 1), 0, 2 * WIN_C - 2)
    mask = jnp.broadcast_to(col_ok[:, :, None, :], (nj, QB_C, wr, band)).reshape(nj, QB_C, n_lat)

    def row_block(inp):
        q_r, r0_r, dr_r = inp

        def band_of(t):
            t = lax.dynamic_slice_in_dim(t, r0_r, wr, axis=1)
            t = jnp.take(t, key_cols, axis=2)
            return jnp.moveaxis(t, 2, 1).reshape(b, nj, n_lat, h, dh)

        k_r, v_r = band_of(kg), band_of(vg)
        bias = rel_bias[:, dr_r[None, None, :, None], dc_idx[:, :, None, :]].reshape(h, nj, QB_C, n_lat)
        s_lat = jnp.einsum('bjqhd,bjkhd->bhjqk', q_r, k_r).astype(F32) + bias.astype(F32)
        s_lat = jnp.where(mask, s_lat, -jnp.inf)
        s_ctx = jnp.einsum('bjqhd,bkhd->bhjqk', q_r, k_ctx).astype(F32)
        p = jax.nn.softmax(jnp.concatenate([s_lat, s_ctx], -1), axis=-1).astype(v.dtype)
        return (jnp.einsum('bhjqk,bjkhd->bjqhd', p[..., :n_lat], v_r)
                + jnp.einsum('bhjqk,bkhd->bjqhd', p[..., n_lat:], v_ctx))

    o = lax.map(row_block, (q_rows, r0, dr_idx))
    return jnp.moveaxis(o, 0, 1).reshape(b, l, h, dh)


def _mix_c(h_ctx, h_lat, w_in, rel_bias, w_out, need_ctx):
    b, l, _ = h_lat.shape
    q, k, v = (_heads(t, C_HEADS) for t in jnp.split(h_lat @ w_in, 3, axis=-1))
    pos = jnp.arange(l)
    row = (pos // GRID_W).astype(F32)
    col = (pos % GRID_W).astype(F32)
    q = _axial_rope(q, row, col)
    k = _axial_rope(k, row, col)
    if need_ctx:
        qc, kc, vc = (_heads(t, C_HEADS) for t in jnp.split(h_ctx @ w_in, 3, axis=-1))
    else:
        kc, vc = (_heads(t, C_HEADS) for t in jnp.split(h_ctx @ w_in[:, D_MODEL:], 2, axis=-1))
    y_lat = _neigh_attn(q, k, v, kc, vc, rel_bias).reshape(b, l, D_MODEL) @ w_out
    y_ctx = None
    if need_ctx:
        y_ctx = _softmax_attn(qc, kc, vc).reshape(b, h_ctx.shape[1], D_MODEL) @ w_out
    return y_ctx, y_lat


def _conv_ffn(h, w_up, w_dw, b_dw, w_down):
    a, gate = jnp.split(h @ w_up, 2, axis=-1)
    a = _dwconv(a, w_dw) + b_dw
    return (jax.nn.gelu(a, approximate=False) * gate) @ w_down


def setup_inputs(seed: int = 0) -> dict:
    key = jax.random.key(seed)
    ks = jax.random.split(key, 24)

    def nrm(k, shape, s):
        return jax.random.normal(k, shape, F32) * s

    dt = jnp.exp(jax.random.uniform(ks[13], (N_EVEN, 2, B_HEADS), F32, math.log(1e-3), math.log(1e-1)))
    return {
        'x': nrm(ks[0], (BATCH, SEQ, D_MODEL), 1.0),
        'c': nrm(ks[1], (BATCH, D_MODEL), 1.0),
        'ctx': nrm(ks[2], (BATCH, CTX_LEN, D_MODEL), 1.0),
        'c_ctx': nrm(ks[3], (D_MODEL,), 1.0),
        'w_ada': nrm(ks[4], (DEPTH, D_MODEL, 6 * D_MODEL), 0.5 * D_MODEL ** -0.5),
        'b_ada': nrm(ks[5], (DEPTH, 6 * D_MODEL), 0.01),
        'ln_g': 1.0 + nrm(ks[6], (DEPTH, 2, D_MODEL), 0.01),
        'ln_b': nrm(ks[7], (DEPTH, 2, D_MODEL), 0.01),
        'w_in_ab': nrm(ks[8], (N_EVEN, D_MODEL, N_IN_AB), D_MODEL ** -0.5),
        'hgrn_lb': nrm(ks[9], (N_EVEN, 2, A_KEY), 0.1),
        'hgrn_norm': 1.0 + nrm(ks[10], (N_EVEN, A_HEAD_V), 0.01),
        'gdn_conv': nrm(ks[11], (N_EVEN, B_CONV_W, 2 * B_KEY + B_VAL), B_CONV_W ** -0.5),
        'gdn_a_log': jnp.log(jax.random.uniform(ks[12], (N_EVEN, 2, B_HEADS), F32, 1.0, 16.0)),
        'gdn_dt_bias': dt + jnp.log(-jnp.expm1(-dt)),
        'gdn_norm': 1.0 + nrm(ks[14], (N_EVEN, B_HEAD_V), 0.01),
        'w_out_ab': nrm(ks[15], (N_EVEN, A_VAL + B_VAL, D_MODEL), BETA * (A_VAL + B_VAL) ** -0.5),
        'w_in_c': nrm(ks[16], (N_ODD, D_MODEL, 3 * D_MODEL), D_MODEL ** -0.5),
        'na_rel_bias': nrm(ks[17], (N_ODD, C_HEADS, 2 * WIN_R - 1, 2 * WIN_C - 1), 0.1),
        'w_out_c': nrm(ks[18], (N_ODD, D_MODEL, D_MODEL), BETA * D_MODEL ** -0.5),
        'ffn_w_up': nrm(ks[19], (DEPTH, D_MODEL, 2 * D_FF), D_MODEL ** -0.5),
        'ffn_w_dw': nrm(ks[20], (DEPTH, FFN_CONV_W, D_FF), FFN_CONV_W ** -0.5),
        'ffn_b_dw': nrm(ks[21], (DEPTH, D_FF), 0.01),
        'ffn_w_down': nrm(ks[22], (DEPTH, D_FF, D_MODEL), BETA * D_FF ** -0.5),
    }


def reference(x, c, ctx, c_ctx, w_ada, b_ada, ln_g, ln_b, w_in_ab, hgrn_lb, hgrn_norm, gdn_conv,
              gdn_a_log, gdn_dt_bias, gdn_norm, w_out_ab, w_in_c, na_rel_bias, w_out_c,
              ffn_w_up, ffn_w_dw, ffn_b_dw, ffn_w_down):
    lb_all = jax.nn.softmax(hgrn_lb.astype(F32), axis=0)
    lb_all = jnp.cumsum(lb_all, axis=0) - lb_all[0]
    s_lat = jax.nn.silu(c)
    s_ctx = jax.nn.silu(c_ctx)
    xl, xc = x, ctx
    for layer in range(DEPTH):
        need_ctx = layer < DEPTH - 1
        ml = jnp.split((s_lat @ w_ada[layer] + b_ada[layer])[:, None, :], 6, axis=-1)
        mc = jnp.split(s_ctx @ w_ada[layer] + b_ada[layer], 6, axis=-1)
        h_lat = xl * (1.0 + ml[1]) + ml[0]
        h_ctx = xc * (1.0 + mc[1]) + mc[0]
        if layer % 2 == 0:
            e = layer // 2
            y_ctx, y_lat = _mix_ab(h_ctx, h_lat, w_in_ab[e], lb_all[e], hgrn_norm[e], gdn_conv[e],
                                   gdn_a_log[e], gdn_dt_bias[e], gdn_norm[e], w_out_ab[e], need_ctx)
        else:
            o = layer // 2
            y_ctx, y_lat = _mix_c(h_ctx, h_lat, w_in_c[o], na_rel_bias[o], w_out_c[o], need_ctx)
        xl = _layernorm(ALPHA * xl + ml[2] * y_lat, ln_g[layer, 0], ln_b[layer, 0])
        f_lat = _conv_ffn(xl * (1.0 + ml[4]) + ml[3], ffn_w_up[layer], ffn_w_dw[layer],
                          ffn_b_dw[layer], ffn_w_down[layer])
        xl = _layernorm(ALPHA * xl + ml[5] * f_lat, ln_g[layer, 1], ln_b[layer, 1])
        if need_ctx:
            xc = _layernorm(ALPHA * xc + mc[2] * y_ctx, ln_g[layer, 0], ln_b[layer, 0])
            f_ctx = _conv_ffn(xc * (1.0 + mc[4]) + mc[3], ffn_w_up[layer], ffn_w_dw[layer],
                              ffn_b_dw[layer], ffn_w_down[layer])
            xc = _layernorm(ALPHA * xc + mc[5] * f_ctx, ln_g[layer, 1], ln_b[layer, 1])
    return xl
```

```python
import numpy as np
from contextlib import ExitStack
import concourse.bass as bass
import concourse.mybir as mybir
from concourse.alu_op_type import AluOpType as ALU
from concourse.bass_utils import run_bass_kernel_spmd

AF = mybir.ActivationFunctionType
F32 = mybir.dt.float32
BF16 = mybir.dt.bfloat16
U32 = mybir.dt.uint32

ENGS = ('pe', 'act', 'dve', 'pool', 'sp')
RING = 12


class Tl:
    __slots__ = ('name', 'w', 'r')

    def __init__(self, name=''):
        self.name = name
        self.w = None
        self.r = []


class Op:
    __slots__ = ('eng', 'fn', 'dma', 'pos', 'cpos', 'waits', 'sig', 'sem', 'val')

    def __init__(self, eng, fn, dma):
        self.eng = eng
        self.fn = fn
        self.dma = dma
        self.waits = []
        self.sig = False
        self.sem = None
        self.val = None


class Prog:
    def __init__(self, nc):
        self.nc = nc
        self.streams = {e: [] for e in ENGS}
        self.ccount = {e: 0 for e in ENGS}
        self.ndma = {e: 0 for e in ENGS}
        self.dmaops = {e: [] for e in ENGS}
        self.wpos = {}
        self.wdma = {}
        self.lastc = {e: None for e in ENGS}

    def _need(self, op, d, kind):
        if d is op:
            return
        if d.dma:
            key = (op.eng, d.eng, d.sem)
            if self.wdma.get(key, 0) >= d.val:
                return
            self.wdma[key] = d.val
            op.waits.append(d)
            return
        if d.eng == op.eng and not op.dma:
            if op.eng == 'pe':
                return
            if kind != 'raw':
                return
            if self.ccount[op.eng] - d.cpos > 2:
                return
        key = (op.eng, d.eng)
        if self.wpos.get(key, -1) >= d.cpos:
            return
        self.wpos[key] = d.cpos
        d.sig = True
        op.waits.append(d)

    def add(self, eng, fn, R=(), W=(), dma=False):
        op = Op(eng, fn, dma)
        op.cpos = self.ccount[eng]
        if dma:
            j = self.ndma[eng]
            self.ndma[eng] += 1
            op.sem = j % RING
            op.val = 16 * (j // RING + 1)
            if j >= RING:
                self._need(op, self.dmaops[eng][j - RING], 'ring')
            self.dmaops[eng].append(op)
        for t in R:
            if t.w is not None:
                self._need(op, t.w, 'raw')
        for t in W:
            if t.w is not None:
                self._need(op, t.w, 'waw')
            for r in t.r:
                self._need(op, r, 'war')
        for t in R:
            t.r.append(op)
        for t in W:
            t.w = op
            t.r = []
        if not dma and fn is not None:
            self.ccount[eng] += 1
            self.lastc[eng] = op
        self.streams[eng].append(op)
        return op

    def barrier(self):
        lasts = [self.lastc[e] for e in ENGS if self.lastc[e] is not None]
        dmas = []
        for e in ENGS:
            dmas += self.dmaops[e][-RING:]
        for e in ENGS:
            op = Op(e, None, False)
            op.cpos = self.ccount[e]
            for d in lasts + dmas:
                if d.eng == e and not d.dma:
                    continue
                self._need(op, d, 'raw')
            self.streams[e].append(op)

    def emit(self, block, csem, dsem):
        nc = self.nc
        for e in ENGS:
            c = 0
            for op in self.streams[e]:
                if not op.dma and op.sig:
                    c += 1
                    op.val = c

        def semof(d):
            return dsem[d.eng][d.sem] if d.dma else csem[d.eng]

        def replay(e):
            def f(eng):
                for op in self.streams[e]:
                    ws = {}
                    for d in op.waits:
                        s = semof(d)
                        k = id(s)
                        if k not in ws or ws[k][1] < d.val:
                            ws[k] = (s, d.val)
                    for s, v in ws.values():
                        eng.wait_ge(s, v)
                    if op.fn is None:
                        continue
                    inst = op.fn(eng)
                    if op.dma:
                        inst.then_inc(dsem[e][op.sem], 16)
                    elif op.sig:
                        inst.then_inc(csem[e], 1)
            return f
        block.tensor(replay('pe'))
        block.scalar(replay('act'))
        block.vector(replay('dve'))
        block.gpsimd(replay('pool'))
        block.sync(replay('sp'))


D = 2048
KC = 16
DEPTH = 4
NBC = 2
LCTX, LLAT = 256, 2048
LSEQ = LCTX + LLAT
DFF = 5504
FC = 43
ALPHA = (2 * DEPTH) ** 0.25
EPS = 1e-6
NAB = 9248


class Ctx:
    def __init__(self, nc):
        self.nc = nc
        self.P = Prog(nc)
        self.dram = {}
        self.stack = None
        self.n = 0

    def din(self, name, shape, dt=F32):
        self.dram[name] = self.nc.dram_tensor(name, list(shape), dt, kind="ExternalInput").ap()
        return self.dram[name]

    def dout(self, name, shape, dt=F32):
        self.dram[name] = self.nc.dram_tensor(name, list(shape), dt, kind="ExternalOutput").ap()
        return self.dram[name]

    def dscr(self, name, shape, dt=F32):
        self.dram[name] = self.nc.dram_tensor(name, list(shape), dt, kind="Internal").ap()
        return self.dram[name]

    def sb(self, shape, dt=F32, name=None):
        self.n += 1
        return self.stack.enter_context(self.nc.sbuf_tensor(name or f"s{self.n}", list(shape), dt))

    def ps(self, shape, dt=F32, name=None):
        self.n += 1
        return self.stack.enter_context(self.nc.psum_tensor(name or f"p{self.n}", list(shape), dt))


def phase_ada(C, layers):
    nc, P = C.nc, C.P
    mod = C.mod
    cT = C.sb([128, KC, 3]); sT = C.sb([128, KC, 3], BF16)
    bt = C.sb([128, DEPTH, 96])
    t_c, t_s, t_b, t_mod = Tl(), Tl(), Tl(), C.t_mod
    P.add('sp', lambda e: e.dma_start(out=cT[:], in_=C.dram['cT'][:]), W=[t_c], dma=True)
    P.add('sp', lambda e: e.dma_start(out=bt[:], in_=C.dram['b_ada_t'][:]), W=[t_b], dma=True)
    P.add('act', lambda e: e.activation(out=sT[:], in_=cT[:], func=AF.Silu), R=[t_c], W=[t_s])
    NW = 3
    wts = [C.sb([128, KC, 128], BF16) for _ in range(NW)]
    t_w = [Tl() for _ in range(NW)]
    pss, t_p = C.bank[:4], C.t_bank[:4]
    i = 0
    for l in layers:
        for oc in range(96):
            w, tw, ps, tp = wts[i % NW], t_w[i % NW], pss[i % 4], t_p[i % 4]
            P.add('pool', lambda e, w=w, l=l, oc=oc: e.dma_start(out=w[:], in_=C.dram['w_ada_t'][l, oc], max_dma_last_dim=4096),
                  W=[tw], dma=True)
            for kc in range(KC):
                P.add('pe', lambda e, w=w, ps=ps, kc=kc: e.matmul(ps[:, 0:3], w[:, kc, :], sT[:, kc, :],
                                                                   start=(kc == 0), stop=(kc == KC - 1)),
                      R=[tw, t_s], W=[tp])
            add1 = 1.0 if (oc // 16) in (1, 4) else 0.0
            P.add('dve', lambda e, ps=ps, l=l, oc=oc, add1=add1: e.tensor_scalar(
                out=mod[:, l, oc, :], in0=ps[:, 0:3], scalar1=bt[:, l, oc:oc + 1], scalar2=add1,
                op0=ALU.add, op1=ALU.add), R=[tp, t_b], W=[t_mod])
            i += 1


def seq_tiles(L):
    return [(o, min(512, L - o)) for o in range(0, L, 512)]


SEQS = ((0, LCTX), (LCTX, LLAT))


def phase_mod0(C, l, xsrc, HT, sh=0, sc=1):
    P, mod = C.P, C.mod
    NB = 3
    xin = [C.sb([128, 512]) for _ in range(NB)]; hb = [C.sb([128, 512], BF16) for _ in range(NB)]
    t_x = [Tl() for _ in range(NB)]; t_h = [Tl() for _ in range(NB)]
    i = 0
    for b in range(C.nb):
        for kc in range(KC):
            for (t0, n) in seq_tiles(LSEQ):
                x, h, tx, th = xin[i % NB], hb[i % NB], t_x[i % NB], t_h[i % NB]
                j = 2 if t0 < LCTX else b
                P.add('sp', lambda e, x=x, b=b, kc=kc, t0=t0, n=n: e.dma_start(
                    out=x[:, :n], in_=xsrc[b, kc * 128:(kc + 1) * 128, t0:t0 + n]), W=[tx], dma=True)
                if t0 < LCTX < t0 + n:
                    for (a, z, jj) in ((0, LCTX - t0, 2), (LCTX - t0, n, b)):
                        P.add('act', lambda e, x=x, h=h, a=a, z=z, jj=jj, kc=kc: e.activation(
                            out=h[:, a:z], in_=x[:, a:z], func=AF.Identity,
                            scale=mod[:, l, sc * 16 + kc, jj:jj + 1], bias=mod[:, l, sh * 16 + kc, jj:jj + 1]),
                            R=[tx, C.t_mod], W=[th])
                else:
                    P.add('act', lambda e, x=x, h=h, n=n, j=j, kc=kc: e.activation(
                        out=h[:, :n], in_=x[:, :n], func=AF.Identity,
                        scale=mod[:, l, sc * 16 + kc, j:j + 1], bias=mod[:, l, sh * 16 + kc, j:j + 1]),
                        R=[tx, C.t_mod], W=[th])
                P.add('act', lambda e, h=h, b=b, kc=kc, t0=t0, n=n: e.dma_start(
                    out=HT[b, kc * 128:(kc + 1) * 128, t0:t0 + n], in_=h[:, :n]), R=[th], W=[C.t_HT], dma=True)
                i += 1


def load_ht(C, HT, b, ht, t_ht):
    for q in range(4):
        C.P.add('sp', lambda e, q=q: e.dma_start(
            out=ht[:, 4 * q:4 * q + 4, :], in_=HT[b, 512 * q:512 * (q + 1), :].rearrange("(kc p) t -> p kc t", p=128)),
            R=[C.t_HT], W=[t_ht], dma=True)


def phase_ffn_up(C, l, HT, HID, skip_ctx=False):
    P = C.P
    ht = C.sb([128, KC, LSEQ], BF16); t_ht = Tl()
    dw = C.sb([128, FC, 4]); t_dw = Tl()
    P.add('sp', lambda e: e.dma_start(out=dw[:], in_=C.dram['ffn_dw_t'][l]), W=[t_dw], dma=True)
    NW = 2
    wa = [C.sb([128, KC, 128], BF16) for _ in range(NW)]; wg = [C.sb([128, KC, 128], BF16) for _ in range(NW)]
    t_wa = [Tl() for _ in range(NW)]; t_wg = [Tl() for _ in range(NW)]
    NA = 2
    a_sb = [C.sb([128, LLAT + 2]) for _ in range(NA)]; g_sb = [C.sb([128, LLAT]) for _ in range(NA)]
    acc = [C.sb([128, LLAT]) for _ in range(NA)]; hid = [C.sb([128, LLAT], BF16) for _ in range(NA)]
    t_a = [Tl() for _ in range(NA)]; t_g = [Tl() for _ in range(NA)]; t_acc = [Tl() for _ in range(NA)]
    t_hid = [Tl() for _ in range(NA)]
    for k in range(NA):
        P.add('pool', lambda e, k=k: e.memset(a_sb[k][:, 0:1], 0.0), W=[t_a[k]])
    ib = 0; ia = 0; iw = 0
    for b in range(C.nb):
        load_ht(C, HT, b, ht, t_ht)
        for j in range(FC):
            w_a, w_g, twa, twg = wa[iw % NW], wg[iw % NW], t_wa[iw % NW], t_wg[iw % NW]; iw += 1
            P.add('pool', lambda e, w=w_a, j=j: e.dma_start(out=w[:], in_=C.dram['w_up_t'][l, j], max_dma_last_dim=4096),
                  W=[twa], dma=True)
            P.add('pool', lambda e, w=w_g, j=j: e.dma_start(out=w[:], in_=C.dram['w_up_t'][l, FC + j], max_dma_last_dim=4096),
                  W=[twg], dma=True)
            for (s0, L) in (SEQS[1:] if skip_ctx else SEQS):
                k = ia % NA; ia += 1
                A, G, AC, HD = a_sb[k], g_sb[k], acc[k], hid[k]
                P.add('pool', lambda e, A=A, L=L: e.memset(A[:, L + 1:L + 2], 0.0), W=[t_a[k]])
                for (t0, n) in seq_tiles(L):
                    pa, pg = C.bank[ib % 8], C.bank[(ib + 1) % 8]
                    tpa, tpg = C.t_bank[ib % 8], C.t_bank[(ib + 1) % 8]; ib += 2
                    for kc in range(KC):
                        P.add('pe', lambda e, pa=pa, w=w_a, kc=kc, s0=s0, t0=t0, n=n: e.matmul(
                            pa[:, :n], w[:, kc, :], ht[:, kc, s0 + t0:s0 + t0 + n], start=(kc == 0), stop=(kc == KC - 1)),
                            R=[twa, t_ht], W=[tpa])
                    for kc in range(KC):
                        P.add('pe', lambda e, pg=pg, w=w_g, kc=kc, s0=s0, t0=t0, n=n: e.matmul(
                            pg[:, :n], w[:, kc, :], ht[:, kc, s0 + t0:s0 + t0 + n], start=(kc == 0), stop=(kc == KC - 1)),
                            R=[twg, t_ht], W=[tpg])
                    P.add('act', lambda e, A=A, pa=pa, t0=t0, n=n: e.copy(out=A[:, 1 + t0:1 + t0 + n], in_=pa[:, :n]),
                          R=[tpa], W=[t_a[k]])
                    P.add('dve', lambda e, G=G, pg=pg, t0=t0, n=n: e.tensor_copy(out=G[:, t0:t0 + n], in_=pg[:, :n]),
                          R=[tpg], W=[t_g[k]])
                P.add('pool', lambda e, A=A, AC=AC, L=L, j=j: e.tensor_scalar(
                    out=AC[:, :L], in0=A[:, 1:L + 1], scalar1=dw[:, j, 1:2], scalar2=dw[:, j, 3:4],
                    op0=ALU.mult, op1=ALU.add), R=[t_a[k], t_dw], W=[t_acc[k]])
                P.add('dve', lambda e, A=A, AC=AC, L=L, j=j: e.scalar_tensor_tensor(
                    out=AC[:, :L], in0=A[:, 0:L], scalar=dw[:, j, 0:1], in1=AC[:, :L],
                    op0=ALU.mult, op1=ALU.add), R=[t_a[k], t_dw, t_acc[k]], W=[t_acc[k]])
                P.add('dve', lambda e, A=A, AC=AC, L=L, j=j: e.scalar_tensor_tensor(
                    out=AC[:, :L], in0=A[:, 2:L + 2], scalar=dw[:, j, 2:3], in1=AC[:, :L],
                    op0=ALU.mult, op1=ALU.add), R=[t_a[k], t_dw, t_acc[k]], W=[t_acc[k]])
                P.add('act', lambda e, AC=AC, L=L: e.activation(out=AC[:, :L], in_=AC[:, :L], func=AF.Gelu),
                      R=[t_acc[k]], W=[t_acc[k]])
                P.add('dve', lambda e, AC=AC, G=G, HD=HD, L=L: e.tensor_tensor(
                    out=HD[:, :L], in0=AC[:, :L], in1=G[:, :L], op=ALU.mult),
                    R=[t_acc[k], t_g[k]], W=[t_hid[k]])
                P.add('sp', lambda e, HD=HD, b=b, j=j, s0=s0, L=L: e.dma_start(
                    out=HID[b, j * 128:(j + 1) * 128, s0:s0 + L], in_=HD[:, :L]),
                    R=[t_hid[k]], W=[C.t_HID], dma=True)


def phase_proj_ln(C, l, SRC, t_src, kcn, wname, widx, mgate, lnidx, xsrc, xdst, t_xdst, HT, hmod, lat_only_out=False, skip_ctx=False):
    P, mod = C.P, C.mod
    act = C.sb([128, kcn, 512], BF16); t_act = Tl()
    NW = 3
    w = [C.sb([128, kcn, 128], BF16) for _ in range(NW)]; t_w = [Tl() for _ in range(NW)]
    r = C.sb([128, KC, 512]); t_r = [Tl() for _ in range(KC)]
    NX = 3
    xo = [C.sb([128, 512]) for _ in range(NX)]; t_xo = [Tl() for _ in range(NX)]
    sq = [C.sb([128, 512]) for _ in range(2)]; t_sq = [Tl() for _ in range(2)]
    mean = C.sb([128, 512]); rstd = C.sb([128, 512]); t_mean = Tl(); t_rstd = Tl()
    xn = [C.sb([128, 512]) for _ in range(NX)]; t_xn = [Tl() for _ in range(NX)]
    hb = [C.sb([128, 512], BF16) for _ in range(NX)]; t_hb = [Tl() for _ in range(NX)]
    ones = C.sb([128, 128]); t_ones = Tl()
    lnp = C.sb([128, KC, 2]); t_lnp = Tl()
    P.add('pool', lambda e: e.memset(ones[:], 1.0 / D), W=[t_ones])
    P.add('sp', lambda e: e.dma_start(out=lnp[:], in_=C.dram['ln_t'][l, lnidx]), W=[t_lnp], dma=True)
    iw = 0; ix = 0; ib = 0; isq = 0
    pS1, pS2, tS1, tS2 = C.bank[6], C.bank[7], C.t_bank[6], C.t_bank[7]
    for b in range(C.nb):
        for (s0, L) in SEQS:
            j = 2 if s0 < LCTX else b
            if (lat_only_out or skip_ctx) and s0 < LCTX:
                continue
            for (t0, n) in seq_tiles(L):
                T0 = s0 + t0
                nq = 4 if kcn >= 16 else 1
                step = (kcn + nq - 1) // nq
                for q in range(0, kcn, step):
                    z = min(kcn, q + step)
                    P.add('sp', lambda e, q=q, z=z, T0=T0, n=n, b=b: e.dma_start(
                        out=act[:, q:z, :n], in_=SRC[b, q * 128:z * 128, T0:T0 + n].rearrange("(kc p) t -> p kc t", p=128)),
                        R=[t_src], W=[t_act], dma=True)
                for o in range(KC):
                    wt, tw = w[iw % NW], t_w[iw % NW]; iw += 1
                    for q in range(0, kcn, 8):
                        z = min(kcn, q + 8)
                        P.add('pool', lambda e, wt=wt, o=o, q=q, z=z: e.dma_start(
                            out=wt[:, q:z, :], in_=C.dram[wname][widx, o, :, q:z, :], max_dma_last_dim=4096), W=[tw], dma=True)
                    pb, tpb = C.bank[ib % 6], C.t_bank[ib % 6]; ib += 1
                    for kc in range(kcn):
                        P.add('pe', lambda e, pb=pb, wt=wt, kc=kc, n=n: e.matmul(
                            pb[:, :n], wt[:, kc, :], act[:, kc, :n], start=(kc == 0), stop=(kc == kcn - 1)),
                            R=[tw, t_act], W=[tpb])
                    x, tx = xo[ix % NX], t_xo[ix % NX]; ix += 1
                    P.add('sp', lambda e, x=x, b=b, o=o, T0=T0, n=n: e.dma_start(
                        out=x[:, :n], in_=xsrc[b, o * 128:(o + 1) * 128, T0:T0 + n]), W=[tx], dma=True)
                    P.add('pool', lambda e, x=x, n=n: e.tensor_scalar(
                        out=x[:, :n], in0=x[:, :n], scalar1=float(ALPHA), scalar2=None, op0=ALU.mult), R=[tx], W=[tx])
                    P.add('dve', lambda e, pb=pb, x=x, o=o, n=n, j=j: e.scalar_tensor_tensor(
                        out=r[:, o, :n], in0=pb[:, :n], scalar=mod[:, l, mgate * 16 + o, j:j + 1], in1=x[:, :n],
                        op0=ALU.mult, op1=ALU.add), R=[tpb, tx, C.t_mod], W=[t_r[o]])
                    s_, ts = sq[isq % 2], t_sq[isq % 2]; isq += 1
                    P.add('act', lambda e, s_=s_, o=o, n=n: e.activation(out=s_[:, :n], in_=r[:, o, :n], func=AF.Square),
                          R=[t_r[o]], W=[ts])
                    P.add('pe', lambda e, o=o, n=n: e.matmul(pS1[:, :n], ones[:], r[:, o, :n], start=(o == 0), stop=(o == KC - 1)),
                          R=[t_ones, t_r[o]], W=[tS1])
                    P.add('pe', lambda e, s_=s_, o=o, n=n: e.matmul(pS2[:, :n], ones[:], s_[:, :n], start=(o == 0), stop=(o == KC - 1)),
                          R=[t_ones, ts], W=[tS2])
                P.add('act', lambda e, n=n: e.copy(out=mean[:, :n], in_=pS1[:, :n]), R=[tS1], W=[t_mean])
                P.add('dve', lambda e, n=n: e.tensor_tensor(out=rstd[:, :n], in0=mean[:, :n], in1=mean[:, :n], op=ALU.mult),
                      R=[t_mean], W=[t_rstd])
                P.add('dve', lambda e, n=n: e.tensor_tensor(out=rstd[:, :n], in0=pS2[:, :n], in1=rstd[:, :n], op=ALU.subtract),
                      R=[tS2, t_rstd], W=[t_rstd])
                P.add('dve', lambda e, n=n: e.tensor_scalar(out=rstd[:, :n], in0=rstd[:, :n], scalar1=float(EPS), scalar2=None,
                                                            op0=ALU.add), R=[t_rstd], W=[t_rstd])
                P.add('act', lambda e, n=n: e.activation(out=rstd[:, :n], in_=rstd[:, :n], func=AF.Sqrt), R=[t_rstd], W=[t_rstd])
                P.add('dve', lambda e, n=n: e.reciprocal(out=rstd[:, :n], in_=rstd[:, :n]), R=[t_rstd], W=[t_rstd])
                for o in range(KC):
                    k = ix % NX; ix += 1
                    X, tX, H, tH = xn[k], t_xn[k], hb[k], t_hb[k]
                    P.add('dve', lambda e, X=X, o=o, n=n: e.tensor_tensor(out=X[:, :n], in0=r[:, o, :n], in1=mean[:, :n], op=ALU.subtract),
                          R=[t_r[o], t_mean], W=[tX])
                    P.add('pool', lambda e, X=X, n=n: e.tensor_tensor(out=X[:, :n], in0=X[:, :n], in1=rstd[:, :n], op=ALU.mult),
                          R=[tX, t_rstd], W=[tX])
                    P.add('act', lambda e, X=X, o=o, n=n: e.activation(out=X[:, :n], in_=X[:, :n], func=AF.Identity,
                                                                       scale=lnp[:, o, 0:1], bias=lnp[:, o, 1:2]),
                          R=[tX, t_lnp], W=[tX])
                    if lat_only_out:
                        P.add('sp', lambda e, X=X, b=b, o=o, t0=t0, n=n: e.dma_start(
                            out=xdst[b, o * 128:(o + 1) * 128, t0:t0 + n], in_=X[:, :n]), R=[tX], W=[t_xdst], dma=True)
                    else:
                        P.add('sp', lambda e, X=X, b=b, o=o, T0=T0, n=n: e.dma_start(
                            out=xdst[b, o * 128:(o + 1) * 128, T0:T0 + n], in_=X[:, :n]), R=[tX], W=[t_xdst], dma=True)
                    if hmod is not None:
                        hl, hsh, hsc = hmod
                        P.add('act', lambda e, X=X, H=H, o=o, n=n, j=j: e.activation(
                            out=H[:, :n], in_=X[:, :n], func=AF.Identity,
                            scale=mod[:, hl, hsc * 16 + o, j:j + 1], bias=mod[:, hl, hsh * 16 + o, j:j + 1]),
                            R=[tX, C.t_mod], W=[tH])
                        P.add('act', lambda e, H=H, b=b, o=o, T0=T0, n=n: e.dma_start(
                            out=HT[b, o * 128:(o + 1) * 128, T0:T0 + n], in_=H[:, :n]), R=[tH], W=[C.t_HT], dma=True)


NH_C = 16
DH = 128
QSCALE = DH ** -0.5
NBT = 12


def na_plan(qt):
    br = 2 * qt
    if br <= 2:
        rows = [0, 2, 4, 6]
        return rows, (rows[0] - br + 6) // 2
    if br >= 28:
        rows = [24, 26, 28, 30]
        return rows, (rows[0] - br + 6) // 2
    return [br - 4, br - 2, br, br + 2, br + 4], 7


def phase_qkv(C, l, HT, need_ctx):
    P = C.P
    o = l // 2
    QT, KT, V = C.dram['QT'], C.dram['KT'], C.dram['V']
    ht = C.sb([128, KC, LSEQ], BF16); t_ht = Tl()
    rope = C.sb([128, 2, LLAT]); t_rope = Tl()
    pmT = C.sb([128, 128]); t_pm = Tl()
    P.add('sp', lambda e: e.dma_start(out=rope[:], in_=C.dram['rope_t'][:]), W=[t_rope], dma=True)
    P.add('sp', lambda e: e.dma_start(out=pmT[:], in_=C.dram['pmT'][:]), W=[t_pm], dma=True)
    NW = 2
    w = [C.sb([128, KC, 128], BF16) for _ in range(NW)]; t_w = [Tl() for _ in range(NW)]
    wv = [C.sb([128, KC, 512], BF16) for _ in range(NW)]; t_wv = [Tl() for _ in range(NW)]
    NX = 3
    xs = [C.sb([128, 512]) for _ in range(NX)]; t_xs = [Tl() for _ in range(NX)]
    t1 = [C.sb([128, 512]) for _ in range(NX)]; t_t1 = [Tl() for _ in range(NX)]
    ob = [C.sb([128, 512], BF16) for _ in range(NX)]; t_ob = [Tl() for _ in range(NX)]
    iw = 0; ix = 0; ib = 0
    for b in range(C.nb):
        load_ht(C, HT, b, ht, t_ht)
        for j in range(32):
            isq = j < 16
            dst = QT if isq else KT
            wt, tw = w[iw % NW], t_w[iw % NW]; iw += 1
            P.add('pool', lambda e, wt=wt, j=j: e.dma_start(out=wt[:], in_=C.dram['w_inc_t'][o, j], max_dma_last_dim=4096),
                  W=[tw], dma=True)
            for (s0, L) in SEQS:
                isctx = s0 < LCTX
                if isctx and isq and not need_ctx:
                    continue
                for (t0, n) in seq_tiles(L):
                    pb, tpb = C.bank[ib % 4], C.t_bank[ib % 4]
                    pr, tpr = C.bank[4 + ib % 4], C.t_bank[4 + ib % 4]; ib += 1
                    for kc in range(KC):
                        P.add('pe', lambda e, pb=pb, wt=wt, kc=kc, s0=s0, t0=t0, n=n: e.matmul(
                            pb[:, :n], wt[:, kc, :], ht[:, kc, s0 + t0:s0 + t0 + n], start=(kc == 0), stop=(kc == KC - 1)),
                            R=[tw, t_ht], W=[tpb])
                    k = ix % NX; ix += 1
                    X, tX, T1, tT1, O, tO = xs[k], t_xs[k], t1[k], t_t1[k], ob[k], t_ob[k]
                    sc = float(QSCALE) if isq else 1.0
                    if isctx:
                        P.add('act', lambda e, O=O, pb=pb, n=n, sc=sc: e.activation(out=O[:, :n], in_=pb[:, :n], func=AF.Identity, scale=sc),
                              R=[tpb], W=[tO])
                    else:
                        P.add('act', lambda e, X=X, pb=pb, n=n, sc=sc: e.activation(out=X[:, :n], in_=pb[:, :n], func=AF.Identity, scale=sc),
                              R=[tpb], W=[tX])
                        P.add('pe', lambda e, pr=pr, X=X, n=n: e.matmul(pr[:, :n], pmT[:], X[:, :n], start=True, stop=True),
                              R=[t_pm, tX], W=[tpr])
                        P.add('pool', lambda e, X=X, T1=T1, t0=t0, n=n: e.tensor_tensor(
                            out=T1[:, :n], in0=X[:, :n], in1=rope[:, 0, t0:t0 + n], op=ALU.mult), R=[tX, t_rope], W=[tT1])
                        P.add('dve', lambda e, X=X, pr=pr, t0=t0, n=n: e.tensor_tensor(
                            out=X[:, :n], in0=pr[:, :n], in1=rope[:, 1, t0:t0 + n], op=ALU.mult), R=[tpr, t_rope, tX], W=[tX])
                        P.add('dve', lambda e, X=X, T1=T1, O=O, n=n: e.tensor_tensor(
                            out=O[:, :n], in0=X[:, :n], in1=T1[:, :n], op=ALU.add), R=[tX, tT1], W=[tO])
                    hh = j % 16
                    P.add('sp', lambda e, O=O, dst=dst, b=b, hh=hh, s0=s0, t0=t0, n=n: e.dma_start(
                        out=dst[b, hh * 128:(hh + 1) * 128, s0 + t0:s0 + t0 + n], in_=O[:, :n]), R=[tO], W=[C.t_QK], dma=True)
        for cg in range(4):
            wt, tw = wv[iw % NW], t_wv[iw % NW]; iw += 1
            for q in range(0, KC, 2):
                P.add('pool', lambda e, wt=wt, cg=cg, q=q: e.dma_start(
                    out=wt[:, q:q + 2, :], in_=C.dram['w_v_t'][o, cg, :, q:q + 2, :], max_dma_last_dim=4096), W=[tw], dma=True)
            for tt in range(LSEQ // 128):
                pb, tpb = C.bank[ib % 8], C.t_bank[ib % 8]; ib += 1
                for kc in range(KC):
                    P.add('pe', lambda e, pb=pb, wt=wt, kc=kc, tt=tt: e.matmul(
                        pb[:, :], ht[:, kc, tt * 128:(tt + 1) * 128], wt[:, kc, :], start=(kc == 0), stop=(kc == KC - 1)),
                        R=[tw, t_ht], W=[tpb])
                k = ix % NX; ix += 1
                O, tO = ob[k], t_ob[k]
                eng = 'act' if tt % 2 == 0 else 'dve'
                if eng == 'act':
                    P.add('act', lambda e, O=O, pb=pb: e.copy(out=O[:, :], in_=pb[:, :]), R=[tpb], W=[tO])
                else:
                    P.add('dve', lambda e, O=O, pb=pb: e.tensor_copy(out=O[:, :], in_=pb[:, :]), R=[tpb], W=[tO])
                P.add('sp', lambda e, O=O, b=b, tt=tt, cg=cg: e.dma_start(
                    out=V[b, tt * 128:(tt + 1) * 128, cg * 512:(cg + 1) * 512], in_=O[:, :]), R=[tO], W=[C.t_QK], dma=True)


def phase_attn(C, l, need_ctx):
    P = C.P
    o = l // 2
    QT, KT, V, YT = C.dram['QT'], C.dram['KT'], C.dram['V'], C.dram['YT']
    NT = LSEQ // 128
    NB2 = 2
    kt = [C.sb([128, LSEQ], BF16) for _ in range(NB2)]; qt_ = [C.sb([128, LSEQ], BF16) for _ in range(NB2)]
    vt = [C.sb([128, NT, 128], BF16) for _ in range(NB2)]; yt = [C.sb([128, LSEQ], BF16) for _ in range(NB2)]
    bt = [C.sb([128, NBT, 128]) for _ in range(NB2)]
    t_kt = [Tl() for _ in range(NB2)]; t_qt = [Tl() for _ in range(NB2)]; t_vt = [Tl() for _ in range(NB2)]
    t_yt = [Tl() for _ in range(NB2)]; t_bt = [Tl() for _ in range(NB2)]
    msk = C.sb([128, NBT, 128]); t_msk = Tl()
    ones = C.sb([128, 128], BF16); t_ones = Tl()
    P.add('sp', lambda e: e.dma_start(out=msk[:], in_=C.dram['na_mask'][:]), W=[t_msk], dma=True)
    P.add('pool', lambda e: e.memset(ones[:], 1.0), W=[t_ones])
    NS = 2
    sc = [C.sb([128, 5, 128]) for _ in range(NS)]; t_sc = [Tl() for _ in range(NS)]
    pT = [C.sb([128, 7, 128], BF16) for _ in range(NS)]; t_pT = [Tl() for _ in range(NS)]
    rec = [C.sb([128, 256]) for _ in range(NS)]; t_rec = [Tl() for _ in range(NS)]
    ih = 0; iq = 0
    for b in range(C.nb):
        for h in range(NH_C):
            k = ih % NB2; ih += 1
            K_, Q_, V_, Y_, B_ = kt[k], qt_[k], vt[k], yt[k], bt[k]
            P.add('sp', lambda e, K_=K_, b=b, h=h: e.dma_start(out=K_[:], in_=KT[b, h * 128:(h + 1) * 128, :]),
                  R=[C.t_QK], W=[t_kt[k]], dma=True)
            P.add('sp', lambda e, Q_=Q_, b=b, h=h: e.dma_start(out=Q_[:], in_=QT[b, h * 128:(h + 1) * 128, :]),
                  R=[C.t_QK], W=[t_qt[k]], dma=True)
            P.add('act', lambda e, V_=V_, b=b, h=h: e.dma_start(
                out=V_[:], in_=V[b, :, h * 128:(h + 1) * 128].rearrange("(tt p) d -> p tt d", p=128)),
                R=[C.t_QK], W=[t_vt[k]], dma=True)
            P.add('act', lambda e, B_=B_, h=h: e.dma_start(out=B_[:], in_=C.dram['na_bias_t'][o, h]), W=[t_bt[k]], dma=True)
            P.add('pool', lambda e, B_=B_: e.tensor_tensor(out=B_[:], in0=B_[:], in1=msk[:], op=ALU.add),
                  R=[t_bt[k], t_msk], W=[t_bt[k]])
            for qi in range(LLAT // 128):
                rows, b0 = na_plan(qi)
                nl = len(rows)
                s = iq % NS; iq += 1
                X, Y, Z = C.bank[3 * s], C.bank[3 * s + 1], C.bank[3 * s + 2]
                tX, tY, tZ = C.t_bank[3 * s], C.t_bank[3 * s + 1], C.t_bank[3 * s + 2]
                q0 = LCTX + qi * 128
                for i, a in enumerate(rows):
                    dstp, tdst, col = (X, tX, i * 128) if i < 4 else (Y, tY, 0)
                    k0 = LCTX + a * 64
                    P.add('pe', lambda e, dstp=dstp, col=col, K_=K_, Q_=Q_, k0=k0, q0=q0: e.matmul(
                        dstp[:, col:col + 128], K_[:, k0:k0 + 128], Q_[:, q0:q0 + 128], start=True, stop=True),
                        R=[t_kt[k], t_qt[k]], W=[tdst])
                for i in range(2):
                    P.add('pe', lambda e, Y=Y, i=i, K_=K_, Q_=Q_, q0=q0: e.matmul(
                        Y[:, 128 + i * 128:256 + i * 128], K_[:, i * 128:(i + 1) * 128], Q_[:, q0:q0 + 128], start=True, stop=True),
                        R=[t_kt[k], t_qt[k]], W=[tY])
                S_, tS, PT, tPT, RC, tRC = sc[s], t_sc[s], pT[s], t_pT[s], rec[s], t_rec[s]
                P.add('dve', lambda e, S_=S_, X=X, B_=B_, b0=b0: e.tensor_tensor(
                    out=S_[:, 0:4, :], in0=X[:, 0:512].rearrange("p (a c) -> p a c", c=128), in1=B_[:, b0:b0 + 4, :], op=ALU.add), R=[tX, t_bt[k]], W=[tS])
                if nl == 5:
                    P.add('dve', lambda e, S_=S_, Y=Y, B_=B_, b0=b0: e.tensor_tensor(
                        out=S_[:, 4, :], in0=Y[:, 0:128], in1=B_[:, b0 + 4, :], op=ALU.add), R=[t_bt[k]], W=[tS, tY])
                P.add('act', lambda e, PT=PT, S_=S_, nl=nl: e.activation(out=PT[:, 0:nl, :], in_=S_[:, 0:nl, :], func=AF.Exp),
                      R=[tS], W=[tPT])
                P.add('act', lambda e, PT=PT, Y=Y, nl=nl: e.activation(out=PT[:, nl:nl + 2, :], in_=Y[:, 128:384].rearrange("p (a c) -> p a c", c=128), func=AF.Exp),
                      R=[], W=[tPT, tY])
                vidx = [2 + a // 2 for a in rows] + [0, 1]
                for i, vi in enumerate(vidx):
                    P.add('pe', lambda e, Z=Z, V_=V_, PT=PT, i=i, vi=vi, nn=len(vidx): e.matmul(
                        Z[:, 0:128], V_[:, vi, :], PT[:, i, :], start=(i == 0), stop=(i == nn - 1)),
                        R=[t_vt[k], tPT], W=[tZ])
                for i in range(len(vidx)):
                    P.add('pe', lambda e, Z=Z, PT=PT, i=i, nn=len(vidx): e.matmul(
                        Z[:, 128:256], ones[:], PT[:, i, :], start=(i == 0), stop=(i == nn - 1)),
                        R=[t_ones, tPT], W=[tZ])
                P.add('dve', lambda e, RC=RC, Z=Z: e.reciprocal(out=RC[:, 0:128], in_=Z[:, 128:256]), R=[tZ], W=[tRC])
                P.add('dve', lambda e, Y_=Y_, Z=Z, RC=RC, q0=q0: e.tensor_tensor(
                    out=Y_[:, q0:q0 + 128], in0=Z[:, 0:128], in1=RC[:, 0:128], op=ALU.mult), R=[tZ, tRC], W=[t_yt[k]])
            if need_ctx:
                s = iq % NS; iq += 1
                X, Z = C.bank[3 * s], C.bank[3 * s + 2]
                tX, tZ = C.t_bank[3 * s], C.t_bank[3 * s + 2]
                PT, tPT, RC, tRC = pT[s], t_pT[s], rec[s], t_rec[s]
                for i in range(2):
                    P.add('pe', lambda e, X=X, i=i, K_=K_, Q_=Q_: e.matmul(
                        X[:, i * 256:(i + 1) * 256], K_[:, i * 128:(i + 1) * 128], Q_[:, 0:256], start=True, stop=True),
                        R=[t_kt[k], t_qt[k]], W=[tX])
                P.add('act', lambda e, PT=PT, X=X: e.activation(out=PT[:, 0:4, :], in_=X[:, 0:512].rearrange("p (a c) -> p a c", c=128), func=AF.Exp), R=[tX], W=[tPT])
                for i in range(2):
                    P.add('pe', lambda e, Z=Z, V_=V_, PT=PT, i=i: e.matmul(
                        Z[:, 0:256], V_[:, i, :], PT[:, 2 * i:2 * i + 2, :], start=(i == 0), stop=(i == 1)),
                        R=[t_vt[k], tPT], W=[tZ])
                for i in range(2):
                    P.add('pe', lambda e, Z=Z, PT=PT, i=i: e.matmul(
                        Z[:, 256:512], ones[:], PT[:, 2 * i:2 * i + 2, :], start=(i == 0), stop=(i == 1)),
                        R=[t_ones, tPT], W=[tZ])
                P.add('dve', lambda e, RC=RC, Z=Z: e.reciprocal(out=RC[:, 0:256], in_=Z[:, 256:512]), R=[tZ], W=[tRC])
                P.add('dve', lambda e, Y_=Y_, Z=Z, RC=RC: e.tensor_tensor(
                    out=Y_[:, 0:256], in0=Z[:, 0:256], in1=RC[:, 0:256], op=ALU.mult), R=[tZ, tRC], W=[t_yt[k]])
            c0 = 0 if need_ctx else LCTX
            P.add('sp', lambda e, Y_=Y_, b=b, h=h, c0=c0: e.dma_start(
                out=YT[b, h * 128:(h + 1) * 128, c0:LSEQ], in_=Y_[:, c0:LSEQ]), R=[t_yt[k]], W=[C.t_YT], dma=True)


NHA = 8


def phase_inab(C, l, HT):
    P = C.P
    e = l // 2
    Dm = C.dram
    ht = C.sb([128, KC, LSEQ], BF16); t_ht = Tl()
    gconv = C.sb([128, 24, 5]); t_gc = Tl()
    lbr = C.sb([128, 2, 2, NHA]); lb = C.sb([128, 2, NHA]); oml = C.sb([128, 2, NHA]); t_lb = Tl()
    ident = C.sb([128, 128]); t_id = Tl()
    ones = C.sb([128, 128]); t_ones = Tl()
    P.add('sp', lambda e_: e_.dma_start(out=gconv[:], in_=Dm['gconv_t'][e]), W=[t_gc], dma=True)
    P.add('sp', lambda e_: e_.dma_start(out=ident[:], in_=Dm['ident'][:]), W=[t_id], dma=True)
    P.add('pool', lambda e_: e_.memset(ones[:], 1.0), W=[t_ones])
    if e == 0:
        P.add('pool', lambda e_: e_.memset(lb[:], 0.0), W=[t_lb])
        P.add('pool', lambda e_: e_.memset(oml[:], 1.0), W=[t_lb])
    else:
        P.add('sp', lambda e_: e_.dma_start(out=lbr[:], in_=Dm['lb_t'][:]), W=[t_lb], dma=True)
        P.add('dve', lambda e_: e_.tensor_tensor(out=lb[:], in0=lbr[:, 1], in1=lbr[:, 0], op=ALU.subtract), R=[t_lb], W=[t_lb])
        P.add('act', lambda e_: e_.activation(out=lb[:], in_=lb[:], func=AF.Sigmoid), R=[t_lb], W=[t_lb])
        P.add('dve', lambda e_: e_.tensor_scalar(out=oml[:], in0=lb[:], scalar1=-1.0, scalar2=1.0, op0=ALU.mult, op1=ALU.add),
              R=[t_lb], W=[t_lb])
    NW = 2
    w = [C.sb([128, KC, 128], BF16) for _ in range(NW)]; t_w = [Tl() for _ in range(NW)]
    wv = [C.sb([128, KC, 512], BF16) for _ in range(NW)]; t_wv = [Tl() for _ in range(NW)]
    wg = C.sb([128, KC, 32], BF16); t_wg = Tl()
    NR = 2
    xr = [C.sb([128, LLAT + 4]) for _ in range(NR)]; yr = [C.sb([128, LLAT]) for _ in range(NR)]
    obf = [C.sb([128, LLAT], BF16) for _ in range(NR)]
    tst = [C.sb([128, LLAT // 128, 128], BF16) for _ in range(NR)]
    t_xr = [Tl() for _ in range(NR)]; t_yr = [Tl() for _ in range(NR)]; t_ob = [Tl() for _ in range(NR)]; t_ts = [Tl() for _ in range(NR)]
    vst = [C.sb([128, 512], BF16) for _ in range(3)]; t_vst = [Tl() for _ in range(3)]
    gst = C.sb([128, LSEQ // 128, 32]); t_gst = Tl()
    for k in range(NR):
        P.add('pool', lambda e_, k=k: e_.memset(xr[k][:, 0:2], 0.0), W=[t_xr[k]])
    iw = 0; ir = 0; ib = 0; iv = 0
    for b in range(C.nb):
        load_ht(C, HT, b, ht, t_ht)
        for j in range(72):
            grp, hd = j // 8, j % 8
            if grp == 3:
                continue
            wt, tw = w[iw % NW], t_w[iw % NW]; iw += 1
            P.add('pool', lambda e_, wt=wt, j=j: e_.dma_start(out=wt[:], in_=Dm['w_inab_t'][e, j], max_dma_last_dim=4096),
                  W=[tw], dma=True)
            kind = {0: 'silu', 1: 'f', 2: 'f', 4: 'silu', 5: 'conv', 6: 'conv', 7: 'conv', 8: 'silu'}[grp]
            for (s0, L) in SEQS:
                k = ir % NR; ir += 1
                XR, YR, OB, TS = xr[k], yr[k], obf[k], tst[k]
                tXR, tYR, tOB, tTS = t_xr[k], t_yr[k], t_ob[k], t_ts[k]
                pad = 2 if kind == 'conv' else 0
                if kind == 'conv':
                    P.add('pool', lambda e_, XR=XR, L=L: e_.memset(XR[:, L + 2:L + 4], 0.0), W=[tXR])
                    if L != LLAT:
                        P.add('pool', lambda e_, XR=XR: e_.memset(XR[:, 0:2], 0.0), W=[tXR])
                for (t0, n) in seq_tiles(L):
                    pb, tpb = C.bank[ib % 6], C.t_bank[ib % 6]; ib += 1
                    for kc in range(KC):
                        P.add('pe', lambda e_, pb=pb, wt=wt, kc=kc, s0=s0, t0=t0, n=n: e_.matmul(
                            pb[:, :n], wt[:, kc, :], ht[:, kc, s0 + t0:s0 + t0 + n], start=(kc == 0), stop=(kc == KC - 1)),
                            R=[tw, t_ht], W=[tpb])
                    fn = {'silu': AF.Silu, 'f': AF.Sigmoid, 'conv': AF.Identity}[kind]
                    P.add('act', lambda e_, XR=XR, pb=pb, t0=t0, n=n, fn=fn, pad=pad: e_.activation(
                        out=XR[:, pad + t0:pad + t0 + n], in_=pb[:, :n], func=fn), R=[tpb], W=[tXR])
                if kind == 'silu':
                    dst = {0: 'QA', 4: 'GA', 8: 'ZB'}[grp]
                    P.add('sp', lambda e_, XR=XR, dst=dst, b=b, hd=hd, s0=s0, L=L: e_.dma_start(
                        out=Dm[dst][b, hd * 128:(hd + 1) * 128, s0:s0 + L], in_=XR[:, :L]), R=[tXR], W=[C.t_AB], dma=True)
                elif kind == 'f':
                    d = grp - 1
                    P.add('dve', lambda e_, XR=XR, L=L, d=d, hd=hd: e_.tensor_scalar(
                        out=XR[:, :L], in0=XR[:, :L], scalar1=oml[:, d, hd:hd + 1], scalar2=lb[:, d, hd:hd + 1],
                        op0=ALU.mult, op1=ALU.add), R=[tXR, t_lb], W=[tXR])
                    P.add('pool', lambda e_, XR=XR, YR=YR, L=L: e_.tensor_scalar(
                        out=YR[:, :L], in0=XR[:, :L], scalar1=-1.0, scalar2=1.0, op0=ALU.mult, op1=ALU.add), R=[tXR], W=[tYR])
                    P.add('sp', lambda e_, YR=YR, b=b, d=d, hd=hd, s0=s0, L=L: e_.dma_start(
                        out=Dm['KF'][b, d, hd * 128:(hd + 1) * 128, s0:s0 + L], in_=YR[:, :L]), R=[tYR], W=[C.t_AB], dma=True)
                    P.add('act', lambda e_, XR=XR, L=L: e_.activation(out=XR[:, :L], in_=XR[:, :L], func=AF.Ln), R=[tXR, tYR], W=[tXR])
                    P.add('sp', lambda e_, XR=XR, b=b, d=d, hd=hd, s0=s0, L=L: e_.dma_start(
                        out=Dm['LF'][b, d, hd * 128:(hd + 1) * 128, s0:s0 + L], in_=XR[:, :L]), R=[tXR], W=[C.t_AB], dma=True)
                else:
                    ci = j - 40
                    P.add('dve', lambda e_, XR=XR, YR=YR, L=L, ci=ci: e_.tensor_scalar(
                        out=YR[:, :L], in0=XR[:, 2:L + 2], scalar1=gconv[:, ci, 2:3], scalar2=None, op0=ALU.mult),
                        R=[tXR, t_gc], W=[tYR])
                    for tap in (0, 1, 3, 4):
                        P.add('dve', lambda e_, XR=XR, YR=YR, L=L, ci=ci, tap=tap: e_.scalar_tensor_tensor(
                            out=YR[:, :L], in0=XR[:, tap:tap + L], scalar=gconv[:, ci, tap:tap + 1], in1=YR[:, :L],
                            op0=ALU.mult, op1=ALU.add), R=[tXR, t_gc, tYR], W=[tYR])
                    P.add('act', lambda e_, YR=YR, L=L: e_.activation(out=YR[:, :L], in_=YR[:, :L], func=AF.Silu), R=[tYR], W=[tYR])
                    if grp in (5, 6):
                        P.add('act', lambda e_, XR=XR, YR=YR, L=L: e_.activation(out=XR[:, :L], in_=YR[:, :L], func=AF.Square),
                              R=[tYR], W=[tXR])
                        for (t0, n) in seq_tiles(L):
                            pb, tpb = C.bank[6 + ib % 2], C.t_bank[6 + ib % 2]; ib += 1
                            P.add('pe', lambda e_, pb=pb, XR=XR, t0=t0, n=n: e_.matmul(pb[:, :n], ones[:], XR[:, t0:t0 + n], start=True, stop=True),
                                  R=[t_ones, tXR], W=[tpb])
                            P.add('dve', lambda e_, pb=pb, XR=XR, t0=t0, n=n: e_.tensor_scalar(
                                out=XR[:, t0:t0 + n], in0=pb[:, :n], scalar1=float(EPS), scalar2=None, op0=ALU.add), R=[tpb], W=[tXR, tpb])
                        P.add('act', lambda e_, XR=XR, L=L: e_.activation(out=XR[:, :L], in_=XR[:, :L], func=AF.Sqrt), R=[tXR], W=[tXR])
                        P.add('dve', lambda e_, XR=XR, L=L: e_.reciprocal(out=XR[:, :L], in_=XR[:, :L]), R=[tXR], W=[tXR])
                        qs = float(QSCALE) if grp == 5 else 1.0
                        P.add('dve', lambda e_, XR=XR, YR=YR, L=L, qs=qs: e_.scalar_tensor_tensor(
                            out=YR[:, :L], in0=YR[:, :L], scalar=qs, in1=XR[:, :L], op0=ALU.mult, op1=ALU.mult),
                            R=[tXR, tYR], W=[tYR])
                        P.add('pool', lambda e_, OB=OB, YR=YR, L=L: e_.tensor_copy(out=OB[:, :L], in_=YR[:, :L]), R=[tYR], W=[tOB])
                        dst = 'QB' if grp == 5 else 'KB'
                        P.add('sp', lambda e_, OB=OB, dst=dst, b=b, hd=hd, s0=s0, L=L: e_.dma_start(
                            out=Dm[dst][b, hd * 128:(hd + 1) * 128, s0:s0 + L], in_=OB[:, :L]), R=[tOB], W=[C.t_AB], dma=True)
                    if grp in (6, 7):
                        for tt in range(L // 128):
                            pb, tpb = C.bank[6 + ib % 2], C.t_bank[6 + ib % 2]; ib += 1
                            P.add('pe', lambda e_, pb=pb, YR=YR, tt=tt: e_.transpose(pb[:, 0:128], YR[:, tt * 128:(tt + 1) * 128], ident[:]),
                                  R=[tYR, t_id], W=[tpb])
                            if tt % 2 == 0:
                                P.add('act', lambda e_, TS=TS, pb=pb, tt=tt: e_.copy(out=TS[:, tt, :], in_=pb[:, 0:128]), R=[], W=[tTS, tpb])
                            else:
                                P.add('dve', lambda e_, TS=TS, pb=pb, tt=tt: e_.tensor_copy(out=TS[:, tt, :], in_=pb[:, 0:128]), R=[], W=[tTS, tpb])
                        dst = 'KBt' if grp == 6 else 'VBt'
                        P.add('sp', lambda e_, TS=TS, dst=dst, b=b, hd=hd, s0=s0, L=L: e_.dma_start(
                            out=Dm[dst][b, s0:s0 + L, hd * 128:(hd + 1) * 128].rearrange("(tt p) d -> p tt d", p=128),
                            in_=TS[:, 0:L // 128, :]), R=[tTS], W=[C.t_AB], dma=True)
        for cg in range(2):
            wt, tw = wv[iw % NW], t_wv[iw % NW]; iw += 1
            for q in range(0, KC, 2):
                P.add('pool', lambda e_, wt=wt, cg=cg, q=q: e_.dma_start(
                    out=wt[:, q:q + 2, :], in_=Dm['w_ia_t'][e, cg, :, q:q + 2, :], max_dma_last_dim=4096), W=[tw], dma=True)
            for tt in range(LSEQ // 128):
                pb, tpb = C.bank[ib % 6], C.t_bank[ib % 6]; ib += 1
                for kc in range(KC):
                    P.add('pe', lambda e_, pb=pb, wt=wt, kc=kc, tt=tt: e_.matmul(
                        pb[:, :], ht[:, kc, tt * 128:(tt + 1) * 128], wt[:, kc, :], start=(kc == 0), stop=(kc == KC - 1)),
                        R=[tw, t_ht], W=[tpb])
                O, tO = vst[iv % 3], t_vst[iv % 3]; iv += 1
                if tt % 2 == 0:
                    P.add('act', lambda e_, O=O, pb=pb: e_.copy(out=O[:, :], in_=pb[:, :]), R=[tpb], W=[tO])
                else:
                    P.add('dve', lambda e_, O=O, pb=pb: e_.tensor_copy(out=O[:, :], in_=pb[:, :]), R=[tpb], W=[tO])
                P.add('sp', lambda e_, O=O, b=b, tt=tt, cg=cg: e_.dma_start(
                    out=Dm['VA'][b, tt * 128:(tt + 1) * 128, cg * 512:(cg + 1) * 512], in_=O[:, :]), R=[tO], W=[C.t_AB], dma=True)
        P.add('pool', lambda e_: e_.dma_start(out=wg[:], in_=Dm['w_gb_t'][e], max_dma_last_dim=4096), W=[t_wg], dma=True)
        for tt in range(LSEQ // 128):
            pb, tpb = C.bank[ib % 6], C.t_bank[ib % 6]; ib += 1
            for kc in range(KC):
                P.add('pe', lambda e_, pb=pb, kc=kc, tt=tt: e_.matmul(
                    pb[:, 0:32], ht[:, kc, tt * 128:(tt + 1) * 128], wg[:, kc, :], start=(kc == 0), stop=(kc == KC - 1)),
                    R=[t_wg, t_ht], W=[tpb])
            P.add('dve', lambda e_, pb=pb, tt=tt: e_.tensor_copy(out=gst[:, tt, :], in_=pb[:, 0:32]), R=[tpb], W=[t_gst])
        P.add('sp', lambda e_, b=b: e_.dma_start(
            out=Dm['GBt'][b].rearrange("(tt p) c -> p tt c", p=128), in_=gst[:]), R=[t_gst], W=[C.t_AB], dma=True)


NCH = LSEQ // 128


def interleave(gens):
    gens = list(gens)
    while gens:
        for g in list(gens):
            try:
                next(g)
            except StopIteration:
                gens.remove(g)


def chunk_order(d):
    ctx, lat = [0, 1], list(range(2, NCH))
    return (ctx + lat) if d == 0 else (ctx[::-1] + lat[::-1])


def merge_head(C, OA, t_oa, gate, t_gate, normw, t_nw, col, ybf, t_ybf, onesm, t_onesm, tmp, t_tmp, bank, t_bank, dst_row, b):
    P = C.P
    for (t0, n) in seq_tiles(LSEQ):
        P.add('act', lambda e, t0=t0, n=n: e.activation(out=tmp[:, :n], in_=OA[:, t0:t0 + n], func=AF.Square), R=[t_oa], W=[t_tmp])
        P.add('pe', lambda e, n=n: e.matmul(bank[:, :n], onesm[:], tmp[:, :n], start=True, stop=True), R=[t_onesm, t_tmp], W=[t_bank])
        P.add('dve', lambda e, n=n: e.tensor_scalar(out=tmp[:, :n], in0=bank[:, :n], scalar1=float(EPS), scalar2=None, op0=ALU.add),
              R=[], W=[t_tmp, t_bank])
        P.add('act', lambda e, n=n: e.activation(out=tmp[:, :n], in_=tmp[:, :n], func=AF.Sqrt), R=[t_tmp], W=[t_tmp])
        P.add('dve', lambda e, n=n: e.reciprocal(out=tmp[:, :n], in_=tmp[:, :n]), R=[t_tmp], W=[t_tmp])
        P.add('dve', lambda e, t0=t0, n=n: e.scalar_tensor_tensor(
            out=tmp[:, :n], in0=OA[:, t0:t0 + n], scalar=normw[:, col:col + 1], in1=tmp[:, :n], op0=ALU.mult, op1=ALU.mult),
            R=[t_oa, t_nw, t_tmp], W=[t_tmp])
        P.add('pool', lambda e, t0=t0, n=n: e.tensor_tensor(out=ybf[:, t0:t0 + n], in0=tmp[:, :n], in1=gate[:, t0:t0 + n], op=ALU.mult),
              R=[t_tmp, t_gate], W=[t_ybf])
    P.add('sp', lambda e: e.dma_start(out=C.dram['YT'][b, dst_row:dst_row + 128, :], in_=ybf[:]), R=[t_ybf], W=[C.t_YT], dma=True)


def phase_gla(C, l):
    CH = 64; NCA = LSEQ // CH; NCTX = LCTX // CH
    P = C.P
    e = l // 2
    Dm = C.dram
    row = lambda: C.sb([128, LSEQ])
    q = row(); lf = [row(), row()]; kf = [row(), row()]; pp = [row(), row()]; OA = row(); ga = row(); rst = row()
    ybf = C.sb([128, LSEQ], BF16); va = C.sb([CH, NCA, 128], BF16)
    t_q = Tl(); t_lf = [Tl(), Tl()]; t_kf = [Tl(), Tl()]; t_pp = [Tl(), Tl()]; t_oa = Tl(); t_ga = Tl(); t_rst = Tl()
    t_ybf = Tl(); t_va = Tl()
    nmid = [C.sb([128, NCA]) for _ in range(2)]; t_nmid = [Tl(), Tl()]
    edec = [C.sb([128, NCA]) for _ in range(2)]; t_edec = [Tl(), Tl()]
    msk = C.sb([128, 2, 128], U32); t_msk = Tl()
    ident = C.sb([128, 128]); t_id = Tl()
    onesm = C.sb([128, 128]); t_onesm = Tl()
    anorm = C.sb([128, 2]); t_an = Tl()
    tmp = C.sb([128, 512]); t_tmp = Tl()
    S = [C.sb([128, 128]) for _ in range(2)]; Sb = [C.sb([128, 128], BF16) for _ in range(2)]
    t_S = [Tl(), Tl()]; t_Sb = [Tl(), Tl()]
    NU = 2
    E = [[[C.sb([128, CH]) for _ in range(4)] for _ in range(NU)] for _ in range(2)]
    t_E = [[[Tl() for _ in range(4)] for _ in range(NU)] for _ in range(2)]
    qe = [[C.sb([128, CH], BF16) for _ in range(NU)] for _ in range(2)]; ke = [[C.sb([128, CH], BF16) for _ in range(NU)] for _ in range(2)]
    qg = [[C.sb([128, CH], BF16) for _ in range(NU)] for _ in range(2)]; kd = [[C.sb([128, CH]) for _ in range(NU)] for _ in range(2)]
    atm = [[C.sb([CH, CH], BF16) for _ in range(NU)] for _ in range(2)]; kdt = [[C.sb([CH, 128], BF16) for _ in range(NU)] for _ in range(2)]
    mk = lambda: [[Tl() for _ in range(NU)] for _ in range(2)]
    t_qe, t_ke, t_qg, t_kd, t_atm, t_kdt = mk(), mk(), mk(), mk(), mk(), mk()
    P.add('sp', lambda e_: e_.dma_start(out=msk[:], in_=Dm['tri_u'][:]), W=[t_msk], dma=True)
    for d_ in range(2):
        for u_ in range(NU):
            P.add('pool', lambda e_, d_=d_, u_=u_: e_.memset(atm[d_][u_][:], 0.0), W=[t_atm[d_][u_]])
    P.add('sp', lambda e_: e_.dma_start(out=ident[:], in_=Dm['ident'][:]), W=[t_id], dma=True)
    P.add('sp', lambda e_: e_.dma_start(out=anorm[:], in_=Dm['anorm_t'][e]), W=[t_an], dma=True)
    P.add('pool', lambda e_: e_.memset(onesm[:], 1.0 / 128), W=[t_onesm])
    P.add('pool', lambda e_: e_.memset(rst[:], 1.0), W=[t_rst])
    rst3 = rst[:].rearrange("p (c t) -> p c t", t=CH)
    P.add('pool', lambda e_: e_.memset(rst3[:, :, 0:1], 0.0), W=[t_rst])
    for b in range(C.nb):
        for hd in range(NHA):
            r0 = hd * 128
            P.add('sp', lambda e_, b=b, r0=r0: e_.dma_start(out=q[:], in_=Dm['QA'][b, r0:r0 + 128, :]), R=[C.t_AB], W=[t_q], dma=True)
            P.add('sp', lambda e_, b=b, r0=r0: e_.dma_start(out=ga[:], in_=Dm['GA'][b, r0:r0 + 128, :]), R=[C.t_AB], W=[t_ga], dma=True)
            P.add('act', lambda e_, b=b, r0=r0: e_.dma_start(
                out=va[:], in_=Dm['VA'][b, :, r0:r0 + 128].rearrange("(tt p) d -> p tt d", p=CH)), R=[C.t_AB], W=[t_va], dma=True)
            for d in range(2):
                P.add('sp', lambda e_, b=b, d=d, r0=r0: e_.dma_start(out=lf[d][:], in_=Dm['LF'][b, d, r0:r0 + 128, :]),
                      R=[C.t_AB], W=[t_lf[d]], dma=True)
                P.add('act', lambda e_, b=b, d=d, r0=r0: e_.dma_start(out=kf[d][:], in_=Dm['KF'][b, d, r0:r0 + 128, :]),
                      R=[C.t_AB], W=[t_kf[d]], dma=True)
                P.add('dve', lambda e_, d=d: e_.tensor_tensor_scan(out=pp[d][:], data0=rst[:], data1=lf[d][:], initial=0.0,
                                                                   op0=ALU.mult, op1=ALU.add), R=[t_rst, t_lf[d]], W=[t_pp[d]])
                pp3 = pp[d][:].rearrange("p (c t) -> p c t", t=CH)
                P.add('act', lambda e_, d=d, pp3=pp3: e_.activation(out=edec[d][:], in_=pp3[:, :, CH - 1], func=AF.Exp),
                      R=[t_pp[d]], W=[t_edec[d]])
            P.add('dve', lambda e_: e_.tensor_tensor(out=lf[1][:], in0=lf[1][:], in1=pp[1][:], op=ALU.subtract),
                  R=[t_pp[1], t_lf[1]], W=[t_lf[1]])
            B0 = [pp[0], lf[1]]; t_B0 = [t_pp[0], t_lf[1]]
            for d in range(2):
                b3 = B0[d][:].rearrange("p (c t) -> p c t", t=CH)
                mc = CH // 2 - 1 if d == 0 else CH // 2
                P.add('dve', lambda e_, d=d, b3=b3, mc=mc: e_.tensor_scalar(out=nmid[d][:], in0=b3[:, :, mc], scalar1=-1.0, scalar2=None,
                                                                            op0=ALU.mult), R=[t_B0[d]], W=[t_nmid[d]])
                P.add('pool', lambda e_, d=d: e_.memset(S[d][:], 0.0), W=[t_S[d]])
                P.add('pool', lambda e_, d=d: e_.memset(Sb[d][:], 0.0), W=[t_Sb[d]])
            first_written = [False] * NCA

            def chain(d):
                for ui, c in enumerate((list(range(NCTX)) + list(range(NCTX, NCA))) if d == 0 else
                                       (list(range(NCTX))[::-1] + list(range(NCTX, NCA))[::-1])):
                    yield from unit(d, ui, c)

            def unit(d, ui, c):
                if True:
                    u = ui % NU
                    cs = slice(c * CH, (c + 1) * CH)
                    mc = c * CH + (CH // 2 - 1 if d == 0 else CH // 2)
                    lc = c * CH + (CH - 1 if d == 0 else 0)
                    bA, tA = C.bank[4 * d + 2 * u], C.t_bank[4 * d + 2 * u]
                    bB, tB = C.bank[4 * d + 2 * u + 1], C.t_bank[4 * d + 2 * u + 1]
                    E1, E2, E3, E4 = E[d][u]; tE1, tE2, tE3, tE4 = t_E[d][u]
                    Bd, tBd = B0[d], t_B0[d]
                    P.add('act', lambda e_: e_.activation(out=E1[:], in_=Bd[:, cs], func=AF.Exp, bias=nmid[d][:, c:c + 1]),
                          R=[tBd, t_nmid[d]], W=[tE1])
                    P.add('act', lambda e_: e_.activation(out=E2[:], in_=Bd[:, cs], func=AF.Exp, scale=-1.0, bias=Bd[:, mc:mc + 1]),
                          R=[tBd], W=[tE2])
                    if d == 0:
                        P.add('act', lambda e_: e_.activation(out=E3[:], in_=Bd[:, cs], func=AF.Exp), R=[tBd], W=[tE3])
                    else:
                        P.add('act', lambda e_: e_.activation(out=E3[:], in_=Bd[:, cs], func=AF.Exp, bias=pp[1][:, c * CH + CH - 1:c * CH + CH]),
                              R=[tBd, t_pp[1]], W=[tE3])
                    P.add('act', lambda e_: e_.activation(out=E4[:], in_=Bd[:, cs], func=AF.Exp, scale=-1.0, bias=Bd[:, lc:lc + 1]),
                          R=[tBd], W=[tE4])
                    yield
                    QE, KE, QG, KD, ATM, KDT = qe[d][u], ke[d][u], qg[d][u], kd[d][u], atm[d][u], kdt[d][u]
                    P.add('pool', lambda e_: e_.tensor_tensor(out=QE[:], in0=q[:, cs], in1=E1[:], op=ALU.mult), R=[t_q, tE1], W=[t_qe[d][u]])
                    P.add('dve', lambda e_: e_.tensor_tensor(out=KE[:], in0=kf[d][:, cs], in1=E2[:], op=ALU.mult), R=[t_kf[d], tE2], W=[t_ke[d][u]])
                    P.add('pool', lambda e_: e_.tensor_tensor(out=QG[:], in0=q[:, cs], in1=E3[:], op=ALU.mult), R=[t_q, tE3], W=[t_qg[d][u]])
                    P.add('dve', lambda e_: e_.tensor_tensor(out=KD[:], in0=kf[d][:, cs], in1=E4[:], op=ALU.mult), R=[t_kf[d], tE4], W=[t_kd[d][u]])
                    yield
                    P.add('pe', lambda e_: e_.matmul(bA[0:CH, 0:CH], KE[:], QE[:], start=True, stop=True), R=[t_ke[d][u], t_qe[d][u]], W=[tA])
                    P.add('pe', lambda e_: e_.transpose(bA[0:CH, 128:256], KD[:], ident[:]), R=[t_kd[d][u], t_id], W=[tA])
                    yield
                    P.add('dve', lambda e_: e_.copy_predicated(ATM[:], msk[0:CH, d, 0:CH], bA[0:CH, 0:CH]),
                          R=[t_msk], W=[t_atm[d][u], tA])
                    P.add('act', lambda e_: e_.copy(out=KDT[:], in_=bA[0:CH, 128:256]), R=[], W=[t_kdt[d][u], tA])
                    yield
                    P.add('pe', lambda e_: e_.matmul(bB[:, 0:CH], Sb[d][:], QG[:], start=True, stop=False), R=[t_Sb[d], t_qg[d][u]], W=[tB])
                    P.add('pe', lambda e_: e_.matmul(bB[:, 0:CH], va[:, c, :], ATM[:], start=False, stop=True), R=[t_va, t_atm[d][u]], W=[tB])
                    P.add('pe', lambda e_: e_.matmul(bB[:, 128:256], KDT[:], va[:, c, :], start=True, stop=True), R=[t_va, t_kdt[d][u]], W=[tB])
                    yield
                    if not first_written[c]:
                        first_written[c] = True
                        P.add('act', lambda e_: e_.copy(out=OA[:, cs], in_=bB[:, 0:CH]), R=[], W=[t_oa, tB])
                    else:
                        P.add('dve', lambda e_: e_.tensor_tensor(out=OA[:, cs], in0=bB[:, 0:CH], in1=OA[:, cs], op=ALU.add), R=[], W=[t_oa, tB])
                    P.add('dve', lambda e_: e_.scalar_tensor_tensor(out=S[d][:], in0=S[d][:], scalar=edec[d][:, c:c + 1], in1=bB[:, 128:256],
                                                                    op0=ALU.mult, op1=ALU.add), R=[t_edec[d]], W=[t_S[d], tB])
                    P.add('pool', lambda e_: e_.tensor_copy(out=Sb[d][:], in_=S[d][:]), R=[t_S[d]], W=[t_Sb[d]])
                    yield

            interleave([chain(0), chain(1)])
            merge_head(C, OA, t_oa, ga, t_ga, anorm, t_an, 0, ybf, t_ybf, onesm, t_onesm, tmp, t_tmp, C.bank[0], C.t_bank[0], r0, b)


def phase_gdn(C, l):
    P = C.P
    e = l // 2
    Dm = C.dram
    CH = 128

    def T_(shape, dt=F32):
        return C.sb(shape, dt), Tl()

    kT, t_kT = T_([128, LSEQ], BF16); qT, t_qT = T_([128, LSEQ], BF16)
    ktm, t_ktm = T_([128, NCH, 128], BF16); vtm, t_vtm = T_([128, NCH, 128], BF16)
    zb, t_zb = T_([128, LSEQ]); OB, t_ob = T_([128, LSEQ]); ybf, t_ybf = T_([128, LSEQ], BF16)
    gbt, t_gbt = T_([128, NCH, 32]); dtb, t_dtb = T_([128, NCH, 16]); alog, t_alog = T_([128, NCH, 16])
    g, t_g = T_([128, NCH, 16]); beta, t_beta = T_([128, NCH, 16]); G, t_G = T_([128, NCH, 16])
    Gtot, t_Gtot = T_([128, NCH, 16]); eG, t_eG = T_([128, NCH, 16]); beG, t_beG = T_([128, NCH, 16])
    khs, t_khs = T_([128, NCH, 16]); sdec, t_sdec = T_([128, NCH, 16]); tmpg, t_tmpg = T_([128, NCH, 16])
    tri, t_tri = T_([128, 2, 128]); nstr, t_nstr = T_([128, 2, 128]); ident, t_id = T_([128, 128])
    onesf, t_onesf = T_([128, 128]); onesm, t_onesm = T_([128, 128]); anorm, t_an = T_([128, 2])
    tmp, t_tmp = T_([128, 512])
    S = [T_([128, 128]) for _ in range(2)]
    names = ('Dg', 'Ib', 'Dmx', 'eGr', 't2', 'NT', 'AT', 'N', 'NjA', 'NjB', 'NjTA', 'NjTB', 'TT', 'rhs1', 'rhs2', 'nwT', 'vnew', 'qg', 'khat', 'KK', 'QK')
    U = [{n: T_([128, 128]) for n in names} for _ in range(2)]
    P.add('sp', lambda e_: e_.dma_start(out=tri[:], in_=Dm['tri_t'][:]), W=[t_tri], dma=True)
    P.add('sp', lambda e_: e_.dma_start(out=ident[:], in_=Dm['ident'][:]), W=[t_id], dma=True)
    P.add('sp', lambda e_: e_.dma_start(out=anorm[:], in_=Dm['anorm_t'][e]), W=[t_an], dma=True)
    P.add('sp', lambda e_: e_.dma_start(out=dtb[:], in_=Dm['dtb_t'][e]), W=[t_dtb], dma=True)
    P.add('sp', lambda e_: e_.dma_start(out=alog[:], in_=Dm['alog_t'][e]), W=[t_alog], dma=True)
    P.add('pool', lambda e_: e_.memset(onesf[:], 1.0), W=[t_onesf])
    P.add('pool', lambda e_: e_.memset(onesm[:], 1.0 / 128), W=[t_onesm])
    for d_ in range(2):
        P.add('pool', lambda e_, d_=d_: e_.tensor_tensor(out=nstr[:, d_, :], in0=ident[:], in1=tri[:, d_, :], op=ALU.subtract),
              R=[t_id, t_tri], W=[t_nstr])
    P.add('act', lambda e_: e_.activation(out=alog[:], in_=alog[:], func=AF.Exp), R=[t_alog], W=[t_alog])
    bG, tbG = C.bank[6], C.t_bank[6]
    for b in range(C.nb):
        P.add('sp', lambda e_, b=b: e_.dma_start(out=gbt[:], in_=Dm['GBt'][b].rearrange("(tt p) c -> p tt c", p=128)),
              R=[C.t_AB], W=[t_gbt], dma=True)
        P.add('dve', lambda e_: e_.tensor_tensor(out=g[:], in0=gbt[:, :, 0:16], in1=dtb[:], op=ALU.add), R=[t_gbt, t_dtb], W=[t_g])
        P.add('act', lambda e_: e_.activation(out=g[:], in_=g[:], func=AF.Exp), R=[t_g], W=[t_g])
        P.add('dve', lambda e_: e_.tensor_scalar(out=g[:], in0=g[:], scalar1=1.0, scalar2=None, op0=ALU.add), R=[t_g], W=[t_g])
        P.add('act', lambda e_: e_.activation(out=g[:], in_=g[:], func=AF.Ln), R=[t_g], W=[t_g])
        P.add('dve', lambda e_: e_.scalar_tensor_tensor(out=g[:], in0=g[:], scalar=-1.0, in1=alog[:], op0=ALU.mult, op1=ALU.mult),
              R=[t_g, t_alog], W=[t_g])
        P.add('act', lambda e_: e_.activation(out=beta[:], in_=gbt[:, :, 16:32], func=AF.Sigmoid), R=[t_gbt], W=[t_beta])
        for tt in range(NCH):
            P.add('pe', lambda e_, tt=tt: e_.matmul(bG[:, 0:8], tri[:, 0, :], g[:, tt, 0:8], start=True, stop=True), R=[t_tri, t_g], W=[tbG])
            P.add('pe', lambda e_, tt=tt: e_.matmul(bG[:, 8:16], tri[:, 1, :], g[:, tt, 8:16], start=True, stop=True), R=[t_tri, t_g], W=[tbG])
            P.add('pe', lambda e_, tt=tt: e_.matmul(bG[:, 16:32], onesf[:], g[:, tt, :], start=True, stop=True), R=[t_onesf, t_g], W=[tbG])
            P.add('dve', lambda e_, tt=tt: e_.tensor_copy(out=G[:, tt, :], in_=bG[:, 0:16]), R=[], W=[t_G, tbG])
            P.add('dve', lambda e_, tt=tt: e_.tensor_copy(out=Gtot[:, tt, :], in_=bG[:, 16:32]), R=[], W=[t_Gtot, tbG])
        P.add('act', lambda e_: e_.activation(out=eG[:], in_=G[:], func=AF.Exp), R=[t_G], W=[t_eG])
        P.add('dve', lambda e_: e_.tensor_tensor(out=beG[:], in0=beta[:], in1=eG[:], op=ALU.mult), R=[t_beta, t_eG], W=[t_beG])
        P.add('dve', lambda e_: e_.tensor_tensor(out=tmpg[:], in0=Gtot[:], in1=G[:], op=ALU.subtract), R=[t_Gtot, t_G], W=[t_tmpg])
        P.add('act', lambda e_: e_.activation(out=khs[:], in_=tmpg[:], func=AF.Exp), R=[t_tmpg], W=[t_khs])
        P.add('act', lambda e_: e_.activation(out=sdec[:], in_=Gtot[:], func=AF.Exp), R=[t_Gtot], W=[t_sdec])
        for hd in range(NHA):
            r0 = hd * 128
            P.add('sp', lambda e_, b=b, r0=r0: e_.dma_start(out=kT[:], in_=Dm['KB'][b, r0:r0 + 128, :]), R=[C.t_AB], W=[t_kT], dma=True)
            P.add('sp', lambda e_, b=b, r0=r0: e_.dma_start(out=qT[:], in_=Dm['QB'][b, r0:r0 + 128, :]), R=[C.t_AB], W=[t_qT], dma=True)
            P.add('sp', lambda e_, b=b, r0=r0: e_.dma_start(out=zb[:], in_=Dm['ZB'][b, r0:r0 + 128, :]), R=[C.t_AB], W=[t_zb], dma=True)
            P.add('act', lambda e_, b=b, r0=r0: e_.dma_start(
                out=ktm[:], in_=Dm['KBt'][b, :, r0:r0 + 128].rearrange("(tt p) d -> p tt d", p=128)), R=[C.t_AB], W=[t_ktm], dma=True)
            P.add('act', lambda e_, b=b, r0=r0: e_.dma_start(
                out=vtm[:], in_=Dm['VBt'][b, :, r0:r0 + 128].rearrange("(tt p) d -> p tt d", p=128)), R=[C.t_AB], W=[t_vtm], dma=True)
            for d in range(2):
                P.add('pool', lambda e_, d=d: e_.memset(S[d][0][:], 0.0), W=[S[d][1]])
            first_written = [False] * NCH

            def unit(d, c):
                col = d * 8 + hd
                cs = slice(c * 128, (c + 1) * 128)
                u = U[d]
                b1, tb1 = C.bank[3 * d], C.t_bank[3 * d]
                b2, tb2 = C.bank[3 * d + 1], C.t_bank[3 * d + 1]
                b3, tb3 = C.bank[3 * d + 2], C.t_bank[3 * d + 2]
                Sd, tS = S[d]
                A = lambda n: u[n][0]
                t = lambda n: u[n][1]
                gc, bc = g[:, c, col:col + 1], beta[:, c, col:col + 1]
                P.add('pe', lambda e_: e_.matmul(b1[:, 256:384], kT[:, cs], kT[:, cs], start=True, stop=True), R=[t_kT], W=[tb1])
                P.add('pe', lambda e_: e_.matmul(b1[:, 384:512], kT[:, cs], qT[:, cs], start=True, stop=True), R=[t_kT, t_qT], W=[tb1])
                P.add('act', lambda e_: e_.copy(out=A('KK')[:], in_=b1[:, 256:384]), R=[], W=[t('KK'), tb1])
                P.add('act', lambda e_: e_.copy(out=A('QK')[:], in_=b1[:, 384:512]), R=[], W=[t('QK'), tb1])
                KK, tKK = u['KK']; QK, tQK = u['QK']
                yield
                P.add('dve', lambda e_: e_.tensor_scalar(out=A('Dg')[:], in0=tri[:, d, :], scalar1=gc, scalar2=None, op0=ALU.mult),
                      R=[t_tri, t_g], W=[t('Dg')])
                P.add('pool', lambda e_: e_.tensor_scalar(out=A('Ib')[:], in0=ident[:], scalar1=bc, scalar2=None, op0=ALU.mult),
                      R=[t_id, t_beta], W=[t('Ib')])
                P.add('pe', lambda e_: e_.matmul(b1[:, 0:128], onesf[:], A('Dg')[:], start=True, stop=True), R=[t_onesf, t('Dg')], W=[tb1])
                P.add('pe', lambda e_: e_.matmul(b1[:, 128:256], onesf[:], A('Ib')[:], start=True, stop=True), R=[t_onesf, t('Ib')], W=[tb1])
                yield
                P.add('dve', lambda e_: e_.tensor_scalar(out=A('Dmx')[:], in0=b1[:, 0:128], scalar1=G[:, c, col:col + 1], scalar2=0.0,
                                                         op0=ALU.subtract, op1=ALU.min), R=[t_G], W=[t('Dmx'), tb1])
                P.add('act', lambda e_: e_.activation(out=A('eGr')[:], in_=b1[:, 0:128], func=AF.Exp), R=[], W=[t('eGr'), tb1])
                P.add('act', lambda e_: e_.activation(out=A('Dmx')[:], in_=A('Dmx')[:], func=AF.Exp), R=[t('Dmx')], W=[t('Dmx')])
                P.add('dve', lambda e_: e_.tensor_tensor(out=A('t2')[:], in0=b1[:, 128:256], in1=A('Dmx')[:], op=ALU.mult),
                      R=[t('Dmx')], W=[t('t2'), tb1])
                yield
                P.add('pool', lambda e_: e_.tensor_tensor(out=A('t2')[:], in0=A('t2')[:], in1=KK[:], op=ALU.mult), R=[t('t2'), tKK], W=[t('t2')])
                P.add('pool', lambda e_: e_.tensor_tensor(out=A('NT')[:], in0=A('t2')[:], in1=nstr[:, d, :], op=ALU.mult),
                      R=[t('t2'), t_nstr], W=[t('NT')])
                P.add('dve', lambda e_: e_.tensor_tensor(out=A('AT')[:], in0=A('Dmx')[:], in1=tri[:, d, :], op=ALU.mult),
                      R=[t('Dmx'), t_tri], W=[t('AT')])
                P.add('dve', lambda e_: e_.tensor_tensor(out=A('AT')[:], in0=A('AT')[:], in1=QK[:], op=ALU.mult), R=[t('AT'), tQK], W=[t('AT')])
                yield
                P.add('pe', lambda e_: e_.transpose(b2[:, 0:128], A('NT')[:], ident[:]), R=[t('NT'), t_id], W=[tb2])
                yield
                P.add('act', lambda e_: e_.copy(out=A('N')[:], in_=b2[:, 0:128]), R=[], W=[t('N'), tb2])
                P.add('dve', lambda e_: e_.tensor_tensor(out=A('TT')[:], in0=A('NT')[:], in1=ident[:], op=ALU.add), R=[t('NT'), t_id], W=[t('TT')])
                yield
                Np, NTp = 'N', 'NT'
                for j in range(1, 7):
                    Nn, NTn = ('NjA', 'NjTA') if j % 2 else ('NjB', 'NjTB')
                    P.add('pe', lambda e_, Np=Np, NTp=NTp: e_.matmul(b2[:, 0:128], A(NTp)[:], A(Np)[:], start=True, stop=True),
                          R=[t(Np), t(NTp)], W=[tb2])
                    if j < 6:
                        P.add('pe', lambda e_, Np=Np, NTp=NTp: e_.matmul(b2[:, 128:256], A(Np)[:], A(NTp)[:], start=True, stop=True),
                              R=[t(Np), t(NTp)], W=[tb2])
                    yield
                    P.add('act', lambda e_, Nn=Nn: e_.copy(out=A(Nn)[:], in_=b2[:, 0:128]), R=[], W=[t(Nn), tb2])
                    if j < 6:
                        P.add('dve', lambda e_, NTn=NTn: e_.tensor_copy(out=A(NTn)[:], in_=b2[:, 128:256]), R=[], W=[t(NTn), tb2])
                    yield
                    P.add('pe', lambda e_, Nn=Nn: e_.matmul(b2[:, 256:384], A(Nn)[:], A('TT')[:], start=True, stop=True),
                          R=[t(Nn), t('TT')], W=[tb2])
                    yield
                    P.add('dve', lambda e_: e_.tensor_tensor(out=A('TT')[:], in0=b2[:, 256:384], in1=A('TT')[:], op=ALU.add),
                          R=[], W=[t('TT'), tb2])
                    yield
                    Np, NTp = Nn, NTn
                P.add('dve', lambda e_: e_.tensor_scalar(out=A('rhs1')[:], in0=vtm[:, c, :], scalar1=bc, scalar2=None, op0=ALU.mult),
                      R=[t_vtm, t_beta], W=[t('rhs1')])
                P.add('pool', lambda e_: e_.tensor_scalar(out=A('rhs2')[:], in0=ktm[:, c, :], scalar1=beG[:, c, col:col + 1], scalar2=None, op0=ALU.mult),
                      R=[t_ktm, t_beG], W=[t('rhs2')])
                P.add('pool', lambda e_: e_.tensor_scalar(out=A('khat')[:], in0=ktm[:, c, :], scalar1=khs[:, c, col:col + 1], scalar2=None, op0=ALU.mult),
                      R=[t_ktm, t_khs], W=[t('khat')])
                P.add('dve', lambda e_: e_.tensor_tensor(out=A('qg')[:], in0=qT[:, cs], in1=A('eGr')[:], op=ALU.mult), R=[t_qT, t('eGr')], W=[t('qg')])
                yield
                P.add('pe', lambda e_: e_.matmul(b3[:, 0:128], A('rhs2')[:], A('TT')[:], start=True, stop=True), R=[t('rhs2'), t('TT')], W=[tb3])
                yield
                P.add('act', lambda e_: e_.activation(out=A('nwT')[:], in_=b3[:, 0:128], func=AF.Identity, scale=-1.0), R=[], W=[t('nwT'), tb3])
                yield
                P.add('pe', lambda e_: e_.matmul(b3[:, 128:256], A('TT')[:], A('rhs1')[:], start=True, stop=False), R=[t('TT'), t('rhs1')], W=[tb3])
                P.add('pe', lambda e_: e_.matmul(b3[:, 128:256], A('nwT')[:], Sd[:], start=False, stop=True), R=[t('nwT'), tS], W=[tb3])
                yield
                P.add('act', lambda e_: e_.copy(out=A('vnew')[:], in_=b3[:, 128:256]), R=[], W=[t('vnew'), tb3])
                yield
                P.add('pe', lambda e_: e_.matmul(b3[:, 256:384], Sd[:], A('qg')[:], start=True, stop=False), R=[tS, t('qg')], W=[tb3])
                P.add('pe', lambda e_: e_.matmul(b3[:, 256:384], A('vnew')[:], A('AT')[:], start=False, stop=True), R=[t('vnew'), t('AT')], W=[tb3])
                P.add('pe', lambda e_: e_.matmul(b3[:, 384:512], A('khat')[:], A('vnew')[:], start=True, stop=True), R=[t('khat'), t('vnew')], W=[tb3])
                yield
                if not first_written[c]:
                    first_written[c] = True
                    P.add('act', lambda e_: e_.copy(out=OB[:, cs], in_=b3[:, 256:384]), R=[], W=[t_ob, tb3])
                else:
                    P.add('dve', lambda e_: e_.tensor_tensor(out=OB[:, cs], in0=b3[:, 256:384], in1=OB[:, cs], op=ALU.add), R=[], W=[t_ob, tb3])
                P.add('dve', lambda e_: e_.scalar_tensor_tensor(out=Sd[:], in0=Sd[:], scalar=sdec[:, c, col:col + 1], in1=b3[:, 384:512],
                                                                op0=ALU.mult, op1=ALU.add), R=[t_sdec], W=[tS, tb3])
                yield

            def chain(d):
                for c in chunk_order(d):
                    yield from unit(d, c)

            interleave([chain(0), chain(1)])
            merge_head(C, OB, t_ob, zb, t_zb, anorm, t_an, 1, ybf, t_ybf, onesm, t_onesm, tmp, t_tmp, C.bank[7], C.t_bank[7],
                       1024 + r0, b)


def build(cfg):
    nc = bass.Bass("TRN2", target_bir_lowering=False)
    C = Ctx(nc)
    C.nb = nb = cfg.get('nb', NBC)
    ext_in, ext_out = cfg.get('ext_in', ()), cfg.get('ext_out', ())

    def dt(name, shape, dtype=F32):
        if name in ext_in:
            return C.din(name, shape, dtype)
        if name in ext_out:
            return C.dout(name, shape, dtype)
        return C.dscr(name, shape, dtype)

    C.din('cT', [128, KC, 3])
    C.din('b_ada_t', [128, DEPTH, 96])
    C.din('w_ada_t', [DEPTH, 96, 128, KC, 128])
    C.din('ln_t', [DEPTH, 2, 128, KC, 2])
    C.din('ffn_dw_t', [DEPTH, 128, FC, 4])
    C.din('w_up_t', [DEPTH, 2 * FC, 128, KC, 128])
    C.din('w_down_t', [DEPTH, KC, 128, FC, 128])
    for nm in ('XA', 'XB', 'XIN'):
        dt(nm, [nb, D, LSEQ])
    dt('OUT', [nb, D, LLAT])
    dt('HT', [nb, D, LSEQ], BF16)
    dt('HID', [nb, DFF, LSEQ], BF16)
    dt('o_mod', [128, DEPTH, 96, 3])
    C.din('w_out_t', [DEPTH, KC, 128, KC, 128])
    C.din('w_inc_t', [2, 32, 128, KC, 128])
    C.din('w_v_t', [2, 4, 128, KC, 512])
    C.din('rope_t', [128, 2, LLAT])
    C.din('pmT', [128, 128])
    C.din('na_mask', [128, NBT, 128])
    C.din('na_bias_t', [2, NH_C, 128, NBT, 128])
    dt('QT', [nb, D, LSEQ], BF16); dt('KT', [nb, D, LSEQ], BF16); dt('V', [nb, LSEQ, D], BF16)
    dt('YT', [nb, D, LSEQ], BF16)
    C.t_QK, C.t_YT = Tl('QK'), Tl('YT')
    C.din('w_inab_t', [2, 72, 128, KC, 128])
    C.din('w_ia_t', [2, 2, 128, KC, 512])
    C.din('w_gb_t', [2, 128, KC, 32])
    C.din('gconv_t', [2, 128, 24, 5])
    C.din('lb_t', [128, 2, 2, NHA])
    C.din('ident', [128, 128])
    for nm in ('QA', 'GA', 'ZB'):
        dt(nm, [nb, 1024, LSEQ])
    dt('LF', [nb, 2, 1024, LSEQ]); dt('KF', [nb, 2, 1024, LSEQ])
    for nm in ('QB', 'KB'):
        dt(nm, [nb, 1024, LSEQ], BF16)
    for nm in ('VA', 'KBt', 'VBt'):
        dt(nm, [nb, LSEQ, 1024], BF16)
    dt('GBt', [nb, LSEQ, 32])
    C.t_AB = Tl('AB')
    C.din('tri_t', [128, 2, 128])
    C.din('tri_u', [128, 2, 128], U32)
    C.din('dtb_t', [2, 128, NCH, 16])
    C.din('alog_t', [2, 128, NCH, 16])
    C.din('anorm_t', [2, 128, 2])
    C.t_HT, C.t_HID = Tl('HT'), Tl('HID')
    C.t_X = {'XA': Tl('XA'), 'XB': Tl('XB'), 'XIN': Tl('XIN'), 'OUT': Tl('OUT')}
    with ExitStack() as top:
        C.stack = top
        C.mod = C.sb([128, DEPTH, 96, 3], name='mod')
        C.t_mod = Tl('mod')
        C.bank = [C.ps([128, 512], name=f'bank{i}') for i in range(8)]
        C.t_bank = [Tl(f'bank{i}') for i in range(8)]
        csem = {e: top.enter_context(nc.semaphore('c_' + e)) for e in ENGS}
        dsem = {e: [top.enter_context(nc.semaphore(f'd_{e}{i}')) for i in range(RING)] for e in ENGS}
        outs = []
        for ph in cfg['phases']:
            with ExitStack() as phs:
                C.stack = phs
                kind = ph[0]
                if kind == 'ada':
                    phase_ada(C, ph[1])
                elif kind == 'mod':
                    _, l, xs, sh, sc = ph
                    phase_mod0(C, l, C.dram[xs], C.dram['HT'], sh, sc)
                elif kind == 'ffn_up':
                    phase_ffn_up(C, ph[1], C.dram['HT'], C.dram['HID'], skip_ctx=(len(ph) > 2 and ph[2]))
                elif kind == 'ffn_down':
                    _, l, xs, xd, hmod = ph[:5]
                    final = len(ph) > 5 and ph[5]
                    phase_proj_ln(C, l, C.dram['HID'], C.t_HID, FC, 'w_down_t', l, 5, 1, C.dram[xs], C.dram[xd],
                                  C.t_X[xd], C.dram['HT'], hmod, lat_only_out=final)
                elif kind == 'inab':
                    phase_inab(C, ph[1], C.dram['HT'])
                elif kind == 'gdn':
                    phase_gdn(C, ph[1])
                elif kind == 'gla':
                    phase_gla(C, ph[1])
                elif kind == 'qkv':
                    phase_qkv(C, ph[1], C.dram['HT'], ph[2])
                elif kind == 'attn':
                    phase_attn(C, ph[1], ph[2])
                elif kind == 'out_proj':
                    _, l, xs, xd, hmod = ph[:5]
                    phase_proj_ln(C, l, C.dram['YT'], C.t_YT, KC, 'w_out_t', l, 2, 0, C.dram[xs], C.dram[xd],
                                  C.t_X[xd], C.dram['HT'], hmod, skip_ctx=(len(ph) > 5 and ph[5]))
                elif kind == 'dump_mod':
                    C.P.add('sp', lambda e: e.dma_start(out=C.dram['o_mod'][:], in_=C.mod[:]), R=[C.t_mod], W=[C.t_HT], dma=True)
                C.P.barrier()
            C.stack = top
        C.P.barrier()
        with nc.Block() as block:
            C.P.emit(block, csem, dsem)
    return nc


def tileW(w):
    Kd, N = w.shape
    return np.ascontiguousarray(w.reshape(Kd // 128, 128, N // 128, 128).transpose(2, 1, 0, 3))


def tileWwide(w, nw):
    Kd, N = w.shape
    return np.ascontiguousarray(w.reshape(Kd // 128, 128, N // nw, nw).transpose(2, 1, 0, 3))


def host_consts():
    c = {}
    pos = np.arange(LLAT)
    row = (pos // 64).astype(np.float32); col = (pos % 64).astype(np.float32)
    inv = (np.float32(10000.0) ** (-np.arange(32, dtype=np.float32) / np.float32(32))).astype(np.float32)
    rope = np.zeros((128, 2, LLAT), np.float32)
    for d in range(128):
        ang = ((row if d < 64 else col) * inv[d % 32]).astype(np.float32)
        rope[d, 0] = np.cos(ang); rope[d, 1] = np.sin(ang)
    c['rope_t'] = rope
    pm = np.zeros((128, 128), np.float32)
    for i in range(32):
        pm[i, 32 + i] = -1; pm[32 + i, i] = 1; pm[64 + i, 96 + i] = -1; pm[96 + i, 64 + i] = 1
    c['pmT'] = np.ascontiguousarray(pm.T)
    kk = np.arange(128); krl, kc = kk // 64, kk % 64
    qq = np.arange(128); qrl, qc = qq // 64, qq % 64
    c0 = np.clip(qc - 8, 0, 48)
    col_ok = (kc[:, None] >= c0[None, :]) & (kc[:, None] < c0[None, :] + 16)
    deltas = [-6, -4, -2, 0, 2, 4, 6] + [-4, -2, 0, 2, 4]
    mask = np.zeros((128, NBT, 128), np.float32)
    dr = np.zeros((NBT, 128, 128), np.int64)
    for t, dlt in enumerate(deltas):
        rel = dlt + krl[:, None] - qrl[None, :]
        ok = col_ok if t < 7 else (col_ok & (rel >= -4) & (rel < 4))
        mask[:, t, :] = np.where(ok, 0.0, -30000.0)
        dr[t] = np.clip(rel + 7, 0, 14)
    c['na_mask'] = mask
    c['_dr'] = dr
    c['_dc'] = np.clip(kc[:, None] - qc[None, :] + 15, 0, 30)
    return c


def host_bias_tiles(rel_bias, consts):
    dr, dc = consts['_dr'], consts['_dc']
    g = rel_bias[:, :, dr, dc[None]]
    return np.ascontiguousarray(g.transpose(0, 1, 3, 2, 4)).astype(np.float32)


def host_even_weights(inputs, e_list=(0, 1)):
    w = {}
    w['w_inab_t'] = np.zeros((2, 72, 128, KC, 128), np.float32)
    w['w_ia_t'] = np.zeros((2, 2, 128, KC, 512), np.float32)
    w['w_gb_t'] = np.zeros((2, 128, KC, 32), np.float32)
    for e in e_list:
        wi = inputs['w_in_ab'][e]
        w['w_inab_t'][e] = tileW(wi[:, :9216])
        w['w_ia_t'][e] = tileWwide(wi[:, 3072:4096], 512)
        w['w_gb_t'][e] = np.ascontiguousarray(wi[:, 9216:9248].reshape(KC, 128, 32).transpose(1, 0, 2))
    gc = inputs['gdn_conv']
    w['gconv_t'] = np.ascontiguousarray(gc.reshape(2, 5, 24, 128).transpose(0, 3, 2, 1)).astype(np.float32)
    lb = inputs['hgrn_lb']
    w['lb_t'] = np.ascontiguousarray(lb.reshape(2, 2, NHA, 128).transpose(3, 0, 1, 2)).astype(np.float32)
    w['ident'] = np.eye(128, dtype=np.float32)
    ii = np.arange(128)
    w['anorm_t'] = np.ascontiguousarray(np.stack([inputs['hgrn_norm'], inputs['gdn_norm']], -1)).astype(np.float32)
    w['tri_t'] = np.ascontiguousarray(np.stack([(ii[:, None] <= ii[None, :]), (ii[:, None] >= ii[None, :])], 1)).astype(np.float32)
    w['tri_u'] = np.ascontiguousarray(w['tri_t'].astype(np.uint32))
    dtb = inputs['gdn_dt_bias'].reshape(2, 16).astype(np.float32)
    alg = inputs['gdn_a_log'].reshape(2, 16).astype(np.float32)
    w['dtb_t'] = np.ascontiguousarray(np.broadcast_to(dtb[:, None, None, :], (2, 128, NCH, 16)))
    w['alog_t'] = np.ascontiguousarray(np.broadcast_to(alg[:, None, None, :], (2, 128, NCH, 16)))
    return w


def full_phases():
    ph = [('ada', list(range(DEPTH))), ('mod', 0, 'XIN', 0, 1)]
    src = 'XIN'
    for l in range(DEPTH):
        last = l == DEPTH - 1
        if l % 2 == 0:
            ph += [('inab', l), ('gla', l), ('gdn', l)]
        else:
            ph += [('qkv', l, not last), ('attn', l, not last)]
        ph += [('out_proj', l, src, 'XB', (l, 3, 4), last), ('ffn_up', l, last)]
        if last:
            ph += [('ffn_down', l, 'XB', 'OUT', None, True)]
        else:
            ph += [('ffn_down', l, 'XB', 'XA', (l + 1, 0, 1))]
        src = 'XA'
    return ph


def host_shared(inputs):
    f32 = np.float32
    m = {}
    m['w_ada_t'] = np.stack([tileW(inputs['w_ada'][l]) for l in range(DEPTH)])
    m['b_ada_t'] = np.ascontiguousarray(inputs['b_ada'].reshape(DEPTH, 96, 128).transpose(2, 0, 1)).astype(f32)
    m['ln_t'] = np.ascontiguousarray(np.stack([inputs['ln_g'], inputs['ln_b']], -1).reshape(DEPTH, 2, KC, 128, 2)
                                     .transpose(0, 1, 3, 2, 4)).astype(f32)
    dw = np.concatenate([inputs['ffn_w_dw'], inputs['ffn_b_dw'][:, None, :]], 1)
    m['ffn_dw_t'] = np.ascontiguousarray(dw.reshape(DEPTH, 4, FC, 128).transpose(0, 3, 2, 1)).astype(f32)
    m['w_up_t'] = np.stack([tileW(inputs['ffn_w_up'][l]) for l in range(DEPTH)])
    m['w_down_t'] = np.stack([tileW(inputs['ffn_w_down'][l]) for l in range(DEPTH)])
    m['w_out_t'] = np.stack([tileW(inputs['w_out_ab'][l // 2] if l % 2 == 0 else inputs['w_out_c'][l // 2]) for l in range(DEPTH)])
    m['w_inc_t'] = np.stack([tileW(inputs['w_in_c'][o][:, :2 * D]) for o in range(2)])
    m['w_v_t'] = np.stack([tileWwide(inputs['w_in_c'][o][:, 2 * D:], 512) for o in range(2)])
    hc = host_consts()
    for k in ('rope_t', 'pmT', 'na_mask'):
        m[k] = hc[k]
    m['na_bias_t'] = host_bias_tiles(inputs['na_rel_bias'], hc)
    m.update(host_even_weights(inputs))
    return m


def host_core(inputs, bs):
    xin = np.stack([np.concatenate([inputs['ctx'][b].T, inputs['x'][b].T], 1) for b in bs]).astype(np.float32)
    cols = [inputs['c'][b] for b in bs]
    while len(cols) < 2:
        cols.append(cols[-1])
    cc = np.stack(cols + [inputs['c_ctx']], -1)
    cT = np.ascontiguousarray(cc.reshape(KC, 128, 3).transpose(1, 0, 2)).astype(np.float32)
    return {'XIN': np.ascontiguousarray(xin), 'cT': cT}


def kernel(**inputs):
    inputs = {k: np.asarray(v) for k, v in inputs.items()}
    n = 8
    shared = host_shared(inputs)
    nc = build({'nb': NBC, 'ext_in': ('XIN',), 'ext_out': ('OUT',), 'phases': full_phases()})
    in_maps = []
    for core in range(n):
        m = dict(shared)
        m.update(host_core(inputs, [NBC * core + i for i in range(NBC)]))
        in_maps.append(m)
    res = run_bass_kernel_spmd(nc, in_maps, core_ids=list(range(n)))
    out = np.empty((n * NBC, LLAT, D), np.float32)
    for core in range(n):
        o = np.asarray(res.results[core]['OUT'])
        for i in range(NBC):
            out[NBC * core + i] = o[i].T
    return out
```

```python
import numpy as np
from contextlib import ExitStack
import concourse.bass as bass
import concourse.mybir as mybir
from concourse.alu_op_type import AluOpType as ALU
from concourse.bass_utils import run_bass_kernel_spmd

AF = mybir.ActivationFunctionType
F32 = mybir.dt.float32
BF16 = mybir.dt.bfloat16
U32 = mybir.dt.uint32

ENGS = ('pe', 'act', 'dve', 'pool', 'sp')
RING = 12


class Tl:
    __slots__ = ('name', 'w', 'r')

    def __init__(self, name=''):
        self.name = name
        self.w = None
        self.r = []


class Op:
    __slots__ = ('eng', 'fn', 'dma', 'pos', 'cpos', 'waits', 'sig', 'sem', 'val')

    def __init__(self, eng, fn, dma):
        self.eng = eng
        self.fn = fn
        self.dma = dma
        self.waits = []
        self.sig = False
        self.sem = None
        self.val = None


class Prog:
    def __init__(self, nc):
        self.nc = nc
        self.streams = {e: [] for e in ENGS}
        self.ccount = {e: 0 for e in ENGS}
        self.ndma = {e: 0 for e in ENGS}
        self.dmaops = {e: [] for e in ENGS}
        self.wpos = {}
        self.wdma = {}
        self.lastc = {e: None for e in ENGS}

    def _need(self, op, d, kind):
        if d is op:
            return
        if d.dma:
            key = (op.eng, d.eng, d.sem)
            if self.wdma.get(key, 0) >= d.val:
                return
            self.wdma[key] = d.val
            op.waits.append(d)
            return
        if d.eng == op.eng and not op.dma:
            if op.eng == 'pe':
                return
            if kind != 'raw':
                return
            if self.ccount[op.eng] - d.cpos > 2:
                return
        key = (op.eng, d.eng)
        if self.wpos.get(key, -1) >= d.cpos:
            return
        self.wpos[key] = d.cpos
        d.sig = True
        op.waits.append(d)

    def add(self, eng, fn, R=(), W=(), dma=False):
        op = Op(eng, fn, dma)
        op.cpos = self.ccount[eng]
        if dma:
            j = self.ndma[eng]
            self.ndma[eng] += 1
            op.sem = j % RING
            op.val = 16 * (j // RING + 1)
            if j >= RING:
                self._need(op, self.dmaops[eng][j - RING], 'ring')
            self.dmaops[eng].append(op)
        for t in R:
            if t.w is not None:
                self._need(op, t.w, 'raw')
        for t in W:
            if t.w is not None:
                self._need(op, t.w, 'waw')
            for r in t.r:
                self._need(op, r, 'war')
        for t in R:
            t.r.append(op)
        for t in W:
            t.w = op
            t.r = []
        if not dma and fn is not None:
            self.ccount[eng] += 1
            self.lastc[eng] = op
        self.streams[eng].append(op)
        return op

    def barrier(self):
        lasts = [self.lastc[e] for e in ENGS if self.lastc[e] is not None]
        dmas = []
        for e in ENGS:
            dmas += self.dmaops[e][-RING:]
        for e in ENGS:
            op = Op(e, None, False)
            op.cpos = self.ccount[e]
            for d in lasts + dmas:
                if d.eng == e and not d.dma:
                    continue
                self._need(op, d, 'raw')
            self.streams[e].append(op)

    def emit(self, block, csem, dsem):
        nc = self.nc
        for e in ENGS:
            c = 0
            for op in self.streams[e]:
                if not op.dma and op.sig:
                    c += 1
                    op.val = c

        def semof(d):
            return dsem[d.eng][d.sem] if d.dma else csem[d.eng]

        def replay(e):
            def f(eng):
                for op in self.streams[e]:
                    ws = {}
                    for d in op.waits:
                        s = semof(d)
                        k = id(s)
                        if k not in ws or ws[k][1] < d.val:
                            ws[k] = (s, d.val)
                    for s, v in ws.values():
                        eng.wait_ge(s, v)
                    if op.fn is None:
                        continue
                    inst = op.fn(eng)
                    if op.dma:
                        inst.then_inc(dsem[e][op.sem], 16)
                    elif op.sig:
                        inst.then_inc(csem[e], 1)
            return f
        block.tensor(replay('pe'))
        block.scalar(replay('act'))
        block.vector(replay('dve'))
        block.gpsimd(replay('pool'))
        block.sync(replay('sp'))


D = 2048
KC = 16
DEPTH = 4
NBC = 2
LCTX, LLAT = 256, 2048
LSEQ = LCTX + LLAT
DFF = 5504
FC = 43
ALPHA = (2 * DEPTH) ** 0.25
EPS = 1e-6
NAB = 9248


class Ctx:
    def __init__(self, nc):
        self.nc = nc
        self.P = Prog(nc)
        self.dram = {}
        self.stack = None
        self.n = 0

    def din(self, name, shape, dt=F32):
        self.dram[name] = self.nc.dram_tensor(name, list(shape), dt, kind="ExternalInput").ap()
        return self.dram[name]

    def dout(self, name, shape, dt=F32):
        self.dram[name] = self.nc.dram_tensor(name, list(shape), dt, kind="ExternalOutput").ap()
        return self.dram[name]

    def dscr(self, name, shape, dt=F32):
        self.dram[name] = self.nc.dram_tensor(name, list(shape), dt, kind="Internal").ap()
        return self.dram[name]

    def sb(self, shape, dt=F32, name=None):
        self.n += 1
        return self.stack.enter_context(self.nc.sbuf_tensor(name or f"s{self.n}", list(shape), dt))

    def ps(self, shape, dt=F32, name=None):
        self.n += 1
        return self.stack.enter_context(self.nc.psum_tensor(name or f"p{self.n}", list(shape), dt))


def phase_ada(C, layers):
    nc, P = C.nc, C.P
    mod = C.mod
    cT = C.sb([128, KC, 3]); sT = C.sb([128, KC, 3], BF16)
    bt = C.sb([128, DEPTH, 96])
    t_c, t_s, t_b, t_mod = Tl(), Tl(), Tl(), C.t_mod
    P.add('sp', lambda e: e.dma_start(out=cT[:], in_=C.dram['cT'][:]), W=[t_c], dma=True)
    P.add('sp', lambda e: e.dma_start(out=bt[:], in_=C.dram['b_ada_t'][:]), W=[t_b], dma=True)
    P.add('act', lambda e: e.activation(out=sT[:], in_=cT[:], func=AF.Silu), R=[t_c], W=[t_s])
    NW = 3
    wts = [C.sb([128, KC, 128], BF16) for _ in range(NW)]
    t_w = [Tl() for _ in range(NW)]
    pss, t_p = C.bank[:4], C.t_bank[:4]
    i = 0
    for l in layers:
        for oc in range(96):
            w, tw, ps, tp = wts[i % NW], t_w[i % NW], pss[i % 4], t_p[i % 4]
            P.add('pool', lambda e, w=w, l=l, oc=oc: e.dma_start(out=w[:], in_=C.dram['w_ada_t'][l, oc], max_dma_last_dim=4096),
                  W=[tw], dma=True)
            for kc in range(KC):
                P.add('pe', lambda e, w=w, ps=ps, kc=kc: e.matmul(ps[:, 0:3], w[:, kc, :], sT[:, kc, :],
                                                                   start=(kc == 0), stop=(kc == KC - 1)),
                      R=[tw, t_s], W=[tp])
            add1 = 1.0 if (oc // 16) in (1, 4) else 0.0
            P.add('dve', lambda e, ps=ps, l=l, oc=oc, add1=add1: e.tensor_scalar(
                out=mod[:, l, oc, :], in0=ps[:, 0:3], scalar1=bt[:, l, oc:oc + 1], scalar2=add1,
                op0=ALU.add, op1=ALU.add), R=[tp, t_b], W=[t_mod])
            i += 1


def seq_tiles(L):
    return [(o, min(512, L - o)) for o in range(0, L, 512)]


SEQS = ((0, LCTX), (LCTX, LLAT))


def phase_mod0(C, l, xsrc, HT, sh=0, sc=1):
    P, mod = C.P, C.mod
    NB = 3
    xin = [C.sb([128, 512]) for _ in range(NB)]; hb = [C.sb([128, 512], BF16) for _ in range(NB)]
    t_x = [Tl() for _ in range(NB)]; t_h = [Tl() for _ in range(NB)]
    i = 0
    for b in range(C.nb):
        for kc in range(KC):
            for (t0, n) in seq_tiles(LSEQ):
                x, h, tx, th = xin[i % NB], hb[i % NB], t_x[i % NB], t_h[i % NB]
                j = 2 if t0 < LCTX else b
                P.add('sp', lambda e, x=x, b=b, kc=kc, t0=t0, n=n: e.dma_start(
                    out=x[:, :n], in_=xsrc[b, kc * 128:(kc + 1) * 128, t0:t0 + n]), W=[tx], dma=True)
                if t0 < LCTX < t0 + n:
                    for (a, z, jj) in ((0, LCTX - t0, 2), (LCTX - t0, n, b)):
                        P.add('act', lambda e, x=x, h=h, a=a, z=z, jj=jj, kc=kc: e.activation(
                            out=h[:, a:z], in_=x[:, a:z], func=AF.Identity,
                            scale=mod[:, l, sc * 16 + kc, jj:jj + 1], bias=mod[:, l, sh * 16 + kc, jj:jj + 1]),
                            R=[tx, C.t_mod], W=[th])
                else:
                    P.add('act', lambda e, x=x, h=h, n=n, j=j, kc=kc: e.activation(
                        out=h[:, :n], in_=x[:, :n], func=AF.Identity,
                        scale=mod[:, l, sc * 16 + kc, j:j + 1], bias=mod[:, l, sh * 16 + kc, j:j + 1]),
                        R=[tx, C.t_mod], W=[th])
                P.add('act', lambda e, h=h, b=b, kc=kc, t0=t0, n=n: e.dma_start(
                    out=HT[b, kc * 128:(kc + 1) * 128, t0:t0 + n], in_=h[:, :n]), R=[th], W=[C.t_HT], dma=True)
                i += 1


def load_ht(C, HT, b, ht, t_ht):
    for q in range(4):
        C.P.add('sp', lambda e, q=q: e.dma_start(
            out=ht[:, 4 * q:4 * q + 4, :], in_=HT[b, 512 * q:512 * (q + 1), :].rearrange("(kc p) t -> p kc t", p=128)),
            R=[C.t_HT], W=[t_ht], dma=True)


def phase_ffn_up(C, l, HT, HID, skip_ctx=False):
    P = C.P
    ht = C.sb([128, KC, LSEQ], BF16); t_ht = Tl()
    dw = C.sb([128, FC, 4]); t_dw = Tl()
    P.add('sp', lambda e: e.dma_start(out=dw[:], in_=C.dram['ffn_dw_t'][l]), W=[t_dw], dma=True)
    NW = 2
    wa = [C.sb([128, KC, 128], BF16) for _ in range(NW)]; wg = [C.sb([128, KC, 128], BF16) for _ in range(NW)]
    t_wa = [Tl() for _ in range(NW)]; t_wg = [Tl() for _ in range(NW)]
    NA = 2
    a_sb = [C.sb([128, LLAT + 2]) for _ in range(NA)]; g_sb = [C.sb([128, LLAT]) for _ in range(NA)]
    acc = [C.sb([128, LLAT]) for _ in range(NA)]; hid = [C.sb([128, LLAT], BF16) for _ in range(NA)]
    t_a = [Tl() for _ in range(NA)]; t_g = [Tl() for _ in range(NA)]; t_acc = [Tl() for _ in range(NA)]
    t_hid = [Tl() for _ in range(NA)]
    for k in range(NA):
        P.add('pool', lambda e, k=k: e.memset(a_sb[k][:, 0:1], 0.0), W=[t_a[k]])
    ib = 0; ia = 0; iw = 0
    for b in range(C.nb):
        load_ht(C, HT, b, ht, t_ht)
        for j in range(FC):
            w_a, w_g, twa, twg = wa[iw % NW], wg[iw % NW], t_wa[iw % NW], t_wg[iw % NW]; iw += 1
            P.add('pool', lambda e, w=w_a, j=j: e.dma_start(out=w[:], in_=C.dram['w_up_t'][l, j], max_dma_last_dim=4096),
                  W=[twa], dma=True)
            P.add('pool', lambda e, w=w_g, j=j: e.dma_start(out=w[:], in_=C.dram['w_up_t'][l, FC + j], max_dma_last_dim=4096),
                  W=[twg], dma=True)
            for (s0, L) in (SEQS[1:] if skip_ctx else SEQS):
                k = ia % NA; ia += 1
                A, G, AC, HD = a_sb[k], g_sb[k], acc[k], hid[k]
                P.add('pool', lambda e, A=A, L=L: e.memset(A[:, L + 1:L + 2], 0.0), W=[t_a[k]])
                for (t0, n) in seq_tiles(L):
                    pa, pg = C.bank[ib % 8], C.bank[(ib + 1) % 8]
                    tpa, tpg = C.t_bank[ib % 8], C.t_bank[(ib + 1) % 8]; ib += 2
                    for kc in range(KC):
                        P.add('pe', lambda e, pa=pa, w=w_a, kc=kc, s0=s0, t0=t0, n=n: e.matmul(
                            pa[:, :n], w[:, kc, :], ht[:, kc, s0 + t0:s0 + t0 + n], start=(kc == 0), stop=(kc == KC - 1)),
                            R=[twa, t_ht], W=[tpa])
                    for kc in range(KC):
                        P.add('pe', lambda e, pg=pg, w=w_g, kc=kc, s0=s0, t0=t0, n=n: e.matmul(
                            pg[:, :n], w[:, kc, :], ht[:, kc, s0 + t0:s0 + t0 + n], start=(kc == 0), stop=(kc == KC - 1)),
                            R=[twg, t_ht], W=[tpg])
                    P.add('act', lambda e, A=A, pa=pa, t0=t0, n=n: e.copy(out=A[:, 1 + t0:1 + t0 + n], in_=pa[:, :n]),
                          R=[tpa], W=[t_a[k]])
                    P.add('dve', lambda e, G=G, pg=pg, t0=t0, n=n: e.tensor_copy(out=G[:, t0:t0 + n], in_=pg[:, :n]),
                          R=[tpg], W=[t_g[k]])
                P.add('pool', lambda e, A=A, AC=AC, L=L, j=j: e.tensor_scalar(
                    out=AC[:, :L], in0=A[:, 1:L + 1], scalar1=dw[:, j, 1:2], scalar2=dw[:, j, 3:4],
                    op0=ALU.mult, op1=ALU.add), R=[t_a[k], t_dw], W=[t_acc[k]])
                P.add('dve', lambda e, A=A, AC=AC, L=L, j=j: e.scalar_tensor_tensor(
                    out=AC[:, :L], in0=A[:, 0:L], scalar=dw[:, j, 0:1], in1=AC[:, :L],
                    op0=ALU.mult, op1=ALU.add), R=[t_a[k], t_dw, t_acc[k]], W=[t_acc[k]])
                P.add('dve', lambda e, A=A, AC=AC, L=L, j=j: e.scalar_tensor_tensor(
                    out=AC[:, :L], in0=A[:, 2:L + 2], scalar=dw[:, j, 2:3], in1=AC[:, :L],
                    op0=ALU.mult, op1=ALU.add), R=[t_a[k], t_dw, t_acc[k]], W=[t_acc[k]])
                P.add('act', lambda e, AC=AC, L=L: e.activation(out=AC[:, :L], in_=AC[:, :L], func=AF.Gelu),
                      R=[t_acc[k]], W=[t_acc[k]])
                P.add('dve', lambda e, AC=AC, G=G, HD=HD, L=L: e.tensor_tensor(
                    out=HD[:, :L], in0=AC[:, :L], in1=G[:, :L], op=ALU.mult),
                    R=[t_acc[k], t_g[k]], W=[t_hid[k]])
                P.add('sp', lambda e, HD=HD, b=b, j=j, s0=s0, L=L: e.dma_start(
                    out=HID[b, j * 128:(j + 1) * 128, s0:s0 + L], in_=HD[:, :L]),
                    R=[t_hid[k]], W=[C.t_HID], dma=True)


def phase_proj_ln(C, l, SRC, t_src, kcn, wname, widx, mgate, lnidx, xsrc, xdst, t_xdst, HT, hmod, lat_only_out=False, skip_ctx=False):
    P, mod = C.P, C.mod
    NACT = 2
    acts = [C.sb([128, kcn, 512], BF16) for _ in range(NACT)]; t_acts = [Tl() for _ in range(NACT)]
    iact = 0
    NW = 3
    w = [C.sb([128, kcn, 128], BF16) for _ in range(NW)]; t_w = [Tl() for _ in range(NW)]
    r = C.sb([128, KC, 512]); t_r = [Tl() for _ in range(KC)]
    NX = 3
    xo = [C.sb([128, 512]) for _ in range(NX)]; t_xo = [Tl() for _ in range(NX)]
    sq = [C.sb([128, 512]) for _ in range(2)]; t_sq = [Tl() for _ in range(2)]
    mean = C.sb([128, 512]); rstd = C.sb([128, 512]); t_mean = Tl(); t_rstd = Tl()
    xn = [C.sb([128, 512]) for _ in range(NX)]; t_xn = [Tl() for _ in range(NX)]
    hb = [C.sb([128, 512], BF16) for _ in range(NX)]; t_hb = [Tl() for _ in range(NX)]
    ones = C.sb([128, 128]); t_ones = Tl()
    lnp = C.sb([128, KC, 2]); t_lnp = Tl()
    P.add('pool', lambda e: e.memset(ones[:], 1.0 / D), W=[t_ones])
    P.add('sp', lambda e: e.dma_start(out=lnp[:], in_=C.dram['ln_t'][l, lnidx]), W=[t_lnp], dma=True)
    iw = 0; ix = 0; ib = 0; isq = 0
    pS1, pS2, tS1, tS2 = C.bank[6], C.bank[7], C.t_bank[6], C.t_bank[7]
    for b in range(C.nb):
        for (s0, L) in SEQS:
            j = 2 if s0 < LCTX else b
            if (lat_only_out or skip_ctx) and s0 < LCTX:
                continue
            for (t0, n) in seq_tiles(L):
                T0 = s0 + t0
                act, t_act = acts[iact % NACT], t_acts[iact % NACT]; iact += 1
                nq = 4 if kcn >= 16 else 1
                step = (kcn + nq - 1) // nq
                for q in range(0, kcn, step):
                    z = min(kcn, q + step)
                    P.add('sp', lambda e, q=q, z=z, T0=T0, n=n, b=b, act=act: e.dma_start(
                        out=act[:, q:z, :n], in_=SRC[b, q * 128:z * 128, T0:T0 + n].rearrange("(kc p) t -> p kc t", p=128)),
                        R=[t_src], W=[t_act], dma=True)
                for o in range(KC):
                    wt, tw = w[iw % NW], t_w[iw % NW]; iw += 1
                    P.add('pool', lambda e, wt=wt, o=o: e.dma_start(
                        out=wt[:], in_=C.dram[wname][widx, o], max_dma_last_dim=4096), W=[tw], dma=True)
                    pb, tpb = C.bank[ib % 6], C.t_bank[ib % 6]; ib += 1
                    for kc in range(kcn):
                        P.add('pe', lambda e, pb=pb, wt=wt, kc=kc, n=n, act=act: e.matmul(
                            pb[:, :n], wt[:, kc, :], act[:, kc, :n], start=(kc == 0), stop=(kc == kcn - 1)),
                            R=[tw, t_act], W=[tpb])
                    x, tx = xo[ix % NX], t_xo[ix % NX]; ix += 1
                    P.add('sp', lambda e, x=x, b=b, o=o, T0=T0, n=n: e.dma_start(
                        out=x[:, :n], in_=xsrc[b, o * 128:(o + 1) * 128, T0:T0 + n]), W=[tx], dma=True)
                    P.add('act', lambda e, x=x, n=n: e.activation(out=x[:, :n], in_=x[:, :n], func=AF.Identity, scale=float(ALPHA)),
                          R=[tx], W=[tx])
                    P.add('dve', lambda e, pb=pb, x=x, o=o, n=n, j=j: e.scalar_tensor_tensor(
                        out=r[:, o, :n], in0=pb[:, :n], scalar=mod[:, l, mgate * 16 + o, j:j + 1], in1=x[:, :n],
                        op0=ALU.mult, op1=ALU.add), R=[tpb, tx, C.t_mod], W=[t_r[o]])
                    s_, ts = sq[isq % 2], t_sq[isq % 2]; isq += 1
                    P.add('act', lambda e, s_=s_, o=o, n=n: e.activation(out=s_[:, :n], in_=r[:, o, :n], func=AF.Square),
                          R=[t_r[o]], W=[ts])
                    P.add('pe', lambda e, o=o, n=n: e.matmul(pS1[:, :n], ones[:], r[:, o, :n], start=(o == 0), stop=(o == KC - 1)),
                          R=[t_ones, t_r[o]], W=[tS1])
                    P.add('pe', lambda e, s_=s_, o=o, n=n: e.matmul(pS2[:, :n], ones[:], s_[:, :n], start=(o == 0), stop=(o == KC - 1)),
                          R=[t_ones, ts], W=[tS2])
                P.add('act', lambda e, n=n: e.copy(out=mean[:, :n], in_=pS1[:, :n]), R=[tS1], W=[t_mean])
                P.add('dve', lambda e, n=n: e.tensor_tensor(out=rstd[:, :n], in0=mean[:, :n], in1=mean[:, :n], op=ALU.mult),
                      R=[t_mean], W=[t_rstd])
                P.add('dve', lambda e, n=n: e.tensor_tensor(out=rstd[:, :n], in0=pS2[:, :n], in1=rstd[:, :n], op=ALU.subtract),
                      R=[tS2, t_rstd], W=[t_rstd])
                P.add('dve', lambda e, n=n: e.tensor_scalar(out=rstd[:, :n], in0=rstd[:, :n], scalar1=float(EPS), scalar2=None,
                                                            op0=ALU.add), R=[t_rstd], W=[t_rstd])
                P.add('act', lambda e, n=n: e.activation(out=rstd[:, :n], in_=rstd[:, :n], func=AF.Sqrt), R=[t_rstd], W=[t_rstd])
                P.add('dve', lambda e, n=n: e.reciprocal(out=rstd[:, :n], in_=rstd[:, :n]), R=[t_rstd], W=[t_rstd])
                for o in range(KC):
                    k = ix % NX; ix += 1
                    X, tX, H, tH = xn[k], t_xn[k], hb[k], t_hb[k]
                    P.add('dve', lambda e, X=X, o=o, n=n: e.tensor_tensor(out=X[:, :n], in0=r[:, o, :n], in1=mean[:, :n], op=ALU.subtract),
                          R=[t_r[o], t_mean], W=[tX])
                    P.add('pool', lambda e, X=X, n=n: e.tensor_tensor(out=X[:, :n], in0=X[:, :n], in1=rstd[:, :n], op=ALU.mult),
                          R=[tX, t_rstd], W=[tX])
                    P.add('act', lambda e, X=X, o=o, n=n: e.activation(out=X[:, :n], in_=X[:, :n], func=AF.Identity,
                                                                       scale=lnp[:, o, 0:1], bias=lnp[:, o, 1:2]),
                          R=[tX, t_lnp], W=[tX])
                    if lat_only_out:
                        P.add('sp', lambda e, X=X, b=b, o=o, t0=t0, n=n: e.dma_start(
                            out=xdst[b, o * 128:(o + 1) * 128, t0:t0 + n], in_=X[:, :n]), R=[tX], W=[t_xdst], dma=True)
                    else:
                        P.add('sp', lambda e, X=X, b=b, o=o, T0=T0, n=n: e.dma_start(
                            out=xdst[b, o * 128:(o + 1) * 128, T0:T0 + n], in_=X[:, :n]), R=[tX], W=[t_xdst], dma=True)
                    if hmod is not None:
                        hl, hsh, hsc = hmod
                        P.add('act', lambda e, X=X, H=H, o=o, n=n, j=j: e.activation(
                            out=H[:, :n], in_=X[:, :n], func=AF.Identity,
                            scale=mod[:, hl, hsc * 16 + o, j:j + 1], bias=mod[:, hl, hsh * 16 + o, j:j + 1]),
                            R=[tX, C.t_mod], W=[tH])
                        P.add('act', lambda e, H=H, b=b, o=o, T0=T0, n=n: e.dma_start(
                            out=HT[b, o * 128:(o + 1) * 128, T0:T0 + n], in_=H[:, :n]), R=[tH], W=[C.t_HT], dma=True)


NH_C = 16
DH = 128
QSCALE = DH ** -0.5
NBT = 12


def na_plan(qt):
    br = 2 * qt
    if br <= 2:
        rows = [0, 2, 4, 6]
        return rows, (rows[0] - br + 6) // 2
    if br >= 28:
        rows = [24, 26, 28, 30]
        return rows, (rows[0] - br + 6) // 2
    return [br - 4, br - 2, br, br + 2, br + 4], 7


def phase_qkv(C, l, HT, need_ctx):
    P = C.P
    o = l // 2
    QT, KT, V = C.dram['QT'], C.dram['KT'], C.dram['V']
    ht = C.sb([128, KC, LSEQ], BF16); t_ht = Tl()
    rope = C.sb([128, 2, LLAT]); t_rope = Tl()
    pmT = C.sb([128, 128]); t_pm = Tl()
    P.add('sp', lambda e: e.dma_start(out=rope[:], in_=C.dram['rope_t'][:]), W=[t_rope], dma=True)
    P.add('sp', lambda e: e.dma_start(out=pmT[:], in_=C.dram['pmT'][:]), W=[t_pm], dma=True)
    NW = 2
    w = [C.sb([128, KC, 128], BF16) for _ in range(NW)]; t_w = [Tl() for _ in range(NW)]
    wv = [C.sb([128, KC, 512], BF16) for _ in range(NW)]; t_wv = [Tl() for _ in range(NW)]
    NX = 3
    xs = [C.sb([128, 512]) for _ in range(NX)]; t_xs = [Tl() for _ in range(NX)]
    t1 = [C.sb([128, 512]) for _ in range(NX)]; t_t1 = [Tl() for _ in range(NX)]
    ob = [C.sb([128, 512], BF16) for _ in range(NX)]; t_ob = [Tl() for _ in range(NX)]
    iw = 0; ix = 0; ib = 0
    for b in range(C.nb):
        load_ht(C, HT, b, ht, t_ht)
        for j in range(32):
            isq = j < 16
            dst = QT if isq else KT
            wt, tw = w[iw % NW], t_w[iw % NW]; iw += 1
            P.add('pool', lambda e, wt=wt, j=j: e.dma_start(out=wt[:], in_=C.dram['w_inc_t'][o, j], max_dma_last_dim=4096),
                  W=[tw], dma=True)
            for (s0, L) in SEQS:
                isctx = s0 < LCTX
                if isctx and isq and not need_ctx:
                    continue
                for (t0, n) in seq_tiles(L):
                    pb, tpb = C.bank[ib % 4], C.t_bank[ib % 4]
                    pr, tpr = C.bank[4 + ib % 4], C.t_bank[4 + ib % 4]; ib += 1
                    for kc in range(KC):
                        P.add('pe', lambda e, pb=pb, wt=wt, kc=kc, s0=s0, t0=t0, n=n: e.matmul(
                            pb[:, :n], wt[:, kc, :], ht[:, kc, s0 + t0:s0 + t0 + n], start=(kc == 0), stop=(kc == KC - 1)),
                            R=[tw, t_ht], W=[tpb])
                    k = ix % NX; ix += 1
                    X, tX, T1, tT1, O, tO = xs[k], t_xs[k], t1[k], t_t1[k], ob[k], t_ob[k]
                    sc = float(QSCALE) if isq else 1.0
                    if isctx:
                        P.add('act', lambda e, O=O, pb=pb, n=n, sc=sc: e.activation(out=O[:, :n], in_=pb[:, :n], func=AF.Identity, scale=sc),
                              R=[tpb], W=[tO])
                    else:
                        P.add('act', lambda e, X=X, pb=pb, n=n, sc=sc: e.activation(out=X[:, :n], in_=pb[:, :n], func=AF.Identity, scale=sc),
                              R=[tpb], W=[tX])
                        P.add('pe', lambda e, pr=pr, X=X, n=n: e.matmul(pr[:, :n], pmT[:], X[:, :n], start=True, stop=True),
                              R=[t_pm, tX], W=[tpr])
                        P.add('pool', lambda e, X=X, T1=T1, t0=t0, n=n: e.tensor_tensor(
                            out=T1[:, :n], in0=X[:, :n], in1=rope[:, 0, t0:t0 + n], op=ALU.mult), R=[tX, t_rope], W=[tT1])
                        P.add('dve', lambda e, X=X, pr=pr, t0=t0, n=n: e.tensor_tensor(
                            out=X[:, :n], in0=pr[:, :n], in1=rope[:, 1, t0:t0 + n], op=ALU.mult), R=[tpr, t_rope, tX], W=[tX])
                        P.add('dve', lambda e, X=X, T1=T1, O=O, n=n: e.tensor_tensor(
                            out=O[:, :n], in0=X[:, :n], in1=T1[:, :n], op=ALU.add), R=[tX, tT1], W=[tO])
                    hh = j % 16
                    P.add('sp', lambda e, O=O, dst=dst, b=b, hh=hh, s0=s0, t0=t0, n=n: e.dma_start(
                        out=dst[b, hh * 128:(hh + 1) * 128, s0 + t0:s0 + t0 + n], in_=O[:, :n]), R=[tO], W=[C.t_QK], dma=True)
        for cg in range(4):
            wt, tw = wv[iw % NW], t_wv[iw % NW]; iw += 1
            for q in range(0, KC, 2):
                P.add('pool', lambda e, wt=wt, cg=cg, q=q: e.dma_start(
                    out=wt[:, q:q + 2, :], in_=C.dram['w_v_t'][o, cg, :, q:q + 2, :], max_dma_last_dim=4096), W=[tw], dma=True)
            for tt in range(LSEQ // 128):
                pb, tpb = C.bank[ib % 8], C.t_bank[ib % 8]; ib += 1
                for kc in range(KC):
                    P.add('pe', lambda e, pb=pb, wt=wt, kc=kc, tt=tt: e.matmul(
                        pb[:, :], ht[:, kc, tt * 128:(tt + 1) * 128], wt[:, kc, :], start=(kc == 0), stop=(kc == KC - 1)),
                        R=[tw, t_ht], W=[tpb])
                k = ix % NX; ix += 1
                O, tO = ob[k], t_ob[k]
                eng = 'act' if tt % 2 == 0 else 'dve'
                if eng == 'act':
                    P.add('act', lambda e, O=O, pb=pb: e.copy(out=O[:, :], in_=pb[:, :]), R=[tpb], W=[tO])
                else:
                    P.add('dve', lambda e, O=O, pb=pb: e.tensor_copy(out=O[:, :], in_=pb[:, :]), R=[tpb], W=[tO])
                P.add('sp', lambda e, O=O, b=b, tt=tt, cg=cg: e.dma_start(
                    out=V[b, tt * 128:(tt + 1) * 128, cg * 512:(cg + 1) * 512], in_=O[:, :]), R=[tO], W=[C.t_QK], dma=True)


def phase_attn(C, l, need_ctx):
    P = C.P
    o = l // 2
    QT, KT, V, YT = C.dram['QT'], C.dram['KT'], C.dram['V'], C.dram['YT']
    NT = LSEQ // 128
    NB2 = 2
    kt = [C.sb([128, LSEQ], BF16) for _ in range(NB2)]; qt_ = [C.sb([128, LSEQ], BF16) for _ in range(NB2)]
    vt = [C.sb([128, NT, 128], BF16) for _ in range(NB2)]; yt = [C.sb([128, LSEQ], BF16) for _ in range(NB2)]
    bt = [C.sb([128, NBT, 128]) for _ in range(NB2)]
    t_kt = [Tl() for _ in range(NB2)]; t_qt = [Tl() for _ in range(NB2)]; t_vt = [Tl() for _ in range(NB2)]
    t_yt = [Tl() for _ in range(NB2)]; t_bt = [Tl() for _ in range(NB2)]
    msk = C.sb([128, NBT, 128]); t_msk = Tl()
    ones = C.sb([128, 128], BF16); t_ones = Tl()
    P.add('sp', lambda e: e.dma_start(out=msk[:], in_=C.dram['na_mask'][:]), W=[t_msk], dma=True)
    P.add('pool', lambda e: e.memset(ones[:], 1.0), W=[t_ones])
    NS = 2
    sc = [C.sb([128, 5, 128]) for _ in range(NS)]; t_sc = [Tl() for _ in range(NS)]
    pT = [C.sb([128, 7, 128], BF16) for _ in range(NS)]; t_pT = [Tl() for _ in range(NS)]
    rec = [C.sb([128, 256]) for _ in range(NS)]; t_rec = [Tl() for _ in range(NS)]
    ih = 0; iq = 0
    for b in range(C.nb):
        for h in range(NH_C):
            k = ih % NB2; ih += 1
            K_, Q_, V_, Y_, B_ = kt[k], qt_[k], vt[k], yt[k], bt[k]
            P.add('sp', lambda e, K_=K_, b=b, h=h: e.dma_start(out=K_[:], in_=KT[b, h * 128:(h + 1) * 128, :]),
                  R=[C.t_QK], W=[t_kt[k]], dma=True)
            P.add('sp', lambda e, Q_=Q_, b=b, h=h: e.dma_start(out=Q_[:], in_=QT[b, h * 128:(h + 1) * 128, :]),
                  R=[C.t_QK], W=[t_qt[k]], dma=True)
            P.add('act', lambda e, V_=V_, b=b, h=h: e.dma_start(
                out=V_[:], in_=V[b, :, h * 128:(h + 1) * 128].rearrange("(tt p) d -> p tt d", p=128)),
                R=[C.t_QK], W=[t_vt[k]], dma=True)
            P.add('act', lambda e, B_=B_, h=h: e.dma_start(out=B_[:], in_=C.dram['na_bias_t'][o, h]), W=[t_bt[k]], dma=True)
            P.add('pool', lambda e, B_=B_: e.tensor_tensor(out=B_[:], in0=B_[:], in1=msk[:], op=ALU.add),
                  R=[t_bt[k], t_msk], W=[t_bt[k]])
            for qi in range(LLAT // 128):
                rows, b0 = na_plan(qi)
                nl = len(rows)
                s = iq % NS; iq += 1
                X, Y, Z = C.bank[3 * s], C.bank[3 * s + 1], C.bank[3 * s + 2]
                tX, tY, tZ = C.t_bank[3 * s], C.t_bank[3 * s + 1], C.t_bank[3 * s + 2]
                q0 = LCTX + qi * 128
                for i, a in enumerate(rows):
                    dstp, tdst, col = (X, tX, i * 128) if i < 4 else (Y, tY, 0)
                    k0 = LCTX + a * 64
                    P.add('pe', lambda e, dstp=dstp, col=col, K_=K_, Q_=Q_, k0=k0, q0=q0: e.matmul(
                        dstp[:, col:col + 128], K_[:, k0:k0 + 128], Q_[:, q0:q0 + 128], start=True, stop=True),
                        R=[t_kt[k], t_qt[k]], W=[tdst])
                for i in range(2):
                    P.add('pe', lambda e, Y=Y, i=i, K_=K_, Q_=Q_, q0=q0: e.matmul(
                        Y[:, 128 + i * 128:256 + i * 128], K_[:, i * 128:(i + 1) * 128], Q_[:, q0:q0 + 128], start=True, stop=True),
                        R=[t_kt[k], t_qt[k]], W=[tY])
                S_, tS, PT, tPT, RC, tRC = sc[s], t_sc[s], pT[s], t_pT[s], rec[s], t_rec[s]
                P.add('dve', lambda e, S_=S_, X=X, B_=B_, b0=b0: e.tensor_tensor(
                    out=S_[:, 0:4, :], in0=X[:, 0:512].rearrange("p (a c) -> p a c", c=128), in1=B_[:, b0:b0 + 4, :], op=ALU.add), R=[tX, t_bt[k]], W=[tS])
                if nl == 5:
                    P.add('dve', lambda e, S_=S_, Y=Y, B_=B_, b0=b0: e.tensor_tensor(
                        out=S_[:, 4, :], in0=Y[:, 0:128], in1=B_[:, b0 + 4, :], op=ALU.add), R=[t_bt[k]], W=[tS, tY])
                P.add('act', lambda e, PT=PT, S_=S_, nl=nl: e.activation(out=PT[:, 0:nl, :], in_=S_[:, 0:nl, :], func=AF.Exp),
                      R=[tS], W=[tPT])
                P.add('act', lambda e, PT=PT, Y=Y, nl=nl: e.activation(out=PT[:, nl:nl + 2, :], in_=Y[:, 128:384].rearrange("p (a c) -> p a c", c=128), func=AF.Exp),
                      R=[], W=[tPT, tY])
                vidx = [2 + a // 2 for a in rows] + [0, 1]
                for i, vi in enumerate(vidx):
                    P.add('pe', lambda e, Z=Z, V_=V_, PT=PT, i=i, vi=vi, nn=len(vidx): e.matmul(
                        Z[:, 0:128], V_[:, vi, :], PT[:, i, :], start=(i == 0), stop=(i == nn - 1)),
                        R=[t_vt[k], tPT], W=[tZ])
                for i in range(len(vidx)):
                    P.add('pe', lambda e, Z=Z, PT=PT, i=i, nn=len(vidx): e.matmul(
                        Z[:, 128:256], ones[:], PT[:, i, :], start=(i == 0), stop=(i == nn - 1)),
                        R=[t_ones, tPT], W=[tZ])
                P.add('dve', lambda e, RC=RC, Z=Z: e.reciprocal(out=RC[:, 0:128], in_=Z[:, 128:256]), R=[tZ], W=[tRC])
                P.add('dve', lambda e, Y_=Y_, Z=Z, RC=RC, q0=q0: e.tensor_tensor(
                    out=Y_[:, q0:q0 + 128], in0=Z[:, 0:128], in1=RC[:, 0:128], op=ALU.mult), R=[tZ, tRC], W=[t_yt[k]])
            if need_ctx:
                s = iq % NS; iq += 1
                X, Z = C.bank[3 * s], C.bank[3 * s + 2]
                tX, tZ = C.t_bank[3 * s], C.t_bank[3 * s + 2]
                PT, tPT, RC, tRC = pT[s], t_pT[s], rec[s], t_rec[s]
                for i in range(2):
                    P.add('pe', lambda e, X=X, i=i, K_=K_, Q_=Q_: e.matmul(
                        X[:, i * 256:(i + 1) * 256], K_[:, i * 128:(i + 1) * 128], Q_[:, 0:256], start=True, stop=True),
                        R=[t_kt[k], t_qt[k]], W=[tX])
                P.add('act', lambda e, PT=PT, X=X: e.activation(out=PT[:, 0:4, :], in_=X[:, 0:512].rearrange("p (a c) -> p a c", c=128), func=AF.Exp), R=[tX], W=[tPT])
                for i in range(2):
                    P.add('pe', lambda e, Z=Z, V_=V_, PT=PT, i=i: e.matmul(
                        Z[:, 0:256], V_[:, i, :], PT[:, 2 * i:2 * i + 2, :], start=(i == 0), stop=(i == 1)),
                        R=[t_vt[k], tPT], W=[tZ])
                for i in range(2):
                    P.add('pe', lambda e, Z=Z, PT=PT, i=i: e.matmul(
                        Z[:, 256:512], ones[:], PT[:, 2 * i:2 * i + 2, :], start=(i == 0), stop=(i == 1)),
                        R=[t_ones, tPT], W=[tZ])
                P.add('dve', lambda e, RC=RC, Z=Z: e.reciprocal(out=RC[:, 0:256], in_=Z[:, 256:512]), R=[tZ], W=[tRC])
                P.add('dve', lambda e, Y_=Y_, Z=Z, RC=RC: e.tensor_tensor(
                    out=Y_[:, 0:256], in0=Z[:, 0:256], in1=RC[:, 0:256], op=ALU.mult), R=[tZ, tRC], W=[t_yt[k]])
            c0 = 0 if need_ctx else LCTX
            P.add('sp', lambda e, Y_=Y_, b=b, h=h, c0=c0: e.dma_start(
                out=YT[b, h * 128:(h + 1) * 128, c0:LSEQ], in_=Y_[:, c0:LSEQ]), R=[t_yt[k]], W=[C.t_YT], dma=True)


NHA = 8


def phase_inab(C, l, HT):
    P = C.P
    e = l // 2
    Dm = C.dram
    ht = C.sb([128, KC, LSEQ], BF16); t_ht = Tl()
    gconv = C.sb([128, 24, 5]); t_gc = Tl()
    lbr = C.sb([128, 2, 2, NHA]); lb = C.sb([128, 2, NHA]); oml = C.sb([128, 2, NHA]); t_lb = Tl()
    ident = C.sb([128, 128]); t_id = Tl()
    ones = C.sb([128, 128]); t_ones = Tl()
    P.add('sp', lambda e_: e_.dma_start(out=gconv[:], in_=Dm['gconv_t'][e]), W=[t_gc], dma=True)
    P.add('sp', lambda e_: e_.dma_start(out=ident[:], in_=Dm['ident'][:]), W=[t_id], dma=True)
    P.add('pool', lambda e_: e_.memset(ones[:], 1.0), W=[t_ones])
    if e == 0:
        P.add('pool', lambda e_: e_.memset(lb[:], 0.0), W=[t_lb])
        P.add('pool', lambda e_: e_.memset(oml[:], 1.0), W=[t_lb])
    else:
        P.add('sp', lambda e_: e_.dma_start(out=lbr[:], in_=Dm['lb_t'][:]), W=[t_lb], dma=True)
        P.add('dve', lambda e_: e_.tensor_tensor(out=lb[:], in0=lbr[:, 1], in1=lbr[:, 0], op=ALU.subtract), R=[t_lb], W=[t_lb])
        P.add('act', lambda e_: e_.activation(out=lb[:], in_=lb[:], func=AF.Sigmoid), R=[t_lb], W=[t_lb])
        P.add('dve', lambda e_: e_.tensor_scalar(out=oml[:], in0=lb[:], scalar1=-1.0, scalar2=1.0, op0=ALU.mult, op1=ALU.add),
              R=[t_lb], W=[t_lb])
    NW = 2
    w = [C.sb([128, KC, 128], BF16) for _ in range(NW)]; t_w = [Tl() for _ in range(NW)]
    wv = [C.sb([128, KC, 512], BF16) for _ in range(NW)]; t_wv = [Tl() for _ in range(NW)]
    wg = C.sb([128, KC, 32], BF16); t_wg = Tl()
    NR = 2
    xr = [C.sb([128, LLAT + 4]) for _ in range(NR)]; yr = [C.sb([128, LLAT]) for _ in range(NR)]
    obf = [C.sb([128, LLAT], BF16) for _ in range(NR)]
    tst = [C.sb([128, LLAT // 128, 128], BF16) for _ in range(NR)]
    t_xr = [Tl() for _ in range(NR)]; t_yr = [Tl() for _ in range(NR)]; t_ob = [Tl() for _ in range(NR)]; t_ts = [Tl() for _ in range(NR)]
    vst = [C.sb([128, 512], BF16) for _ in range(3)]; t_vst = [Tl() for _ in range(3)]
    gst = C.sb([128, LSEQ // 128, 32]); t_gst = Tl()
    for k in range(NR):
        P.add('pool', lambda e_, k=k: e_.memset(xr[k][:, 0:2], 0.0), W=[t_xr[k]])
    iw = 0; ir = 0; ib = 0; iv = 0
    for b in range(C.nb):
        load_ht(C, HT, b, ht, t_ht)
        for j in range(72):
            grp, hd = j // 8, j % 8
            if grp == 3:
                continue
            wt, tw = w[iw % NW], t_w[iw % NW]; iw += 1
            P.add('pool', lambda e_, wt=wt, j=j: e_.dma_start(out=wt[:], in_=Dm['w_inab_t'][e, j], max_dma_last_dim=4096),
                  W=[tw], dma=True)
            kind = {0: 'silu', 1: 'f', 2: 'f', 4: 'silu', 5: 'conv', 6: 'conv', 7: 'conv', 8: 'silu'}[grp]
            for (s0, L) in SEQS:
                k = ir % NR; ir += 1
                XR, YR, OB, TS = xr[k], yr[k], obf[k], tst[k]
                tXR, tYR, tOB, tTS = t_xr[k], t_yr[k], t_ob[k], t_ts[k]
                pad = 2 if kind == 'conv' else 0
                if kind == 'conv':
                    P.add('pool', lambda e_, XR=XR, L=L: e_.memset(XR[:, L + 2:L + 4], 0.0), W=[tXR])
                    if L != LLAT:
                        P.add('pool', lambda e_, XR=XR: e_.memset(XR[:, 0:2], 0.0), W=[tXR])
                for (t0, n) in seq_tiles(L):
                    pb, tpb = C.bank[ib % 6], C.t_bank[ib % 6]; ib += 1
                    for kc in range(KC):
                        P.add('pe', lambda e_, pb=pb, wt=wt, kc=kc, s0=s0, t0=t0, n=n: e_.matmul(
                            pb[:, :n], wt[:, kc, :], ht[:, kc, s0 + t0:s0 + t0 + n], start=(kc == 0), stop=(kc == KC - 1)),
                            R=[tw, t_ht], W=[tpb])
                    fn = {'silu': AF.Silu, 'f': AF.Sigmoid, 'conv': AF.Identity}[kind]
                    P.add('act', lambda e_, XR=XR, pb=pb, t0=t0, n=n, fn=fn, pad=pad: e_.activation(
                        out=XR[:, pad + t0:pad + t0 + n], in_=pb[:, :n], func=fn), R=[tpb], W=[tXR])
                if kind == 'silu':
                    dst = {0: 'QA', 4: 'GA', 8: 'ZB'}[grp]
                    P.add('sp', lambda e_, XR=XR, dst=dst, b=b, hd=hd, s0=s0, L=L: e_.dma_start(
                        out=Dm[dst][b, hd * 128:(hd + 1) * 128, s0:s0 + L], in_=XR[:, :L]), R=[tXR], W=[C.t_AB], dma=True)
                elif kind == 'f':
                    d = grp - 1
                    P.add('dve', lambda e_, XR=XR, L=L, d=d, hd=hd: e_.tensor_scalar(
                        out=XR[:, :L], in0=XR[:, :L], scalar1=oml[:, d, hd:hd + 1], scalar2=lb[:, d, hd:hd + 1],
                        op0=ALU.mult, op1=ALU.add), R=[tXR, t_lb], W=[tXR])
                    P.add('pool', lambda e_, XR=XR, YR=YR, L=L: e_.tensor_scalar(
                        out=YR[:, :L], in0=XR[:, :L], scalar1=-1.0, scalar2=1.0, op0=ALU.mult, op1=ALU.add), R=[tXR], W=[tYR])
                    P.add('sp', lambda e_, YR=YR, b=b, d=d, hd=hd, s0=s0, L=L: e_.dma_start(
                        out=Dm['KF'][b, d, hd * 128:(hd + 1) * 128, s0:s0 + L], in_=YR[:, :L]), R=[tYR], W=[C.t_AB], dma=True)
                    P.add('act', lambda e_, XR=XR, L=L: e_.activation(out=XR[:, :L], in_=XR[:, :L], func=AF.Ln), R=[tXR, tYR], W=[tXR])
                    P.add('sp', lambda e_, XR=XR, b=b, d=d, hd=hd, s0=s0, L=L: e_.dma_start(
                        out=Dm['LF'][b, d, hd * 128:(hd + 1) * 128, s0:s0 + L], in_=XR[:, :L]), R=[tXR], W=[C.t_AB], dma=True)
                else:
                    ci = j - 40
                    P.add('dve', lambda e_, XR=XR, YR=YR, L=L, ci=ci: e_.tensor_scalar(
                        out=YR[:, :L], in0=XR[:, 2:L + 2], scalar1=gconv[:, ci, 2:3], scalar2=None, op0=ALU.mult),
                        R=[tXR, t_gc], W=[tYR])
                    for tap in (0, 1, 3, 4):
                        P.add('dve', lambda e_, XR=XR, YR=YR, L=L, ci=ci, tap=tap: e_.scalar_tensor_tensor(
                            out=YR[:, :L], in0=XR[:, tap:tap + L], scalar=gconv[:, ci, tap:tap + 1], in1=YR[:, :L],
                            op0=ALU.mult, op1=ALU.add), R=[tXR, t_gc, tYR], W=[tYR])
                    P.add('act', lambda e_, YR=YR, L=L: e_.activation(out=YR[:, :L], in_=YR[:, :L], func=AF.Silu), R=[tYR], W=[tYR])
                    if grp in (5, 6):
                        P.add('act', lambda e_, XR=XR, YR=YR, L=L: e_.activation(out=XR[:, :L], in_=YR[:, :L], func=AF.Square),
                              R=[tYR], W=[tXR])
                        for (t0, n) in seq_tiles(L):
                            pb, tpb = C.bank[6 + ib % 2], C.t_bank[6 + ib % 2]; ib += 1
                            P.add('pe', lambda e_, pb=pb, XR=XR, t0=t0, n=n: e_.matmul(pb[:, :n], ones[:], XR[:, t0:t0 + n], start=True, stop=True),
                                  R=[t_ones, tXR], W=[tpb])
                            P.add('dve', lambda e_, pb=pb, XR=XR, t0=t0, n=n: e_.tensor_scalar(
                                out=XR[:, t0:t0 + n], in0=pb[:, :n], scalar1=float(EPS), scalar2=None, op0=ALU.add), R=[tpb], W=[tXR, tpb])
                        P.add('act', lambda e_, XR=XR, L=L: e_.activation(out=XR[:, :L], in_=XR[:, :L], func=AF.Sqrt), R=[tXR], W=[tXR])
                        P.add('dve', lambda e_, XR=XR, L=L: e_.reciprocal(out=XR[:, :L], in_=XR[:, :L]), R=[tXR], W=[tXR])
                        qs = float(QSCALE) if grp == 5 else 1.0
                        P.add('dve', lambda e_, XR=XR, YR=YR, L=L, qs=qs: e_.scalar_tensor_tensor(
                            out=YR[:, :L], in0=YR[:, :L], scalar=qs, in1=XR[:, :L], op0=ALU.mult, op1=ALU.mult),
                            R=[tXR, tYR], W=[tYR])
                        P.add('pool', lambda e_, OB=OB, YR=YR, L=L: e_.tensor_copy(out=OB[:, :L], in_=YR[:, :L]), R=[tYR], W=[tOB])
                        dst = 'QB' if grp == 5 else 'KB'
                        P.add('sp', lambda e_, OB=OB, dst=dst, b=b, hd=hd, s0=s0, L=L: e_.dma_start(
                            out=Dm[dst][b, hd * 128:(hd + 1) * 128, s0:s0 + L], in_=OB[:, :L]), R=[tOB], W=[C.t_AB], dma=True)
                    if grp in (6, 7):
                        for tt in range(L // 128):
                            pb, tpb = C.bank[6 + ib % 2], C.t_bank[6 + ib % 2]; ib += 1
                            P.add('pe', lambda e_, pb=pb, YR=YR, tt=tt: e_.transpose(pb[:, 0:128], YR[:, tt * 128:(tt + 1) * 128], ident[:]),
                                  R=[tYR, t_id], W=[tpb])
                            if tt % 2 == 0:
                                P.add('act', lambda e_, TS=TS, pb=pb, tt=tt: e_.copy(out=TS[:, tt, :], in_=pb[:, 0:128]), R=[], W=[tTS, tpb])
                            else:
                                P.add('dve', lambda e_, TS=TS, pb=pb, tt=tt: e_.tensor_copy(out=TS[:, tt, :], in_=pb[:, 0:128]), R=[], W=[tTS, tpb])
                        dst = 'KBt' if grp == 6 else 'VBt'
                        P.add('sp', lambda e_, TS=TS, dst=dst, b=b, hd=hd, s0=s0, L=L: e_.dma_start(
                            out=Dm[dst][b, s0:s0 + L, hd * 128:(hd + 1) * 128].rearrange("(tt p) d -> p tt d", p=128),
                            in_=TS[:, 0:L // 128, :]), R=[tTS], W=[C.t_AB], dma=True)
        for cg in range(2):
            wt, tw = wv[iw % NW], t_wv[iw % NW]; iw += 1
            for q in range(0, KC, 2):
                P.add('pool', lambda e_, wt=wt, cg=cg, q=q: e_.dma_start(
                    out=wt[:, q:q + 2, :], in_=Dm['w_ia_t'][e, cg, :, q:q + 2, :], max_dma_last_dim=4096), W=[tw], dma=True)
            for tt in range(LSEQ // 128):
                pb, tpb = C.bank[ib % 6], C.t_bank[ib % 6]; ib += 1
                for kc in range(KC):
                    P.add('pe', lambda e_, pb=pb, wt=wt, kc=kc, tt=tt: e_.matmul(
                        pb[:, :], ht[:, kc, tt * 128:(tt + 1) * 128], wt[:, kc, :], start=(kc == 0), stop=(kc == KC - 1)),
                        R=[tw, t_ht], W=[tpb])
                O, tO = vst[iv % 3], t_vst[iv % 3]; iv += 1
                if tt % 2 == 0:
                    P.add('act', lambda e_, O=O, pb=pb: e_.copy(out=O[:, :], in_=pb[:, :]), R=[tpb], W=[tO])
                else:
                    P.add('dve', lambda e_, O=O, pb=pb: e_.tensor_copy(out=O[:, :], in_=pb[:, :]), R=[tpb], W=[tO])
                P.add('sp', lambda e_, O=O, b=b, tt=tt, cg=cg: e_.dma_start(
                    out=Dm['VA'][b, tt * 128:(tt + 1) * 128, cg * 512:(cg + 1) * 512], in_=O[:, :]), R=[tO], W=[C.t_AB], dma=True)
        P.add('pool', lambda e_: e_.dma_start(out=wg[:], in_=Dm['w_gb_t'][e], max_dma_last_dim=4096), W=[t_wg], dma=True)
        for tt in range(LSEQ // 128):
            pb, tpb = C.bank[ib % 6], C.t_bank[ib % 6]; ib += 1
            for kc in range(KC):
                P.add('pe', lambda e_, pb=pb, kc=kc, tt=tt: e_.matmul(
                    pb[:, 0:32], ht[:, kc, tt * 128:(tt + 1) * 128], wg[:, kc, :], start=(kc == 0), stop=(kc == KC - 1)),
                    R=[t_wg, t_ht], W=[tpb])
            P.add('dve', lambda e_, pb=pb, tt=tt: e_.tensor_copy(out=gst[:, tt, :], in_=pb[:, 0:32]), R=[tpb], W=[t_gst])
        P.add('sp', lambda e_, b=b: e_.dma_start(
            out=Dm['GBt'][b].rearrange("(tt p) c -> p tt c", p=128), in_=gst[:]), R=[t_gst], W=[C.t_AB], dma=True)


NCH = LSEQ // 128


def interleave(gens):
    gens = list(gens)
    while gens:
        for g in list(gens):
            try:
                next(g)
            except StopIteration:
                gens.remove(g)


def chunk_order(d):
    ctx, lat = [0, 1], list(range(2, NCH))
    return (ctx + lat) if d == 0 else (ctx[::-1] + lat[::-1])


def merge_head(C, OA, t_oa, gate, t_gate, normw, t_nw, col, ybf, t_ybf, onesm, t_onesm, tmp, t_tmp, bank, t_bank, dst_row, b):
    P = C.P
    for (t0, n) in seq_tiles(LSEQ):
        P.add('act', lambda e, t0=t0, n=n: e.activation(out=tmp[:, :n], in_=OA[:, t0:t0 + n], func=AF.Square), R=[t_oa], W=[t_tmp])
        P.add('pe', lambda e, n=n: e.matmul(bank[:, :n], onesm[:], tmp[:, :n], start=True, stop=True), R=[t_onesm, t_tmp], W=[t_bank])
        P.add('dve', lambda e, n=n: e.tensor_scalar(out=tmp[:, :n], in0=bank[:, :n], scalar1=float(EPS), scalar2=None, op0=ALU.add),
              R=[], W=[t_tmp, t_bank])
        P.add('act', lambda e, n=n: e.activation(out=tmp[:, :n], in_=tmp[:, :n], func=AF.Sqrt), R=[t_tmp], W=[t_tmp])
        P.add('dve', lambda e, n=n: e.reciprocal(out=tmp[:, :n], in_=tmp[:, :n]), R=[t_tmp], W=[t_tmp])
        P.add('dve', lambda e, t0=t0, n=n: e.scalar_tensor_tensor(
            out=tmp[:, :n], in0=OA[:, t0:t0 + n], scalar=normw[:, col:col + 1], in1=tmp[:, :n], op0=ALU.mult, op1=ALU.mult),
            R=[t_oa, t_nw, t_tmp], W=[t_tmp])
        P.add('pool', lambda e, t0=t0, n=n: e.tensor_tensor(out=ybf[:, t0:t0 + n], in0=tmp[:, :n], in1=gate[:, t0:t0 + n], op=ALU.mult),
              R=[t_tmp, t_gate], W=[t_ybf])
    P.add('sp', lambda e: e.dma_start(out=C.dram['YT'][b, dst_row:dst_row + 128, :], in_=ybf[:]), R=[t_ybf], W=[C.t_YT], dma=True)


def phase_gla(C, l):
    CH = 64; NCA = LSEQ // CH; NCTX = LCTX // CH
    P = C.P
    e = l // 2
    Dm = C.dram
    row = lambda: C.sb([128, LSEQ])
    q = row(); lf = [row(), row()]; kf = [row(), row()]; pp = [row(), row()]; OA = row(); ga = row(); rst = row()
    ybf = C.sb([128, LSEQ], BF16); va = C.sb([CH, NCA, 128], BF16)
    t_q = Tl(); t_lf = [Tl(), Tl()]; t_kf = [Tl(), Tl()]; t_pp = [Tl(), Tl()]; t_oa = Tl(); t_ga = Tl(); t_rst = Tl()
    t_ybf = Tl(); t_va = Tl()
    nmid = [C.sb([128, NCA]) for _ in range(2)]; t_nmid = [Tl(), Tl()]
    edec = [C.sb([128, NCA]) for _ in range(2)]; t_edec = [Tl(), Tl()]
    msk = C.sb([128, 2, 128], U32); t_msk = Tl()
    ident = C.sb([128, 128]); t_id = Tl()
    onesm = C.sb([128, 128]); t_onesm = Tl()
    anorm = C.sb([128, 2]); t_an = Tl()
    tmp = C.sb([128, 512]); t_tmp = Tl()
    S = [C.sb([128, 128]) for _ in range(2)]; Sb = [C.sb([128, 128], BF16) for _ in range(2)]
    t_S = [Tl(), Tl()]; t_Sb = [Tl(), Tl()]
    NU = 2
    E = [[[C.sb([128, CH]) for _ in range(4)] for _ in range(NU)] for _ in range(2)]
    t_E = [[[Tl() for _ in range(4)] for _ in range(NU)] for _ in range(2)]
    qe = [[C.sb([128, CH], BF16) for _ in range(NU)] for _ in range(2)]; ke = [[C.sb([128, CH], BF16) for _ in range(NU)] for _ in range(2)]
    qg = [[C.sb([128, CH], BF16) for _ in range(NU)] for _ in range(2)]; kd = [[C.sb([128, CH]) for _ in range(NU)] for _ in range(2)]
    atm = [[C.sb([CH, CH], BF16) for _ in range(NU)] for _ in range(2)]; kdt = [[C.sb([CH, 128], BF16) for _ in range(NU)] for _ in range(2)]
    mk = lambda: [[Tl() for _ in range(NU)] for _ in range(2)]
    t_qe, t_ke, t_qg, t_kd, t_atm, t_kdt = mk(), mk(), mk(), mk(), mk(), mk()
    P.add('sp', lambda e_: e_.dma_start(out=msk[:], in_=Dm['tri_u'][:]), W=[t_msk], dma=True)
    for d_ in range(2):
        for u_ in range(NU):
            P.add('pool', lambda e_, d_=d_, u_=u_: e_.memset(atm[d_][u_][:], 0.0), W=[t_atm[d_][u_]])
    P.add('sp', lambda e_: e_.dma_start(out=ident[:], in_=Dm['ident'][:]), W=[t_id], dma=True)
    P.add('sp', lambda e_: e_.dma_start(out=anorm[:], in_=Dm['anorm_t'][e]), W=[t_an], dma=True)
    P.add('pool', lambda e_: e_.memset(onesm[:], 1.0 / 128), W=[t_onesm])
    P.add('pool', lambda e_: e_.memset(rst[:], 1.0), W=[t_rst])
    rst3 = rst[:].rearrange("p (c t) -> p c t", t=CH)
    P.add('pool', lambda e_: e_.memset(rst3[:, :, 0:1], 0.0), W=[t_rst])
    for b in range(C.nb):
        for hd in range(NHA):
            r0 = hd * 128
            P.add('sp', lambda e_, b=b, r0=r0: e_.dma_start(out=q[:], in_=Dm['QA'][b, r0:r0 + 128, :]), R=[C.t_AB], W=[t_q], dma=True)
            P.add('sp', lambda e_, b=b, r0=r0: e_.dma_start(out=ga[:], in_=Dm['GA'][b, r0:r0 + 128, :]), R=[C.t_AB], W=[t_ga], dma=True)
            P.add('act', lambda e_, b=b, r0=r0: e_.dma_start(
                out=va[:], in_=Dm['VA'][b, :, r0:r0 + 128].rearrange("(tt p) d -> p tt d", p=CH)), R=[C.t_AB], W=[t_va], dma=True)
            for d in range(2):
                P.add('sp', lambda e_, b=b, d=d, r0=r0: e_.dma_start(out=lf[d][:], in_=Dm['LF'][b, d, r0:r0 + 128, :]),
                      R=[C.t_AB], W=[t_lf[d]], dma=True)
                P.add('act', lambda e_, b=b, d=d, r0=r0: e_.dma_start(out=kf[d][:], in_=Dm['KF'][b, d, r0:r0 + 128, :]),
                      R=[C.t_AB], W=[t_kf[d]], dma=True)
                P.add('dve', lambda e_, d=d: e_.tensor_tensor_scan(out=pp[d][:], data0=rst[:], data1=lf[d][:], initial=0.0,
                                                                   op0=ALU.mult, op1=ALU.add), R=[t_rst, t_lf[d]], W=[t_pp[d]])
                pp3 = pp[d][:].rearrange("p (c t) -> p c t", t=CH)
                P.add('act', lambda e_, d=d, pp3=pp3: e_.activation(out=edec[d][:], in_=pp3[:, :, CH - 1], func=AF.Exp),
                      R=[t_pp[d]], W=[t_edec[d]])
            P.add('dve', lambda e_: e_.tensor_tensor(out=lf[1][:], in0=lf[1][:], in1=pp[1][:], op=ALU.subtract),
                  R=[t_pp[1], t_lf[1]], W=[t_lf[1]])
            B0 = [pp[0], lf[1]]; t_B0 = [t_pp[0], t_lf[1]]
            for d in range(2):
                b3 = B0[d][:].rearrange("p (c t) -> p c t", t=CH)
                mc = CH // 2 - 1 if d == 0 else CH // 2
                P.add('dve', lambda e_, d=d, b3=b3, mc=mc: e_.tensor_scalar(out=nmid[d][:], in0=b3[:, :, mc], scalar1=-1.0, scalar2=None,
                                                                            op0=ALU.mult), R=[t_B0[d]], W=[t_nmid[d]])
                P.add('pool', lambda e_, d=d: e_.memset(S[d][:], 0.0), W=[t_S[d]])
                P.add('pool', lambda e_, d=d: e_.memset(Sb[d][:], 0.0), W=[t_Sb[d]])
            first_written = [False] * NCA

            def chain(d):
                for ui, c in enumerate((list(range(NCTX)) + list(range(NCTX, NCA))) if d == 0 else
                                       (list(range(NCTX))[::-1] + list(range(NCTX, NCA))[::-1])):
                    yield from unit(d, ui, c)

            def unit(d, ui, c):
                if True:
                    u = ui % NU
                    cs = slice(c * CH, (c + 1) * CH)
                    mc = c * CH + (CH // 2 - 1 if d == 0 else CH // 2)
                    lc = c * CH + (CH - 1 if d == 0 else 0)
                    bA, tA = C.bank[4 * d + 2 * u], C.t_bank[4 * d + 2 * u]
                    bB, tB = C.bank[4 * d + 2 * u + 1], C.t_bank[4 * d + 2 * u + 1]
                    E1, E2, E3, E4 = E[d][u]; tE1, tE2, tE3, tE4 = t_E[d][u]
                    Bd, tBd = B0[d], t_B0[d]
                    P.add('act', lambda e_: e_.activation(out=E1[:], in_=Bd[:, cs], func=AF.Exp, bias=nmid[d][:, c:c + 1]),
                          R=[tBd, t_nmid[d]], W=[tE1])
                    P.add('act', lambda e_: e_.activation(out=E2[:], in_=Bd[:, cs], func=AF.Exp, scale=-1.0, bias=Bd[:, mc:mc + 1]),
                          R=[tBd], W=[tE2])
                    if d == 0:
                        P.add('act', lambda e_: e_.activation(out=E3[:], in_=Bd[:, cs], func=AF.Exp), R=[tBd], W=[tE3])
                    else:
                        P.add('act', lambda e_: e_.activation(out=E3[:], in_=Bd[:, cs], func=AF.Exp, bias=pp[1][:, c * CH + CH - 1:c * CH + CH]),
                              R=[tBd, t_pp[1]], W=[tE3])
                    P.add('act', lambda e_: e_.activation(out=E4[:], in_=Bd[:, cs], func=AF.Exp, scale=-1.0, bias=Bd[:, lc:lc + 1]),
                          R=[tBd], W=[tE4])
                    yield
                    QE, KE, QG, KD, ATM, KDT = qe[d][u], ke[d][u], qg[d][u], kd[d][u], atm[d][u], kdt[d][u]
                    P.add('pool', lambda e_: e_.tensor_tensor(out=QE[:], in0=q[:, cs], in1=E1[:], op=ALU.mult), R=[t_q, tE1], W=[t_qe[d][u]])
                    P.add('dve', lambda e_: e_.tensor_tensor(out=KE[:], in0=kf[d][:, cs], in1=E2[:], op=ALU.mult), R=[t_kf[d], tE2], W=[t_ke[d][u]])
                    P.add('pool', lambda e_: e_.tensor_tensor(out=QG[:], in0=q[:, cs], in1=E3[:], op=ALU.mult), R=[t_q, tE3], W=[t_qg[d][u]])
                    P.add('dve', lambda e_: e_.tensor_tensor(out=KD[:], in0=kf[d][:, cs], in1=E4[:], op=ALU.mult), R=[t_kf[d], tE4], W=[t_kd[d][u]])
                    yield
                    P.add('pe', lambda e_: e_.matmul(bA[0:CH, 0:CH], KE[:], QE[:], start=True, stop=True), R=[t_ke[d][u], t_qe[d][u]], W=[tA])
                    P.add('pe', lambda e_: e_.transpose(bA[0:CH, 128:256], KD[:], ident[:]), R=[t_kd[d][u], t_id], W=[tA])
                    yield
                    P.add('dve', lambda e_: e_.copy_predicated(ATM[:], msk[0:CH, d, 0:CH], bA[0:CH, 0:CH]),
                          R=[t_msk], W=[t_atm[d][u], tA])
                    P.add('act', lambda e_: e_.copy(out=KDT[:], in_=bA[0:CH, 128:256]), R=[], W=[t_kdt[d][u], tA])
                    yield
                    P.add('pe', lambda e_: e_.matmul(bB[:, 0:CH], Sb[d][:], QG[:], start=True, stop=False), R=[t_Sb[d], t_qg[d][u]], W=[tB])
                    P.add('pe', lambda e_: e_.matmul(bB[:, 0:CH], va[:, c, :], ATM[:], start=False, stop=True), R=[t_va, t_atm[d][u]], W=[tB])
                    P.add('pe', lambda e_: e_.matmul(bB[:, 128:256], KDT[:], va[:, c, :], start=True, stop=True), R=[t_va, t_kdt[d][u]], W=[tB])
                    yield
                    if not first_written[c]:
                        first_written[c] = True
                        P.add('act', lambda e_: e_.copy(out=OA[:, cs], in_=bB[:, 0:CH]), R=[], W=[t_oa, tB])
                    else:
                        P.add('dve', lambda e_: e_.tensor_tensor(out=OA[:, cs], in0=bB[:, 0:CH], in1=OA[:, cs], op=ALU.add), R=[], W=[t_oa, tB])
                    P.add('dve', lambda e_: e_.scalar_tensor_tensor(out=S[d][:], in0=S[d][:], scalar=edec[d][:, c:c + 1], in1=bB[:, 128:256],
                                                                    op0=ALU.mult, op1=ALU.add), R=[t_edec[d]], W=[t_S[d], tB])
                    P.add('pool', lambda e_: e_.tensor_copy(out=Sb[d][:], in_=S[d][:]), R=[t_S[d]], W=[t_Sb[d]])
                    yield

            interleave([chain(0), chain(1)])
            merge_head(C, OA, t_oa, ga, t_ga, anorm, t_an, 0, ybf, t_ybf, onesm, t_onesm, tmp, t_tmp, C.bank[0], C.t_bank[0], r0, b)


def phase_gdn(C, l):
    P = C.P
    e = l // 2
    Dm = C.dram
    CH = 128

    def T_(shape, dt=F32):
        return C.sb(shape, dt), Tl()

    kT, t_kT = T_([128, LSEQ], BF16); qT, t_qT = T_([128, LSEQ], BF16)
    ktm, t_ktm = T_([128, NCH, 128], BF16); vtm, t_vtm = T_([128, NCH, 128], BF16)
    zb, t_zb = T_([128, LSEQ]); OB, t_ob = T_([128, LSEQ]); ybf, t_ybf = T_([128, LSEQ], BF16)
    gbt, t_gbt = T_([128, NCH, 32]); dtb, t_dtb = T_([128, NCH, 16]); alog, t_alog = T_([128, NCH, 16])
    g, t_g = T_([128, NCH, 16]); beta, t_beta = T_([128, NCH, 16]); G, t_G = T_([128, NCH, 16])
    Gtot, t_Gtot = T_([128, NCH, 16]); eG, t_eG = T_([128, NCH, 16]); beG, t_beG = T_([128, NCH, 16])
    khs, t_khs = T_([128, NCH, 16]); sdec, t_sdec = T_([128, NCH, 16]); tmpg, t_tmpg = T_([128, NCH, 16])
    tri, t_tri = T_([128, 2, 128]); nstr, t_nstr = T_([128, 2, 128]); ident, t_id = T_([128, 128])
    onesf, t_onesf = T_([128, 128]); onesm, t_onesm = T_([128, 128]); anorm, t_an = T_([128, 2])
    tmp, t_tmp = T_([128, 512])
    S = [T_([128, 128]) for _ in range(2)]
    names = ('Dg', 'Ib', 'Dmx', 'eGr', 't2', 'NT', 'AT', 'N', 'NjA', 'NjB', 'NjTA', 'NjTB', 'TT', 'rhs1', 'rhs2', 'nwT', 'vnew', 'qg', 'khat', 'KK', 'QK')
    U = [{n: T_([128, 128]) for n in names} for _ in range(2)]
    P.add('sp', lambda e_: e_.dma_start(out=tri[:], in_=Dm['tri_t'][:]), W=[t_tri], dma=True)
    P.add('sp', lambda e_: e_.dma_start(out=ident[:], in_=Dm['ident'][:]), W=[t_id], dma=True)
    P.add('sp', lambda e_: e_.dma_start(out=anorm[:], in_=Dm['anorm_t'][e]), W=[t_an], dma=True)
    P.add('sp', lambda e_: e_.dma_start(out=dtb[:], in_=Dm['dtb_t'][e]), W=[t_dtb], dma=True)
    P.add('sp', lambda e_: e_.dma_start(out=alog[:], in_=Dm['alog_t'][e]), W=[t_alog], dma=True)
    P.add('pool', lambda e_: e_.memset(onesf[:], 1.0), W=[t_onesf])
    P.add('pool', lambda e_: e_.memset(onesm[:], 1.0 / 128), W=[t_onesm])
    for d_ in range(2):
        P.add('pool', lambda e_, d_=d_: e_.tensor_tensor(out=nstr[:, d_, :], in0=ident[:], in1=tri[:, d_, :], op=ALU.subtract),
              R=[t_id, t_tri], W=[t_nstr])
    P.add('act', lambda e_: e_.activation(out=alog[:], in_=alog[:], func=AF.Exp), R=[t_alog], W=[t_alog])
    bG, tbG = C.bank[6], C.t_bank[6]
    for b in range(C.nb):
        P.add('sp', lambda e_, b=b: e_.dma_start(out=gbt[:], in_=Dm['GBt'][b].rearrange("(tt p) c -> p tt c", p=128)),
              R=[C.t_AB], W=[t_gbt], dma=True)
        P.add('dve', lambda e_: e_.tensor_tensor(out=g[:], in0=gbt[:, :, 0:16], in1=dtb[:], op=ALU.add), R=[t_gbt, t_dtb], W=[t_g])
        P.add('act', lambda e_: e_.activation(out=g[:], in_=g[:], func=AF.Exp), R=[t_g], W=[t_g])
        P.add('dve', lambda e_: e_.tensor_scalar(out=g[:], in0=g[:], scalar1=1.0, scalar2=None, op0=ALU.add), R=[t_g], W=[t_g])
        P.add('act', lambda e_: e_.activation(out=g[:], in_=g[:], func=AF.Ln), R=[t_g], W=[t_g])
        P.add('dve', lambda e_: e_.scalar_tensor_tensor(out=g[:], in0=g[:], scalar=-1.0, in1=alog[:], op0=ALU.mult, op1=ALU.mult),
              R=[t_g, t_alog], W=[t_g])
        P.add('act', lambda e_: e_.activation(out=beta[:], in_=gbt[:, :, 16:32], func=AF.Sigmoid), R=[t_gbt], W=[t_beta])
        for tt in range(NCH):
            P.add('pe', lambda e_, tt=tt: e_.matmul(bG[:, 0:8], tri[:, 0, :], g[:, tt, 0:8], start=True, stop=True), R=[t_tri, t_g], W=[tbG])
            P.add('pe', lambda e_, tt=tt: e_.matmul(bG[:, 8:16], tri[:, 1, :], g[:, tt, 8:16], start=True, stop=True), R=[t_tri, t_g], W=[tbG])
            P.add('pe', lambda e_, tt=tt: e_.matmul(bG[:, 16:32], onesf[:], g[:, tt, :], start=True, stop=True), R=[t_onesf, t_g], W=[tbG])
            P.add('dve', lambda e_, tt=tt: e_.tensor_copy(out=G[:, tt, :], in_=bG[:, 0:16]), R=[], W=[t_G, tbG])
            P.add('dve', lambda e_, tt=tt: e_.tensor_copy(out=Gtot[:, tt, :], in_=bG[:, 16:32]), R=[], W=[t_Gtot, tbG])
        P.add('act', lambda e_: e_.activation(out=eG[:], in_=G[:], func=AF.Exp), R=[t_G], W=[t_eG])
        P.add('dve', lambda e_: e_.tensor_tensor(out=beG[:], in0=beta[:], in1=eG[:], op=ALU.mult), R=[t_beta, t_eG], W=[t_beG])
        P.add('dve', lambda e_: e_.tensor_tensor(out=tmpg[:], in0=Gtot[:], in1=G[:], op=ALU.subtract), R=[t_Gtot, t_G], W=[t_tmpg])
        P.add('act', lambda e_: e_.activation(out=khs[:], in_=tmpg[:], func=AF.Exp), R=[t_tmpg], W=[t_khs])
        P.add('act', lambda e_: e_.activation(out=sdec[:], in_=Gtot[:], func=AF.Exp), R=[t_Gtot], W=[t_sdec])
        for hd in range(NHA):
            r0 = hd * 128
            P.add('sp', lambda e_, b=b, r0=r0: e_.dma_start(out=kT[:], in_=Dm['KB'][b, r0:r0 + 128, :]), R=[C.t_AB], W=[t_kT], dma=True)
            P.add('sp', lambda e_, b=b, r0=r0: e_.dma_start(out=qT[:], in_=Dm['QB'][b, r0:r0 + 128, :]), R=[C.t_AB], W=[t_qT], dma=True)
            P.add('sp', lambda e_, b=b, r0=r0: e_.dma_start(out=zb[:], in_=Dm['ZB'][b, r0:r0 + 128, :]), R=[C.t_AB], W=[t_zb], dma=True)
            P.add('act', lambda e_, b=b, r0=r0: e_.dma_start(
                out=ktm[:], in_=Dm['KBt'][b, :, r0:r0 + 128].rearrange("(tt p) d -> p tt d", p=128)), R=[C.t_AB], W=[t_ktm], dma=True)
            P.add('act', lambda e_, b=b, r0=r0: e_.dma_start(
                out=vtm[:], in_=Dm['VBt'][b, :, r0:r0 + 128].rearrange("(tt p) d -> p tt d", p=128)), R=[C.t_AB], W=[t_vtm], dma=True)
            for d in range(2):
                P.add('pool', lambda e_, d=d: e_.memset(S[d][0][:], 0.0), W=[S[d][1]])
            first_written = [False] * NCH

            def unit(d, c):
                col = d * 8 + hd
                cs = slice(c * 128, (c + 1) * 128)
                u = U[d]
                b1, tb1 = C.bank[3 * d], C.t_bank[3 * d]
                b2, tb2 = C.bank[3 * d + 1], C.t_bank[3 * d + 1]
                b3, tb3 = C.bank[3 * d + 2], C.t_bank[3 * d + 2]
                Sd, tS = S[d]
                A = lambda n: u[n][0]
                t = lambda n: u[n][1]
                gc, bc = g[:, c, col:col + 1], beta[:, c, col:col + 1]
                P.add('pe', lambda e_: e_.matmul(b1[:, 256:384], kT[:, cs], kT[:, cs], start=True, stop=True), R=[t_kT], W=[tb1])
                P.add('pe', lambda e_: e_.matmul(b1[:, 384:512], kT[:, cs], qT[:, cs], start=True, stop=True), R=[t_kT, t_qT], W=[tb1])
                P.add('act', lambda e_: e_.copy(out=A('KK')[:], in_=b1[:, 256:384]), R=[], W=[t('KK'), tb1])
                P.add('act', lambda e_: e_.copy(out=A('QK')[:], in_=b1[:, 384:512]), R=[], W=[t('QK'), tb1])
                KK, tKK = u['KK']; QK, tQK = u['QK']
                yield
                P.add('dve', lambda e_: e_.tensor_scalar(out=A('Dg')[:], in0=tri[:, d, :], scalar1=gc, scalar2=None, op0=ALU.mult),
                      R=[t_tri, t_g], W=[t('Dg')])
                P.add('pool', lambda e_: e_.tensor_scalar(out=A('Ib')[:], in0=ident[:], scalar1=bc, scalar2=None, op0=ALU.mult),
                      R=[t_id, t_beta], W=[t('Ib')])
                P.add('pe', lambda e_: e_.matmul(b1[:, 0:128], onesf[:], A('Dg')[:], start=True, stop=True), R=[t_onesf, t('Dg')], W=[tb1])
                P.add('pe', lambda e_: e_.matmul(b1[:, 128:256], onesf[:], A('Ib')[:], start=True, stop=True), R=[t_onesf, t('Ib')], W=[tb1])
                yield
                P.add('dve', lambda e_: e_.tensor_scalar(out=A('Dmx')[:], in0=b1[:, 0:128], scalar1=G[:, c, col:col + 1], scalar2=0.0,
                                                         op0=ALU.subtract, op1=ALU.min), R=[t_G], W=[t('Dmx'), tb1])
                P.add('act', lambda e_: e_.activation(out=A('eGr')[:], in_=b1[:, 0:128], func=AF.Exp), R=[], W=[t('eGr'), tb1])
                P.add('act', lambda e_: e_.activation(out=A('Dmx')[:], in_=A('Dmx')[:], func=AF.Exp), R=[t('Dmx')], W=[t('Dmx')])
                P.add('dve', lambda e_: e_.tensor_tensor(out=A('t2')[:], in0=b1[:, 128:256], in1=A('Dmx')[:], op=ALU.mult),
                      R=[t('Dmx')], W=[t('t2'), tb1])
                yield
                P.add('pool', lambda e_: e_.tensor_tensor(out=A('t2')[:], in0=A('t2')[:], in1=KK[:], op=ALU.mult), R=[t('t2'), tKK], W=[t('t2')])
                P.add('pool', lambda e_: e_.tensor_tensor(out=A('NT')[:], in0=A('t2')[:], in1=nstr[:, d, :], op=ALU.mult),
                      R=[t('t2'), t_nstr], W=[t('NT')])
                P.add('dve', lambda e_: e_.tensor_tensor(out=A('AT')[:], in0=A('Dmx')[:], in1=tri[:, d, :], op=ALU.mult),
                      R=[t('Dmx'), t_tri], W=[t('AT')])
                P.add('dve', lambda e_: e_.tensor_tensor(out=A('AT')[:], in0=A('AT')[:], in1=QK[:], op=ALU.mult), R=[t('AT'), tQK], W=[t('AT')])
                yield
                P.add('pe', lambda e_: e_.transpose(b2[:, 0:128], A('NT')[:], ident[:]), R=[t('NT'), t_id], W=[tb2])
                yield
                P.add('act', lambda e_: e_.copy(out=A('N')[:], in_=b2[:, 0:128]), R=[], W=[t('N'), tb2])
                P.add('dve', lambda e_: e_.tensor_tensor(out=A('TT')[:], in0=A('NT')[:], in1=ident[:], op=ALU.add), R=[t('NT'), t_id], W=[t('TT')])
                yield
                Np, NTp = 'N', 'NT'
                for j in range(1, 7):
                    Nn, NTn = ('NjA', 'NjTA') if j % 2 else ('NjB', 'NjTB')
                    P.add('pe', lambda e_, Np=Np, NTp=NTp: e_.matmul(b2[:, 0:128], A(NTp)[:], A(Np)[:], start=True, stop=True),
                          R=[t(Np), t(NTp)], W=[tb2])
                    if j < 6:
                        P.add('pe', lambda e_, Np=Np, NTp=NTp: e_.matmul(b2[:, 128:256], A(Np)[:], A(NTp)[:], start=True, stop=True),
                              R=[t(Np), t(NTp)], W=[tb2])
                    yield
                    P.add('act', lambda e_, Nn=Nn: e_.copy(out=A(Nn)[:], in_=b2[:, 0:128]), R=[], W=[t(Nn), tb2])
                    if j < 6:
                        P.add('dve', lambda e_, NTn=NTn: e_.tensor_copy(out=A(NTn)[:], in_=b2[:, 128:256]), R=[], W=[t(NTn), tb2])
                    yield
                    P.add('pe', lambda e_, Nn=Nn: e_.matmul(b2[:, 256:384], A(Nn)[:], A('TT')[:], start=True, stop=True),
                          R=[t(Nn), t('TT')], W=[tb2])
                    yield
                    P.add('dve', lambda e_: e_.tensor_tensor(out=A('TT')[:], in0=b2[:, 256:384], in1=A('TT')[:], op=ALU.add),
                          R=[], W=[t('TT'), tb2])
                    yield
                    Np, NTp = Nn, NTn
                P.add('dve', lambda e_: e_.tensor_scalar(out=A('rhs1')[:], in0=vtm[:, c, :], scalar1=bc, scalar2=None, op0=ALU.mult),
                      R=[t_vtm, t_beta], W=[t('rhs1')])
                P.add('pool', lambda e_: e_.tensor_scalar(out=A('rhs2')[:], in0=ktm[:, c, :], scalar1=beG[:, c, col:col + 1], scalar2=None, op0=ALU.mult),
                      R=[t_ktm, t_beG], W=[t('rhs2')])
                P.add('pool', lambda e_: e_.tensor_scalar(out=A('khat')[:], in0=ktm[:, c, :], scalar1=khs[:, c, col:col + 1], scalar2=None, op0=ALU.mult),
                      R=[t_ktm, t_khs], W=[t('khat')])
                P.add('dve', lambda e_: e_.tensor_tensor(out=A('qg')[:], in0=qT[:, cs], in1=A('eGr')[:], op=ALU.mult), R=[t_qT, t('eGr')], W=[t('qg')])
                yield
                P.add('pe', lambda e_: e_.matmul(b3[:, 0:128], A('rhs2')[:], A('TT')[:], start=True, stop=True), R=[t('rhs2'), t('TT')], W=[tb3])
                yield
                P.add('act', lambda e_: e_.activation(out=A('nwT')[:], in_=b3[:, 0:128], func=AF.Identity, scale=-1.0), R=[], W=[t('nwT'), tb3])
                yield
                P.add('pe', lambda e_: e_.matmul(b3[:, 128:256], A('TT')[:], A('rhs1')[:], start=True, stop=False), R=[t('TT'), t('rhs1')], W=[tb3])
                P.add('pe', lambda e_: e_.matmul(b3[:, 128:256], A('nwT')[:], Sd[:], start=False, stop=True), R=[t('nwT'), tS], W=[tb3])
                yield
                P.add('act', lambda e_: e_.copy(out=A('vnew')[:], in_=b3[:, 128:256]), R=[], W=[t('vnew'), tb3])
                yield
                P.add('pe', lambda e_: e_.matmul(b3[:, 256:384], Sd[:], A('qg')[:], start=True, stop=False), R=[tS, t('qg')], W=[tb3])
                P.add('pe', lambda e_: e_.matmul(b3[:, 256:384], A('vnew')[:], A('AT')[:], start=False, stop=True), R=[t('vnew'), t('AT')], W=[tb3])
                P.add('pe', lambda e_: e_.matmul(b3[:, 384:512], A('khat')[:], A('vnew')[:], start=True, stop=True), R=[t('khat'), t('vnew')], W=[tb3])
                yield
                if not first_written[c]:
                    first_written[c] = True
                    P.add('act', lambda e_: e_.copy(out=OB[:, cs], in_=b3[:, 256:384]), R=[], W=[t_ob, tb3])
                else:
                    P.add('dve', lambda e_: e_.tensor_tensor(out=OB[:, cs], in0=b3[:, 256:384], in1=OB[:, cs], op=ALU.add), R=[], W=[t_ob, tb3])
                P.add('dve', lambda e_: e_.scalar_tensor_tensor(out=Sd[:], in0=Sd[:], scalar=sdec[:, c, col:col + 1], in1=b3[:, 384:512],
                                                                op0=ALU.mult, op1=ALU.add), R=[t_sdec], W=[tS, tb3])
                yield

            def chain(d):
                for c in chunk_order(d):
                    yield from unit(d, c)

            interleave([chain(0), chain(1)])
            merge_head(C, OB, t_ob, zb, t_zb, anorm, t_an, 1, ybf, t_ybf, onesm, t_onesm, tmp, t_tmp, C.bank[7], C.t_bank[7],
                       1024 + r0, b)


def build(cfg):
    nc = bass.Bass("TRN2", target_bir_lowering=False)
    C = Ctx(nc)
    C.nb = nb = cfg.get('nb', NBC)
    ext_in, ext_out = cfg.get('ext_in', ()), cfg.get('ext_out', ())

    def dt(name, shape, dtype=F32):
        if name in ext_in:
            return C.din(name, shape, dtype)
        if name in ext_out:
            return C.dout(name, shape, dtype)
        return C.dscr(name, shape, dtype)

    C.din('cT', [128, KC, 3])
    C.din('b_ada_t', [128, DEPTH, 96])
    C.din('w_ada_t', [DEPTH, 96, 128, KC, 128])
    C.din('ln_t', [DEPTH, 2, 128, KC, 2])
    C.din('ffn_dw_t', [DEPTH, 128, FC, 4])
    C.din('w_up_t', [DEPTH, 2 * FC, 128, KC, 128])
    C.din('w_down_t', [DEPTH, KC, 128, FC, 128])
    for nm in ('XA', 'XB', 'XIN'):
        dt(nm, [nb, D, LSEQ])
    dt('OUT', [nb, D, LLAT])
    dt('HT', [nb, D, LSEQ], BF16)
    dt('HID', [nb, DFF, LSEQ], BF16)
    dt('o_mod', [128, DEPTH, 96, 3])
    C.din('w_out_t', [DEPTH, KC, 128, KC, 128])
    C.din('w_inc_t', [2, 32, 128, KC, 128])
    C.din('w_v_t', [2, 4, 128, KC, 512])
    C.din('rope_t', [128, 2, LLAT])
    C.din('pmT', [128, 128])
    C.din('na_mask', [128, NBT, 128])
    C.din('na_bias_t', [2, NH_C, 128, NBT, 128])
    dt('QT', [nb, D, LSEQ], BF16); dt('KT', [nb, D, LSEQ], BF16); dt('V', [nb, LSEQ, D], BF16)
    dt('YT', [nb, D, LSEQ], BF16)
    C.t_QK, C.t_YT = Tl('QK'), Tl('YT')
    C.din('w_inab_t', [2, 72, 128, KC, 128])
    C.din('w_ia_t', [2, 2, 128, KC, 512])
    C.din('w_gb_t', [2, 128, KC, 32])
    C.din('gconv_t', [2, 128, 24, 5])
    C.din('lb_t', [128, 2, 2, NHA])
    C.din('ident', [128, 128])
    for nm in ('QA', 'GA', 'ZB'):
        dt(nm, [nb, 1024, LSEQ])
    dt('LF', [nb, 2, 1024, LSEQ]); dt('KF', [nb, 2, 1024, LSEQ])
    for nm in ('QB', 'KB'):
        dt(nm, [nb, 1024, LSEQ], BF16)
    for nm in ('VA', 'KBt', 'VBt'):
        dt(nm, [nb, LSEQ, 1024], BF16)
    dt('GBt', [nb, LSEQ, 32])
    C.t_AB = Tl('AB')
    C.din('tri_t', [128, 2, 128])
    C.din('tri_u', [128, 2, 128], U32)
    C.din('dtb_t', [2, 128, NCH, 16])
    C.din('alog_t', [2, 128, NCH, 16])
    C.din('anorm_t', [2, 128, 2])
    C.t_HT, C.t_HID = Tl('HT'), Tl('HID')
    C.t_X = {'XA': Tl('XA'), 'XB': Tl('XB'), 'XIN': Tl('XIN'), 'OUT': Tl('OUT')}
    with ExitStack() as top:
        C.stack = top
        C.mod = C.sb([128, DEPTH, 96, 3], name='mod')
        C.t_mod = Tl('mod')
        C.bank = [C.ps([128, 512], name=f'bank{i}') for i in range(8)]
        C.t_bank = [Tl(f'bank{i}') for i in range(8)]
        csem = {e: top.enter_context(nc.semaphore('c_' + e)) for e in ENGS}
        dsem = {e: [top.enter_context(nc.semaphore(f'd_{e}{i}')) for i in range(RING)] for e in ENGS}
        outs = []
        for ph in cfg['phases']:
            with ExitStack() as phs:
                C.stack = phs
                kind = ph[0]
                if kind == 'ada':
                    phase_ada(C, ph[1])
                elif kind == 'mod':
                    _, l, xs, sh, sc = ph
                    phase_mod0(C, l, C.dram[xs], C.dram['HT'], sh, sc)
                elif kind == 'ffn_up':
                    phase_ffn_up(C, ph[1], C.dram['HT'], C.dram['HID'], skip_ctx=(len(ph) > 2 and ph[2]))
                elif kind == 'ffn_down':
                    _, l, xs, xd, hmod = ph[:5]
                    final = len(ph) > 5 and ph[5]
                    phase_proj_ln(C, l, C.dram['HID'], C.t_HID, FC, 'w_down_t', l, 5, 1, C.dram[xs], C.dram[xd],
                                  C.t_X[xd], C.dram['HT'], hmod, lat_only_out=final)
                elif kind == 'inab':
                    phase_inab(C, ph[1], C.dram['HT'])
                elif kind == 'gdn':
                    phase_gdn(C, ph[1])
                elif kind == 'gla':
                    phase_gla(C, ph[1])
                elif kind == 'qkv':
                    phase_qkv(C, ph[1], C.dram['HT'], ph[2])
                elif kind == 'attn':
                    phase_attn(C, ph[1], ph[2])
                elif kind == 'out_proj':
                    _, l, xs, xd, hmod = ph[:5]
                    phase_proj_ln(C, l, C.dram['YT'], C.t_YT, KC, 'w_out_t', l, 2, 0, C.dram[xs], C.dram[xd],
                                  C.t_X[xd], C.dram['HT'], hmod, skip_ctx=(len(ph) > 5 and ph[5]))
                elif kind == 'dump_mod':
                    C.P.add('sp', lambda e: e.dma_start(out=C.dram['o_mod'][:], in_=C.mod[:]), R=[C.t_mod], W=[C.t_HT], dma=True)
                C.P.barrier()
            C.stack = top
        C.P.barrier()
        with nc.Block() as block:
            C.P.emit(block, csem, dsem)
    return nc


def tileW(w):
    Kd, N = w.shape
    return np.ascontiguousarray(w.reshape(Kd // 128, 128, N // 128, 128).transpose(2, 1, 0, 3))


def tileWwide(w, nw):
    Kd, N = w.shape
    return np.ascontiguousarray(w.reshape(Kd // 128, 128, N // nw, nw).transpose(2, 1, 0, 3))


def host_consts():
    c = {}
    pos = np.arange(LLAT)
    row = (pos // 64).astype(np.float32); col = (pos % 64).astype(np.float32)
    inv = (np.float32(10000.0) ** (-np.arange(32, dtype=np.float32) / np.float32(32))).astype(np.float32)
    rope = np.zeros((128, 2, LLAT), np.float32)
    for d in range(128):
        ang = ((row if d < 64 else col) * inv[d % 32]).astype(np.float32)
        rope[d, 0] = np.cos(ang); rope[d, 1] = np.sin(ang)
    c['rope_t'] = rope
    pm = np.zeros((128, 128), np.float32)
    for i in range(32):
        pm[i, 32 + i] = -1; pm[32 + i, i] = 1; pm[64 + i, 96 + i] = -1; pm[96 + i, 64 + i] = 1
    c['pmT'] = np.ascontiguousarray(pm.T)
    kk = np.arange(128); krl, kc = kk // 64, kk % 64
    qq = np.arange(128); qrl, qc = qq // 64, qq % 64
    c0 = np.clip(qc - 8, 0, 48)
    col_ok = (kc[:, None] >= c0[None, :]) & (kc[:, None] < c0[None, :] + 16)
    deltas = [-6, -4, -2, 0, 2, 4, 6] + [-4, -2, 0, 2, 4]
    mask = np.zeros((128, NBT, 128), np.float32)
    dr = np.zeros((NBT, 128, 128), np.int64)
    for t, dlt in enumerate(deltas):
        rel = dlt + krl[:, None] - qrl[None, :]
        ok = col_ok if t < 7 else (col_ok & (rel >= -4) & (rel < 4))
        mask[:, t, :] = np.where(ok, 0.0, -30000.0)
        dr[t] = np.clip(rel + 7, 0, 14)
    c['na_mask'] = mask
    c['_dr'] = dr
    c['_dc'] = np.clip(kc[:, None] - qc[None, :] + 15, 0, 30)
    return c


def host_bias_tiles(rel_bias, consts):
    dr, dc = consts['_dr'], consts['_dc']
    g = rel_bias[:, :, dr, dc[None]]
    return np.ascontiguousarray(g.transpose(0, 1, 3, 2, 4)).astype(np.float32)


def host_even_weights(inputs, e_list=(0, 1)):
    w = {}
    w['w_inab_t'] = np.zeros((2, 72, 128, KC, 128), np.float32)
    w['w_ia_t'] = np.zeros((2, 2, 128, KC, 512), np.float32)
    w['w_gb_t'] = np.zeros((2, 128, KC, 32), np.float32)
    for e in e_list:
        wi = inputs['w_in_ab'][e]
        w['w_inab_t'][e] = tileW(wi[:, :9216])
        w['w_ia_t'][e] = tileWwide(wi[:, 3072:4096], 512)
        w['w_gb_t'][e] = np.ascontiguousarray(wi[:, 9216:9248].reshape(KC, 128, 32).transpose(1, 0, 2))
    gc = inputs['gdn_conv']
    w['gconv_t'] = np.ascontiguousarray(gc.reshape(2, 5, 24, 128).transpose(0, 3, 2, 1)).astype(np.float32)
    lb = inputs['hgrn_lb']
    w['lb_t'] = np.ascontiguousarray(lb.reshape(2, 2, NHA, 128).transpose(3, 0, 1, 2)).astype(np.float32)
    w['ident'] = np.eye(128, dtype=np.float32)
    ii = np.arange(128)
    w['anorm_t'] = np.ascontiguousarray(np.stack([inputs['hgrn_norm'], inputs['gdn_norm']], -1)).astype(np.float32)
    w['tri_t'] = np.ascontiguousarray(np.stack([(ii[:, None] <= ii[None, :]), (ii[:, None] >= ii[None, :])], 1)).astype(np.float32)
    w['tri_u'] = np.ascontiguousarray(w['tri_t'].astype(np.uint32))
    dtb = inputs['gdn_dt_bias'].reshape(2, 16).astype(np.float32)
    alg = inputs['gdn_a_log'].reshape(2, 16).astype(np.float32)
    w['dtb_t'] = np.ascontiguousarray(np.broadcast_to(dtb[:, None, None, :], (2, 128, NCH, 16)))
    w['alog_t'] = np.ascontiguousarray(np.broadcast_to(alg[:, None, None, :], (2, 128, NCH, 16)))
    return w


def full_phases():
    ph = [('ada', list(range(DEPTH))), ('mod', 0, 'XIN', 0, 1)]
    src = 'XIN'
    for l in range(DEPTH):
        last = l == DEPTH - 1
        if l % 2 == 0:
            ph += [('inab', l), ('gla', l), ('gdn', l)]
        else:
            ph += [('qkv', l, not last), ('attn', l, not last)]
        ph += [('out_proj', l, src, 'XB', (l, 3, 4), last), ('ffn_up', l, last)]
        if last:
            ph += [('ffn_down', l, 'XB', 'OUT', None, True)]
        else:
            ph += [('ffn_down', l, 'XB', 'XA', (l + 1, 0, 1))]
        src = 'XA'
    return ph


def host_shared(inputs):
    f32 = np.float32
    m = {}
    m['w_ada_t'] = np.stack([tileW(inputs['w_ada'][l]) for l in range(DEPTH)])
    m['b_ada_t'] = np.ascontiguousarray(inputs['b_ada'].reshape(DEPTH, 96, 128).transpose(2, 0, 1)).astype(f32)
    m['ln_t'] = np.ascontiguousarray(np.stack([inputs['ln_g'], inputs['ln_b']], -1).reshape(DEPTH, 2, KC, 128, 2)
                                     .transpose(0, 1, 3, 2, 4)).astype(f32)
    dw = np.concatenate([inputs['ffn_w_dw'], inputs['ffn_b_dw'][:, None, :]], 1)
    m['ffn_dw_t'] = np.ascontiguousarray(dw.reshape(DEPTH, 4, FC, 128).transpose(0, 3, 2, 1)).astype(f32)
    m['w_up_t'] = np.stack([tileW(inputs['ffn_w_up'][l]) for l in range(DEPTH)])
    m['w_down_t'] = np.stack([tileW(inputs['ffn_w_down'][l]) for l in range(DEPTH)])
    m['w_out_t'] = np.stack([tileW(inputs['w_out_ab'][l // 2] if l % 2 == 0 else inputs['w_out_c'][l // 2]) for l in range(DEPTH)])
    m['w_inc_t'] = np.stack([tileW(inputs['w_in_c'][o][:, :2 * D]) for o in range(2)])
    m['w_v_t'] = np.stack([tileWwide(inputs['w_in_c'][o][:, 2 * D:], 512) for o in range(2)])
    hc = host_consts()
    for k in ('rope_t', 'pmT', 'na_mask'):
        m[k] = hc[k]
    m['na_bias_t'] = host_bias_tiles(inputs['na_rel_bias'], hc)
    m.update(host_even_weights(inputs))
    return m


def host_core(inputs, bs):
    xin = np.stack([np.concatenate([inputs['ctx'][b].T, inputs['x'][b].T], 1) for b in bs]).astype(np.float32)
    cols = [inputs['c'][b] for b in bs]
    while len(cols) < 2:
        cols.append(cols[-1])
    cc = np.stack(cols + [inputs['c_ctx']], -1)
    cT = np.ascontiguousarray(cc.reshape(KC, 128, 3).transpose(1, 0, 2)).astype(np.float32)
    return {'XIN': np.ascontiguousarray(xin), 'cT': cT}


def kernel(**inputs):
    inputs = {k: np.asarray(v) for k, v in inputs.items()}
    n = 8
    shared = host_shared(inputs)
    nc = build({'nb': NBC, 'ext_in': ('XIN',), 'ext_out': ('OUT',), 'phases': full_phases()})
    in_maps = []
    for core in range(n):
        m = dict(shared)
        m.update(host_core(inputs, [NBC * core + i for i in range(NBC)]))
        in_maps.append(m)
    res = run_bass_kernel_spmd(nc, in_maps, core_ids=list(range(n)))
    out = np.empty((n * NBC, LLAT, D), np.float32)
    for core in range(n):
        o = np.asarray(res.results[core]['OUT'])
        for i in range(NBC):
            out[NBC * core + i] = o[i].T
    return out
```

```python
import numpy as np
from contextlib import ExitStack
import concourse.bass as bass
import concourse.mybir as mybir
from concourse.alu_op_type import AluOpType as ALU
from concourse.bass_utils import run_bass_kernel_spmd

AF = mybir.ActivationFunctionType
F32 = mybir.dt.float32
BF16 = mybir.dt.bfloat16
U32 = mybir.dt.uint32

ENGS = ('pe', 'act', 'dve', 'pool', 'sp')
RING = 12


class Tl:
    __slots__ = ('name', 'w', 'r')

    def __init__(self, name=''):
        self.name = name
        self.w = None
        self.r = []


class Op:
    __slots__ = ('eng', 'fn', 'dma', 'pos', 'cpos', 'waits', 'sig', 'sem', 'val')

    def __init__(self, eng, fn, dma):
        self.eng = eng
        self.fn = fn
        self.dma = dma
        self.waits = []
        self.sig = False
        self.sem = None
        self.val = None


class Prog:
    def __init__(self, nc):
        self.nc = nc
        self.streams = {e: [] for e in ENGS}
        self.ccount = {e: 0 for e in ENGS}
        self.ndma = {e: 0 for e in ENGS}
        self.dmaops = {e: [] for e in ENGS}
        self.wpos = {}
        self.wdma = {}
        self.lastc = {e: None for e in ENGS}

    def _need(self, op, d, kind):
        if d is op:
            return
        if d.dma:
            key = (op.eng, d.eng, d.sem)
            if self.wdma.get(key, 0) >= d.val:
                return
            self.wdma[key] = d.val
            op.waits.append(d)
            return
        if d.eng == op.eng and not op.dma:
            if op.eng == 'pe':
                return
            if kind != 'raw':
                return
            if self.ccount[op.eng] - d.cpos > 2:
                return
        key = (op.eng, d.eng)
        if self.wpos.get(key, -1) >= d.cpos:
            return
        self.wpos[key] = d.cpos
        d.sig = True
        op.waits.append(d)

    def add(self, eng, fn, R=(), W=(), dma=False):
        op = Op(eng, fn, dma)
        op.cpos = self.ccount[eng]
        if dma:
            j = self.ndma[eng]
            self.ndma[eng] += 1
            op.sem = j % RING
            op.val = 16 * (j // RING + 1)
            if j >= RING:
                self._need(op, self.dmaops[eng][j - RING], 'ring')
            self.dmaops[eng].append(op)
        for t in R:
            if t.w is not None:
                self._need(op, t.w, 'raw')
        for t in W:
            if t.w is not None:
                self._need(op, t.w, 'waw')
            for r in t.r:
                self._need(op, r, 'war')
        for t in R:
            t.r.append(op)
        for t in W:
            t.w = op
            t.r = []
        if not dma and fn is not None:
            self.ccount[eng] += 1
            self.lastc[eng] = op
        self.streams[eng].append(op)
        return op

    def barrier(self):
        lasts = [self.lastc[e] for e in ENGS if self.lastc[e] is not None]
        dmas = []
        for e in ENGS:
            dmas += self.dmaops[e][-RING:]
        for e in ENGS:
            op = Op(e, None, False)
            op.cpos = self.ccount[e]
            for d in lasts + dmas:
                if d.eng == e and not d.dma:
                    continue
                self._need(op, d, 'raw')
            self.streams[e].append(op)

    def emit(self, block, csem, dsem):
        nc = self.nc
        for e in ENGS:
            c = 0
            for op in self.streams[e]:
                if not op.dma and op.sig:
                    c += 1
                    op.val = c

        def semof(d):
            return dsem[d.eng][d.sem] if d.dma else csem[d.eng]

        def replay(e):
            def f(eng):
                for op in self.streams[e]:
                    ws = {}
                    for d in op.waits:
                        s = semof(d)
                        k = id(s)
                        if k not in ws or ws[k][1] < d.val:
                            ws[k] = (s, d.val)
                    for s, v in ws.values():
                        eng.wait_ge(s, v)
                    if op.fn is None:
                        continue
                    inst = op.fn(eng)
                    if op.dma:
                        inst.then_inc(dsem[e][op.sem], 16)
                    elif op.sig:
                        inst.then_inc(csem[e], 1)
            return f
        block.tensor(replay('pe'))
        block.scalar(replay('act'))
        block.vector(replay('dve'))
        block.gpsimd(replay('pool'))
        block.sync(replay('sp'))


D = 2048
KC = 16
DEPTH = 4
NBC = 2
LCTX, LLAT = 256, 2048
LSEQ = LCTX + LLAT
DFF = 5504
FC = 43
ALPHA = (2 * DEPTH) ** 0.25
EPS = 1e-6
NAB = 9248


class Ctx:
    def __init__(self, nc):
        self.nc = nc
        self.P = Prog(nc)
        self.dram = {}
        self.stack = None
        self.n = 0

    def din(self, name, shape, dt=F32):
        self.dram[name] = self.nc.dram_tensor(name, list(shape), dt, kind="ExternalInput").ap()
        return self.dram[name]

    def dout(self, name, shape, dt=F32):
        self.dram[name] = self.nc.dram_tensor(name, list(shape), dt, kind="ExternalOutput").ap()
        return self.dram[name]

    def dscr(self, name, shape, dt=F32):
        self.dram[name] = self.nc.dram_tensor(name, list(shape), dt, kind="Internal").ap()
        return self.dram[name]

    def sb(self, shape, dt=F32, name=None):
        self.n += 1
        return self.stack.enter_context(self.nc.sbuf_tensor(name or f"s{self.n}", list(shape), dt))

    def ps(self, shape, dt=F32, name=None):
        self.n += 1
        return self.stack.enter_context(self.nc.psum_tensor(name or f"p{self.n}", list(shape), dt))


def phase_ada(C, layers):
    nc, P = C.nc, C.P
    mod = C.mod
    cT = C.sb([128, KC, 3]); sT = C.sb([128, KC, 3], BF16)
    bt = C.sb([128, DEPTH, 96])
    t_c, t_s, t_b, t_mod = Tl(), Tl(), Tl(), C.t_mod
    P.add('sp', lambda e: e.dma_start(out=cT[:], in_=C.dram['cT'][:]), W=[t_c], dma=True)
    P.add('sp', lambda e: e.dma_start(out=bt[:], in_=C.dram['b_ada_t'][:]), W=[t_b], dma=True)
    P.add('act', lambda e: e.activation(out=sT[:], in_=cT[:], func=AF.Silu), R=[t_c], W=[t_s])
    NW = 3
    wts = [C.sb([128, KC, 128], BF16) for _ in range(NW)]
    t_w = [Tl() for _ in range(NW)]
    pss, t_p = C.bank[:4], C.t_bank[:4]
    i = 0
    for l in layers:
        for oc in range(96):
            w, tw, ps, tp = wts[i % NW], t_w[i % NW], pss[i % 4], t_p[i % 4]
            P.add('pool', lambda e, w=w, l=l, oc=oc: e.dma_start(out=w[:], in_=C.dram['w_ada_t'][l, oc], max_dma_last_dim=4096),
                  W=[tw], dma=True)
            for kc in range(KC):
                P.add('pe', lambda e, w=w, ps=ps, kc=kc: e.matmul(ps[:, 0:3], w[:, kc, :], sT[:, kc, :],
                                                                   start=(kc == 0), stop=(kc == KC - 1)),
                      R=[tw, t_s], W=[tp])
            add1 = 1.0 if (oc // 16) in (1, 4) else 0.0
            P.add('dve', lambda e, ps=ps, l=l, oc=oc, add1=add1: e.tensor_scalar(
                out=mod[:, l, oc, :], in0=ps[:, 0:3], scalar1=bt[:, l, oc:oc + 1], scalar2=add1,
                op0=ALU.add, op1=ALU.add), R=[tp, t_b], W=[t_mod])
            i += 1


def seq_tiles(L):
    return [(o, min(512, L - o)) for o in range(0, L, 512)]


SEQS = ((0, LCTX), (LCTX, LLAT))


def phase_mod0(C, l, xsrc, HT, sh=0, sc=1):
    P, mod = C.P, C.mod
    NB = 3
    xin = [C.sb([128, 512]) for _ in range(NB)]; hb = [C.sb([128, 512], BF16) for _ in range(NB)]
    t_x = [Tl() for _ in range(NB)]; t_h = [Tl() for _ in range(NB)]
    i = 0
    for b in range(C.nb):
        for kc in range(KC):
            for (t0, n) in seq_tiles(LSEQ):
                x, h, tx, th = xin[i % NB], hb[i % NB], t_x[i % NB], t_h[i % NB]
                j = 2 if t0 < LCTX else b
                P.add('sp', lambda e, x=x, b=b, kc=kc, t0=t0, n=n: e.dma_start(
                    out=x[:, :n], in_=xsrc[b, kc * 128:(kc + 1) * 128, t0:t0 + n]), W=[tx], dma=True)
                if t0 < LCTX < t0 + n:
                    for (a, z, jj) in ((0, LCTX - t0, 2), (LCTX - t0, n, b)):
                        P.add('act', lambda e, x=x, h=h, a=a, z=z, jj=jj, kc=kc: e.activation(
                            out=h[:, a:z], in_=x[:, a:z], func=AF.Identity,
                            scale=mod[:, l, sc * 16 + kc, jj:jj + 1], bias=mod[:, l, sh * 16 + kc, jj:jj + 1]),
                            R=[tx, C.t_mod], W=[th])
                else:
                    P.add('act', lambda e, x=x, h=h, n=n, j=j, kc=kc: e.activation(
                        out=h[:, :n], in_=x[:, :n], func=AF.Identity,
                        scale=mod[:, l, sc * 16 + kc, j:j + 1], bias=mod[:, l, sh * 16 + kc, j:j + 1]),
                        R=[tx, C.t_mod], W=[th])
                P.add('act', lambda e, h=h, b=b, kc=kc, t0=t0, n=n: e.dma_start(
                    out=HT[b, kc * 128:(kc + 1) * 128, t0:t0 + n], in_=h[:, :n]), R=[th], W=[C.t_HT], dma=True)
                i += 1


def load_ht(C, HT, b, ht, t_ht):
    for q in range(4):
        C.P.add('sp', lambda e, q=q: e.dma_start(
            out=ht[:, 4 * q:4 * q + 4, :], in_=HT[b, 512 * q:512 * (q + 1), :].rearrange("(kc p) t -> p kc t", p=128)),
            R=[C.t_HT], W=[t_ht], dma=True)


def phase_ffn_up(C, l, HT, HID, skip_ctx=False):
    P = C.P
    ht = C.sb([128, KC, LSEQ], BF16); t_ht = Tl()
    dw = C.sb([128, FC, 4]); t_dw = Tl()
    P.add('sp', lambda e: e.dma_start(out=dw[:], in_=C.dram['ffn_dw_t'][l]), W=[t_dw], dma=True)
    NW = 2
    wa = [C.sb([128, KC, 128], BF16) for _ in range(NW)]; wg = [C.sb([128, KC, 128], BF16) for _ in range(NW)]
    t_wa = [Tl() for _ in range(NW)]; t_wg = [Tl() for _ in range(NW)]
    NA = 2
    a_sb = [C.sb([128, LLAT + 2]) for _ in range(NA)]; g_sb = [C.sb([128, LLAT]) for _ in range(NA)]
    acc = [C.sb([128, LLAT]) for _ in range(NA)]; hid = [C.sb([128, LLAT], BF16) for _ in range(NA)]
    t_a = [Tl() for _ in range(NA)]; t_g = [Tl() for _ in range(NA)]; t_acc = [Tl() for _ in range(NA)]
    t_hid = [Tl() for _ in range(NA)]
    for k in range(NA):
        P.add('pool', lambda e, k=k: e.memset(a_sb[k][:, 0:1], 0.0), W=[t_a[k]])
    ib = 0; ia = 0; iw = 0
    for b in range(C.nb):
        load_ht(C, HT, b, ht, t_ht)
        for j in range(FC):
            w_a, w_g, twa, twg = wa[iw % NW], wg[iw % NW], t_wa[iw % NW], t_wg[iw % NW]; iw += 1
            P.add('pool', lambda e, w=w_a, j=j: e.dma_start(out=w[:], in_=C.dram['w_up_t'][l, j], max_dma_last_dim=4096),
                  W=[twa], dma=True)
            P.add('pool', lambda e, w=w_g, j=j: e.dma_start(out=w[:], in_=C.dram['w_up_t'][l, FC + j], max_dma_last_dim=4096),
                  W=[twg], dma=True)
            for (s0, L) in (SEQS[1:] if skip_ctx else SEQS):
                k = ia % NA; ia += 1
                A, G, AC, HD = a_sb[k], g_sb[k], acc[k], hid[k]
                P.add('pool', lambda e, A=A, L=L: e.memset(A[:, L + 1:L + 2], 0.0), W=[t_a[k]])
                for (t0, n) in seq_tiles(L):
                    pa, pg = C.bank[ib % 8], C.bank[(ib + 1) % 8]
                    tpa, tpg = C.t_bank[ib % 8], C.t_bank[(ib + 1) % 8]; ib += 2
                    for kc in range(KC):
                        P.add('pe', lambda e, pa=pa, w=w_a, kc=kc, s0=s0, t0=t0, n=n: e.matmul(
                            pa[:, :n], w[:, kc, :], ht[:, kc, s0 + t0:s0 + t0 + n], start=(kc == 0), stop=(kc == KC - 1)),
                            R=[twa, t_ht], W=[tpa])
                    for kc in range(KC):
                        P.add('pe', lambda e, pg=pg, w=w_g, kc=kc, s0=s0, t0=t0, n=n: e.matmul(
                            pg[:, :n], w[:, kc, :], ht[:, kc, s0 + t0:s0 + t0 + n], start=(kc == 0), stop=(kc == KC - 1)),
                            R=[twg, t_ht], W=[tpg])
                    P.add('act', lambda e, A=A, pa=pa, t0=t0, n=n: e.copy(out=A[:, 1 + t0:1 + t0 + n], in_=pa[:, :n]),
                          R=[tpa], W=[t_a[k]])
                    P.add('dve', lambda e, G=G, pg=pg, t0=t0, n=n: e.tensor_copy(out=G[:, t0:t0 + n], in_=pg[:, :n]),
                          R=[tpg], W=[t_g[k]])
                P.add('dve', lambda e, A=A, AC=AC, L=L, j=j: e.tensor_scalar(
                    out=AC[:, :L], in0=A[:, 1:L + 1], scalar1=dw[:, j, 1:2], scalar2=dw[:, j, 3:4],
                    op0=ALU.mult, op1=ALU.add), R=[t_a[k], t_dw], W=[t_acc[k]])
                P.add('dve', lambda e, A=A, AC=AC, L=L, j=j: e.scalar_tensor_tensor(
                    out=AC[:, :L], in0=A[:, 0:L], scalar=dw[:, j, 0:1], in1=AC[:, :L],
                    op0=ALU.mult, op1=ALU.add), R=[t_a[k], t_dw, t_acc[k]], W=[t_acc[k]])
                P.add('dve', lambda e, A=A, AC=AC, L=L, j=j: e.scalar_tensor_tensor(
                    out=AC[:, :L], in0=A[:, 2:L + 2], scalar=dw[:, j, 2:3], in1=AC[:, :L],
                    op0=ALU.mult, op1=ALU.add), R=[t_a[k], t_dw, t_acc[k]], W=[t_acc[k]])
                P.add('act', lambda e, AC=AC, L=L: e.activation(out=AC[:, :L], in_=AC[:, :L], func=AF.Gelu),
                      R=[t_acc[k]], W=[t_acc[k]])
                P.add('dve', lambda e, AC=AC, G=G, HD=HD, L=L: e.tensor_tensor(
                    out=HD[:, :L], in0=AC[:, :L], in1=G[:, :L], op=ALU.mult),
                    R=[t_acc[k], t_g[k]], W=[t_hid[k]])
                P.add('sp', lambda e, HD=HD, b=b, j=j, s0=s0, L=L: e.dma_start(
                    out=HID[b, j * 128:(j + 1) * 128, s0:s0 + L], in_=HD[:, :L]),
                    R=[t_hid[k]], W=[C.t_HID], dma=True)


def phase_proj_ln(C, l, SRC, t_src, kcn, wname, widx, mgate, lnidx, xsrc, xdst, t_xdst, HT, hmod, lat_only_out=False, skip_ctx=False):
    P, mod = C.P, C.mod
    NACT = 2
    acts = [C.sb([128, kcn, 512], BF16) for _ in range(NACT)]; t_acts = [Tl() for _ in range(NACT)]
    iact = 0
    resident = kcn <= KC
    NW = KC if resident else 3
    w = [C.sb([128, kcn, 128], BF16) for _ in range(NW)]; t_w = [Tl() for _ in range(NW)]
    if resident:
        for o_ in range(KC):
            P.add('pool', lambda e, o_=o_: e.dma_start(out=w[o_][:], in_=C.dram[wname][widx, o_], max_dma_last_dim=4096),
                  W=[t_w[o_]], dma=True)
    r = C.sb([128, KC, 512]); t_r = [Tl() for _ in range(KC)]
    NX = 3
    xo = [C.sb([128, 512]) for _ in range(NX)]; t_xo = [Tl() for _ in range(NX)]
    sq = [C.sb([128, 512]) for _ in range(2)]; t_sq = [Tl() for _ in range(2)]
    mean = C.sb([128, 512]); rstd = C.sb([128, 512]); t_mean = Tl(); t_rstd = Tl()
    xn = [C.sb([128, 512]) for _ in range(NX)]; t_xn = [Tl() for _ in range(NX)]
    hb = [C.sb([128, 512], BF16) for _ in range(NX)]; t_hb = [Tl() for _ in range(NX)]
    ones = C.sb([128, 128]); t_ones = Tl()
    lnp = C.sb([128, KC, 2]); t_lnp = Tl()
    P.add('pool', lambda e: e.memset(ones[:], 1.0 / D), W=[t_ones])
    P.add('sp', lambda e: e.dma_start(out=lnp[:], in_=C.dram['ln_t'][l, lnidx]), W=[t_lnp], dma=True)
    iw = 0; ix = 0; ib = 0; isq = 0
    pS1, pS2, tS1, tS2 = C.bank[6], C.bank[7], C.t_bank[6], C.t_bank[7]
    for b in range(C.nb):
        for (s0, L) in SEQS:
            j = 2 if s0 < LCTX else b
            if (lat_only_out or skip_ctx) and s0 < LCTX:
                continue
            for (t0, n) in seq_tiles(L):
                T0 = s0 + t0
                act, t_act = acts[iact % NACT], t_acts[iact % NACT]; iact += 1
                nq = 4 if kcn >= 16 else 1
                step = (kcn + nq - 1) // nq
                for q in range(0, kcn, step):
                    z = min(kcn, q + step)
                    P.add('sp', lambda e, q=q, z=z, T0=T0, n=n, b=b, act=act: e.dma_start(
                        out=act[:, q:z, :n], in_=SRC[b, q * 128:z * 128, T0:T0 + n].rearrange("(kc p) t -> p kc t", p=128)),
                        R=[t_src], W=[t_act], dma=True)
                for o in range(KC):
                    wt, tw = w[iw % NW], t_w[iw % NW]; iw += 1
                    if not resident:
                        P.add('pool', lambda e, wt=wt, o=o: e.dma_start(
                            out=wt[:], in_=C.dram[wname][widx, o], max_dma_last_dim=4096), W=[tw], dma=True)
                    pb, tpb = C.bank[ib % 6], C.t_bank[ib % 6]; ib += 1
                    for kc in range(kcn):
                        P.add('pe', lambda e, pb=pb, wt=wt, kc=kc, n=n, act=act: e.matmul(
                            pb[:, :n], wt[:, kc, :], act[:, kc, :n], start=(kc == 0), stop=(kc == kcn - 1)),
                            R=[tw, t_act], W=[tpb])
                    x, tx = xo[ix % NX], t_xo[ix % NX]; ix += 1
                    P.add('sp', lambda e, x=x, b=b, o=o, T0=T0, n=n: e.dma_start(
                        out=x[:, :n], in_=xsrc[b, o * 128:(o + 1) * 128, T0:T0 + n]), W=[tx], dma=True)
                    P.add('act', lambda e, x=x, n=n: e.activation(out=x[:, :n], in_=x[:, :n], func=AF.Identity, scale=float(ALPHA)),
                          R=[tx], W=[tx])
                    P.add('dve', lambda e, pb=pb, x=x, o=o, n=n, j=j: e.scalar_tensor_tensor(
                        out=r[:, o, :n], in0=pb[:, :n], scalar=mod[:, l, mgate * 16 + o, j:j + 1], in1=x[:, :n],
                        op0=ALU.mult, op1=ALU.add), R=[tpb, tx, C.t_mod], W=[t_r[o]])
                    s_, ts = sq[isq % 2], t_sq[isq % 2]; isq += 1
                    P.add('act', lambda e, s_=s_, o=o, n=n: e.activation(out=s_[:, :n], in_=r[:, o, :n], func=AF.Square),
                          R=[t_r[o]], W=[ts])
                    P.add('pe', lambda e, o=o, n=n: e.matmul(pS1[:, :n], ones[:], r[:, o, :n], start=(o == 0), stop=(o == KC - 1)),
                          R=[t_ones, t_r[o]], W=[tS1])
                    P.add('pe', lambda e, s_=s_, o=o, n=n: e.matmul(pS2[:, :n], ones[:], s_[:, :n], start=(o == 0), stop=(o == KC - 1)),
                          R=[t_ones, ts], W=[tS2])
                P.add('act', lambda e, n=n: e.copy(out=mean[:, :n], in_=pS1[:, :n]), R=[tS1], W=[t_mean])
                P.add('dve', lambda e, n=n: e.tensor_tensor(out=rstd[:, :n], in0=mean[:, :n], in1=mean[:, :n], op=ALU.mult),
                      R=[t_mean], W=[t_rstd])
                P.add('dve', lambda e, n=n: e.tensor_tensor(out=rstd[:, :n], in0=pS2[:, :n], in1=rstd[:, :n], op=ALU.subtract),
                      R=[tS2, t_rstd], W=[t_rstd])
                P.add('dve', lambda e, n=n: e.tensor_scalar(out=rstd[:, :n], in0=rstd[:, :n], scalar1=float(EPS), scalar2=None,
                                                            op0=ALU.add), R=[t_rstd], W=[t_rstd])
                P.add('act', lambda e, n=n: e.activation(out=rstd[:, :n], in_=rstd[:, :n], func=AF.Sqrt), R=[t_rstd], W=[t_rstd])
                P.add('dve', lambda e, n=n: e.reciprocal(out=rstd[:, :n], in_=rstd[:, :n]), R=[t_rstd], W=[t_rstd])
                for o in range(KC):
                    k = ix % NX; ix += 1
                    X, tX, H, tH = xn[k], t_xn[k], hb[k], t_hb[k]
                    P.add('dve', lambda e, X=X, o=o, n=n: e.tensor_tensor(out=X[:, :n], in0=r[:, o, :n], in1=mean[:, :n], op=ALU.subtract),
                          R=[t_r[o], t_mean], W=[tX])
                    P.add('pool', lambda e, X=X, n=n: e.tensor_tensor(out=X[:, :n], in0=X[:, :n], in1=rstd[:, :n], op=ALU.mult),
                          R=[tX, t_rstd], W=[tX])
                    P.add('act', lambda e, X=X, o=o, n=n: e.activation(out=X[:, :n], in_=X[:, :n], func=AF.Identity,
                                                                       scale=lnp[:, o, 0:1], bias=lnp[:, o, 1:2]),
                          R=[tX, t_lnp], W=[tX])
                    if lat_only_out:
                        P.add('sp', lambda e, X=X, b=b, o=o, t0=t0, n=n: e.dma_start(
                            out=xdst[b, o * 128:(o + 1) * 128, t0:t0 + n], in_=X[:, :n]), R=[tX], W=[t_xdst], dma=True)
                    else:
                        P.add('sp', lambda e, X=X, b=b, o=o, T0=T0, n=n: e.dma_start(
                            out=xdst[b, o * 128:(o + 1) * 128, T0:T0 + n], in_=X[:, :n]), R=[tX], W=[t_xdst], dma=True)
                    if hmod is not None:
                        hl, hsh, hsc = hmod
                        P.add('act', lambda e, X=X, H=H, o=o, n=n, j=j: e.activation(
                            out=H[:, :n], in_=X[:, :n], func=AF.Identity,
                            scale=mod[:, hl, hsc * 16 + o, j:j + 1], bias=mod[:, hl, hsh * 16 + o, j:j + 1]),
                            R=[tX, C.t_mod], W=[tH])
                        P.add('act', lambda e, H=H, b=b, o=o, T0=T0, n=n: e.dma_start(
                            out=HT[b, o * 128:(o + 1) * 128, T0:T0 + n], in_=H[:, :n]), R=[tH], W=[C.t_HT], dma=True)


NH_C = 16
DH = 128
QSCALE = DH ** -0.5
NBT = 12


def na_plan(qt):
    br = 2 * qt
    if br <= 2:
        rows = [0, 2, 4, 6]
        return rows, (rows[0] - br + 6) // 2
    if br >= 28:
        rows = [24, 26, 28, 30]
        return rows, (rows[0] - br + 6) // 2
    return [br - 4, br - 2, br, br + 2, br + 4], 7


def phase_qkv(C, l, HT, need_ctx):
    P = C.P
    o = l // 2
    QT, KT, V = C.dram['QT'], C.dram['KT'], C.dram['V']
    ht = C.sb([128, KC, LSEQ], BF16); t_ht = Tl()
    rope = C.sb([128, 2, LLAT]); t_rope = Tl()
    pmT = C.sb([128, 128]); t_pm = Tl()
    P.add('sp', lambda e: e.dma_start(out=rope[:], in_=C.dram['rope_t'][:]), W=[t_rope], dma=True)
    P.add('sp', lambda e: e.dma_start(out=pmT[:], in_=C.dram['pmT'][:]), W=[t_pm], dma=True)
    NW = 2
    w = [C.sb([128, KC, 128], BF16) for _ in range(NW)]; t_w = [Tl() for _ in range(NW)]
    wv = [C.sb([128, KC, 512], BF16) for _ in range(NW)]; t_wv = [Tl() for _ in range(NW)]
    NX = 3
    xs = [C.sb([128, 512]) for _ in range(NX)]; t_xs = [Tl() for _ in range(NX)]
    t1 = [C.sb([128, 512]) for _ in range(NX)]; t_t1 = [Tl() for _ in range(NX)]
    ob = [C.sb([128, 512], BF16) for _ in range(NX)]; t_ob = [Tl() for _ in range(NX)]
    iw = 0; ix = 0; ib = 0
    for b in range(C.nb):
        load_ht(C, HT, b, ht, t_ht)
        for j in range(32):
            isq = j < 16
            dst = QT if isq else KT
            wt, tw = w[iw % NW], t_w[iw % NW]; iw += 1
            P.add('pool', lambda e, wt=wt, j=j: e.dma_start(out=wt[:], in_=C.dram['w_inc_t'][o, j], max_dma_last_dim=4096),
                  W=[tw], dma=True)
            for (s0, L) in SEQS:
                isctx = s0 < LCTX
                if isctx and isq and not need_ctx:
                    continue
                for (t0, n) in seq_tiles(L):
                    pb, tpb = C.bank[ib % 4], C.t_bank[ib % 4]
                    pr, tpr = C.bank[4 + ib % 4], C.t_bank[4 + ib % 4]; ib += 1
                    for kc in range(KC):
                        P.add('pe', lambda e, pb=pb, wt=wt, kc=kc, s0=s0, t0=t0, n=n: e.matmul(
                            pb[:, :n], wt[:, kc, :], ht[:, kc, s0 + t0:s0 + t0 + n], start=(kc == 0), stop=(kc == KC - 1)),
                            R=[tw, t_ht], W=[tpb])
                    k = ix % NX; ix += 1
                    X, tX, T1, tT1, O, tO = xs[k], t_xs[k], t1[k], t_t1[k], ob[k], t_ob[k]
                    sc = float(QSCALE) if isq else 1.0
                    if isctx:
                        P.add('act', lambda e, O=O, pb=pb, n=n, sc=sc: e.activation(out=O[:, :n], in_=pb[:, :n], func=AF.Identity, scale=sc),
                              R=[tpb], W=[tO])
                    else:
                        P.add('act', lambda e, X=X, pb=pb, n=n, sc=sc: e.activation(out=X[:, :n], in_=pb[:, :n], func=AF.Identity, scale=sc),
                              R=[tpb], W=[tX])
                        P.add('pe', lambda e, pr=pr, X=X, n=n: e.matmul(pr[:, :n], pmT[:], X[:, :n], start=True, stop=True),
                              R=[t_pm, tX], W=[tpr])
                        P.add('pool', lambda e, X=X, T1=T1, t0=t0, n=n: e.tensor_tensor(
                            out=T1[:, :n], in0=X[:, :n], in1=rope[:, 0, t0:t0 + n], op=ALU.mult), R=[tX, t_rope], W=[tT1])
                        P.add('dve', lambda e, X=X, pr=pr, t0=t0, n=n: e.tensor_tensor(
                            out=X[:, :n], in0=pr[:, :n], in1=rope[:, 1, t0:t0 + n], op=ALU.mult), R=[tpr, t_rope, tX], W=[tX])
                        P.add('dve', lambda e, X=X, T1=T1, O=O, n=n: e.tensor_tensor(
                            out=O[:, :n], in0=X[:, :n], in1=T1[:, :n], op=ALU.add), R=[tX, tT1], W=[tO])
                    hh = j % 16
                    P.add('sp', lambda e, O=O, dst=dst, b=b, hh=hh, s0=s0, t0=t0, n=n: e.dma_start(
                        out=dst[b, hh * 128:(hh + 1) * 128, s0 + t0:s0 + t0 + n], in_=O[:, :n]), R=[tO], W=[C.t_QK], dma=True)
        for cg in range(4):
            wt, tw = wv[iw % NW], t_wv[iw % NW]; iw += 1
            for q in range(0, KC, 2):
                P.add('pool', lambda e, wt=wt, cg=cg, q=q: e.dma_start(
                    out=wt[:, q:q + 2, :], in_=C.dram['w_v_t'][o, cg, :, q:q + 2, :], max_dma_last_dim=4096), W=[tw], dma=True)
            for tt in range(LSEQ // 128):
                pb, tpb = C.bank[ib % 8], C.t_bank[ib % 8]; ib += 1
                for kc in range(KC):
                    P.add('pe', lambda e, pb=pb, wt=wt, kc=kc, tt=tt: e.matmul(
                        pb[:, :], ht[:, kc, tt * 128:(tt + 1) * 128], wt[:, kc, :], start=(kc == 0), stop=(kc == KC - 1)),
                        R=[tw, t_ht], W=[tpb])
                k = ix % NX; ix += 1
                O, tO = ob[k], t_ob[k]
                eng = 'act' if tt % 2 == 0 else 'dve'
                if eng == 'act':
                    P.add('act', lambda e, O=O, pb=pb: e.copy(out=O[:, :], in_=pb[:, :]), R=[tpb], W=[tO])
                else:
                    P.add('dve', lambda e, O=O, pb=pb: e.tensor_copy(out=O[:, :], in_=pb[:, :]), R=[tpb], W=[tO])
                P.add('sp', lambda e, O=O, b=b, tt=tt, cg=cg: e.dma_start(
                    out=V[b, tt * 128:(tt + 1) * 128, cg * 512:(cg + 1) * 512], in_=O[:, :]), R=[tO], W=[C.t_QK], dma=True)


def phase_attn(C, l, need_ctx):
    P = C.P
    o = l // 2
    QT, KT, V, YT = C.dram['QT'], C.dram['KT'], C.dram['V'], C.dram['YT']
    NT = LSEQ // 128
    NB2 = 2
    kt = [C.sb([128, LSEQ], BF16) for _ in range(NB2)]; qt_ = [C.sb([128, LSEQ], BF16) for _ in range(NB2)]
    vt = [C.sb([128, NT, 128], BF16) for _ in range(NB2)]; yt = [C.sb([128, LSEQ], BF16) for _ in range(NB2)]
    bt = [C.sb([128, NBT, 128]) for _ in range(NB2)]
    t_kt = [Tl() for _ in range(NB2)]; t_qt = [Tl() for _ in range(NB2)]; t_vt = [Tl() for _ in range(NB2)]
    t_yt = [Tl() for _ in range(NB2)]; t_bt = [Tl() for _ in range(NB2)]
    msk = C.sb([128, NBT, 128]); t_msk = Tl()
    ones = C.sb([128, 128], BF16); t_ones = Tl()
    P.add('sp', lambda e: e.dma_start(out=msk[:], in_=C.dram['na_mask'][:]), W=[t_msk], dma=True)
    P.add('pool', lambda e: e.memset(ones[:], 1.0), W=[t_ones])
    NS = 2
    sc = [C.sb([128, 5, 128]) for _ in range(NS)]; t_sc = [Tl() for _ in range(NS)]
    pT = [C.sb([128, 7, 128], BF16) for _ in range(NS)]; t_pT = [Tl() for _ in range(NS)]
    rec = [C.sb([128, 256]) for _ in range(NS)]; t_rec = [Tl() for _ in range(NS)]
    ih = 0; iq = 0
    for b in range(C.nb):
        for h in range(NH_C):
            k = ih % NB2; ih += 1
            K_, Q_, V_, Y_, B_ = kt[k], qt_[k], vt[k], yt[k], bt[k]
            P.add('sp', lambda e, K_=K_, b=b, h=h: e.dma_start(out=K_[:], in_=KT[b, h * 128:(h + 1) * 128, :]),
                  R=[C.t_QK], W=[t_kt[k]], dma=True)
            P.add('sp', lambda e, Q_=Q_, b=b, h=h: e.dma_start(out=Q_[:], in_=QT[b, h * 128:(h + 1) * 128, :]),
                  R=[C.t_QK], W=[t_qt[k]], dma=True)
            P.add('act', lambda e, V_=V_, b=b, h=h: e.dma_start(
                out=V_[:], in_=V[b, :, h * 128:(h + 1) * 128].rearrange("(tt p) d -> p tt d", p=128)),
                R=[C.t_QK], W=[t_vt[k]], dma=True)
            P.add('act', lambda e, B_=B_, h=h: e.dma_start(out=B_[:], in_=C.dram['na_bias_t'][o, h]), W=[t_bt[k]], dma=True)
            P.add('pool', lambda e, B_=B_: e.tensor_tensor(out=B_[:], in0=B_[:], in1=msk[:], op=ALU.add),
                  R=[t_bt[k], t_msk], W=[t_bt[k]])
            for qi in range(LLAT // 128):
                rows, b0 = na_plan(qi)
                nl = len(rows)
                s = iq % NS; iq += 1
                X, Y, Z = C.bank[3 * s], C.bank[3 * s + 1], C.bank[3 * s + 2]
                tX, tY, tZ = C.t_bank[3 * s], C.t_bank[3 * s + 1], C.t_bank[3 * s + 2]
                q0 = LCTX + qi * 128
                for i, a in enumerate(rows):
                    dstp, tdst, col = (X, tX, i * 128) if i < 4 else (Y, tY, 0)
                    k0 = LCTX + a * 64
                    P.add('pe', lambda e, dstp=dstp, col=col, K_=K_, Q_=Q_, k0=k0, q0=q0: e.matmul(
                        dstp[:, col:col + 128], K_[:, k0:k0 + 128], Q_[:, q0:q0 + 128], start=True, stop=True),
                        R=[t_kt[k], t_qt[k]], W=[tdst])
                for i in range(2):
                    P.add('pe', lambda e, Y=Y, i=i, K_=K_, Q_=Q_, q0=q0: e.matmul(
                        Y[:, 128 + i * 128:256 + i * 128], K_[:, i * 128:(i + 1) * 128], Q_[:, q0:q0 + 128], start=True, stop=True),
                        R=[t_kt[k], t_qt[k]], W=[tY])
                S_, tS, PT, tPT, RC, tRC = sc[s], t_sc[s], pT[s], t_pT[s], rec[s], t_rec[s]
                P.add('dve', lambda e, S_=S_, X=X, B_=B_, b0=b0: e.tensor_tensor(
                    out=S_[:, 0:4, :], in0=X[:, 0:512].rearrange("p (a c) -> p a c", c=128), in1=B_[:, b0:b0 + 4, :], op=ALU.add), R=[tX, t_bt[k]], W=[tS])
                if nl == 5:
                    P.add('dve', lambda e, S_=S_, Y=Y, B_=B_, b0=b0: e.tensor_tensor(
                        out=S_[:, 4, :], in0=Y[:, 0:128], in1=B_[:, b0 + 4, :], op=ALU.add), R=[t_bt[k]], W=[tS, tY])
                P.add('act', lambda e, PT=PT, S_=S_, nl=nl: e.activation(out=PT[:, 0:nl, :], in_=S_[:, 0:nl, :], func=AF.Exp),
                      R=[tS], W=[tPT])
                P.add('act', lambda e, PT=PT, Y=Y, nl=nl: e.activation(out=PT[:, nl:nl + 2, :], in_=Y[:, 128:384].rearrange("p (a c) -> p a c", c=128), func=AF.Exp),
                      R=[], W=[tPT, tY])
                vidx = [2 + a // 2 for a in rows] + [0, 1]
                for i, vi in enumerate(vidx):
                    P.add('pe', lambda e, Z=Z, V_=V_, PT=PT, i=i, vi=vi, nn=len(vidx): e.matmul(
                        Z[:, 0:128], V_[:, vi, :], PT[:, i, :], start=(i == 0), stop=(i == nn - 1)),
                        R=[t_vt[k], tPT], W=[tZ])
                for i in range(len(vidx)):
                    P.add('pe', lambda e, Z=Z, PT=PT, i=i, nn=len(vidx): e.matmul(
                        Z[:, 128:256], ones[:], PT[:, i, :], start=(i == 0), stop=(i == nn - 1)),
                        R=[t_ones, tPT], W=[tZ])
                P.add('dve', lambda e, RC=RC, Z=Z: e.reciprocal(out=RC[:, 0:128], in_=Z[:, 128:256]), R=[tZ], W=[tRC])
                P.add('dve', lambda e, Y_=Y_, Z=Z, RC=RC, q0=q0: e.tensor_tensor(
                    out=Y_[:, q0:q0 + 128], in0=Z[:, 0:128], in1=RC[:, 0:128], op=ALU.mult), R=[tZ, tRC], W=[t_yt[k]])
            if need_ctx:
                s = iq % NS; iq += 1
                X, Z = C.bank[3 * s], C.bank[3 * s + 2]
                tX, tZ = C.t_bank[3 * s], C.t_bank[3 * s + 2]
                PT, tPT, RC, tRC = pT[s], t_pT[s], rec[s], t_rec[s]
                for i in range(2):
                    P.add('pe', lambda e, X=X, i=i, K_=K_, Q_=Q_: e.matmul(
                        X[:, i * 256:(i + 1) * 256], K_[:, i * 128:(i + 1) * 128], Q_[:, 0:256], start=True, stop=True),
                        R=[t_kt[k], t_qt[k]], W=[tX])
                P.add('act', lambda e, PT=PT, X=X: e.activation(out=PT[:, 0:4, :], in_=X[:, 0:512].rearrange("p (a c) -> p a c", c=128), func=AF.Exp), R=[tX], W=[tPT])
                for i in range(2):
                    P.add('pe', lambda e, Z=Z, V_=V_, PT=PT, i=i: e.matmul(
                        Z[:, 0:256], V_[:, i, :], PT[:, 2 * i:2 * i + 2, :], start=(i == 0), stop=(i == 1)),
                        R=[t_vt[k], tPT], W=[tZ])
                for i in range(2):
                    P.add('pe', lambda e, Z=Z, PT=PT, i=i: e.matmul(
                        Z[:, 256:512], ones[:], PT[:, 2 * i:2 * i + 2, :], start=(i == 0), stop=(i == 1)),
                        R=[t_ones, tPT], W=[tZ])
                P.add('dve', lambda e, RC=RC, Z=Z: e.reciprocal(out=RC[:, 0:256], in_=Z[:, 256:512]), R=[tZ], W=[tRC])
                P.add('dve', lambda e, Y_=Y_, Z=Z, RC=RC: e.tensor_tensor(
                    out=Y_[:, 0:256], in0=Z[:, 0:256], in1=RC[:, 0:256], op=ALU.mult), R=[tZ, tRC], W=[t_yt[k]])
            c0 = 0 if need_ctx else LCTX
            P.add('sp', lambda e, Y_=Y_, b=b, h=h, c0=c0: e.dma_start(
                out=YT[b, h * 128:(h + 1) * 128, c0:LSEQ], in_=Y_[:, c0:LSEQ]), R=[t_yt[k]], W=[C.t_YT], dma=True)


NHA = 8


def phase_inab(C, l, HT):
    P = C.P
    e = l // 2
    Dm = C.dram
    ht = C.sb([128, KC, LSEQ], BF16); t_ht = Tl()
    gconv = C.sb([128, 24, 5]); t_gc = Tl()
    lbr = C.sb([128, 2, 2, NHA]); lb = C.sb([128, 2, NHA]); oml = C.sb([128, 2, NHA]); t_lb = Tl()
    ident = C.sb([128, 128]); t_id = Tl()
    ones = C.sb([128, 128]); t_ones = Tl()
    P.add('sp', lambda e_: e_.dma_start(out=gconv[:], in_=Dm['gconv_t'][e]), W=[t_gc], dma=True)
    P.add('sp', lambda e_: e_.dma_start(out=ident[:], in_=Dm['ident'][:]), W=[t_id], dma=True)
    P.add('pool', lambda e_: e_.memset(ones[:], 1.0), W=[t_ones])
    if e == 0:
        P.add('pool', lambda e_: e_.memset(lb[:], 0.0), W=[t_lb])
        P.add('pool', lambda e_: e_.memset(oml[:], 1.0), W=[t_lb])
    else:
        P.add('sp', lambda e_: e_.dma_start(out=lbr[:], in_=Dm['lb_t'][:]), W=[t_lb], dma=True)
        P.add('dve', lambda e_: e_.tensor_tensor(out=lb[:], in0=lbr[:, 1], in1=lbr[:, 0], op=ALU.subtract), R=[t_lb], W=[t_lb])
        P.add('act', lambda e_: e_.activation(out=lb[:], in_=lb[:], func=AF.Sigmoid), R=[t_lb], W=[t_lb])
        P.add('dve', lambda e_: e_.tensor_scalar(out=oml[:], in0=lb[:], scalar1=-1.0, scalar2=1.0, op0=ALU.mult, op1=ALU.add),
              R=[t_lb], W=[t_lb])
    NW = 2
    w = [C.sb([128, KC, 128], BF16) for _ in range(NW)]; t_w = [Tl() for _ in range(NW)]
    wv = [C.sb([128, KC, 512], BF16) for _ in range(NW)]; t_wv = [Tl() for _ in range(NW)]
    wg = C.sb([128, KC, 32], BF16); t_wg = Tl()
    NR = 2
    xr = [C.sb([128, LLAT + 4]) for _ in range(NR)]; yr = [C.sb([128, LLAT]) for _ in range(NR)]
    obf = [C.sb([128, LLAT], BF16) for _ in range(NR)]
    tst = [C.sb([128, LLAT // 128, 128], BF16) for _ in range(NR)]
    t_xr = [Tl() for _ in range(NR)]; t_yr = [Tl() for _ in range(NR)]; t_ob = [Tl() for _ in range(NR)]; t_ts = [Tl() for _ in range(NR)]
    vst = [C.sb([128, 512], BF16) for _ in range(3)]; t_vst = [Tl() for _ in range(3)]
    gst = C.sb([128, LSEQ // 128, 32]); t_gst = Tl()
    for k in range(NR):
        P.add('pool', lambda e_, k=k: e_.memset(xr[k][:, 0:2], 0.0), W=[t_xr[k]])
    iw = 0; ir = 0; ib = 0; iv = 0
    for b in range(C.nb):
        load_ht(C, HT, b, ht, t_ht)
        for j in range(72):
            grp, hd = j // 8, j % 8
            if grp == 3:
                continue
            wt, tw = w[iw % NW], t_w[iw % NW]; iw += 1
            P.add('pool', lambda e_, wt=wt, j=j: e_.dma_start(out=wt[:], in_=Dm['w_inab_t'][e, j], max_dma_last_dim=4096),
                  W=[tw], dma=True)
            kind = {0: 'silu', 1: 'f', 2: 'f', 4: 'silu', 5: 'conv', 6: 'conv', 7: 'conv', 8: 'silu'}[grp]
            for (s0, L) in SEQS:
                k = ir % NR; ir += 1
                XR, YR, OB, TS = xr[k], yr[k], obf[k], tst[k]
                tXR, tYR, tOB, tTS = t_xr[k], t_yr[k], t_ob[k], t_ts[k]
                pad = 2 if kind == 'conv' else 0
                if kind == 'conv':
                    P.add('pool', lambda e_, XR=XR, L=L: e_.memset(XR[:, L + 2:L + 4], 0.0), W=[tXR])
                    if L != LLAT:
                        P.add('pool', lambda e_, XR=XR: e_.memset(XR[:, 0:2], 0.0), W=[tXR])
                for (t0, n) in seq_tiles(L):
                    pb, tpb = C.bank[ib % 6], C.t_bank[ib % 6]; ib += 1
                    for kc in range(KC):
                        P.add('pe', lambda e_, pb=pb, wt=wt, kc=kc, s0=s0, t0=t0, n=n: e_.matmul(
                            pb[:, :n], wt[:, kc, :], ht[:, kc, s0 + t0:s0 + t0 + n], start=(kc == 0), stop=(kc == KC - 1)),
                            R=[tw, t_ht], W=[tpb])
                    fn = {'silu': AF.Silu, 'f': AF.Sigmoid, 'conv': AF.Identity}[kind]
                    P.add('act', lambda e_, XR=XR, pb=pb, t0=t0, n=n, fn=fn, pad=pad: e_.activation(
                        out=XR[:, pad + t0:pad + t0 + n], in_=pb[:, :n], func=fn), R=[tpb], W=[tXR])
                if kind == 'silu':
                    dst = {0: 'QA', 4: 'GA', 8: 'ZB'}[grp]
                    P.add('sp', lambda e_, XR=XR, dst=dst, b=b, hd=hd, s0=s0, L=L: e_.dma_start(
                        out=Dm[dst][b, hd * 128:(hd + 1) * 128, s0:s0 + L], in_=XR[:, :L]), R=[tXR], W=[C.t_AB], dma=True)
                elif kind == 'f':
                    d = grp - 1
                    P.add('dve', lambda e_, XR=XR, L=L, d=d, hd=hd: e_.tensor_scalar(
                        out=XR[:, :L], in0=XR[:, :L], scalar1=oml[:, d, hd:hd + 1], scalar2=lb[:, d, hd:hd + 1],
                        op0=ALU.mult, op1=ALU.add), R=[tXR, t_lb], W=[tXR])
                    P.add('dve', lambda e_, XR=XR, YR=YR, L=L: e_.tensor_scalar(
                        out=YR[:, :L], in0=XR[:, :L], scalar1=-1.0, scalar2=1.0, op0=ALU.mult, op1=ALU.add), R=[tXR], W=[tYR])
                    P.add('sp', lambda e_, YR=YR, b=b, d=d, hd=hd, s0=s0, L=L: e_.dma_start(
                        out=Dm['KF'][b, d, hd * 128:(hd + 1) * 128, s0:s0 + L], in_=YR[:, :L]), R=[tYR], W=[C.t_AB], dma=True)
                    P.add('act', lambda e_, XR=XR, L=L: e_.activation(out=XR[:, :L], in_=XR[:, :L], func=AF.Ln), R=[tXR, tYR], W=[tXR])
                    P.add('sp', lambda e_, XR=XR, b=b, d=d, hd=hd, s0=s0, L=L: e_.dma_start(
                        out=Dm['LF'][b, d, hd * 128:(hd + 1) * 128, s0:s0 + L], in_=XR[:, :L]), R=[tXR], W=[C.t_AB], dma=True)
                else:
                    ci = j - 40
                    P.add('dve', lambda e_, XR=XR, YR=YR, L=L, ci=ci: e_.tensor_scalar(
                        out=YR[:, :L], in0=XR[:, 2:L + 2], scalar1=gconv[:, ci, 2:3], scalar2=None, op0=ALU.mult),
                        R=[tXR, t_gc], W=[tYR])
                    for tap in (0, 1, 3, 4):
                        P.add('dve', lambda e_, XR=XR, YR=YR, L=L, ci=ci, tap=tap: e_.scalar_tensor_tensor(
                            out=YR[:, :L], in0=XR[:, tap:tap + L], scalar=gconv[:, ci, tap:tap + 1], in1=YR[:, :L],
                            op0=ALU.mult, op1=ALU.add), R=[tXR, t_gc, tYR], W=[tYR])
                    P.add('act', lambda e_, YR=YR, L=L: e_.activation(out=YR[:, :L], in_=YR[:, :L], func=AF.Silu), R=[tYR], W=[tYR])
                    if grp in (5, 6):
                        P.add('act', lambda e_, XR=XR, YR=YR, L=L: e_.activation(out=XR[:, :L], in_=YR[:, :L], func=AF.Square),
                              R=[tYR], W=[tXR])
                        for (t0, n) in seq_tiles(L):
                            pb, tpb = C.bank[6 + ib % 2], C.t_bank[6 + ib % 2]; ib += 1
                            P.add('pe', lambda e_, pb=pb, XR=XR, t0=t0, n=n: e_.matmul(pb[:, :n], ones[:], XR[:, t0:t0 + n], start=True, stop=True),
                                  R=[t_ones, tXR], W=[tpb])
                            P.add('dve', lambda e_, pb=pb, XR=XR, t0=t0, n=n: e_.tensor_scalar(
                                out=XR[:, t0:t0 + n], in0=pb[:, :n], scalar1=float(EPS), scalar2=None, op0=ALU.add), R=[tpb], W=[tXR, tpb])
                        P.add('act', lambda e_, XR=XR, L=L: e_.activation(out=XR[:, :L], in_=XR[:, :L], func=AF.Sqrt), R=[tXR], W=[tXR])
                        P.add('dve', lambda e_, XR=XR, L=L: e_.reciprocal(out=XR[:, :L], in_=XR[:, :L]), R=[tXR], W=[tXR])
                        qs = float(QSCALE) if grp == 5 else 1.0
                        P.add('dve', lambda e_, XR=XR, YR=YR, L=L, qs=qs: e_.scalar_tensor_tensor(
                            out=YR[:, :L], in0=YR[:, :L], scalar=qs, in1=XR[:, :L], op0=ALU.mult, op1=ALU.mult),
                            R=[tXR, tYR], W=[tYR])
                        P.add('pool', lambda e_, OB=OB, YR=YR, L=L: e_.tensor_copy(out=OB[:, :L], in_=YR[:, :L]), R=[tYR], W=[tOB])
                        dst = 'QB' if grp == 5 else 'KB'
                        P.add('sp', lambda e_, OB=OB, dst=dst, b=b, hd=hd, s0=s0, L=L: e_.dma_start(
                            out=Dm[dst][b, hd * 128:(hd + 1) * 128, s0:s0 + L], in_=OB[:, :L]), R=[tOB], W=[C.t_AB], dma=True)
                    if grp in (6, 7):
                        for tt in range(L // 128):
                            pb, tpb = C.bank[6 + ib % 2], C.t_bank[6 + ib % 2]; ib += 1
                            P.add('pe', lambda e_, pb=pb, YR=YR, tt=tt: e_.transpose(pb[:, 0:128], YR[:, tt * 128:(tt + 1) * 128], ident[:]),
                                  R=[tYR, t_id], W=[tpb])
                            if tt % 2 == 0:
                                P.add('act', lambda e_, TS=TS, pb=pb, tt=tt: e_.copy(out=TS[:, tt, :], in_=pb[:, 0:128]), R=[], W=[tTS, tpb])
                            else:
                                P.add('dve', lambda e_, TS=TS, pb=pb, tt=tt: e_.tensor_copy(out=TS[:, tt, :], in_=pb[:, 0:128]), R=[], W=[tTS, tpb])
                        dst = 'KBt' if grp == 6 else 'VBt'
                        P.add('sp', lambda e_, TS=TS, dst=dst, b=b, hd=hd, s0=s0, L=L: e_.dma_start(
                            out=Dm[dst][b, s0:s0 + L, hd * 128:(hd + 1) * 128].rearrange("(tt p) d -> p tt d", p=128),
                            in_=TS[:, 0:L // 128, :]), R=[tTS], W=[C.t_AB], dma=True)
        for cg in range(2):
            wt, tw = wv[iw % NW], t_wv[iw % NW]; iw += 1
            for q in range(0, KC, 2):
                P.add('pool', lambda e_, wt=wt, cg=cg, q=q: e_.dma_start(
                    out=wt[:, q:q + 2, :], in_=Dm['w_ia_t'][e, cg, :, q:q + 2, :], max_dma_last_dim=4096), W=[tw], dma=True)
            for tt in range(LSEQ // 128):
                pb, tpb = C.bank[ib % 6], C.t_bank[ib % 6]; ib += 1
                for kc in range(KC):
                    P.add('pe', lambda e_, pb=pb, wt=wt, kc=kc, tt=tt: e_.matmul(
                        pb[:, :], ht[:, kc, tt * 128:(tt + 1) * 128], wt[:, kc, :], start=(kc == 0), stop=(kc == KC - 1)),
                        R=[tw, t_ht], W=[tpb])
                O, tO = vst[iv % 3], t_vst[iv % 3]; iv += 1
                if tt % 2 == 0:
                    P.add('act', lambda e_, O=O, pb=pb: e_.copy(out=O[:, :], in_=pb[:, :]), R=[tpb], W=[tO])
                else:
                    P.add('dve', lambda e_, O=O, pb=pb: e_.tensor_copy(out=O[:, :], in_=pb[:, :]), R=[tpb], W=[tO])
                P.add('sp', lambda e_, O=O, b=b, tt=tt, cg=cg: e_.dma_start(
                    out=Dm['VA'][b, tt * 128:(tt + 1) * 128, cg * 512:(cg + 1) * 512], in_=O[:, :]), R=[tO], W=[C.t_AB], dma=True)
        P.add('pool', lambda e_: e_.dma_start(out=wg[:], in_=Dm['w_gb_t'][e], max_dma_last_dim=4096), W=[t_wg], dma=True)
        for tt in range(LSEQ // 128):
            pb, tpb = C.bank[ib % 6], C.t_bank[ib % 6]; ib += 1
            for kc in range(KC):
                P.add('pe', lambda e_, pb=pb, kc=kc, tt=tt: e_.matmul(
                    pb[:, 0:32], ht[:, kc, tt * 128:(tt + 1) * 128], wg[:, kc, :], start=(kc == 0), stop=(kc == KC - 1)),
                    R=[t_wg, t_ht], W=[tpb])
            P.add('dve', lambda e_, pb=pb, tt=tt: e_.tensor_copy(out=gst[:, tt, :], in_=pb[:, 0:32]), R=[tpb], W=[t_gst])
        P.add('sp', lambda e_, b=b: e_.dma_start(
            out=Dm['GBt'][b].rearrange("(tt p) c -> p tt c", p=128), in_=gst[:]), R=[t_gst], W=[C.t_AB], dma=True)


NCH = LSEQ // 128


def interleave(gens):
    gens = list(gens)
    while gens:
        for g in list(gens):
            try:
                next(g)
            except StopIteration:
                gens.remove(g)


def chunk_order(d):
    ctx, lat = [0, 1], list(range(2, NCH))
    return (ctx + lat) if d == 0 else (ctx[::-1] + lat[::-1])


def merge_head(C, OA, t_oa, gate, t_gate, normw, t_nw, col, ybf, t_ybf, onesm, t_onesm, tmp, t_tmp, bank, t_bank, dst_row, b):
    P = C.P
    for (t0, n) in seq_tiles(LSEQ):
        P.add('act', lambda e, t0=t0, n=n: e.activation(out=tmp[:, :n], in_=OA[:, t0:t0 + n], func=AF.Square), R=[t_oa], W=[t_tmp])
        P.add('pe', lambda e, n=n: e.matmul(bank[:, :n], onesm[:], tmp[:, :n], start=True, stop=True), R=[t_onesm, t_tmp], W=[t_bank])
        P.add('dve', lambda e, n=n: e.tensor_scalar(out=tmp[:, :n], in0=bank[:, :n], scalar1=float(EPS), scalar2=None, op0=ALU.add),
              R=[], W=[t_tmp, t_bank])
        P.add('act', lambda e, n=n: e.activation(out=tmp[:, :n], in_=tmp[:, :n], func=AF.Sqrt), R=[t_tmp], W=[t_tmp])
        P.add('dve', lambda e, n=n: e.reciprocal(out=tmp[:, :n], in_=tmp[:, :n]), R=[t_tmp], W=[t_tmp])
        P.add('dve', lambda e, t0=t0, n=n: e.scalar_tensor_tensor(
            out=tmp[:, :n], in0=OA[:, t0:t0 + n], scalar=normw[:, col:col + 1], in1=tmp[:, :n], op0=ALU.mult, op1=ALU.mult),
            R=[t_oa, t_nw, t_tmp], W=[t_tmp])
        P.add('pool', lambda e, t0=t0, n=n: e.tensor_tensor(out=ybf[:, t0:t0 + n], in0=tmp[:, :n], in1=gate[:, t0:t0 + n], op=ALU.mult),
              R=[t_tmp, t_gate], W=[t_ybf])
    P.add('sp', lambda e: e.dma_start(out=C.dram['YT'][b, dst_row:dst_row + 128, :], in_=ybf[:]), R=[t_ybf], W=[C.t_YT], dma=True)


def phase_gla(C, l):
    CH = 64; NCA = LSEQ // CH; NCTX = LCTX // CH
    P = C.P
    e = l // 2
    Dm = C.dram
    row = lambda: C.sb([128, LSEQ])
    q = row(); lf = [row(), row()]; kf = [row(), row()]; pp = [row(), row()]; OA = row(); ga = row(); rst = row()
    ybf = C.sb([128, LSEQ], BF16); va = C.sb([CH, NCA, 128], BF16)
    t_q = Tl(); t_lf = [Tl(), Tl()]; t_kf = [Tl(), Tl()]; t_pp = [Tl(), Tl()]; t_oa = Tl(); t_ga = Tl(); t_rst = Tl()
    t_ybf = Tl(); t_va = Tl()
    nmid = [C.sb([128, NCA]) for _ in range(2)]; t_nmid = [Tl(), Tl()]
    edec = [C.sb([128, NCA]) for _ in range(2)]; t_edec = [Tl(), Tl()]
    msk = C.sb([128, 2, 128], U32); t_msk = Tl()
    ident = C.sb([128, 128]); t_id = Tl()
    onesm = C.sb([128, 128]); t_onesm = Tl()
    anorm = C.sb([128, 2]); t_an = Tl()
    tmp = C.sb([128, 512]); t_tmp = Tl()
    S = [C.sb([128, 128]) for _ in range(2)]; Sb = [C.sb([128, 128], BF16) for _ in range(2)]
    t_S = [Tl(), Tl()]; t_Sb = [Tl(), Tl()]
    NU = 2
    E = [[[C.sb([128, CH]) for _ in range(4)] for _ in range(NU)] for _ in range(2)]
    t_E = [[[Tl() for _ in range(4)] for _ in range(NU)] for _ in range(2)]
    qe = [[C.sb([128, CH], BF16) for _ in range(NU)] for _ in range(2)]; ke = [[C.sb([128, CH], BF16) for _ in range(NU)] for _ in range(2)]
    qg = [[C.sb([128, CH], BF16) for _ in range(NU)] for _ in range(2)]; kd = [[C.sb([128, CH]) for _ in range(NU)] for _ in range(2)]
    atm = [[C.sb([CH, CH], BF16) for _ in range(NU)] for _ in range(2)]; kdt = [[C.sb([CH, 128], BF16) for _ in range(NU)] for _ in range(2)]
    mk = lambda: [[Tl() for _ in range(NU)] for _ in range(2)]
    t_qe, t_ke, t_qg, t_kd, t_atm, t_kdt = mk(), mk(), mk(), mk(), mk(), mk()
    P.add('sp', lambda e_: e_.dma_start(out=msk[:], in_=Dm['tri_u'][:]), W=[t_msk], dma=True)
    for d_ in range(2):
        for u_ in range(NU):
            P.add('pool', lambda e_, d_=d_, u_=u_: e_.memset(atm[d_][u_][:], 0.0), W=[t_atm[d_][u_]])
    P.add('sp', lambda e_: e_.dma_start(out=ident[:], in_=Dm['ident'][:]), W=[t_id], dma=True)
    P.add('sp', lambda e_: e_.dma_start(out=anorm[:], in_=Dm['anorm_t'][e]), W=[t_an], dma=True)
    P.add('pool', lambda e_: e_.memset(onesm[:], 1.0 / 128), W=[t_onesm])
    P.add('pool', lambda e_: e_.memset(rst[:], 1.0), W=[t_rst])
    rst3 = rst[:].rearrange("p (c t) -> p c t", t=CH)
    P.add('pool', lambda e_: e_.memset(rst3[:, :, 0:1], 0.0), W=[t_rst])
    for b in range(C.nb):
        for hd in range(NHA):
            r0 = hd * 128
            P.add('sp', lambda e_, b=b, r0=r0: e_.dma_start(out=q[:], in_=Dm['QA'][b, r0:r0 + 128, :]), R=[C.t_AB], W=[t_q], dma=True)
            P.add('sp', lambda e_, b=b, r0=r0: e_.dma_start(out=ga[:], in_=Dm['GA'][b, r0:r0 + 128, :]), R=[C.t_AB], W=[t_ga], dma=True)
            P.add('act', lambda e_, b=b, r0=r0: e_.dma_start(
                out=va[:], in_=Dm['VA'][b, :, r0:r0 + 128].rearrange("(tt p) d -> p tt d", p=CH)), R=[C.t_AB], W=[t_va], dma=True)
            for d in range(2):
                P.add('sp', lambda e_, b=b, d=d, r0=r0: e_.dma_start(out=lf[d][:], in_=Dm['LF'][b, d, r0:r0 + 128, :]),
                      R=[C.t_AB], W=[t_lf[d]], dma=True)
                P.add('act', lambda e_, b=b, d=d, r0=r0: e_.dma_start(out=kf[d][:], in_=Dm['KF'][b, d, r0:r0 + 128, :]),
                      R=[C.t_AB], W=[t_kf[d]], dma=True)
                P.add('dve', lambda e_, d=d: e_.tensor_tensor_scan(out=pp[d][:], data0=rst[:], data1=lf[d][:], initial=0.0,
                                                                   op0=ALU.mult, op1=ALU.add), R=[t_rst, t_lf[d]], W=[t_pp[d]])
                pp3 = pp[d][:].rearrange("p (c t) -> p c t", t=CH)
                P.add('act', lambda e_, d=d, pp3=pp3: e_.activation(out=edec[d][:], in_=pp3[:, :, CH - 1], func=AF.Exp),
                      R=[t_pp[d]], W=[t_edec[d]])
            P.add('dve', lambda e_: e_.tensor_tensor(out=lf[1][:], in0=lf[1][:], in1=pp[1][:], op=ALU.subtract),
                  R=[t_pp[1], t_lf[1]], W=[t_lf[1]])
            B0 = [pp[0], lf[1]]; t_B0 = [t_pp[0], t_lf[1]]
            for d in range(2):
                b3 = B0[d][:].rearrange("p (c t) -> p c t", t=CH)
                mc = CH // 2 - 1 if d == 0 else CH // 2
                P.add('dve', lambda e_, d=d, b3=b3, mc=mc: e_.tensor_scalar(out=nmid[d][:], in0=b3[:, :, mc], scalar1=-1.0, scalar2=None,
                                                                            op0=ALU.mult), R=[t_B0[d]], W=[t_nmid[d]])
                P.add('pool', lambda e_, d=d: e_.memset(S[d][:], 0.0), W=[t_S[d]])
                P.add('pool', lambda e_, d=d: e_.memset(Sb[d][:], 0.0), W=[t_Sb[d]])
            first_written = [False] * NCA

            def chain(d):
                for ui, c in enumerate((list(range(NCTX)) + list(range(NCTX, NCA))) if d == 0 else
                                       (list(range(NCTX))[::-1] + list(range(NCTX, NCA))[::-1])):
                    yield from unit(d, ui, c)

            def unit(d, ui, c):
                if True:
                    u = ui % NU
                    cs = slice(c * CH, (c + 1) * CH)
                    mc = c * CH + (CH // 2 - 1 if d == 0 else CH // 2)
                    lc = c * CH + (CH - 1 if d == 0 else 0)
                    bA, tA = C.bank[4 * d + 2 * u], C.t_bank[4 * d + 2 * u]
                    bB, tB = C.bank[4 * d + 2 * u + 1], C.t_bank[4 * d + 2 * u + 1]
                    E1, E2, E3, E4 = E[d][u]; tE1, tE2, tE3, tE4 = t_E[d][u]
                    Bd, tBd = B0[d], t_B0[d]
                    P.add('act', lambda e_: e_.activation(out=E1[:], in_=Bd[:, cs], func=AF.Exp, bias=nmid[d][:, c:c + 1]),
                          R=[tBd, t_nmid[d]], W=[tE1])
                    P.add('act', lambda e_: e_.activation(out=E2[:], in_=Bd[:, cs], func=AF.Exp, scale=-1.0, bias=Bd[:, mc:mc + 1]),
                          R=[tBd], W=[tE2])
                    if d == 0:
                        P.add('act', lambda e_: e_.activation(out=E3[:], in_=Bd[:, cs], func=AF.Exp), R=[tBd], W=[tE3])
                    else:
                        P.add('act', lambda e_: e_.activation(out=E3[:], in_=Bd[:, cs], func=AF.Exp, bias=pp[1][:, c * CH + CH - 1:c * CH + CH]),
                              R=[tBd, t_pp[1]], W=[tE3])
                    P.add('act', lambda e_: e_.activation(out=E4[:], in_=Bd[:, cs], func=AF.Exp, scale=-1.0, bias=Bd[:, lc:lc + 1]),
                          R=[tBd], W=[tE4])
                    yield
                    QE, KE, QG, KD, ATM, KDT = qe[d][u], ke[d][u], qg[d][u], kd[d][u], atm[d][u], kdt[d][u]
                    P.add('pool', lambda e_: e_.tensor_tensor(out=QE[:], in0=q[:, cs], in1=E1[:], op=ALU.mult), R=[t_q, tE1], W=[t_qe[d][u]])
                    P.add('dve', lambda e_: e_.tensor_tensor(out=KE[:], in0=kf[d][:, cs], in1=E2[:], op=ALU.mult), R=[t_kf[d], tE2], W=[t_ke[d][u]])
                    P.add('pool', lambda e_: e_.tensor_tensor(out=QG[:], in0=q[:, cs], in1=E3[:], op=ALU.mult), R=[t_q, tE3], W=[t_qg[d][u]])
                    P.add('dve', lambda e_: e_.tensor_tensor(out=KD[:], in0=kf[d][:, cs], in1=E4[:], op=ALU.mult), R=[t_kf[d], tE4], W=[t_kd[d][u]])
                    yield
                    P.add('pe', lambda e_: e_.matmul(bA[0:CH, 0:CH], KE[:], QE[:], start=True, stop=True), R=[t_ke[d][u], t_qe[d][u]], W=[tA])
                    P.add('pe', lambda e_: e_.transpose(bA[0:CH, 128:256], KD[:], ident[:]), R=[t_kd[d][u], t_id], W=[tA])
                    yield
                    P.add('dve', lambda e_: e_.copy_predicated(ATM[:], msk[0:CH, d, 0:CH], bA[0:CH, 0:CH]),
                          R=[t_msk], W=[t_atm[d][u], tA])
                    P.add('act', lambda e_: e_.copy(out=KDT[:], in_=bA[0:CH, 128:256]), R=[], W=[t_kdt[d][u], tA])
                    yield
                    P.add('pe', lambda e_: e_.matmul(bB[:, 0:CH], Sb[d][:], QG[:], start=True, stop=False), R=[t_Sb[d], t_qg[d][u]], W=[tB])
                    P.add('pe', lambda e_: e_.matmul(bB[:, 0:CH], va[:, c, :], ATM[:], start=False, stop=True), R=[t_va, t_atm[d][u]], W=[tB])
                    P.add('pe', lambda e_: e_.matmul(bB[:, 128:256], KDT[:], va[:, c, :], start=True, stop=True), R=[t_va, t_kdt[d][u]], W=[tB])
                    yield
                    if not first_written[c]:
                        first_written[c] = True
                        P.add('act', lambda e_: e_.copy(out=OA[:, cs], in_=bB[:, 0:CH]), R=[], W=[t_oa, tB])
                    else:
                        P.add('dve', lambda e_: e_.tensor_tensor(out=OA[:, cs], in0=bB[:, 0:CH], in1=OA[:, cs], op=ALU.add), R=[], W=[t_oa, tB])
                    P.add('dve', lambda e_: e_.scalar_tensor_tensor(out=S[d][:], in0=S[d][:], scalar=edec[d][:, c:c + 1], in1=bB[:, 128:256],
                                                                    op0=ALU.mult, op1=ALU.add), R=[t_edec[d]], W=[t_S[d], tB])
                    P.add('pool', lambda e_: e_.tensor_copy(out=Sb[d][:], in_=S[d][:]), R=[t_S[d]], W=[t_Sb[d]])
                    yield

            interleave([chain(0), chain(1)])
            merge_head(C, OA, t_oa, ga, t_ga, anorm, t_an, 0, ybf, t_ybf, onesm, t_onesm, tmp, t_tmp, C.bank[0], C.t_bank[0], r0, b)


def phase_gdn(C, l):
    P = C.P
    e = l // 2
    Dm = C.dram
    CH = 128

    def T_(shape, dt=F32):
        return C.sb(shape, dt), Tl()

    kT, t_kT = T_([128, LSEQ], BF16); qT, t_qT = T_([128, LSEQ], BF16)
    ktm, t_ktm = T_([128, NCH, 128], BF16); vtm, t_vtm = T_([128, NCH, 128], BF16)
    zb, t_zb = T_([128, LSEQ]); OB, t_ob = T_([128, LSEQ]); ybf, t_ybf = T_([128, LSEQ], BF16)
    gbt, t_gbt = T_([128, NCH, 32]); dtb, t_dtb = T_([128, NCH, 16]); alog, t_alog = T_([128, NCH, 16])
    g, t_g = T_([128, NCH, 16]); beta, t_beta = T_([128, NCH, 16]); G, t_G = T_([128, NCH, 16])
    Gtot, t_Gtot = T_([128, NCH, 16]); eG, t_eG = T_([128, NCH, 16]); beG, t_beG = T_([128, NCH, 16])
    khs, t_khs = T_([128, NCH, 16]); sdec, t_sdec = T_([128, NCH, 16]); tmpg, t_tmpg = T_([128, NCH, 16])
    tri, t_tri = T_([128, 2, 128]); nstr, t_nstr = T_([128, 2, 128]); ident, t_id = T_([128, 128])
    onesf, t_onesf = T_([128, 128]); onesm, t_onesm = T_([128, 128]); anorm, t_an = T_([128, 2])
    tmp, t_tmp = T_([128, 512])
    S = [T_([128, 128]) for _ in range(2)]
    names = ('Dg', 'Ib', 'Dmx', 'eGr', 't2', 'NT', 'AT', 'N', 'NjA', 'NjB', 'NjTA', 'NjTB', 'TT', 'rhs1', 'rhs2', 'nwT', 'vnew', 'qg', 'khat', 'KK', 'QK')
    U = [{n: T_([128, 128]) for n in names} for _ in range(2)]
    P.add('sp', lambda e_: e_.dma_start(out=tri[:], in_=Dm['tri_t'][:]), W=[t_tri], dma=True)
    P.add('sp', lambda e_: e_.dma_start(out=ident[:], in_=Dm['ident'][:]), W=[t_id], dma=True)
    P.add('sp', lambda e_: e_.dma_start(out=anorm[:], in_=Dm['anorm_t'][e]), W=[t_an], dma=True)
    P.add('sp', lambda e_: e_.dma_start(out=dtb[:], in_=Dm['dtb_t'][e]), W=[t_dtb], dma=True)
    P.add('sp', lambda e_: e_.dma_start(out=alog[:], in_=Dm['alog_t'][e]), W=[t_alog], dma=True)
    P.add('pool', lambda e_: e_.memset(onesf[:], 1.0), W=[t_onesf])
    P.add('pool', lambda e_: e_.memset(onesm[:], 1.0 / 128), W=[t_onesm])
    for d_ in range(2):
        P.add('pool', lambda e_, d_=d_: e_.tensor_tensor(out=nstr[:, d_, :], in0=ident[:], in1=tri[:, d_, :], op=ALU.subtract),
              R=[t_id, t_tri], W=[t_nstr])
    P.add('act', lambda e_: e_.activation(out=alog[:], in_=alog[:], func=AF.Exp), R=[t_alog], W=[t_alog])
    bG, tbG = C.bank[6], C.t_bank[6]
    for b in range(C.nb):
        P.add('sp', lambda e_, b=b: e_.dma_start(out=gbt[:], in_=Dm['GBt'][b].rearrange("(tt p) c -> p tt c", p=128)),
              R=[C.t_AB], W=[t_gbt], dma=True)
        P.add('dve', lambda e_: e_.tensor_tensor(out=g[:], in0=gbt[:, :, 0:16], in1=dtb[:], op=ALU.add), R=[t_gbt, t_dtb], W=[t_g])
        P.add('act', lambda e_: e_.activation(out=g[:], in_=g[:], func=AF.Exp), R=[t_g], W=[t_g])
        P.add('dve', lambda e_: e_.tensor_scalar(out=g[:], in0=g[:], scalar1=1.0, scalar2=None, op0=ALU.add), R=[t_g], W=[t_g])
        P.add('act', lambda e_: e_.activation(out=g[:], in_=g[:], func=AF.Ln), R=[t_g], W=[t_g])
        P.add('dve', lambda e_: e_.scalar_tensor_tensor(out=g[:], in0=g[:], scalar=-1.0, in1=alog[:], op0=ALU.mult, op1=ALU.mult),
              R=[t_g, t_alog], W=[t_g])
        P.add('act', lambda e_: e_.activation(out=beta[:], in_=gbt[:, :, 16:32], func=AF.Sigmoid), R=[t_gbt], W=[t_beta])
        for tt in range(NCH):
            P.add('pe', lambda e_, tt=tt: e_.matmul(bG[:, 0:8], tri[:, 0, :], g[:, tt, 0:8], start=True, stop=True), R=[t_tri, t_g], W=[tbG])
            P.add('pe', lambda e_, tt=tt: e_.matmul(bG[:, 8:16], tri[:, 1, :], g[:, tt, 8:16], start=True, stop=True), R=[t_tri, t_g], W=[tbG])
            P.add('pe', lambda e_, tt=tt: e_.matmul(bG[:, 16:32], onesf[:], g[:, tt, :], start=True, stop=True), R=[t_onesf, t_g], W=[tbG])
            P.add('dve', lambda e_, tt=tt: e_.tensor_copy(out=G[:, tt, :], in_=bG[:, 0:16]), R=[], W=[t_G, tbG])
            P.add('dve', lambda e_, tt=tt: e_.tensor_copy(out=Gtot[:, tt, :], in_=bG[:, 16:32]), R=[], W=[t_Gtot, tbG])
        P.add('act', lambda e_: e_.activation(out=eG[:], in_=G[:], func=AF.Exp), R=[t_G], W=[t_eG])
        P.add('dve', lambda e_: e_.tensor_tensor(out=beG[:], in0=beta[:], in1=eG[:], op=ALU.mult), R=[t_beta, t_eG], W=[t_beG])
        P.add('dve', lambda e_: e_.tensor_tensor(out=tmpg[:], in0=Gtot[:], in1=G[:], op=ALU.subtract), R=[t_Gtot, t_G], W=[t_tmpg])
        P.add('act', lambda e_: e_.activation(out=khs[:], in_=tmpg[:], func=AF.Exp), R=[t_tmpg], W=[t_khs])
        P.add('act', lambda e_: e_.activation(out=sdec[:], in_=Gtot[:], func=AF.Exp), R=[t_Gtot], W=[t_sdec])
        for hd in range(NHA):
            r0 = hd * 128
            P.add('sp', lambda e_, b=b, r0=r0: e_.dma_start(out=kT[:], in_=Dm['KB'][b, r0:r0 + 128, :]), R=[C.t_AB], W=[t_kT], dma=True)
            P.add('sp', lambda e_, b=b, r0=r0: e_.dma_start(out=qT[:], in_=Dm['QB'][b, r0:r0 + 128, :]), R=[C.t_AB], W=[t_qT], dma=True)
            P.add('sp', lambda e_, b=b, r0=r0: e_.dma_start(out=zb[:], in_=Dm['ZB'][b, r0:r0 + 128, :]), R=[C.t_AB], W=[t_zb], dma=True)
            P.add('act', lambda e_, b=b, r0=r0: e_.dma_start(
                out=ktm[:], in_=Dm['KBt'][b, :, r0:r0 + 128].rearrange("(tt p) d -> p tt d", p=128)), R=[C.t_AB], W=[t_ktm], dma=True)
            P.add('act', lambda e_, b=b, r0=r0: e_.dma_start(
                out=vtm[:], in_=Dm['VBt'][b, :, r0:r0 + 128].rearrange("(tt p) d -> p tt d", p=128)), R=[C.t_AB], W=[t_vtm], dma=True)
            for d in range(2):
                P.add('pool', lambda e_, d=d: e_.memset(S[d][0][:], 0.0), W=[S[d][1]])
            first_written = [False] * NCH

            def unit(d, c):
                col = d * 8 + hd
                cs = slice(c * 128, (c + 1) * 128)
                u = U[d]
                b1, tb1 = C.bank[3 * d], C.t_bank[3 * d]
                b2, tb2 = C.bank[3 * d + 1], C.t_bank[3 * d + 1]
                b3, tb3 = C.bank[3 * d + 2], C.t_bank[3 * d + 2]
                Sd, tS = S[d]
                A = lambda n: u[n][0]
                t = lambda n: u[n][1]
                gc, bc = g[:, c, col:col + 1], beta[:, c, col:col + 1]
                P.add('pe', lambda e_: e_.matmul(b1[:, 256:384], kT[:, cs], kT[:, cs], start=True, stop=True), R=[t_kT], W=[tb1])
                P.add('pe', lambda e_: e_.matmul(b1[:, 384:512], kT[:, cs], qT[:, cs], start=True, stop=True), R=[t_kT, t_qT], W=[tb1])
                P.add('act', lambda e_: e_.copy(out=A('KK')[:], in_=b1[:, 256:384]), R=[], W=[t('KK'), tb1])
                P.add('act', lambda e_: e_.copy(out=A('QK')[:], in_=b1[:, 384:512]), R=[], W=[t('QK'), tb1])
                KK, tKK = u['KK']; QK, tQK = u['QK']
                yield
                P.add('dve', lambda e_: e_.tensor_scalar(out=A('Dg')[:], in0=tri[:, d, :], scalar1=gc, scalar2=None, op0=ALU.mult),
                      R=[t_tri, t_g], W=[t('Dg')])
                P.add('act', lambda e_: e_.activation(out=A('Ib')[:], in_=ident[:], func=AF.Identity, scale=bc),
                      R=[t_id, t_beta], W=[t('Ib')])
                P.add('pe', lambda e_: e_.matmul(b1[:, 0:128], onesf[:], A('Dg')[:], start=True, stop=True), R=[t_onesf, t('Dg')], W=[tb1])
                P.add('pe', lambda e_: e_.matmul(b1[:, 128:256], onesf[:], A('Ib')[:], start=True, stop=True), R=[t_onesf, t('Ib')], W=[tb1])
                yield
                P.add('dve', lambda e_: e_.tensor_scalar(out=A('Dmx')[:], in0=b1[:, 0:128], scalar1=G[:, c, col:col + 1], scalar2=0.0,
                                                         op0=ALU.subtract, op1=ALU.min), R=[t_G], W=[t('Dmx'), tb1])
                P.add('act', lambda e_: e_.activation(out=A('eGr')[:], in_=b1[:, 0:128], func=AF.Exp), R=[], W=[t('eGr'), tb1])
                P.add('act', lambda e_: e_.activation(out=A('Dmx')[:], in_=A('Dmx')[:], func=AF.Exp), R=[t('Dmx')], W=[t('Dmx')])
                P.add('dve', lambda e_: e_.tensor_tensor(out=A('t2')[:], in0=b1[:, 128:256], in1=A('Dmx')[:], op=ALU.mult),
                      R=[t('Dmx')], W=[t('t2'), tb1])
                yield
                P.add('pool', lambda e_: e_.tensor_tensor(out=A('t2')[:], in0=A('t2')[:], in1=KK[:], op=ALU.mult), R=[t('t2'), tKK], W=[t('t2')])
                P.add('pool', lambda e_: e_.tensor_tensor(out=A('NT')[:], in0=A('t2')[:], in1=nstr[:, d, :], op=ALU.mult),
                      R=[t('t2'), t_nstr], W=[t('NT')])
                P.add('dve', lambda e_: e_.tensor_tensor(out=A('AT')[:], in0=A('Dmx')[:], in1=tri[:, d, :], op=ALU.mult),
                      R=[t('Dmx'), t_tri], W=[t('AT')])
                P.add('dve', lambda e_: e_.tensor_tensor(out=A('AT')[:], in0=A('AT')[:], in1=QK[:], op=ALU.mult), R=[t('AT'), tQK], W=[t('AT')])
                yield
                P.add('pe', lambda e_: e_.transpose(b2[:, 0:128], A('NT')[:], ident[:]), R=[t('NT'), t_id], W=[tb2])
                yield
                P.add('act', lambda e_: e_.copy(out=A('N')[:], in_=b2[:, 0:128]), R=[], W=[t('N'), tb2])
                P.add('dve', lambda e_: e_.tensor_tensor(out=A('TT')[:], in0=A('NT')[:], in1=ident[:], op=ALU.add), R=[t('NT'), t_id], W=[t('TT')])
                yield
                Np, NTp = 'N', 'NT'
                for j in range(1, 7):
                    Nn, NTn = ('NjA', 'NjTA') if j % 2 else ('NjB', 'NjTB')
                    P.add('pe', lambda e_, Np=Np, NTp=NTp: e_.matmul(b2[:, 0:128], A(NTp)[:], A(Np)[:], start=True, stop=True),
                          R=[t(Np), t(NTp)], W=[tb2])
                    if j < 6:
                        P.add('pe', lambda e_, Np=Np, NTp=NTp: e_.matmul(b2[:, 128:256], A(Np)[:], A(NTp)[:], start=True, stop=True),
                              R=[t(Np), t(NTp)], W=[tb2])
                    yield
                    P.add('act', lambda e_, Nn=Nn: e_.copy(out=A(Nn)[:], in_=b2[:, 0:128]), R=[], W=[t(Nn), tb2])
                    if j < 6:
                        P.add('dve', lambda e_, NTn=NTn: e_.tensor_copy(out=A(NTn)[:], in_=b2[:, 128:256]), R=[], W=[t(NTn), tb2])
                    yield
                    P.add('pe', lambda e_, Nn=Nn: e_.matmul(b2[:, 256:384], A(Nn)[:], A('TT')[:], start=True, stop=True),
                          R=[t(Nn), t('TT')], W=[tb2])
                    yield
                    P.add('dve', lambda e_: e_.tensor_tensor(out=A('TT')[:], in0=b2[:, 256:384], in1=A('TT')[:], op=ALU.add),
                          R=[], W=[t('TT'), tb2])
                    yield
                    Np, NTp = Nn, NTn
                P.add('dve', lambda e_: e_.tensor_scalar(out=A('rhs1')[:], in0=vtm[:, c, :], scalar1=bc, scalar2=None, op0=ALU.mult),
                      R=[t_vtm, t_beta], W=[t('rhs1')])
                P.add('act', lambda e_: e_.activation(out=A('rhs2')[:], in_=ktm[:, c, :], func=AF.Identity, scale=beG[:, c, col:col + 1]),
                      R=[t_ktm, t_beG], W=[t('rhs2')])
                P.add('act', lambda e_: e_.activation(out=A('khat')[:], in_=ktm[:, c, :], func=AF.Identity, scale=khs[:, c, col:col + 1]),
                      R=[t_ktm, t_khs], W=[t('khat')])
                P.add('dve', lambda e_: e_.tensor_tensor(out=A('qg')[:], in0=qT[:, cs], in1=A('eGr')[:], op=ALU.mult), R=[t_qT, t('eGr')], W=[t('qg')])
                yield
                P.add('pe', lambda e_: e_.matmul(b3[:, 0:128], A('rhs2')[:], A('TT')[:], start=True, stop=True), R=[t('rhs2'), t('TT')], W=[tb3])
                yield
                P.add('act', lambda e_: e_.activation(out=A('nwT')[:], in_=b3[:, 0:128], func=AF.Identity, scale=-1.0), R=[], W=[t('nwT'), tb3])
                yield
                P.add('pe', lambda e_: e_.matmul(b3[:, 128:256], A('TT')[:], A('rhs1')[:], start=True, stop=False), R=[t('TT'), t('rhs1')], W=[tb3])
                P.add('pe', lambda e_: e_.matmul(b3[:, 128:256], A('nwT')[:], Sd[:], start=False, stop=True), R=[t('nwT'), tS], W=[tb3])
                yield
                P.add('act', lambda e_: e_.copy(out=A('vnew')[:], in_=b3[:, 128:256]), R=[], W=[t('vnew'), tb3])
                yield
                P.add('pe', lambda e_: e_.matmul(b3[:, 256:384], Sd[:], A('qg')[:], start=True, stop=False), R=[tS, t('qg')], W=[tb3])
                P.add('pe', lambda e_: e_.matmul(b3[:, 256:384], A('vnew')[:], A('AT')[:], start=False, stop=True), R=[t('vnew'), t('AT')], W=[tb3])
                P.add('pe', lambda e_: e_.matmul(b3[:, 384:512], A('khat')[:], A('vnew')[:], start=True, stop=True), R=[t('khat'), t('vnew')], W=[tb3])
                yield
                if not first_written[c]:
                    first_written[c] = True
                    P.add('act', lambda e_: e_.copy(out=OB[:, cs], in_=b3[:, 256:384]), R=[], W=[t_ob, tb3])
                else:
                    P.add('dve', lambda e_: e_.tensor_tensor(out=OB[:, cs], in0=b3[:, 256:384], in1=OB[:, cs], op=ALU.add), R=[], W=[t_ob, tb3])
                P.add('dve', lambda e_: e_.scalar_tensor_tensor(out=Sd[:], in0=Sd[:], scalar=sdec[:, c, col:col + 1], in1=b3[:, 384:512],
                                                                op0=ALU.mult, op1=ALU.add), R=[t_sdec], W=[tS, tb3])
                yield

            def chain(d):
                for c in chunk_order(d):
                    yield from unit(d, c)

            interleave([chain(0), chain(1)])
            merge_head(C, OB, t_ob, zb, t_zb, anorm, t_an, 1, ybf, t_ybf, onesm, t_onesm, tmp, t_tmp, C.bank[7], C.t_bank[7],
                       1024 + r0, b)


def build(cfg):
    nc = bass.Bass("TRN2", target_bir_lowering=False)
    C = Ctx(nc)
    C.nb = nb = cfg.get('nb', NBC)
    ext_in, ext_out = cfg.get('ext_in', ()), cfg.get('ext_out', ())

    def dt(name, shape, dtype=F32):
        if name in ext_in:
            return C.din(name, shape, dtype)
        if name in ext_out:
            return C.dout(name, shape, dtype)
        return C.dscr(name, shape, dtype)

    C.din('cT', [128, KC, 3])
    C.din('b_ada_t', [128, DEPTH, 96])
    C.din('w_ada_t', [DEPTH, 96, 128, KC, 128])
    C.din('ln_t', [DEPTH, 2, 128, KC, 2])
    C.din('ffn_dw_t', [DEPTH, 128, FC, 4])
    C.din('w_up_t', [DEPTH, 2 * FC, 128, KC, 128])
    C.din('w_down_t', [DEPTH, KC, 128, FC, 128])
    for nm in ('XA', 'XB', 'XIN'):
        dt(nm, [nb, D, LSEQ])
    dt('OUT', [nb, D, LLAT])
    dt('HT', [nb, D, LSEQ], BF16)
    dt('HID', [nb, DFF, LSEQ], BF16)
    dt('o_mod', [128, DEPTH, 96, 3])
    C.din('w_out_t', [DEPTH, KC, 128, KC, 128])
    C.din('w_inc_t', [2, 32, 128, KC, 128])
    C.din('w_v_t', [2, 4, 128, KC, 512])
    C.din('rope_t', [128, 2, LLAT])
    C.din('pmT', [128, 128])
    C.din('na_mask', [128, NBT, 128])
    C.din('na_bias_t', [2, NH_C, 128, NBT, 128])
    dt('QT', [nb, D, LSEQ], BF16); dt('KT', [nb, D, LSEQ], BF16); dt('V', [nb, LSEQ, D], BF16)
    dt('YT', [nb, D, LSEQ], BF16)
    C.t_QK, C.t_YT = Tl('QK'), Tl('YT')
    C.din('w_inab_t', [2, 72, 128, KC, 128])
    C.din('w_ia_t', [2, 2, 128, KC, 512])
    C.din('w_gb_t', [2, 128, KC, 32])
    C.din('gconv_t', [2, 128, 24, 5])
    C.din('lb_t', [128, 2, 2, NHA])
    C.din('ident', [128, 128])
    for nm in ('QA', 'GA', 'ZB'):
        dt(nm, [nb, 1024, LSEQ])
    dt('LF', [nb, 2, 1024, LSEQ]); dt('KF', [nb, 2, 1024, LSEQ])
    for nm in ('QB', 'KB'):
        dt(nm, [nb, 1024, LSEQ], BF16)
    for nm in ('VA', 'KBt', 'VBt'):
        dt(nm, [nb, LSEQ, 1024], BF16)
    dt('GBt', [nb, LSEQ, 32])
    C.t_AB = Tl('AB')
    C.din('tri_t', [128, 2, 128])
    C.din('tri_u', [128, 2, 128], U32)
    C.din('dtb_t', [2, 128, NCH, 16])
    C.din('alog_t', [2, 128, NCH, 16])
    C.din('anorm_t', [2, 128, 2])
    C.t_HT, C.t_HID = Tl('HT'), Tl('HID')
    C.t_X = {'XA': Tl('XA'), 'XB': Tl('XB'), 'XIN': Tl('XIN'), 'OUT': Tl('OUT')}
    with ExitStack() as top:
        C.stack = top
        C.mod = C.sb([128, DEPTH, 96, 3], name='mod')
        C.t_mod = Tl('mod')
        C.bank = [C.ps([128, 512], name=f'bank{i}') for i in range(8)]
        C.t_bank = [Tl(f'bank{i}') for i in range(8)]
        csem = {e: top.enter_context(nc.semaphore('c_' + e)) for e in ENGS}
        dsem = {e: [top.enter_context(nc.semaphore(f'd_{e}{i}')) for i in range(RING)] for e in ENGS}
        outs = []
        for ph in cfg['phases']:
            with ExitStack() as phs:
                C.stack = phs
                kind = ph[0]
                if kind == 'ada':
                    phase_ada(C, ph[1])
                elif kind == 'mod':
                    _, l, xs, sh, sc = ph
                    phase_mod0(C, l, C.dram[xs], C.dram['HT'], sh, sc)
                elif kind == 'ffn_up':
                    phase_ffn_up(C, ph[1], C.dram['HT'], C.dram['HID'], skip_ctx=(len(ph) > 2 and ph[2]))
                elif kind == 'ffn_down':
                    _, l, xs, xd, hmod = ph[:5]
                    final = len(ph) > 5 and ph[5]
                    phase_proj_ln(C, l, C.dram['HID'], C.t_HID, FC, 'w_down_t', l, 5, 1, C.dram[xs], C.dram[xd],
                                  C.t_X[xd], C.dram['HT'], hmod, lat_only_out=final)
                elif kind == 'inab':
                    phase_inab(C, ph[1], C.dram['HT'])
                elif kind == 'gdn':
                    phase_gdn(C, ph[1])
                elif kind == 'gla':
                    phase_gla(C, ph[1])
                elif kind == 'qkv':
                    phase_qkv(C, ph[1], C.dram['HT'], ph[2])
                elif kind == 'attn':
                    phase_attn(C, ph[1], ph[2])
                elif kind == 'out_proj':
                    _, l, xs, xd, hmod = ph[:5]
                    phase_proj_ln(C, l, C.dram['YT'], C.t_YT, KC, 'w_out_t', l, 2, 0, C.dram[xs], C.dram[xd],
                                  C.t_X[xd], C.dram['HT'], hmod, skip_ctx=(len(ph) > 5 and ph[5]))
                elif kind == 'dump_mod':
                    C.P.add('sp', lambda e: e.dma_start(out=C.dram['o_mod'][:], in_=C.mod[:]), R=[C.t_mod], W=[C.t_HT], dma=True)
                C.P.barrier()
            C.stack = top
        C.P.barrier()
        with nc.Block() as block:
            C.P.emit(block, csem, dsem)
    return nc


def tileW(w):
    Kd, N = w.shape
    return np.ascontiguousarray(w.reshape(Kd // 128, 128, N // 128, 128).transpose(2, 1, 0, 3))


def tileWwide(w, nw):
    Kd, N = w.shape
    return np.ascontiguousarray(w.reshape(Kd // 128, 128, N // nw, nw).transpose(2, 1, 0, 3))


def host_consts():
    c = {}
    pos = np.arange(LLAT)
    row = (pos // 64).astype(np.float32); col = (pos % 64).astype(np.float32)
    inv = (np.float32(10000.0) ** (-np.arange(32, dtype=np.float32) / np.float32(32))).astype(np.float32)
    rope = np.zeros((128, 2, LLAT), np.float32)
    for d in range(128):
        ang = ((row if d < 64 else col) * inv[d % 32]).astype(np.float32)
        rope[d, 0] = np.cos(ang); rope[d, 1] = np.sin(ang)
    c['rope_t'] = rope
    pm = np.zeros((128, 128), np.float32)
    for i in range(32):
        pm[i, 32 + i] = -1; pm[32 + i, i] = 1; pm[64 + i, 96 + i] = -1; pm[96 + i, 64 + i] = 1
    c['pmT'] = np.ascontiguousarray(pm.T)
    kk = np.arange(128); krl, kc = kk // 64, kk % 64
    qq = np.arange(128); qrl, qc = qq // 64, qq % 64
    c0 = np.clip(qc - 8, 0, 48)
    col_ok = (kc[:, None] >= c0[None, :]) & (kc[:, None] < c0[None, :] + 16)
    deltas = [-6, -4, -2, 0, 2, 4, 6] + [-4, -2, 0, 2, 4]
    mask = np.zeros((128, NBT, 128), np.float32)
    dr = np.zeros((NBT, 128, 128), np.int64)
    for t, dlt in enumerate(deltas):
        rel = dlt + krl[:, None] - qrl[None, :]
        ok = col_ok if t < 7 else (col_ok & (rel >= -4) & (rel < 4))
        mask[:, t, :] = np.where(ok, 0.0, -30000.0)
        dr[t] = np.clip(rel + 7, 0, 14)
    c['na_mask'] = mask
    c['_dr'] = dr
    c['_dc'] = np.clip(kc[:, None] - qc[None, :] + 15, 0, 30)
    return c


def host_bias_tiles(rel_bias, consts):
    dr, dc = consts['_dr'], consts['_dc']
    g = rel_bias[:, :, dr, dc[None]]
    return np.ascontiguousarray(g.transpose(0, 1, 3, 2, 4)).astype(np.float32)


def host_even_weights(inputs, e_list=(0, 1)):
    w = {}
    w['w_inab_t'] = np.zeros((2, 72, 128, KC, 128), np.float32)
    w['w_ia_t'] = np.zeros((2, 2, 128, KC, 512), np.float32)
    w['w_gb_t'] = np.zeros((2, 128, KC, 32), np.float32)
    for e in e_list:
        wi = inputs['w_in_ab'][e]
        w['w_inab_t'][e] = tileW(wi[:, :9216])
        w['w_ia_t'][e] = tileWwide(wi[:, 3072:4096], 512)
        w['w_gb_t'][e] = np.ascontiguousarray(wi[:, 9216:9248].reshape(KC, 128, 32).transpose(1, 0, 2))
    gc = inputs['gdn_conv']
    w['gconv_t'] = np.ascontiguousarray(gc.reshape(2, 5, 24, 128).transpose(0, 3, 2, 1)).astype(np.float32)
    lb = inputs['hgrn_lb']
    w['lb_t'] = np.ascontiguousarray(lb.reshape(2, 2, NHA, 128).transpose(3, 0, 1, 2)).astype(np.float32)
    w['ident'] = np.eye(128, dtype=np.float32)
    ii = np.arange(128)
    w['anorm_t'] = np.ascontiguousarray(np.stack([inputs['hgrn_norm'], inputs['gdn_norm']], -1)).astype(np.float32)
    w['tri_t'] = np.ascontiguousarray(np.stack([(ii[:, None] <= ii[None, :]), (ii[:, None] >= ii[None, :])], 1)).astype(np.float32)
    w['tri_u'] = np.ascontiguousarray(w['tri_t'].astype(np.uint32))
    dtb = inputs['gdn_dt_bias'].reshape(2, 16).astype(np.float32)
    alg = inputs['gdn_a_log'].reshape(2, 16).astype(np.float32)
    w['dtb_t'] = np.ascontiguousarray(np.broadcast_to(dtb[:, None, None, :], (2, 128, NCH, 16)))
    w['alog_t'] = np.ascontiguousarray(np.broadcast_to(alg[:, None, None, :], (2, 128, NCH, 16)))
    return w


def full_phases():
    ph = [('ada', list(range(DEPTH))), ('mod', 0, 'XIN', 0, 1)]
    src = 'XIN'
    for l in range(DEPTH):
        last = l == DEPTH - 1
        if l % 2 == 0:
            ph += [('inab', l), ('gla', l), ('gdn', l)]
        else:
            ph += [('qkv', l, not last), ('attn', l, not last)]
        ph += [('out_proj', l, src, 'XB', (l, 3, 4), last), ('ffn_up', l, last)]
        if last:
            ph += [('ffn_down', l, 'XB', 'OUT', None, True)]
        else:
            ph += [('ffn_down', l, 'XB', 'XA', (l + 1, 0, 1))]
        src = 'XA'
    return ph


def host_shared(inputs):
    f32 = np.float32
    m = {}
    m['w_ada_t'] = np.stack([tileW(inputs['w_ada'][l]) for l in range(DEPTH)])
    m['b_ada_t'] = np.ascontiguousarray(inputs['b_ada'].reshape(DEPTH, 96, 128).transpose(2, 0, 1)).astype(f32)
    m['ln_t'] = np.ascontiguousarray(np.stack([inputs['ln_g'], inputs['ln_b']], -1).reshape(DEPTH, 2, KC, 128, 2)
                                     .transpose(0, 1, 3, 2, 4)).astype(f32)
    dw = np.concatenate([inputs['ffn_w_dw'], inputs['ffn_b_dw'][:, None, :]], 1)
    m['ffn_dw_t'] = np.ascontiguousarray(dw.reshape(DEPTH, 4, FC, 128).transpose(0, 3, 2, 1)).astype(f32)
    m['w_up_t'] = np.stack([tileW(inputs['ffn_w_up'][l]) for l in range(DEPTH)])
    m['w_down_t'] = np.stack([tileW(inputs['ffn_w_down'][l]) for l in range(DEPTH)])
    m['w_out_t'] = np.stack([tileW(inputs['w_out_ab'][l // 2] if l % 2 == 0 else inputs['w_out_c'][l // 2]) for l in range(DEPTH)])
    m['w_inc_t'] = np.stack([tileW(inputs['w_in_c'][o][:, :2 * D]) for o in range(2)])
    m['w_v_t'] = np.stack([tileWwide(inputs['w_in_c'][o][:, 2 * D:], 512) for o in range(2)])
    hc = host_consts()
    for k in ('rope_t', 'pmT', 'na_mask'):
        m[k] = hc[k]
    m['na_bias_t'] = host_bias_tiles(inputs['na_rel_bias'], hc)
    m.update(host_even_weights(inputs))
    return m


def host_core(inputs, bs):
    xin = np.stack([np.concatenate([inputs['ctx'][b].T, inputs['x'][b].T], 1) for b in bs]).astype(np.float32)
    cols = [inputs['c'][b] for b in bs]
    while len(cols) < 2:
        cols.append(cols[-1])
    cc = np.stack(cols + [inputs['c_ctx']], -1)
    cT = np.ascontiguousarray(cc.reshape(KC, 128, 3).transpose(1, 0, 2)).astype(np.float32)
    return {'XIN': np.ascontiguousarray(xin), 'cT': cT}


def kernel(**inputs):
    inputs = {k: np.asarray(v) for k, v in inputs.items()}
    n = 8
    shared = host_shared(inputs)
    nc = build({'nb': NBC, 'ext_in': ('XIN',), 'ext_out': ('OUT',), 'phases': full_phases()})
    in_maps = []
    for core in range(n):
        m = dict(shared)
        m.update(host_core(inputs, [NBC * core + i for i in range(NBC)]))
        in_maps.append(m)
    res = run_bass_kernel_spmd(nc, in_maps, core_ids=list(range(n)))
    out = np.empty((n * NBC, LLAT, D), np.float32)
    for core in range(n):
        o = np.asarray(res.results[core]['OUT'])
        for i in range(NBC):
            out[NBC * core + i] = o[i].T
    return out
```

```python
import numpy as np
from contextlib import ExitStack
import concourse.bass as bass
import concourse.mybir as mybir
from concourse.alu_op_type import AluOpType as ALU
from concourse.bass_utils import run_bass_kernel_spmd

AF = mybir.ActivationFunctionType
F32 = mybir.dt.float32
BF16 = mybir.dt.bfloat16
U32 = mybir.dt.uint32

ENGS = ('pe', 'act', 'dve', 'pool', 'sp')
RING = 12


class Tl:
    __slots__ = ('name', 'w', 'r')

    def __init__(self, name=''):
        self.name = name
        self.w = None
        self.r = []


class Op:
    __slots__ = ('eng', 'fn', 'dma', 'pos', 'cpos', 'waits', 'sig', 'sem', 'val')

    def __init__(self, eng, fn, dma):
        self.eng = eng
        self.fn = fn
        self.dma = dma
        self.waits = []
        self.sig = False
        self.sem = None
        self.val = None


class Prog:
    def __init__(self, nc):
        self.nc = nc
        self.streams = {e: [] for e in ENGS}
        self.ccount = {e: 0 for e in ENGS}
        self.ndma = {e: 0 for e in ENGS}
        self.dmaops = {e: [] for e in ENGS}
        self.wpos = {}
        self.wdma = {}
        self.lastc = {e: None for e in ENGS}

    def _need(self, op, d, kind):
        if d is op:
            return
        if d.dma:
            key = (op.eng, d.eng, d.sem)
            if self.wdma.get(key, 0) >= d.val:
                return
            self.wdma[key] = d.val
            op.waits.append(d)
            return
        if d.eng == op.eng and not op.dma:
            if op.eng == 'pe':
                return
            if kind != 'raw':
                return
            if self.ccount[op.eng] - d.cpos > 2:
                return
        key = (op.eng, d.eng)
        if self.wpos.get(key, -1) >= d.cpos:
            return
        self.wpos[key] = d.cpos
        d.sig = True
        op.waits.append(d)

    def add(self, eng, fn, R=(), W=(), dma=False):
        op = Op(eng, fn, dma)
        op.cpos = self.ccount[eng]
        if dma:
            j = self.ndma[eng]
            self.ndma[eng] += 1
            op.sem = j % RING
            op.val = 16 * (j // RING + 1)
            if j >= RING:
                self._need(op, self.dmaops[eng][j - RING], 'ring')
            self.dmaops[eng].append(op)
        for t in R:
            if t.w is not None:
                self._need(op, t.w, 'raw')
        for t in W:
            if t.w is not None:
                self._need(op, t.w, 'waw')
            for r in t.r:
                self._need(op, r, 'war')
        for t in R:
            t.r.append(op)
        for t in W:
            t.w = op
            t.r = []
        if not dma and fn is not None:
            self.ccount[eng] += 1
            self.lastc[eng] = op
        self.streams[eng].append(op)
        return op

    def barrier(self):
        lasts = [self.lastc[e] for e in ENGS if self.lastc[e] is not None]
        dmas = []
        for e in ENGS:
            dmas += self.dmaops[e][-RING:]
        for e in ENGS:
            op = Op(e, None, False)
            op.cpos = self.ccount[e]
            for d in lasts + dmas:
                if d.eng == e and not d.dma:
                    continue
                self._need(op, d, 'raw')
            self.streams[e].append(op)

    def emit(self, block, csem, dsem):
        nc = self.nc
        for e in ENGS:
            c = 0
            for op in self.streams[e]:
                if not op.dma and op.sig:
                    c += 1
                    op.val = c

        def semof(d):
            return dsem[d.eng][d.sem] if d.dma else csem[d.eng]

        def replay(e):
            def f(eng):
                for op in self.streams[e]:
                    ws = {}
                    for d in op.waits:
                        s = semof(d)
                        k = id(s)
                        if k not in ws or ws[k][1] < d.val:
                            ws[k] = (s, d.val)
                    for s, v in ws.values():
                        eng.wait_ge(s, v)
                    if op.fn is None:
                        continue
                    inst = op.fn(eng)
                    if op.dma:
                        inst.then_inc(dsem[e][op.sem], 16)
                    elif op.sig:
                        inst.then_inc(csem[e], 1)
            return f
        block.tensor(replay('pe'))
        block.scalar(replay('act'))
        block.vector(replay('dve'))
        block.gpsimd(replay('pool'))
        block.sync(replay('sp'))


D = 2048
KC = 16
DEPTH = 4
NBC = 2
LCTX, LLAT = 256, 2048
LSEQ = LCTX + LLAT
DFF = 5504
FC = 43
ALPHA = (2 * DEPTH) ** 0.25
EPS = 1e-6
NAB = 9248


class Ctx:
    def __init__(self, nc):
        self.nc = nc
        self.P = Prog(nc)
        self.dram = {}
        self.stack = None
        self.n = 0

    def din(self, name, shape, dt=F32):
        self.dram[name] = self.nc.dram_tensor(name, list(shape), dt, kind="ExternalInput").ap()
        return self.dram[name]

    def dout(self, name, shape, dt=F32):
        self.dram[name] = self.nc.dram_tensor(name, list(shape), dt, kind="ExternalOutput").ap()
        return self.dram[name]

    def dscr(self, name, shape, dt=F32):
        self.dram[name] = self.nc.dram_tensor(name, list(shape), dt, kind="Internal").ap()
        return self.dram[name]

    def sb(self, shape, dt=F32, name=None):
        self.n += 1
        return self.stack.enter_context(self.nc.sbuf_tensor(name or f"s{self.n}", list(shape), dt))

    def ps(self, shape, dt=F32, name=None):
        self.n += 1
        return self.stack.enter_context(self.nc.psum_tensor(name or f"p{self.n}", list(shape), dt))


def phase_ada(C, layers):
    nc, P = C.nc, C.P
    mod = C.mod
    cT = C.sb([128, KC, 3]); sT = C.sb([128, KC, 3], BF16)
    bt = C.sb([128, DEPTH, 96])
    t_c, t_s, t_b, t_mod = Tl(), Tl(), Tl(), C.t_mod
    P.add('sp', lambda e: e.dma_start(out=cT[:], in_=C.dram['cT'][:]), W=[t_c], dma=True)
    P.add('sp', lambda e: e.dma_start(out=bt[:], in_=C.dram['b_ada_t'][:]), W=[t_b], dma=True)
    P.add('act', lambda e: e.activation(out=sT[:], in_=cT[:], func=AF.Silu), R=[t_c], W=[t_s])
    NW = 3
    wts = [C.sb([128, KC, 128], BF16) for _ in range(NW)]
    t_w = [Tl() for _ in range(NW)]
    pss, t_p = C.bank[:4], C.t_bank[:4]
    i = 0
    for l in layers:
        for oc in range(96):
            w, tw, ps, tp = wts[i % NW], t_w[i % NW], pss[i % 4], t_p[i % 4]
            P.add('pool', lambda e, w=w, l=l, oc=oc: e.dma_start(out=w[:], in_=C.dram['w_ada_t'][l, oc], max_dma_last_dim=4096),
                  W=[tw], dma=True)
            for kc in range(KC):
                P.add('pe', lambda e, w=w, ps=ps, kc=kc: e.matmul(ps[:, 0:3], w[:, kc, :], sT[:, kc, :],
                                                                   start=(kc == 0), stop=(kc == KC - 1)),
                      R=[tw, t_s], W=[tp])
            add1 = 1.0 if (oc // 16) in (1, 4) else 0.0
            P.add('dve', lambda e, ps=ps, l=l, oc=oc, add1=add1: e.tensor_scalar(
                out=mod[:, l, oc, :], in0=ps[:, 0:3], scalar1=bt[:, l, oc:oc + 1], scalar2=add1,
                op0=ALU.add, op1=ALU.add), R=[tp, t_b], W=[t_mod])
            i += 1


def seq_tiles(L):
    return [(o, min(512, L - o)) for o in range(0, L, 512)]


SEQS = ((0, LCTX), (LCTX, LLAT))


def phase_mod0(C, l, xsrc, HT, sh=0, sc=1):
    P, mod = C.P, C.mod
    NB = 3
    xin = [C.sb([128, 512]) for _ in range(NB)]; hb = [C.sb([128, 512], BF16) for _ in range(NB)]
    t_x = [Tl() for _ in range(NB)]; t_h = [Tl() for _ in range(NB)]
    i = 0
    for b in range(C.nb):
        for kc in range(KC):
            for (t0, n) in seq_tiles(LSEQ):
                x, h, tx, th = xin[i % NB], hb[i % NB], t_x[i % NB], t_h[i % NB]
                j = 2 if t0 < LCTX else b
                P.add('sp', lambda e, x=x, b=b, kc=kc, t0=t0, n=n: e.dma_start(
                    out=x[:, :n], in_=xsrc[b, kc * 128:(kc + 1) * 128, t0:t0 + n]), W=[tx], dma=True)
                if t0 < LCTX < t0 + n:
                    for (a, z, jj) in ((0, LCTX - t0, 2), (LCTX - t0, n, b)):
                        P.add('act', lambda e, x=x, h=h, a=a, z=z, jj=jj, kc=kc: e.activation(
                            out=h[:, a:z], in_=x[:, a:z], func=AF.Identity,
                            scale=mod[:, l, sc * 16 + kc, jj:jj + 1], bias=mod[:, l, sh * 16 + kc, jj:jj + 1]),
                            R=[tx, C.t_mod], W=[th])
                else:
                    P.add('act', lambda e, x=x, h=h, n=n, j=j, kc=kc: e.activation(
                        out=h[:, :n], in_=x[:, :n], func=AF.Identity,
                        scale=mod[:, l, sc * 16 + kc, j:j + 1], bias=mod[:, l, sh * 16 + kc, j:j + 1]),
                        R=[tx, C.t_mod], W=[th])
                P.add('act', lambda e, h=h, b=b, kc=kc, t0=t0, n=n: e.dma_start(
                    out=HT[b, kc * 128:(kc + 1) * 128, t0:t0 + n], in_=h[:, :n]), R=[th], W=[C.t_HT], dma=True)
                i += 1


def load_ht(C, HT, b, ht, t_ht):
    for q in range(4):
        C.P.add('sp', lambda e, q=q: e.dma_start(
            out=ht[:, 4 * q:4 * q + 4, :], in_=HT[b, 512 * q:512 * (q + 1), :].rearrange("(kc p) t -> p kc t", p=128)),
            R=[C.t_HT], W=[t_ht], dma=True)


def phase_ffn_up(C, l, HT, HID, skip_ctx=False):
    P = C.P
    ht = C.sb([128, KC, LSEQ], BF16); t_ht = Tl()
    dw = C.sb([128, FC, 4]); t_dw = Tl()
    P.add('sp', lambda e: e.dma_start(out=dw[:], in_=C.dram['ffn_dw_t'][l]), W=[t_dw], dma=True)
    NW = 2
    wa = [C.sb([128, KC, 128], BF16) for _ in range(NW)]; wg = [C.sb([128, KC, 128], BF16) for _ in range(NW)]
    t_wa = [Tl() for _ in range(NW)]; t_wg = [Tl() for _ in range(NW)]
    NA = 2
    a_sb = [C.sb([128, LLAT + 2]) for _ in range(NA)]; g_sb = [C.sb([128, LLAT]) for _ in range(NA)]
    acc = [C.sb([128, LLAT]) for _ in range(NA)]; hid = [C.sb([128, LLAT], BF16) for _ in range(NA)]
    t_a = [Tl() for _ in range(NA)]; t_g = [Tl() for _ in range(NA)]; t_acc = [Tl() for _ in range(NA)]
    t_hid = [Tl() for _ in range(NA)]
    for k in range(NA):
        P.add('pool', lambda e, k=k: e.memset(a_sb[k][:, 0:1], 0.0), W=[t_a[k]])
    ib = 0; ia = 0; iw = 0
    for b in range(C.nb):
        load_ht(C, HT, b, ht, t_ht)
        for j in range(FC):
            w_a, w_g, twa, twg = wa[iw % NW], wg[iw % NW], t_wa[iw % NW], t_wg[iw % NW]; iw += 1
            P.add('pool', lambda e, w=w_a, j=j: e.dma_start(out=w[:], in_=C.dram['w_up_t'][l, j], max_dma_last_dim=4096),
                  W=[twa], dma=True)
            P.add('pool', lambda e, w=w_g, j=j: e.dma_start(out=w[:], in_=C.dram['w_up_t'][l, FC + j], max_dma_last_dim=4096),
                  W=[twg], dma=True)
            for (s0, L) in (SEQS[1:] if skip_ctx else SEQS):
                k = ia % NA; ia += 1
                A, G, AC, HD = a_sb[k], g_sb[k], acc[k], hid[k]
                P.add('pool', lambda e, A=A, L=L: e.memset(A[:, L + 1:L + 2], 0.0), W=[t_a[k]])
                for (t0, n) in seq_tiles(L):
                    pa, pg = C.bank[ib % 8], C.bank[(ib + 1) % 8]
                    tpa, tpg = C.t_bank[ib % 8], C.t_bank[(ib + 1) % 8]; ib += 2
                    for kc in range(KC):
                        P.add('pe', lambda e, pa=pa, w=w_a, kc=kc, s0=s0, t0=t0, n=n: e.matmul(
                            pa[:, :n], w[:, kc, :], ht[:, kc, s0 + t0:s0 + t0 + n], start=(kc == 0), stop=(kc == KC - 1)),
                            R=[twa, t_ht], W=[tpa])
                    for kc in range(KC):
                        P.add('pe', lambda e, pg=pg, w=w_g, kc=kc, s0=s0, t0=t0, n=n: e.matmul(
                            pg[:, :n], w[:, kc, :], ht[:, kc, s0 + t0:s0 + t0 + n], start=(kc == 0), stop=(kc == KC - 1)),
                            R=[twg, t_ht], W=[tpg])
                    P.add('act', lambda e, A=A, pa=pa, t0=t0, n=n: e.copy(out=A[:, 1 + t0:1 + t0 + n], in_=pa[:, :n]),
                          R=[tpa], W=[t_a[k]])
                    P.add('dve', lambda e, G=G, pg=pg, t0=t0, n=n: e.tensor_copy(out=G[:, t0:t0 + n], in_=pg[:, :n]),
                          R=[tpg], W=[t_g[k]])
                P.add('dve', lambda e, A=A, AC=AC, L=L, j=j: e.tensor_scalar(
                    out=AC[:, :L], in0=A[:, 1:L + 1], scalar1=dw[:, j, 1:2], scalar2=dw[:, j, 3:4],
                    op0=ALU.mult, op1=ALU.add), R=[t_a[k], t_dw], W=[t_acc[k]])
                P.add('dve', lambda e, A=A, AC=AC, L=L, j=j: e.scalar_tensor_tensor(
                    out=AC[:, :L], in0=A[:, 0:L], scalar=dw[:, j, 0:1], in1=AC[:, :L],
                    op0=ALU.mult, op1=ALU.add), R=[t_a[k], t_dw, t_acc[k]], W=[t_acc[k]])
                P.add('dve', lambda e, A=A, AC=AC, L=L, j=j: e.scalar_tensor_tensor(
                    out=AC[:, :L], in0=A[:, 2:L + 2], scalar=dw[:, j, 2:3], in1=AC[:, :L],
                    op0=ALU.mult, op1=ALU.add), R=[t_a[k], t_dw, t_acc[k]], W=[t_acc[k]])
                P.add('act', lambda e, AC=AC, L=L: e.activation(out=AC[:, :L], in_=AC[:, :L], func=AF.Gelu),
                      R=[t_acc[k]], W=[t_acc[k]])
                P.add('dve', lambda e, AC=AC, G=G, HD=HD, L=L: e.tensor_tensor(
                    out=HD[:, :L], in0=AC[:, :L], in1=G[:, :L], op=ALU.mult),
                    R=[t_acc[k], t_g[k]], W=[t_hid[k]])
                P.add('sp', lambda e, HD=HD, b=b, j=j, s0=s0, L=L: e.dma_start(
                    out=HID[b, j * 128:(j + 1) * 128, s0:s0 + L], in_=HD[:, :L]),
                    R=[t_hid[k]], W=[C.t_HID], dma=True)


def phase_proj_ln(C, l, SRC, t_src, kcn, wname, widx, mgate, lnidx, xsrc, xdst, t_xdst, HT, hmod, lat_only_out=False, skip_ctx=False, w16=None, t_w16=None):
    P, mod = C.P, C.mod
    NACT = 2
    acts = [C.sb([128, kcn, 512], BF16) for _ in range(NACT)]; t_acts = [Tl() for _ in range(NACT)]
    iact = 0
    resident = kcn <= KC
    NW = KC if resident else 3
    w = [C.sb([128, kcn, 128], BF16) for _ in range(NW)]; t_w = [Tl() for _ in range(NW)]
    if resident:
        for o_ in range(KC):
            P.add('pool', lambda e, o_=o_: e.dma_start(out=w[o_][:], in_=C.dram[wname][widx, o_], max_dma_last_dim=4096),
                  W=[t_w[o_]], dma=True)
    r = C.sb([128, KC, 512]); t_r = [Tl() for _ in range(KC)]
    NX = 3
    xo = [C.sb([128, 512]) for _ in range(NX)]; t_xo = [Tl() for _ in range(NX)]
    sq = [C.sb([128, 512]) for _ in range(2)]; t_sq = [Tl() for _ in range(2)]
    mean = C.sb([128, 512]); rstd = C.sb([128, 512]); t_mean = Tl(); t_rstd = Tl()
    xn = [C.sb([128, 512]) for _ in range(NX)]; t_xn = [Tl() for _ in range(NX)]
    hb = [C.sb([128, 512], BF16) for _ in range(NX)]; t_hb = [Tl() for _ in range(NX)]
    ones = C.sb([128, 128]); t_ones = Tl()
    lnp = C.sb([128, KC, 2]); t_lnp = Tl()
    P.add('pool', lambda e: e.memset(ones[:], 1.0 / D), W=[t_ones])
    P.add('sp', lambda e: e.dma_start(out=lnp[:], in_=C.dram['ln_t'][l, lnidx]), W=[t_lnp], dma=True)
    iw = 0; ix = 0; ib = 0; isq = 0
    pS1, pS2, tS1, tS2 = C.bank[6], C.bank[7], C.t_bank[6], C.t_bank[7]
    for b in range(C.nb):
        for (s0, L) in SEQS:
            j = 2 if s0 < LCTX else b
            if (lat_only_out or skip_ctx) and s0 < LCTX:
                continue
            for (t0, n) in seq_tiles(L):
                T0 = s0 + t0
                act, t_act = acts[iact % NACT], t_acts[iact % NACT]; iact += 1
                nq = 4 if kcn >= 16 else 1
                step = (kcn + nq - 1) // nq
                for q in range(0, kcn, step):
                    z = min(kcn, q + step)
                    P.add('sp', lambda e, q=q, z=z, T0=T0, n=n, b=b, act=act: e.dma_start(
                        out=act[:, q:z, :n], in_=SRC[b, q * 128:z * 128, T0:T0 + n].rearrange("(kc p) t -> p kc t", p=128)),
                        R=[t_src], W=[t_act], dma=True)
                for o in range(KC):
                    wt, tw = w[iw % NW], t_w[iw % NW]; iw += 1
                    if not resident:
                        if w16 is None or iact == 1:
                            P.add('pool', lambda e, wt=wt, o=o: e.dma_start(
                                out=wt[:], in_=C.dram[wname][widx, o], max_dma_last_dim=4096), W=[tw], dma=True)
                            if w16 is not None:
                                P.add('pool', lambda e, wt=wt, o=o: e.dma_start(out=w16[o], in_=wt[:]), R=[tw], W=[t_w16[o]], dma=True)
                        else:
                            P.add('pool', lambda e, wt=wt, o=o: e.dma_start(out=wt[:], in_=w16[o]), R=[t_w16[o]], W=[tw], dma=True)
                    pb, tpb = C.bank[ib % 6], C.t_bank[ib % 6]; ib += 1
                    for kc in range(kcn):
                        P.add('pe', lambda e, pb=pb, wt=wt, kc=kc, n=n, act=act: e.matmul(
                            pb[:, :n], wt[:, kc, :], act[:, kc, :n], start=(kc == 0), stop=(kc == kcn - 1)),
                            R=[tw, t_act], W=[tpb])
                    x, tx = xo[ix % NX], t_xo[ix % NX]; ix += 1
                    P.add('sp', lambda e, x=x, b=b, o=o, T0=T0, n=n: e.dma_start(
                        out=x[:, :n], in_=xsrc[b, o * 128:(o + 1) * 128, T0:T0 + n]), W=[tx], dma=True)
                    P.add('act', lambda e, x=x, n=n: e.activation(out=x[:, :n], in_=x[:, :n], func=AF.Identity, scale=float(ALPHA)),
                          R=[tx], W=[tx])
                    P.add('dve', lambda e, pb=pb, x=x, o=o, n=n, j=j: e.scalar_tensor_tensor(
                        out=r[:, o, :n], in0=pb[:, :n], scalar=mod[:, l, mgate * 16 + o, j:j + 1], in1=x[:, :n],
                        op0=ALU.mult, op1=ALU.add), R=[tpb, tx, C.t_mod], W=[t_r[o]])
                    s_, ts = sq[isq % 2], t_sq[isq % 2]; isq += 1
                    P.add('act', lambda e, s_=s_, o=o, n=n: e.activation(out=s_[:, :n], in_=r[:, o, :n], func=AF.Square),
                          R=[t_r[o]], W=[ts])
                    P.add('pe', lambda e, o=o, n=n: e.matmul(pS1[:, :n], ones[:], r[:, o, :n], start=(o == 0), stop=(o == KC - 1)),
                          R=[t_ones, t_r[o]], W=[tS1])
                    P.add('pe', lambda e, s_=s_, o=o, n=n: e.matmul(pS2[:, :n], ones[:], s_[:, :n], start=(o == 0), stop=(o == KC - 1)),
                          R=[t_ones, ts], W=[tS2])
                P.add('act', lambda e, n=n: e.copy(out=mean[:, :n], in_=pS1[:, :n]), R=[tS1], W=[t_mean])
                P.add('dve', lambda e, n=n: e.tensor_tensor(out=rstd[:, :n], in0=mean[:, :n], in1=mean[:, :n], op=ALU.mult),
                      R=[t_mean], W=[t_rstd])
                P.add('dve', lambda e, n=n: e.tensor_tensor(out=rstd[:, :n], in0=pS2[:, :n], in1=rstd[:, :n], op=ALU.subtract),
                      R=[tS2, t_rstd], W=[t_rstd])
                P.add('dve', lambda e, n=n: e.tensor_scalar(out=rstd[:, :n], in0=rstd[:, :n], scalar1=float(EPS), scalar2=None,
                                                            op0=ALU.add), R=[t_rstd], W=[t_rstd])
                P.add('act', lambda e, n=n: e.activation(out=rstd[:, :n], in_=rstd[:, :n], func=AF.Sqrt), R=[t_rstd], W=[t_rstd])
                P.add('dve', lambda e, n=n: e.reciprocal(out=rstd[:, :n], in_=rstd[:, :n]), R=[t_rstd], W=[t_rstd])
                for o in range(KC):
                    k = ix % NX; ix += 1
                    X, tX, H, tH = xn[k], t_xn[k], hb[k], t_hb[k]
                    P.add('dve', lambda e, X=X, o=o, n=n: e.tensor_tensor(out=X[:, :n], in0=r[:, o, :n], in1=mean[:, :n], op=ALU.subtract),
                          R=[t_r[o], t_mean], W=[tX])
                    P.add('pool', lambda e, X=X, n=n: e.tensor_tensor(out=X[:, :n], in0=X[:, :n], in1=rstd[:, :n], op=ALU.mult),
                          R=[tX, t_rstd], W=[tX])
                    P.add('act', lambda e, X=X, o=o, n=n: e.activation(out=X[:, :n], in_=X[:, :n], func=AF.Identity,
                                                                       scale=lnp[:, o, 0:1], bias=lnp[:, o, 1:2]),
                          R=[tX, t_lnp], W=[tX])
                    if lat_only_out:
                        P.add('sp', lambda e, X=X, b=b, o=o, t0=t0, n=n: e.dma_start(
                            out=xdst[b, o * 128:(o + 1) * 128, t0:t0 + n], in_=X[:, :n]), R=[tX], W=[t_xdst], dma=True)
                    else:
                        P.add('sp', lambda e, X=X, b=b, o=o, T0=T0, n=n: e.dma_start(
                            out=xdst[b, o * 128:(o + 1) * 128, T0:T0 + n], in_=X[:, :n]), R=[tX], W=[t_xdst], dma=True)
                    if hmod is not None:
                        hl, hsh, hsc = hmod
                        P.add('act', lambda e, X=X, H=H, o=o, n=n, j=j: e.activation(
                            out=H[:, :n], in_=X[:, :n], func=AF.Identity,
                            scale=mod[:, hl, hsc * 16 + o, j:j + 1], bias=mod[:, hl, hsh * 16 + o, j:j + 1]),
                            R=[tX, C.t_mod], W=[tH])
                        P.add('act', lambda e, H=H, b=b, o=o, T0=T0, n=n: e.dma_start(
                            out=HT[b, o * 128:(o + 1) * 128, T0:T0 + n], in_=H[:, :n]), R=[tH], W=[C.t_HT], dma=True)


NH_C = 16
DH = 128
QSCALE = DH ** -0.5
NBT = 12


def na_plan(qt):
    br = 2 * qt
    if br <= 2:
        rows = [0, 2, 4, 6]
        return rows, (rows[0] - br + 6) // 2
    if br >= 28:
        rows = [24, 26, 28, 30]
        return rows, (rows[0] - br + 6) // 2
    return [br - 4, br - 2, br, br + 2, br + 4], 7


def phase_qkv(C, l, HT, need_ctx):
    P = C.P
    o = l // 2
    QT, KT, V = C.dram['QT'], C.dram['KT'], C.dram['V']
    ht = C.sb([128, KC, LSEQ], BF16); t_ht = Tl()
    rope = C.sb([128, 2, LLAT]); t_rope = Tl()
    pmT = C.sb([128, 128]); t_pm = Tl()
    P.add('sp', lambda e: e.dma_start(out=rope[:], in_=C.dram['rope_t'][:]), W=[t_rope], dma=True)
    P.add('sp', lambda e: e.dma_start(out=pmT[:], in_=C.dram['pmT'][:]), W=[t_pm], dma=True)
    NW = 2
    w = [C.sb([128, KC, 128], BF16) for _ in range(NW)]; t_w = [Tl() for _ in range(NW)]
    wv = [C.sb([128, KC, 512], BF16) for _ in range(NW)]; t_wv = [Tl() for _ in range(NW)]
    NX = 3
    xs = [C.sb([128, 512]) for _ in range(NX)]; t_xs = [Tl() for _ in range(NX)]
    t1 = [C.sb([128, 512]) for _ in range(NX)]; t_t1 = [Tl() for _ in range(NX)]
    ob = [C.sb([128, 512], BF16) for _ in range(NX)]; t_ob = [Tl() for _ in range(NX)]
    iw = 0; ix = 0; ib = 0
    for b in range(C.nb):
        load_ht(C, HT, b, ht, t_ht)
        for j in range(32):
            isq = j < 16
            dst = QT if isq else KT
            wt, tw = w[iw % NW], t_w[iw % NW]; iw += 1
            P.add('pool', lambda e, wt=wt, j=j: e.dma_start(out=wt[:], in_=C.dram['w_inc_t'][o, j], max_dma_last_dim=4096),
                  W=[tw], dma=True)
            for (s0, L) in SEQS:
                isctx = s0 < LCTX
                if isctx and isq and not need_ctx:
                    continue
                for (t0, n) in seq_tiles(L):
                    pb, tpb = C.bank[ib % 4], C.t_bank[ib % 4]
                    pr, tpr = C.bank[4 + ib % 4], C.t_bank[4 + ib % 4]; ib += 1
                    for kc in range(KC):
                        P.add('pe', lambda e, pb=pb, wt=wt, kc=kc, s0=s0, t0=t0, n=n: e.matmul(
                            pb[:, :n], wt[:, kc, :], ht[:, kc, s0 + t0:s0 + t0 + n], start=(kc == 0), stop=(kc == KC - 1)),
                            R=[tw, t_ht], W=[tpb])
                    k = ix % NX; ix += 1
                    X, tX, T1, tT1, O, tO = xs[k], t_xs[k], t1[k], t_t1[k], ob[k], t_ob[k]
                    sc = float(QSCALE) if isq else 1.0
                    if isctx:
                        P.add('act', lambda e, O=O, pb=pb, n=n, sc=sc: e.activation(out=O[:, :n], in_=pb[:, :n], func=AF.Identity, scale=sc),
                              R=[tpb], W=[tO])
                    else:
                        P.add('act', lambda e, X=X, pb=pb, n=n, sc=sc: e.activation(out=X[:, :n], in_=pb[:, :n], func=AF.Identity, scale=sc),
                              R=[tpb], W=[tX])
                        P.add('pe', lambda e, pr=pr, X=X, n=n: e.matmul(pr[:, :n], pmT[:], X[:, :n], start=True, stop=True),
                              R=[t_pm, tX], W=[tpr])
                        P.add('pool', lambda e, X=X, T1=T1, t0=t0, n=n: e.tensor_tensor(
                            out=T1[:, :n], in0=X[:, :n], in1=rope[:, 0, t0:t0 + n], op=ALU.mult), R=[tX, t_rope], W=[tT1])
                        P.add('dve', lambda e, X=X, pr=pr, t0=t0, n=n: e.tensor_tensor(
                            out=X[:, :n], in0=pr[:, :n], in1=rope[:, 1, t0:t0 + n], op=ALU.mult), R=[tpr, t_rope, tX], W=[tX])
                        P.add('dve', lambda e, X=X, T1=T1, O=O, n=n: e.tensor_tensor(
                            out=O[:, :n], in0=X[:, :n], in1=T1[:, :n], op=ALU.add), R=[tX, tT1], W=[tO])
                    hh = j % 16
                    P.add('sp', lambda e, O=O, dst=dst, b=b, hh=hh, s0=s0, t0=t0, n=n: e.dma_start(
                        out=dst[b, hh * 128:(hh + 1) * 128, s0 + t0:s0 + t0 + n], in_=O[:, :n]), R=[tO], W=[C.t_QK], dma=True)
        for cg in range(4):
            wt, tw = wv[iw % NW], t_wv[iw % NW]; iw += 1
            for q in range(0, KC, 2):
                P.add('pool', lambda e, wt=wt, cg=cg, q=q: e.dma_start(
                    out=wt[:, q:q + 2, :], in_=C.dram['w_v_t'][o, cg, :, q:q + 2, :], max_dma_last_dim=4096), W=[tw], dma=True)
            for tt in range(LSEQ // 128):
                pb, tpb = C.bank[ib % 8], C.t_bank[ib % 8]; ib += 1
                for kc in range(KC):
                    P.add('pe', lambda e, pb=pb, wt=wt, kc=kc, tt=tt: e.matmul(
                        pb[:, :], ht[:, kc, tt * 128:(tt + 1) * 128], wt[:, kc, :], start=(kc == 0), stop=(kc == KC - 1)),
                        R=[tw, t_ht], W=[tpb])
                k = ix % NX; ix += 1
                O, tO = ob[k], t_ob[k]
                eng = 'act' if tt % 2 == 0 else 'dve'
                if eng == 'act':
                    P.add('act', lambda e, O=O, pb=pb: e.copy(out=O[:, :], in_=pb[:, :]), R=[tpb], W=[tO])
                else:
                    P.add('dve', lambda e, O=O, pb=pb: e.tensor_copy(out=O[:, :], in_=pb[:, :]), R=[tpb], W=[tO])
                P.add('sp', lambda e, O=O, b=b, tt=tt, cg=cg: e.dma_start(
                    out=V[b, tt * 128:(tt + 1) * 128, cg * 512:(cg + 1) * 512], in_=O[:, :]), R=[tO], W=[C.t_QK], dma=True)


def phase_attn(C, l, need_ctx):
    P = C.P
    o = l // 2
    QT, KT, V, YT = C.dram['QT'], C.dram['KT'], C.dram['V'], C.dram['YT']
    NT = LSEQ // 128
    NB2 = 2
    kt = [C.sb([128, LSEQ], BF16) for _ in range(NB2)]; qt_ = [C.sb([128, LSEQ], BF16) for _ in range(NB2)]
    vt = [C.sb([128, NT, 128], BF16) for _ in range(NB2)]; yt = [C.sb([128, LSEQ], BF16) for _ in range(NB2)]
    bt = [C.sb([128, NBT, 128]) for _ in range(NB2)]
    t_kt = [Tl() for _ in range(NB2)]; t_qt = [Tl() for _ in range(NB2)]; t_vt = [Tl() for _ in range(NB2)]
    t_yt = [Tl() for _ in range(NB2)]; t_bt = [Tl() for _ in range(NB2)]
    msk = C.sb([128, NBT, 128]); t_msk = Tl()
    ones = C.sb([128, 128], BF16); t_ones = Tl()
    P.add('sp', lambda e: e.dma_start(out=msk[:], in_=C.dram['na_mask'][:]), W=[t_msk], dma=True)
    P.add('pool', lambda e: e.memset(ones[:], 1.0), W=[t_ones])
    NS = 2
    sc = [C.sb([128, 5, 128]) for _ in range(NS)]; t_sc = [Tl() for _ in range(NS)]
    pT = [C.sb([128, 7, 128], BF16) for _ in range(NS)]; t_pT = [Tl() for _ in range(NS)]
    rec = [C.sb([128, 256]) for _ in range(NS)]; t_rec = [Tl() for _ in range(NS)]
    ih = 0; iq = 0
    for b in range(C.nb):
        for h in range(NH_C):
            k = ih % NB2; ih += 1
            K_, Q_, V_, Y_, B_ = kt[k], qt_[k], vt[k], yt[k], bt[k]
            P.add('sp', lambda e, K_=K_, b=b, h=h: e.dma_start(out=K_[:], in_=KT[b, h * 128:(h + 1) * 128, :]),
                  R=[C.t_QK], W=[t_kt[k]], dma=True)
            P.add('sp', lambda e, Q_=Q_, b=b, h=h: e.dma_start(out=Q_[:], in_=QT[b, h * 128:(h + 1) * 128, :]),
                  R=[C.t_QK], W=[t_qt[k]], dma=True)
            P.add('act', lambda e, V_=V_, b=b, h=h: e.dma_start(
                out=V_[:], in_=V[b, :, h * 128:(h + 1) * 128].rearrange("(tt p) d -> p tt d", p=128)),
                R=[C.t_QK], W=[t_vt[k]], dma=True)
            P.add('act', lambda e, B_=B_, h=h: e.dma_start(out=B_[:], in_=C.dram['na_bias_t'][o, h]), W=[t_bt[k]], dma=True)
            P.add('pool', lambda e, B_=B_: e.tensor_tensor(out=B_[:], in0=B_[:], in1=msk[:], op=ALU.add),
                  R=[t_bt[k], t_msk], W=[t_bt[k]])
            def stage1(qi, s):
                    rows, b0 = na_plan(qi)
                    nl = len(rows)
                    X, Y, Z = C.bank[3 * s], C.bank[3 * s + 1], C.bank[3 * s + 2]
                    tX, tY, tZ = C.t_bank[3 * s], C.t_bank[3 * s + 1], C.t_bank[3 * s + 2]
                    q0 = LCTX + qi * 128
                    for i, a in enumerate(rows):
                        dstp, tdst, col = (X, tX, i * 128) if i < 4 else (Y, tY, 0)
                        k0 = LCTX + a * 64
                        P.add('pe', lambda e, dstp=dstp, col=col, K_=K_, Q_=Q_, k0=k0, q0=q0: e.matmul(
                            dstp[:, col:col + 128], K_[:, k0:k0 + 128], Q_[:, q0:q0 + 128], start=True, stop=True),
                            R=[t_kt[k], t_qt[k]], W=[tdst])
                    for i in range(2):
                        P.add('pe', lambda e, Y=Y, i=i, K_=K_, Q_=Q_, q0=q0: e.matmul(
                            Y[:, 128 + i * 128:256 + i * 128], K_[:, i * 128:(i + 1) * 128], Q_[:, q0:q0 + 128], start=True, stop=True),
                            R=[t_kt[k], t_qt[k]], W=[tY])
                    S_, tS, PT, tPT, RC, tRC = sc[s], t_sc[s], pT[s], t_pT[s], rec[s], t_rec[s]
                    P.add('dve', lambda e, S_=S_, X=X, B_=B_, b0=b0: e.tensor_tensor(
                        out=S_[:, 0:4, :], in0=X[:, 0:512].rearrange("p (a c) -> p a c", c=128), in1=B_[:, b0:b0 + 4, :], op=ALU.add), R=[tX, t_bt[k]], W=[tS])
                    if nl == 5:
                        P.add('dve', lambda e, S_=S_, Y=Y, B_=B_, b0=b0: e.tensor_tensor(
                            out=S_[:, 4, :], in0=Y[:, 0:128], in1=B_[:, b0 + 4, :], op=ALU.add), R=[t_bt[k]], W=[tS, tY])
                    P.add('act', lambda e, PT=PT, S_=S_, nl=nl: e.activation(out=PT[:, 0:nl, :], in_=S_[:, 0:nl, :], func=AF.Exp),
                          R=[tS], W=[tPT])
                    P.add('act', lambda e, PT=PT, Y=Y, nl=nl: e.activation(out=PT[:, nl:nl + 2, :], in_=Y[:, 128:384].rearrange("p (a c) -> p a c", c=128), func=AF.Exp),
                          R=[], W=[tPT, tY])
                    return (rows, nl, Z, tZ, PT, tPT, RC, tRC, q0)

            def stage2(st):
                    rows, nl, Z, tZ, PT, tPT, RC, tRC, q0 = st
                    vidx = [2 + a // 2 for a in rows] + [0, 1]
                    for i, vi in enumerate(vidx):
                        P.add('pe', lambda e, Z=Z, V_=V_, PT=PT, i=i, vi=vi, nn=len(vidx): e.matmul(
                            Z[:, 0:128], V_[:, vi, :], PT[:, i, :], start=(i == 0), stop=(i == nn - 1)),
                            R=[t_vt[k], tPT], W=[tZ])
                    for i in range(len(vidx)):
                        P.add('pe', lambda e, Z=Z, PT=PT, i=i, nn=len(vidx): e.matmul(
                            Z[:, 128:256], ones[:], PT[:, i, :], start=(i == 0), stop=(i == nn - 1)),
                            R=[t_ones, tPT], W=[tZ])
                    P.add('dve', lambda e, RC=RC, Z=Z: e.reciprocal(out=RC[:, 0:128], in_=Z[:, 128:256]), R=[tZ], W=[tRC])
                    P.add('dve', lambda e, Y_=Y_, Z=Z, RC=RC, q0=q0: e.tensor_tensor(
                        out=Y_[:, q0:q0 + 128], in0=Z[:, 0:128], in1=RC[:, 0:128], op=ALU.mult), R=[tZ, tRC], W=[t_yt[k]])

            pend = None
            for qi in range(LLAT // 128):
                s_ = iq % NS; iq += 1
                cur = stage1(qi, s_)
                if pend is not None:
                    stage2(pend)
                pend = cur
            stage2(pend)
            if need_ctx:
                s = iq % NS; iq += 1
                X, Z = C.bank[3 * s], C.bank[3 * s + 2]
                tX, tZ = C.t_bank[3 * s], C.t_bank[3 * s + 2]
                PT, tPT, RC, tRC = pT[s], t_pT[s], rec[s], t_rec[s]
                for i in range(2):
                    P.add('pe', lambda e, X=X, i=i, K_=K_, Q_=Q_: e.matmul(
                        X[:, i * 256:(i + 1) * 256], K_[:, i * 128:(i + 1) * 128], Q_[:, 0:256], start=True, stop=True),
                        R=[t_kt[k], t_qt[k]], W=[tX])
                P.add('act', lambda e, PT=PT, X=X: e.activation(out=PT[:, 0:4, :], in_=X[:, 0:512].rearrange("p (a c) -> p a c", c=128), func=AF.Exp), R=[tX], W=[tPT])
                for i in range(2):
                    P.add('pe', lambda e, Z=Z, V_=V_, PT=PT, i=i: e.matmul(
                        Z[:, 0:256], V_[:, i, :], PT[:, 2 * i:2 * i + 2, :], start=(i == 0), stop=(i == 1)),
                        R=[t_vt[k], tPT], W=[tZ])
                for i in range(2):
                    P.add('pe', lambda e, Z=Z, PT=PT, i=i: e.matmul(
                        Z[:, 256:512], ones[:], PT[:, 2 * i:2 * i + 2, :], start=(i == 0), stop=(i == 1)),
                        R=[t_ones, tPT], W=[tZ])
                P.add('dve', lambda e, RC=RC, Z=Z: e.reciprocal(out=RC[:, 0:256], in_=Z[:, 256:512]), R=[tZ], W=[tRC])
                P.add('dve', lambda e, Y_=Y_, Z=Z, RC=RC: e.tensor_tensor(
                    out=Y_[:, 0:256], in0=Z[:, 0:256], in1=RC[:, 0:256], op=ALU.mult), R=[tZ, tRC], W=[t_yt[k]])
            c0 = 0 if need_ctx else LCTX
            P.add('sp', lambda e, Y_=Y_, b=b, h=h, c0=c0: e.dma_start(
                out=YT[b, h * 128:(h + 1) * 128, c0:LSEQ], in_=Y_[:, c0:LSEQ]), R=[t_yt[k]], W=[C.t_YT], dma=True)


NHA = 8


def phase_inab(C, l, HT):
    P = C.P
    e = l // 2
    Dm = C.dram
    ht = C.sb([128, KC, LSEQ], BF16); t_ht = Tl()
    gconv = C.sb([128, 24, 5]); t_gc = Tl()
    lbr = C.sb([128, 2, 2, NHA]); lb = C.sb([128, 2, NHA]); oml = C.sb([128, 2, NHA]); t_lb = Tl()
    ident = C.sb([128, 128]); t_id = Tl()
    ones = C.sb([128, 128]); t_ones = Tl()
    P.add('sp', lambda e_: e_.dma_start(out=gconv[:], in_=Dm['gconv_t'][e]), W=[t_gc], dma=True)
    P.add('sp', lambda e_: e_.dma_start(out=ident[:], in_=Dm['ident'][:]), W=[t_id], dma=True)
    P.add('pool', lambda e_: e_.memset(ones[:], 1.0), W=[t_ones])
    if e == 0:
        P.add('pool', lambda e_: e_.memset(lb[:], 0.0), W=[t_lb])
        P.add('pool', lambda e_: e_.memset(oml[:], 1.0), W=[t_lb])
    else:
        P.add('sp', lambda e_: e_.dma_start(out=lbr[:], in_=Dm['lb_t'][:]), W=[t_lb], dma=True)
        P.add('dve', lambda e_: e_.tensor_tensor(out=lb[:], in0=lbr[:, 1], in1=lbr[:, 0], op=ALU.subtract), R=[t_lb], W=[t_lb])
        P.add('act', lambda e_: e_.activation(out=lb[:], in_=lb[:], func=AF.Sigmoid), R=[t_lb], W=[t_lb])
        P.add('dve', lambda e_: e_.tensor_scalar(out=oml[:], in0=lb[:], scalar1=-1.0, scalar2=1.0, op0=ALU.mult, op1=ALU.add),
              R=[t_lb], W=[t_lb])
    NW = 2
    w = [C.sb([128, KC, 128], BF16) for _ in range(NW)]; t_w = [Tl() for _ in range(NW)]
    wv = [C.sb([128, KC, 512], BF16) for _ in range(NW)]; t_wv = [Tl() for _ in range(NW)]
    wg = C.sb([128, KC, 32], BF16); t_wg = Tl()
    NR = 2
    xr = [C.sb([128, LLAT + 4]) for _ in range(NR)]; yr = [C.sb([128, LLAT]) for _ in range(NR)]
    obf = [C.sb([128, LLAT], BF16) for _ in range(NR)]
    tst = [C.sb([128, LLAT // 128, 128], BF16) for _ in range(NR)]
    t_xr = [Tl() for _ in range(NR)]; t_yr = [Tl() for _ in range(NR)]; t_ob = [Tl() for _ in range(NR)]; t_ts = [Tl() for _ in range(NR)]
    vst = [C.sb([128, 512], BF16) for _ in range(3)]; t_vst = [Tl() for _ in range(3)]
    gst = C.sb([128, LSEQ // 128, 32]); t_gst = Tl()
    for k in range(NR):
        P.add('pool', lambda e_, k=k: e_.memset(xr[k][:, 0:2], 0.0), W=[t_xr[k]])
    iw = 0; ir = 0; ib = 0; iv = 0
    for b in range(C.nb):
        load_ht(C, HT, b, ht, t_ht)
        for j in range(72):
            grp, hd = j // 8, j % 8
            if grp == 3:
                continue
            wt, tw = w[iw % NW], t_w[iw % NW]; iw += 1
            P.add('pool', lambda e_, wt=wt, j=j: e_.dma_start(out=wt[:], in_=Dm['w_inab_t'][e, j], max_dma_last_dim=4096),
                  W=[tw], dma=True)
            kind = {0: 'silu', 1: 'f', 2: 'f', 4: 'silu', 5: 'conv', 6: 'conv', 7: 'conv', 8: 'silu'}[grp]
            for (s0, L) in SEQS:
                k = ir % NR; ir += 1
                XR, YR, OB, TS = xr[k], yr[k], obf[k], tst[k]
                tXR, tYR, tOB, tTS = t_xr[k], t_yr[k], t_ob[k], t_ts[k]
                pad = 2 if kind == 'conv' else 0
                if kind == 'conv':
                    P.add('pool', lambda e_, XR=XR, L=L: e_.memset(XR[:, L + 2:L + 4], 0.0), W=[tXR])
                    if L != LLAT:
                        P.add('pool', lambda e_, XR=XR: e_.memset(XR[:, 0:2], 0.0), W=[tXR])
                for (t0, n) in seq_tiles(L):
                    pb, tpb = C.bank[ib % 6], C.t_bank[ib % 6]; ib += 1
                    for kc in range(KC):
                        P.add('pe', lambda e_, pb=pb, wt=wt, kc=kc, s0=s0, t0=t0, n=n: e_.matmul(
                            pb[:, :n], wt[:, kc, :], ht[:, kc, s0 + t0:s0 + t0 + n], start=(kc == 0), stop=(kc == KC - 1)),
                            R=[tw, t_ht], W=[tpb])
                    fn = {'silu': AF.Silu, 'f': AF.Sigmoid, 'conv': AF.Identity}[kind]
                    P.add('act', lambda e_, XR=XR, pb=pb, t0=t0, n=n, fn=fn, pad=pad: e_.activation(
                        out=XR[:, pad + t0:pad + t0 + n], in_=pb[:, :n], func=fn), R=[tpb], W=[tXR])
                if kind == 'silu':
                    dst = {0: 'QA', 4: 'GA', 8: 'ZB'}[grp]
                    P.add('sp', lambda e_, XR=XR, dst=dst, b=b, hd=hd, s0=s0, L=L: e_.dma_start(
                        out=Dm[dst][b, hd * 128:(hd + 1) * 128, s0:s0 + L], in_=XR[:, :L]), R=[tXR], W=[C.t_AB], dma=True)
                elif kind == 'f':
                    d = grp - 1
                    P.add('dve', lambda e_, XR=XR, L=L, d=d, hd=hd: e_.tensor_scalar(
                        out=XR[:, :L], in0=XR[:, :L], scalar1=oml[:, d, hd:hd + 1], scalar2=lb[:, d, hd:hd + 1],
                        op0=ALU.mult, op1=ALU.add), R=[tXR, t_lb], W=[tXR])
                    P.add('dve', lambda e_, XR=XR, YR=YR, L=L: e_.tensor_scalar(
                        out=YR[:, :L], in0=XR[:, :L], scalar1=-1.0, scalar2=1.0, op0=ALU.mult, op1=ALU.add), R=[tXR], W=[tYR])
                    P.add('sp', lambda e_, YR=YR, b=b, d=d, hd=hd, s0=s0, L=L: e_.dma_start(
                        out=Dm['KF'][b, d, hd * 128:(hd + 1) * 128, s0:s0 + L], in_=YR[:, :L]), R=[tYR], W=[C.t_AB], dma=True)
                    P.add('act', lambda e_, XR=XR, L=L: e_.activation(out=XR[:, :L], in_=XR[:, :L], func=AF.Ln), R=[tXR, tYR], W=[tXR])
                    P.add('sp', lambda e_, XR=XR, b=b, d=d, hd=hd, s0=s0, L=L: e_.dma_start(
                        out=Dm['LF'][b, d, hd * 128:(hd + 1) * 128, s0:s0 + L], in_=XR[:, :L]), R=[tXR], W=[C.t_AB], dma=True)
                else:
                    ci = j - 40
                    P.add('dve', lambda e_, XR=XR, YR=YR, L=L, ci=ci: e_.tensor_scalar(
                        out=YR[:, :L], in0=XR[:, 2:L + 2], scalar1=gconv[:, ci, 2:3], scalar2=None, op0=ALU.mult),
                        R=[tXR, t_gc], W=[tYR])
                    for tap in (0, 1, 3, 4):
                        P.add('dve', lambda e_, XR=XR, YR=YR, L=L, ci=ci, tap=tap: e_.scalar_tensor_tensor(
                            out=YR[:, :L], in0=XR[:, tap:tap + L], scalar=gconv[:, ci, tap:tap + 1], in1=YR[:, :L],
                            op0=ALU.mult, op1=ALU.add), R=[tXR, t_gc, tYR], W=[tYR])
                    P.add('act', lambda e_, YR=YR, L=L: e_.activation(out=YR[:, :L], in_=YR[:, :L], func=AF.Silu), R=[tYR], W=[tYR])
                    if grp in (5, 6):
                        P.add('act', lambda e_, XR=XR, YR=YR, L=L: e_.activation(out=XR[:, :L], in_=YR[:, :L], func=AF.Square),
                              R=[tYR], W=[tXR])
                        for (t0, n) in seq_tiles(L):
                            pb, tpb = C.bank[6 + ib % 2], C.t_bank[6 + ib % 2]; ib += 1
                            P.add('pe', lambda e_, pb=pb, XR=XR, t0=t0, n=n: e_.matmul(pb[:, :n], ones[:], XR[:, t0:t0 + n], start=True, stop=True),
                                  R=[t_ones, tXR], W=[tpb])
                            P.add('dve', lambda e_, pb=pb, XR=XR, t0=t0, n=n: e_.tensor_scalar(
                                out=XR[:, t0:t0 + n], in0=pb[:, :n], scalar1=float(EPS), scalar2=None, op0=ALU.add), R=[tpb], W=[tXR, tpb])
                        P.add('act', lambda e_, XR=XR, L=L: e_.activation(out=XR[:, :L], in_=XR[:, :L], func=AF.Sqrt), R=[tXR], W=[tXR])
                        P.add('dve', lambda e_, XR=XR, L=L: e_.reciprocal(out=XR[:, :L], in_=XR[:, :L]), R=[tXR], W=[tXR])
                        qs = float(QSCALE) if grp == 5 else 1.0
                        P.add('dve', lambda e_, XR=XR, YR=YR, L=L, qs=qs: e_.scalar_tensor_tensor(
                            out=YR[:, :L], in0=YR[:, :L], scalar=qs, in1=XR[:, :L], op0=ALU.mult, op1=ALU.mult),
                            R=[tXR, tYR], W=[tYR])
                        P.add('pool', lambda e_, OB=OB, YR=YR, L=L: e_.tensor_copy(out=OB[:, :L], in_=YR[:, :L]), R=[tYR], W=[tOB])
                        dst = 'QB' if grp == 5 else 'KB'
                        P.add('sp', lambda e_, OB=OB, dst=dst, b=b, hd=hd, s0=s0, L=L: e_.dma_start(
                            out=Dm[dst][b, hd * 128:(hd + 1) * 128, s0:s0 + L], in_=OB[:, :L]), R=[tOB], W=[C.t_AB], dma=True)
                    if grp in (6, 7):
                        for tt in range(L // 128):
                            pb, tpb = C.bank[6 + ib % 2], C.t_bank[6 + ib % 2]; ib += 1
                            P.add('pe', lambda e_, pb=pb, YR=YR, tt=tt: e_.transpose(pb[:, 0:128], YR[:, tt * 128:(tt + 1) * 128], ident[:]),
                                  R=[tYR, t_id], W=[tpb])
                            if tt % 2 == 0:
                                P.add('act', lambda e_, TS=TS, pb=pb, tt=tt: e_.copy(out=TS[:, tt, :], in_=pb[:, 0:128]), R=[], W=[tTS, tpb])
                            else:
                                P.add('dve', lambda e_, TS=TS, pb=pb, tt=tt: e_.tensor_copy(out=TS[:, tt, :], in_=pb[:, 0:128]), R=[], W=[tTS, tpb])
                        dst = 'KBt' if grp == 6 else 'VBt'
                        P.add('sp', lambda e_, TS=TS, dst=dst, b=b, hd=hd, s0=s0, L=L: e_.dma_start(
                            out=Dm[dst][b, s0:s0 + L, hd * 128:(hd + 1) * 128].rearrange("(tt p) d -> p tt d", p=128),
                            in_=TS[:, 0:L // 128, :]), R=[tTS], W=[C.t_AB], dma=True)
        for cg in range(2):
            wt, tw = wv[iw % NW], t_wv[iw % NW]; iw += 1
            for q in range(0, KC, 2):
                P.add('pool', lambda e_, wt=wt, cg=cg, q=q: e_.dma_start(
                    out=wt[:, q:q + 2, :], in_=Dm['w_ia_t'][e, cg, :, q:q + 2, :], max_dma_last_dim=4096), W=[tw], dma=True)
            for tt in range(LSEQ // 128):
                pb, tpb = C.bank[ib % 6], C.t_bank[ib % 6]; ib += 1
                for kc in range(KC):
                    P.add('pe', lambda e_, pb=pb, wt=wt, kc=kc, tt=tt: e_.matmul(
                        pb[:, :], ht[:, kc, tt * 128:(tt + 1) * 128], wt[:, kc, :], start=(kc == 0), stop=(kc == KC - 1)),
                        R=[tw, t_ht], W=[tpb])
                O, tO = vst[iv % 3], t_vst[iv % 3]; iv += 1
                if tt % 2 == 0:
                    P.add('act', lambda e_, O=O, pb=pb: e_.copy(out=O[:, :], in_=pb[:, :]), R=[tpb], W=[tO])
                else:
                    P.add('dve', lambda e_, O=O, pb=pb: e_.tensor_copy(out=O[:, :], in_=pb[:, :]), R=[tpb], W=[tO])
                P.add('sp', lambda e_, O=O, b=b, tt=tt, cg=cg: e_.dma_start(
                    out=Dm['VA'][b, tt * 128:(tt + 1) * 128, cg * 512:(cg + 1) * 512], in_=O[:, :]), R=[tO], W=[C.t_AB], dma=True)
        P.add('pool', lambda e_: e_.dma_start(out=wg[:], in_=Dm['w_gb_t'][e], max_dma_last_dim=4096), W=[t_wg], dma=True)
        for tt in range(LSEQ // 128):
            pb, tpb = C.bank[ib % 6], C.t_bank[ib % 6]; ib += 1
            for kc in range(KC):
                P.add('pe', lambda e_, pb=pb, kc=kc, tt=tt: e_.matmul(
                    pb[:, 0:32], ht[:, kc, tt * 128:(tt + 1) * 128], wg[:, kc, :], start=(kc == 0), stop=(kc == KC - 1)),
                    R=[t_wg, t_ht], W=[tpb])
            P.add('dve', lambda e_, pb=pb, tt=tt: e_.tensor_copy(out=gst[:, tt, :], in_=pb[:, 0:32]), R=[tpb], W=[t_gst])
        P.add('sp', lambda e_, b=b: e_.dma_start(
            out=Dm['GBt'][b].rearrange("(tt p) c -> p tt c", p=128), in_=gst[:]), R=[t_gst], W=[C.t_AB], dma=True)


NCH = LSEQ // 128


def interleave(gens):
    gens = list(gens)
    while gens:
        for g in list(gens):
            try:
                next(g)
            except StopIteration:
                gens.remove(g)


def chunk_order(d):
    ctx, lat = [0, 1], list(range(2, NCH))
    return (ctx + lat) if d == 0 else (ctx[::-1] + lat[::-1])


def merge_head(C, OA, t_oa, gate, t_gate, normw, t_nw, col, ybf, t_ybf, onesm, t_onesm, tmp, t_tmp, bank, t_bank, dst_row, b):
    P = C.P
    for (t0, n) in seq_tiles(LSEQ):
        P.add('act', lambda e, t0=t0, n=n: e.activation(out=tmp[:, :n], in_=OA[:, t0:t0 + n], func=AF.Square), R=[t_oa], W=[t_tmp])
        P.add('pe', lambda e, n=n: e.matmul(bank[:, :n], onesm[:], tmp[:, :n], start=True, stop=True), R=[t_onesm, t_tmp], W=[t_bank])
        P.add('dve', lambda e, n=n: e.tensor_scalar(out=tmp[:, :n], in0=bank[:, :n], scalar1=float(EPS), scalar2=None, op0=ALU.add),
              R=[], W=[t_tmp, t_bank])
        P.add('act', lambda e, n=n: e.activation(out=tmp[:, :n], in_=tmp[:, :n], func=AF.Sqrt), R=[t_tmp], W=[t_tmp])
        P.add('dve', lambda e, n=n: e.reciprocal(out=tmp[:, :n], in_=tmp[:, :n]), R=[t_tmp], W=[t_tmp])
        P.add('dve', lambda e, t0=t0, n=n: e.scalar_tensor_tensor(
            out=tmp[:, :n], in0=OA[:, t0:t0 + n], scalar=normw[:, col:col + 1], in1=tmp[:, :n], op0=ALU.mult, op1=ALU.mult),
            R=[t_oa, t_nw, t_tmp], W=[t_tmp])
        P.add('pool', lambda e, t0=t0, n=n: e.tensor_tensor(out=ybf[:, t0:t0 + n], in0=tmp[:, :n], in1=gate[:, t0:t0 + n], op=ALU.mult),
              R=[t_tmp, t_gate], W=[t_ybf])
    P.add('sp', lambda e: e.dma_start(out=C.dram['YT'][b, dst_row:dst_row + 128, :], in_=ybf[:]), R=[t_ybf], W=[C.t_YT], dma=True)


def phase_gla(C, l):
    CH = 64; NCA = LSEQ // CH; NCTX = LCTX // CH
    P = C.P
    e = l // 2
    Dm = C.dram
    row = lambda: C.sb([128, LSEQ])
    q = row(); lf = [row(), row()]; kf = [row(), row()]; pp = [row(), row()]; OA = row(); ga = row(); rst = row()
    ybf = C.sb([128, LSEQ], BF16); va = C.sb([CH, NCA, 128], BF16)
    t_q = Tl(); t_lf = [Tl(), Tl()]; t_kf = [Tl(), Tl()]; t_pp = [Tl(), Tl()]; t_oa = Tl(); t_ga = Tl(); t_rst = Tl()
    t_ybf = Tl(); t_va = Tl()
    nmid = [C.sb([128, NCA]) for _ in range(2)]; t_nmid = [Tl(), Tl()]
    edec = [C.sb([128, NCA]) for _ in range(2)]; t_edec = [Tl(), Tl()]
    msk = C.sb([128, 2, 128], U32); t_msk = Tl()
    ident = C.sb([128, 128]); t_id = Tl()
    onesm = C.sb([128, 128]); t_onesm = Tl()
    anorm = C.sb([128, 2]); t_an = Tl()
    tmp = C.sb([128, 512]); t_tmp = Tl()
    S = [C.sb([128, 128]) for _ in range(2)]; Sb = [C.sb([128, 128], BF16) for _ in range(2)]
    t_S = [Tl(), Tl()]; t_Sb = [Tl(), Tl()]
    NU = 2
    E = [[[C.sb([128, CH]) for _ in range(4)] for _ in range(NU)] for _ in range(2)]
    t_E = [[[Tl() for _ in range(4)] for _ in range(NU)] for _ in range(2)]
    qe = [[C.sb([128, CH], BF16) for _ in range(NU)] for _ in range(2)]; ke = [[C.sb([128, CH], BF16) for _ in range(NU)] for _ in range(2)]
    qg = [[C.sb([128, CH], BF16) for _ in range(NU)] for _ in range(2)]; kd = [[C.sb([128, CH]) for _ in range(NU)] for _ in range(2)]
    atm = [[C.sb([CH, CH], BF16) for _ in range(NU)] for _ in range(2)]; kdt = [[C.sb([CH, 128], BF16) for _ in range(NU)] for _ in range(2)]
    mk = lambda: [[Tl() for _ in range(NU)] for _ in range(2)]
    t_qe, t_ke, t_qg, t_kd, t_atm, t_kdt = mk(), mk(), mk(), mk(), mk(), mk()
    P.add('sp', lambda e_: e_.dma_start(out=msk[:], in_=Dm['tri_u'][:]), W=[t_msk], dma=True)
    for d_ in range(2):
        for u_ in range(NU):
            P.add('pool', lambda e_, d_=d_, u_=u_: e_.memset(atm[d_][u_][:], 0.0), W=[t_atm[d_][u_]])
    P.add('sp', lambda e_: e_.dma_start(out=ident[:], in_=Dm['ident'][:]), W=[t_id], dma=True)
    P.add('sp', lambda e_: e_.dma_start(out=anorm[:], in_=Dm['anorm_t'][e]), W=[t_an], dma=True)
    P.add('pool', lambda e_: e_.memset(onesm[:], 1.0 / 128), W=[t_onesm])
    P.add('pool', lambda e_: e_.memset(rst[:], 1.0), W=[t_rst])
    rst3 = rst[:].rearrange("p (c t) -> p c t", t=CH)
    P.add('pool', lambda e_: e_.memset(rst3[:, :, 0:1], 0.0), W=[t_rst])
    for b in range(C.nb):
        for hd in range(NHA):
            r0 = hd * 128
            P.add('sp', lambda e_, b=b, r0=r0: e_.dma_start(out=q[:], in_=Dm['QA'][b, r0:r0 + 128, :]), R=[C.t_AB], W=[t_q], dma=True)
            P.add('sp', lambda e_, b=b, r0=r0: e_.dma_start(out=ga[:], in_=Dm['GA'][b, r0:r0 + 128, :]), R=[C.t_AB], W=[t_ga], dma=True)
            P.add('act', lambda e_, b=b, r0=r0: e_.dma_start(
                out=va[:], in_=Dm['VA'][b, :, r0:r0 + 128].rearrange("(tt p) d -> p tt d", p=CH)), R=[C.t_AB], W=[t_va], dma=True)
            for d in range(2):
                P.add('sp', lambda e_, b=b, d=d, r0=r0: e_.dma_start(out=lf[d][:], in_=Dm['LF'][b, d, r0:r0 + 128, :]),
                      R=[C.t_AB], W=[t_lf[d]], dma=True)
                P.add('act', lambda e_, b=b, d=d, r0=r0: e_.dma_start(out=kf[d][:], in_=Dm['KF'][b, d, r0:r0 + 128, :]),
                      R=[C.t_AB], W=[t_kf[d]], dma=True)
                P.add('dve', lambda e_, d=d: e_.tensor_tensor_scan(out=pp[d][:], data0=rst[:], data1=lf[d][:], initial=0.0,
                                                                   op0=ALU.mult, op1=ALU.add), R=[t_rst, t_lf[d]], W=[t_pp[d]])
                pp3 = pp[d][:].rearrange("p (c t) -> p c t", t=CH)
                P.add('act', lambda e_, d=d, pp3=pp3: e_.activation(out=edec[d][:], in_=pp3[:, :, CH - 1], func=AF.Exp),
                      R=[t_pp[d]], W=[t_edec[d]])
            P.add('dve', lambda e_: e_.tensor_tensor(out=lf[1][:], in0=lf[1][:], in1=pp[1][:], op=ALU.subtract),
                  R=[t_pp[1], t_lf[1]], W=[t_lf[1]])
            B0 = [pp[0], lf[1]]; t_B0 = [t_pp[0], t_lf[1]]
            for d in range(2):
                b3 = B0[d][:].rearrange("p (c t) -> p c t", t=CH)
                mc = CH // 2 - 1 if d == 0 else CH // 2
                P.add('dve', lambda e_, d=d, b3=b3, mc=mc: e_.tensor_scalar(out=nmid[d][:], in0=b3[:, :, mc], scalar1=-1.0, scalar2=None,
                                                                            op0=ALU.mult), R=[t_B0[d]], W=[t_nmid[d]])
                P.add('pool', lambda e_, d=d: e_.memset(S[d][:], 0.0), W=[t_S[d]])
                P.add('pool', lambda e_, d=d: e_.memset(Sb[d][:], 0.0), W=[t_Sb[d]])
            first_written = [False] * NCA

            def chain(d):
                for ui, c in enumerate((list(range(NCTX)) + list(range(NCTX, NCA))) if d == 0 else
                                       (list(range(NCTX))[::-1] + list(range(NCTX, NCA))[::-1])):
                    yield from unit(d, ui, c)

            def unit(d, ui, c):
                if True:
                    u = ui % NU
                    cs = slice(c * CH, (c + 1) * CH)
                    mc = c * CH + (CH // 2 - 1 if d == 0 else CH // 2)
                    lc = c * CH + (CH - 1 if d == 0 else 0)
                    bA, tA = C.bank[4 * d + 2 * u], C.t_bank[4 * d + 2 * u]
                    bB, tB = C.bank[4 * d + 2 * u + 1], C.t_bank[4 * d + 2 * u + 1]
                    E1, E2, E3, E4 = E[d][u]; tE1, tE2, tE3, tE4 = t_E[d][u]
                    Bd, tBd = B0[d], t_B0[d]
                    P.add('act', lambda e_: e_.activation(out=E1[:], in_=Bd[:, cs], func=AF.Exp, bias=nmid[d][:, c:c + 1]),
                          R=[tBd, t_nmid[d]], W=[tE1])
                    P.add('act', lambda e_: e_.activation(out=E2[:], in_=Bd[:, cs], func=AF.Exp, scale=-1.0, bias=Bd[:, mc:mc + 1]),
                          R=[tBd], W=[tE2])
                    if d == 0:
                        P.add('act', lambda e_: e_.activation(out=E3[:], in_=Bd[:, cs], func=AF.Exp), R=[tBd], W=[tE3])
                    else:
                        P.add('act', lambda e_: e_.activation(out=E3[:], in_=Bd[:, cs], func=AF.Exp, bias=pp[1][:, c * CH + CH - 1:c * CH + CH]),
                              R=[tBd, t_pp[1]], W=[tE3])
                    P.add('act', lambda e_: e_.activation(out=E4[:], in_=Bd[:, cs], func=AF.Exp, scale=-1.0, bias=Bd[:, lc:lc + 1]),
                          R=[tBd], W=[tE4])
                    yield
                    QE, KE, QG, KD, ATM, KDT = qe[d][u], ke[d][u], qg[d][u], kd[d][u], atm[d][u], kdt[d][u]
                    P.add('pool', lambda e_: e_.tensor_tensor(out=QE[:], in0=q[:, cs], in1=E1[:], op=ALU.mult), R=[t_q, tE1], W=[t_qe[d][u]])
                    P.add('dve', lambda e_: e_.tensor_tensor(out=KE[:], in0=kf[d][:, cs], in1=E2[:], op=ALU.mult), R=[t_kf[d], tE2], W=[t_ke[d][u]])
                    P.add('pool', lambda e_: e_.tensor_tensor(out=QG[:], in0=q[:, cs], in1=E3[:], op=ALU.mult), R=[t_q, tE3], W=[t_qg[d][u]])
                    P.add('dve', lambda e_: e_.tensor_tensor(out=KD[:], in0=kf[d][:, cs], in1=E4[:], op=ALU.mult), R=[t_kf[d], tE4], W=[t_kd[d][u]])
                    yield
                    P.add('pe', lambda e_: e_.matmul(bA[0:CH, 0:CH], KE[:], QE[:], start=True, stop=True), R=[t_ke[d][u], t_qe[d][u]], W=[tA])
                    P.add('pe', lambda e_: e_.transpose(bA[0:CH, 128:256], KD[:], ident[:]), R=[t_kd[d][u], t_id], W=[tA])
                    yield
                    P.add('dve', lambda e_: e_.copy_predicated(ATM[:], msk[0:CH, d, 0:CH], bA[0:CH, 0:CH]),
                          R=[t_msk], W=[t_atm[d][u], tA])
                    P.add('act', lambda e_: e_.copy(out=KDT[:], in_=bA[0:CH, 128:256]), R=[], W=[t_kdt[d][u], tA])
                    yield
                    P.add('pe', lambda e_: e_.matmul(bB[:, 0:CH], Sb[d][:], QG[:], start=True, stop=False), R=[t_Sb[d], t_qg[d][u]], W=[tB])
                    P.add('pe', lambda e_: e_.matmul(bB[:, 0:CH], va[:, c, :], ATM[:], start=False, stop=True), R=[t_va, t_atm[d][u]], W=[tB])
                    P.add('pe', lambda e_: e_.matmul(bB[:, 128:256], KDT[:], va[:, c, :], start=True, stop=True), R=[t_va, t_kdt[d][u]], W=[tB])
                    yield
                    if not first_written[c]:
                        first_written[c] = True
                        P.add('act', lambda e_: e_.copy(out=OA[:, cs], in_=bB[:, 0:CH]), R=[], W=[t_oa, tB])
                    else:
                        P.add('dve', lambda e_: e_.tensor_tensor(out=OA[:, cs], in0=bB[:, 0:CH], in1=OA[:, cs], op=ALU.add), R=[], W=[t_oa, tB])
                    P.add('dve', lambda e_: e_.scalar_tensor_tensor(out=S[d][:], in0=S[d][:], scalar=edec[d][:, c:c + 1], in1=bB[:, 128:256],
                                                                    op0=ALU.mult, op1=ALU.add), R=[t_edec[d]], W=[t_S[d], tB])
                    P.add('pool', lambda e_: e_.tensor_copy(out=Sb[d][:], in_=S[d][:]), R=[t_S[d]], W=[t_Sb[d]])
                    yield

            interleave([chain(0), chain(1)])
            merge_head(C, OA, t_oa, ga, t_ga, anorm, t_an, 0, ybf, t_ybf, onesm, t_onesm, tmp, t_tmp, C.bank[0], C.t_bank[0], r0, b)


def phase_gdn(C, l):
    P = C.P
    e = l // 2
    Dm = C.dram
    CH = 128

    def T_(shape, dt=F32):
        return C.sb(shape, dt), Tl()

    kT, t_kT = T_([128, LSEQ], BF16); qT, t_qT = T_([128, LSEQ], BF16)
    ktm, t_ktm = T_([128, NCH, 128], BF16); vtm, t_vtm = T_([128, NCH, 128], BF16)
    zb, t_zb = T_([128, LSEQ]); OB, t_ob = T_([128, LSEQ]); ybf, t_ybf = T_([128, LSEQ], BF16)
    gbt, t_gbt = T_([128, NCH, 32]); dtb, t_dtb = T_([128, NCH, 16]); alog, t_alog = T_([128, NCH, 16])
    g, t_g = T_([128, NCH, 16]); beta, t_beta = T_([128, NCH, 16]); G, t_G = T_([128, NCH, 16])
    Gtot, t_Gtot = T_([128, NCH, 16]); eG, t_eG = T_([128, NCH, 16]); beG, t_beG = T_([128, NCH, 16])
    khs, t_khs = T_([128, NCH, 16]); sdec, t_sdec = T_([128, NCH, 16]); tmpg, t_tmpg = T_([128, NCH, 16])
    tri, t_tri = T_([128, 2, 128]); nstr, t_nstr = T_([128, 2, 128]); ident, t_id = T_([128, 128])
    onesf, t_onesf = T_([128, 128]); onesm, t_onesm = T_([128, 128]); anorm, t_an = T_([128, 2])
    tmp, t_tmp = T_([128, 512])
    S = [T_([128, 128]) for _ in range(2)]
    names = ('Dg', 'Ib', 'Dmx', 'eGr', 't2', 'NT', 'AT', 'N', 'NjA', 'NjB', 'NjTA', 'NjTB', 'TT', 'rhs1', 'rhs2', 'nwT', 'vnew', 'qg', 'khat', 'KK', 'QK')
    U = [{n: T_([128, 128]) for n in names} for _ in range(2)]
    P.add('sp', lambda e_: e_.dma_start(out=tri[:], in_=Dm['tri_t'][:]), W=[t_tri], dma=True)
    P.add('sp', lambda e_: e_.dma_start(out=ident[:], in_=Dm['ident'][:]), W=[t_id], dma=True)
    P.add('sp', lambda e_: e_.dma_start(out=anorm[:], in_=Dm['anorm_t'][e]), W=[t_an], dma=True)
    P.add('sp', lambda e_: e_.dma_start(out=dtb[:], in_=Dm['dtb_t'][e]), W=[t_dtb], dma=True)
    P.add('sp', lambda e_: e_.dma_start(out=alog[:], in_=Dm['alog_t'][e]), W=[t_alog], dma=True)
    P.add('pool', lambda e_: e_.memset(onesf[:], 1.0), W=[t_onesf])
    P.add('pool', lambda e_: e_.memset(onesm[:], 1.0 / 128), W=[t_onesm])
    for d_ in range(2):
        P.add('pool', lambda e_, d_=d_: e_.tensor_tensor(out=nstr[:, d_, :], in0=ident[:], in1=tri[:, d_, :], op=ALU.subtract),
              R=[t_id, t_tri], W=[t_nstr])
    P.add('act', lambda e_: e_.activation(out=alog[:], in_=alog[:], func=AF.Exp), R=[t_alog], W=[t_alog])
    bG, tbG = C.bank[6], C.t_bank[6]
    for b in range(C.nb):
        P.add('sp', lambda e_, b=b: e_.dma_start(out=gbt[:], in_=Dm['GBt'][b].rearrange("(tt p) c -> p tt c", p=128)),
              R=[C.t_AB], W=[t_gbt], dma=True)
        P.add('dve', lambda e_: e_.tensor_tensor(out=g[:], in0=gbt[:, :, 0:16], in1=dtb[:], op=ALU.add), R=[t_gbt, t_dtb], W=[t_g])
        P.add('act', lambda e_: e_.activation(out=g[:], in_=g[:], func=AF.Exp), R=[t_g], W=[t_g])
        P.add('dve', lambda e_: e_.tensor_scalar(out=g[:], in0=g[:], scalar1=1.0, scalar2=None, op0=ALU.add), R=[t_g], W=[t_g])
        P.add('act', lambda e_: e_.activation(out=g[:], in_=g[:], func=AF.Ln), R=[t_g], W=[t_g])
        P.add('dve', lambda e_: e_.scalar_tensor_tensor(out=g[:], in0=g[:], scalar=-1.0, in1=alog[:], op0=ALU.mult, op1=ALU.mult),
              R=[t_g, t_alog], W=[t_g])
        P.add('act', lambda e_: e_.activation(out=beta[:], in_=gbt[:, :, 16:32], func=AF.Sigmoid), R=[t_gbt], W=[t_beta])
        for tt in range(NCH):
            P.add('pe', lambda e_, tt=tt: e_.matmul(bG[:, 0:8], tri[:, 0, :], g[:, tt, 0:8], start=True, stop=True), R=[t_tri, t_g], W=[tbG])
            P.add('pe', lambda e_, tt=tt: e_.matmul(bG[:, 8:16], tri[:, 1, :], g[:, tt, 8:16], start=True, stop=True), R=[t_tri, t_g], W=[tbG])
            P.add('pe', lambda e_, tt=tt: e_.matmul(bG[:, 16:32], onesf[:], g[:, tt, :], start=True, stop=True), R=[t_onesf, t_g], W=[tbG])
            P.add('dve', lambda e_, tt=tt: e_.tensor_copy(out=G[:, tt, :], in_=bG[:, 0:16]), R=[], W=[t_G, tbG])
            P.add('dve', lambda e_, tt=tt: e_.tensor_copy(out=Gtot[:, tt, :], in_=bG[:, 16:32]), R=[], W=[t_Gtot, tbG])
        P.add('act', lambda e_: e_.activation(out=eG[:], in_=G[:], func=AF.Exp), R=[t_G], W=[t_eG])
        P.add('dve', lambda e_: e_.tensor_tensor(out=beG[:], in0=beta[:], in1=eG[:], op=ALU.mult), R=[t_beta, t_eG], W=[t_beG])
        P.add('dve', lambda e_: e_.tensor_tensor(out=tmpg[:], in0=Gtot[:], in1=G[:], op=ALU.subtract), R=[t_Gtot, t_G], W=[t_tmpg])
        P.add('act', lambda e_: e_.activation(out=khs[:], in_=tmpg[:], func=AF.Exp), R=[t_tmpg], W=[t_khs])
        P.add('act', lambda e_: e_.activation(out=sdec[:], in_=Gtot[:], func=AF.Exp), R=[t_Gtot], W=[t_sdec])
        for hd in range(NHA):
            r0 = hd * 128
            P.add('sp', lambda e_, b=b, r0=r0: e_.dma_start(out=kT[:], in_=Dm['KB'][b, r0:r0 + 128, :]), R=[C.t_AB], W=[t_kT], dma=True)
            P.add('sp', lambda e_, b=b, r0=r0: e_.dma_start(out=qT[:], in_=Dm['QB'][b, r0:r0 + 128, :]), R=[C.t_AB], W=[t_qT], dma=True)
            P.add('sp', lambda e_, b=b, r0=r0: e_.dma_start(out=zb[:], in_=Dm['ZB'][b, r0:r0 + 128, :]), R=[C.t_AB], W=[t_zb], dma=True)
            P.add('act', lambda e_, b=b, r0=r0: e_.dma_start(
                out=ktm[:], in_=Dm['KBt'][b, :, r0:r0 + 128].rearrange("(tt p) d -> p tt d", p=128)), R=[C.t_AB], W=[t_ktm], dma=True)
            P.add('act', lambda e_, b=b, r0=r0: e_.dma_start(
                out=vtm[:], in_=Dm['VBt'][b, :, r0:r0 + 128].rearrange("(tt p) d -> p tt d", p=128)), R=[C.t_AB], W=[t_vtm], dma=True)
            for d in range(2):
                P.add('pool', lambda e_, d=d: e_.memset(S[d][0][:], 0.0), W=[S[d][1]])
            first_written = [False] * NCH

            def unit(d, c):
                col = d * 8 + hd
                cs = slice(c * 128, (c + 1) * 128)
                u = U[d]
                b1, tb1 = C.bank[3 * d], C.t_bank[3 * d]
                b2, tb2 = C.bank[3 * d + 1], C.t_bank[3 * d + 1]
                b3, tb3 = C.bank[3 * d + 2], C.t_bank[3 * d + 2]
                Sd, tS = S[d]
                A = lambda n: u[n][0]
                t = lambda n: u[n][1]
                gc, bc = g[:, c, col:col + 1], beta[:, c, col:col + 1]
                P.add('pe', lambda e_: e_.matmul(b1[:, 256:384], kT[:, cs], kT[:, cs], start=True, stop=True), R=[t_kT], W=[tb1])
                P.add('pe', lambda e_: e_.matmul(b1[:, 384:512], kT[:, cs], qT[:, cs], start=True, stop=True), R=[t_kT, t_qT], W=[tb1])
                P.add('act', lambda e_: e_.copy(out=A('KK')[:], in_=b1[:, 256:384]), R=[], W=[t('KK'), tb1])
                P.add('act', lambda e_: e_.copy(out=A('QK')[:], in_=b1[:, 384:512]), R=[], W=[t('QK'), tb1])
                KK, tKK = u['KK']; QK, tQK = u['QK']
                yield
                P.add('dve', lambda e_: e_.tensor_scalar(out=A('Dg')[:], in0=tri[:, d, :], scalar1=gc, scalar2=None, op0=ALU.mult),
                      R=[t_tri, t_g], W=[t('Dg')])
                P.add('act', lambda e_: e_.activation(out=A('Ib')[:], in_=ident[:], func=AF.Identity, scale=bc),
                      R=[t_id, t_beta], W=[t('Ib')])
                P.add('pe', lambda e_: e_.matmul(b1[:, 0:128], onesf[:], A('Dg')[:], start=True, stop=True), R=[t_onesf, t('Dg')], W=[tb1])
                P.add('pe', lambda e_: e_.matmul(b1[:, 128:256], onesf[:], A('Ib')[:], start=True, stop=True), R=[t_onesf, t('Ib')], W=[tb1])
                yield
                P.add('dve', lambda e_: e_.tensor_scalar(out=A('Dmx')[:], in0=b1[:, 0:128], scalar1=G[:, c, col:col + 1], scalar2=0.0,
                                                         op0=ALU.subtract, op1=ALU.min), R=[t_G], W=[t('Dmx'), tb1])
                P.add('act', lambda e_: e_.activation(out=A('eGr')[:], in_=b1[:, 0:128], func=AF.Exp), R=[], W=[t('eGr'), tb1])
                P.add('act', lambda e_: e_.activation(out=A('Dmx')[:], in_=A('Dmx')[:], func=AF.Exp), R=[t('Dmx')], W=[t('Dmx')])
                P.add('dve', lambda e_: e_.tensor_tensor(out=A('t2')[:], in0=b1[:, 128:256], in1=A('Dmx')[:], op=ALU.mult),
                      R=[t('Dmx')], W=[t('t2'), tb1])
                yield
                P.add('pool', lambda e_: e_.tensor_tensor(out=A('t2')[:], in0=A('t2')[:], in1=KK[:], op=ALU.mult), R=[t('t2'), tKK], W=[t('t2')])
                P.add('pool', lambda e_: e_.tensor_tensor(out=A('NT')[:], in0=A('t2')[:], in1=nstr[:, d, :], op=ALU.mult),
                      R=[t('t2'), t_nstr], W=[t('NT')])
                P.add('dve', lambda e_: e_.tensor_tensor(out=A('AT')[:], in0=A('Dmx')[:], in1=tri[:, d, :], op=ALU.mult),
                      R=[t('Dmx'), t_tri], W=[t('AT')])
                P.add('dve', lambda e_: e_.tensor_tensor(out=A('AT')[:], in0=A('AT')[:], in1=QK[:], op=ALU.mult), R=[t('AT'), tQK], W=[t('AT')])
                yield
                P.add('pe', lambda e_: e_.transpose(b2[:, 0:128], A('NT')[:], ident[:]), R=[t('NT'), t_id], W=[tb2])
                yield
                P.add('act', lambda e_: e_.copy(out=A('N')[:], in_=b2[:, 0:128]), R=[], W=[t('N'), tb2])
                P.add('dve', lambda e_: e_.tensor_tensor(out=A('TT')[:], in0=A('NT')[:], in1=ident[:], op=ALU.add), R=[t('NT'), t_id], W=[t('TT')])
                yield
                Np, NTp = 'N', 'NT'
                for j in range(1, 7):
                    Nn, NTn = ('NjA', 'NjTA') if j % 2 else ('NjB', 'NjTB')
                    P.add('pe', lambda e_, Np=Np, NTp=NTp: e_.matmul(b2[:, 0:128], A(NTp)[:], A(Np)[:], start=True, stop=True),
                          R=[t(Np), t(NTp)], W=[tb2])
                    if j < 6:
                        P.add('pe', lambda e_, Np=Np, NTp=NTp: e_.matmul(b2[:, 128:256], A(Np)[:], A(NTp)[:], start=True, stop=True),
                              R=[t(Np), t(NTp)], W=[tb2])
                    yield
                    P.add('act', lambda e_, Nn=Nn: e_.copy(out=A(Nn)[:], in_=b2[:, 0:128]), R=[], W=[t(Nn), tb2])
                    if j < 6:
                        P.add('dve', lambda e_, NTn=NTn: e_.tensor_copy(out=A(NTn)[:], in_=b2[:, 128:256]), R=[], W=[t(NTn), tb2])
                    yield
                    P.add('pe', lambda e_, Nn=Nn: e_.matmul(b2[:, 256:384], A(Nn)[:], A('TT')[:], start=True, stop=True),
                          R=[t(Nn), t('TT')], W=[tb2])
                    yield
                    P.add('dve', lambda e_: e_.tensor_tensor(out=A('TT')[:], in0=b2[:, 256:384], in1=A('TT')[:], op=ALU.add),
                          R=[], W=[t('TT'), tb2])
                    yield
                    Np, NTp = Nn, NTn
                P.add('dve', lambda e_: e_.tensor_scalar(out=A('rhs1')[:], in0=vtm[:, c, :], scalar1=bc, scalar2=None, op0=ALU.mult),
                      R=[t_vtm, t_beta], W=[t('rhs1')])
                P.add('act', lambda e_: e_.activation(out=A('rhs2')[:], in_=ktm[:, c, :], func=AF.Identity, scale=beG[:, c, col:col + 1]),
                      R=[t_ktm, t_beG], W=[t('rhs2')])
                P.add('act', lambda e_: e_.activation(out=A('khat')[:], in_=ktm[:, c, :], func=AF.Identity, scale=khs[:, c, col:col + 1]),
                      R=[t_ktm, t_khs], W=[t('khat')])
                P.add('dve', lambda e_: e_.tensor_tensor(out=A('qg')[:], in0=qT[:, cs], in1=A('eGr')[:], op=ALU.mult), R=[t_qT, t('eGr')], W=[t('qg')])
                yield
                P.add('pe', lambda e_: e_.matmul(b3[:, 0:128], A('rhs2')[:], A('TT')[:], start=True, stop=True), R=[t('rhs2'), t('TT')], W=[tb3])
                yield
                P.add('act', lambda e_: e_.activation(out=A('nwT')[:], in_=b3[:, 0:128], func=AF.Identity, scale=-1.0), R=[], W=[t('nwT'), tb3])
                yield
                P.add('pe', lambda e_: e_.matmul(b3[:, 128:256], A('TT')[:], A('rhs1')[:], start=True, stop=False), R=[t('TT'), t('rhs1')], W=[tb3])
                P.add('pe', lambda e_: e_.matmul(b3[:, 128:256], A('nwT')[:], Sd[:], start=False, stop=True), R=[t('nwT'), tS], W=[tb3])
                yield
                P.add('act', lambda e_: e_.copy(out=A('vnew')[:], in_=b3[:, 128:256]), R=[], W=[t('vnew'), tb3])
                yield
                P.add('pe', lambda e_: e_.matmul(b3[:, 256:384], Sd[:], A('qg')[:], start=True, stop=False), R=[tS, t('qg')], W=[tb3])
                P.add('pe', lambda e_: e_.matmul(b3[:, 256:384], A('vnew')[:], A('AT')[:], start=False, stop=True), R=[t('vnew'), t('AT')], W=[tb3])
                P.add('pe', lambda e_: e_.matmul(b3[:, 384:512], A('khat')[:], A('vnew')[:], start=True, stop=True), R=[t('khat'), t('vnew')], W=[tb3])
                yield
                if not first_written[c]:
                    first_written[c] = True
                    P.add('act', lambda e_: e_.copy(out=OB[:, cs], in_=b3[:, 256:384]), R=[], W=[t_ob, tb3])
                else:
                    P.add('dve', lambda e_: e_.tensor_tensor(out=OB[:, cs], in0=b3[:, 256:384], in1=OB[:, cs], op=ALU.add), R=[], W=[t_ob, tb3])
                P.add('dve', lambda e_: e_.scalar_tensor_tensor(out=Sd[:], in0=Sd[:], scalar=sdec[:, c, col:col + 1], in1=b3[:, 384:512],
                                                                op0=ALU.mult, op1=ALU.add), R=[t_sdec], W=[tS, tb3])
                yield

            def chain(d):
                for c in chunk_order(d):
                    yield from unit(d, c)

            interleave([chain(0), chain(1)])
            merge_head(C, OB, t_ob, zb, t_zb, anorm, t_an, 1, ybf, t_ybf, onesm, t_onesm, tmp, t_tmp, C.bank[7], C.t_bank[7],
                       1024 + r0, b)


def build(cfg):
    nc = bass.Bass("TRN2", target_bir_lowering=False)
    C = Ctx(nc)
    C.nb = nb = cfg.get('nb', NBC)
    ext_in, ext_out = cfg.get('ext_in', ()), cfg.get('ext_out', ())

    def dt(name, shape, dtype=F32):
        if name in ext_in:
            return C.din(name, shape, dtype)
        if name in ext_out:
            return C.dout(name, shape, dtype)
        return C.dscr(name, shape, dtype)

    C.din('cT', [128, KC, 3])
    C.din('b_ada_t', [128, DEPTH, 96])
    C.din('w_ada_t', [DEPTH, 96, 128, KC, 128])
    C.din('ln_t', [DEPTH, 2, 128, KC, 2])
    C.din('ffn_dw_t', [DEPTH, 128, FC, 4])
    C.din('w_up_t', [DEPTH, 2 * FC, 128, KC, 128])
    C.din('w_down_t', [DEPTH, KC, 128, FC, 128])
    for nm in ('XA', 'XB', 'XIN'):
        dt(nm, [nb, D, LSEQ])
    dt('OUT', [nb, D, LLAT])
    dt('W16', [KC, 128, FC, 128], BF16)
    C.t_W16 = [Tl(f'W16_{i}') for i in range(KC)]
    dt('HT', [nb, D, LSEQ], BF16)
    dt('HID', [nb, DFF, LSEQ], BF16)
    dt('o_mod', [128, DEPTH, 96, 3])
    C.din('w_out_t', [DEPTH, KC, 128, KC, 128])
    C.din('w_inc_t', [2, 32, 128, KC, 128])
    C.din('w_v_t', [2, 4, 128, KC, 512])
    C.din('rope_t', [128, 2, LLAT])
    C.din('pmT', [128, 128])
    C.din('na_mask', [128, NBT, 128])
    C.din('na_bias_t', [2, NH_C, 128, NBT, 128])
    dt('QT', [nb, D, LSEQ], BF16); dt('KT', [nb, D, LSEQ], BF16); dt('V', [nb, LSEQ, D], BF16)
    dt('YT', [nb, D, LSEQ], BF16)
    C.t_QK, C.t_YT = Tl('QK'), Tl('YT')
    C.din('w_inab_t', [2, 72, 128, KC, 128])
    C.din('w_ia_t', [2, 2, 128, KC, 512])
    C.din('w_gb_t', [2, 128, KC, 32])
    C.din('gconv_t', [2, 128, 24, 5])
    C.din('lb_t', [128, 2, 2, NHA])
    C.din('ident', [128, 128])
    for nm in ('QA', 'GA', 'ZB'):
        dt(nm, [nb, 1024, LSEQ])
    dt('LF', [nb, 2, 1024, LSEQ]); dt('KF', [nb, 2, 1024, LSEQ])
    for nm in ('QB', 'KB'):
        dt(nm, [nb, 1024, LSEQ], BF16)
    for nm in ('VA', 'KBt', 'VBt'):
        dt(nm, [nb, LSEQ, 1024], BF16)
    dt('GBt', [nb, LSEQ, 32])
    C.t_AB = Tl('AB')
    C.din('tri_t', [128, 2, 128])
    C.din('tri_u', [128, 2, 128], U32)
    C.din('dtb_t', [2, 128, NCH, 16])
    C.din('alog_t', [2, 128, NCH, 16])
    C.din('anorm_t', [2, 128, 2])
    C.t_HT, C.t_HID = Tl('HT'), Tl('HID')
    C.t_X = {'XA': Tl('XA'), 'XB': Tl('XB'), 'XIN': Tl('XIN'), 'OUT': Tl('OUT')}
    with ExitStack() as top:
        C.stack = top
        C.mod = C.sb([128, DEPTH, 96, 3], name='mod')
        C.t_mod = Tl('mod')
        C.bank = [C.ps([128, 512], name=f'bank{i}') for i in range(8)]
        C.t_bank = [Tl(f'bank{i}') for i in range(8)]
        csem = {e: top.enter_context(nc.semaphore('c_' + e)) for e in ENGS}
        dsem = {e: [top.enter_context(nc.semaphore(f'd_{e}{i}')) for i in range(RING)] for e in ENGS}
        outs = []
        for ph in cfg['phases']:
            with ExitStack() as phs:
                C.stack = phs
                kind = ph[0]
                if kind == 'ada':
                    phase_ada(C, ph[1])
                elif kind == 'mod':
                    _, l, xs, sh, sc = ph
                    phase_mod0(C, l, C.dram[xs], C.dram['HT'], sh, sc)
                elif kind == 'ffn_up':
                    phase_ffn_up(C, ph[1], C.dram['HT'], C.dram['HID'], skip_ctx=(len(ph) > 2 and ph[2]))
                elif kind == 'ffn_down':
                    _, l, xs, xd, hmod = ph[:5]
                    final = len(ph) > 5 and ph[5]
                    phase_proj_ln(C, l, C.dram['HID'], C.t_HID, FC, 'w_down_t', l, 5, 1, C.dram[xs], C.dram[xd],
                                  C.t_X[xd], C.dram['HT'], hmod, lat_only_out=final, w16=C.dram['W16'], t_w16=C.t_W16)
                elif kind == 'inab':
                    phase_inab(C, ph[1], C.dram['HT'])
                elif kind == 'gdn':
                    phase_gdn(C, ph[1])
                elif kind == 'gla':
                    phase_gla(C, ph[1])
                elif kind == 'qkv':
                    phase_qkv(C, ph[1], C.dram['HT'], ph[2])
                elif kind == 'attn':
                    phase_attn(C, ph[1], ph[2])
                elif kind == 'out_proj':
                    _, l, xs, xd, hmod = ph[:5]
                    phase_proj_ln(C, l, C.dram['YT'], C.t_YT, KC, 'w_out_t', l, 2, 0, C.dram[xs], C.dram[xd],
                                  C.t_X[xd], C.dram['HT'], hmod, skip_ctx=(len(ph) > 5 and ph[5]))
                elif kind == 'dump_mod':
                    C.P.add('sp', lambda e: e.dma_start(out=C.dram['o_mod'][:], in_=C.mod[:]), R=[C.t_mod], W=[C.t_HT], dma=True)
                C.P.barrier()
            C.stack = top
        C.P.barrier()
        with nc.Block() as block:
            C.P.emit(block, csem, dsem)
    return nc


def tileW(w):
    Kd, N = w.shape
    return np.ascontiguousarray(w.reshape(Kd // 128, 128, N // 128, 128).transpose(2, 1, 0, 3))


def tileWwide(w, nw):
    Kd, N = w.shape
    return np.ascontiguousarray(w.reshape(Kd // 128, 128, N // nw, nw).transpose(2, 1, 0, 3))


def host_consts():
    c = {}
    pos = np.arange(LLAT)
    row = (pos // 64).astype(np.float32); col = (pos % 64).astype(np.float32)
    inv = (np.float32(10000.0) ** (-np.arange(32, dtype=np.float32) / np.float32(32))).astype(np.float32)
    rope = np.zeros((128, 2, LLAT), np.float32)
    for d in range(128):
        ang = ((row if d < 64 else col) * inv[d % 32]).astype(np.float32)
        rope[d, 0] = np.cos(ang); rope[d, 1] = np.sin(ang)
    c['rope_t'] = rope
    pm = np.zeros((128, 128), np.float32)
    for i in range(32):
        pm[i, 32 + i] = -1; pm[32 + i, i] = 1; pm[64 + i, 96 + i] = -1; pm[96 + i, 64 + i] = 1
    c['pmT'] = np.ascontiguousarray(pm.T)
    kk = np.arange(128); krl, kc = kk // 64, kk % 64
    qq = np.arange(128); qrl, qc = qq // 64, qq % 64
    c0 = np.clip(qc - 8, 0, 48)
    col_ok = (kc[:, None] >= c0[None, :]) & (kc[:, None] < c0[None, :] + 16)
    deltas = [-6, -4, -2, 0, 2, 4, 6] + [-4, -2, 0, 2, 4]
    mask = np.zeros((128, NBT, 128), np.float32)
    dr = np.zeros((NBT, 128, 128), np.int64)
    for t, dlt in enumerate(deltas):
        rel = dlt + krl[:, None] - qrl[None, :]
        ok = col_ok if t < 7 else (col_ok & (rel >= -4) & (rel < 4))
        mask[:, t, :] = np.where(ok, 0.0, -30000.0)
        dr[t] = np.clip(rel + 7, 0, 14)
    c['na_mask'] = mask
    c['_dr'] = dr
    c['_dc'] = np.clip(kc[:, None] - qc[None, :] + 15, 0, 30)
    return c


def host_bias_tiles(rel_bias, consts):
    dr, dc = consts['_dr'], consts['_dc']
    g = rel_bias[:, :, dr, dc[None]]
    return np.ascontiguousarray(g.transpose(0, 1, 3, 2, 4)).astype(np.float32)


def host_even_weights(inputs, e_list=(0, 1)):
    w = {}
    w['w_inab_t'] = np.zeros((2, 72, 128, KC, 128), np.float32)
    w['w_ia_t'] = np.zeros((2, 2, 128, KC, 512), np.float32)
    w['w_gb_t'] = np.zeros((2, 128, KC, 32), np.float32)
    for e in e_list:
        wi = inputs['w_in_ab'][e]
        w['w_inab_t'][e] = tileW(wi[:, :9216])
        w['w_ia_t'][e] = tileWwide(wi[:, 3072:4096], 512)
        w['w_gb_t'][e] = np.ascontiguousarray(wi[:, 9216:9248].reshape(KC, 128, 32).transpose(1, 0, 2))
    gc = inputs['gdn_conv']
    w['gconv_t'] = np.ascontiguousarray(gc.reshape(2, 5, 24, 128).transpose(0, 3, 2, 1)).astype(np.float32)
    lb = inputs['hgrn_lb']
    w['lb_t'] = np.ascontiguousarray(lb.reshape(2, 2, NHA, 128).transpose(3, 0, 1, 2)).astype(np.float32)
    w['ident'] = np.eye(128, dtype=np.float32)
    ii = np.arange(128)
    w['anorm_t'] = np.ascontiguousarray(np.stack([inputs['hgrn_norm'], inputs['gdn_norm']], -1)).astype(np.float32)
    w['tri_t'] = np.ascontiguousarray(np.stack([(ii[:, None] <= ii[None, :]), (ii[:, None] >= ii[None, :])], 1)).astype(np.float32)
    w['tri_u'] = np.ascontiguousarray(w['tri_t'].astype(np.uint32))
    dtb = inputs['gdn_dt_bias'].reshape(2, 16).astype(np.float32)
    alg = inputs['gdn_a_log'].reshape(2, 16).astype(np.float32)
    w['dtb_t'] = np.ascontiguousarray(np.broadcast_to(dtb[:, None, None, :], (2, 128, NCH, 16)))
    w['alog_t'] = np.ascontiguousarray(np.broadcast_to(alg[:, None, None, :], (2, 128, NCH, 16)))
    return w


def full_phases():
    ph = [('ada', list(range(DEPTH))), ('mod', 0, 'XIN', 0, 1)]
    src = 'XIN'
    for l in range(DEPTH):
        last = l == DEPTH - 1
        if l % 2 == 0:
            ph += [('inab', l), ('gla', l), ('gdn', l)]
        else:
            ph += [('qkv', l, not last), ('attn', l, not last)]
        ph += [('out_proj', l, src, 'XB', (l, 3, 4), last), ('ffn_up', l, last)]
        if last:
            ph += [('ffn_down', l, 'XB', 'OUT', None, True)]
        else:
            ph += [('ffn_down', l, 'XB', 'XA', (l + 1, 0, 1))]
        src = 'XA'
    return ph


def host_shared(inputs):
    f32 = np.float32
    m = {}
    m['w_ada_t'] = np.stack([tileW(inputs['w_ada'][l]) for l in range(DEPTH)])
    m['b_ada_t'] = np.ascontiguousarray(inputs['b_ada'].reshape(DEPTH, 96, 128).transpose(2, 0, 1)).astype(f32)
    m['ln_t'] = np.ascontiguousarray(np.stack([inputs['ln_g'], inputs['ln_b']], -1).reshape(DEPTH, 2, KC, 128, 2)
                                     .transpose(0, 1, 3, 2, 4)).astype(f32)
    dw = np.concatenate([inputs['ffn_w_dw'], inputs['ffn_b_dw'][:, None, :]], 1)
    m['ffn_dw_t'] = np.ascontiguousarray(dw.reshape(DEPTH, 4, FC, 128).transpose(0, 3, 2, 1)).astype(f32)
    m['w_up_t'] = np.stack([tileW(inputs['ffn_w_up'][l]) for l in range(DEPTH)])
    m['w_down_t'] = np.stack([tileW(inputs['ffn_w_down'][l]) for l in range(DEPTH)])
    m['w_out_t'] = np.stack([tileW(inputs['w_out_ab'][l // 2] if l % 2 == 0 else inputs['w_out_c'][l // 2]) for l in range(DEPTH)])
    m['w_inc_t'] = np.stack([tileW(inputs['w_in_c'][o][:, :2 * D]) for o in range(2)])
    m['w_v_t'] = np.stack([tileWwide(inputs['w_in_c'][o][:, 2 * D:], 512) for o in range(2)])
    hc = host_consts()
    for k in ('rope_t', 'pmT', 'na_mask'):
        m[k] = hc[k]
    m['na_bias_t'] = host_bias_tiles(inputs['na_rel_bias'], hc)
    m.update(host_even_weights(inputs))
    return m


def host_core(inputs, bs):
    xin = np.stack([np.concatenate([inputs['ctx'][b].T, inputs['x'][b].T], 1) for b in bs]).astype(np.float32)
    cols = [inputs['c'][b] for b in bs]
    while len(cols) < 2:
        cols.append(cols[-1])
    cc = np.stack(cols + [inputs['c_ctx']], -1)
    cT = np.ascontiguousarray(cc.reshape(KC, 128, 3).transpose(1, 0, 2)).astype(np.float32)
    return {'XIN': np.ascontiguousarray(xin), 'cT': cT}


def kernel(**inputs):
    inputs = {k: np.asarray(v) for k, v in inputs.items()}
    n = 8
    shared = host_shared(inputs)
    nc = build({'nb': NBC, 'ext_in': ('XIN',), 'ext_out': ('OUT',), 'phases': full_phases()})
    in_maps = []
    for core in range(n):
        m = dict(shared)
        m.update(host_core(inputs, [NBC * core + i for i in range(NBC)]))
        in_maps.append(m)
    res = run_bass_kernel_spmd(nc, in_maps, core_ids=list(range(n)))
    out = np.empty((n * NBC, LLAT, D), np.float32)
    for core in range(n):
        o = np.asarray(res.results[core]['OUT'])
        for i in range(NBC):
            out[NBC * core + i] = o[i].T
    return out
```
